# Optimizing a Trainium2 kernel written in Bass

```python
import math
import jax, jax.numpy as jnp
from jax import lax
import numpy as np

D_MODEL = 1024
BATCH = 16
SEQ = 4096
DEPTH = 4

CHUNK = 64
N_MIXERS = 4
Q_BLOCK = 128
NORM_EPS = 1e-6
ROPE_BASE = 10000.0
PLE_DIM = 256

SSD_D_INNER = 2 * D_MODEL
SSD_HEAD_DIM = 64
SSD_HEADS = SSD_D_INNER // SSD_HEAD_DIM
SSD_GROUPS = 4
SSD_HEADS_PER_GROUP = SSD_HEADS // SSD_GROUPS
SSD_STATE = 128
SSD_CONV = 4
SSD_CONV_DIM = SSD_D_INNER + 2 * SSD_GROUPS * SSD_STATE
SSD_IN_DIM = SSD_D_INNER + SSD_CONV_DIM + SSD_HEADS

RET_HEADS = 4
RET_QK_DIM = D_MODEL // RET_HEADS
RET_V_DIM = 2 * RET_QK_DIM
RET_V_WIDTH = RET_HEADS * RET_V_DIM
RET_IN_DIM = 2 * D_MODEL + 2 * RET_V_WIDTH

DIFF_HEADS = 8
DIFF_HEAD_DIM = D_MODEL // DIFF_HEADS // 2
DIFF_IN_DIM = 3 * D_MODEL

FOX_HEADS = 16
FOX_HEAD_DIM = D_MODEL // FOX_HEADS
FOX_IN_DIM = 3 * D_MODEL + FOX_HEADS

REL_BUCKETS = 32
REL_MAX_DIST = 128

PEER_KEYS = 128
PEER_EXPERTS = PEER_KEYS * PEER_KEYS
PEER_HEADS = 8
PEER_TOPK = 16
PEER_QUERY_DIM = 256
PEER_HALF = PEER_QUERY_DIM // 2
PEER_BLOCK = 128

kernel_name = 'hybrid_chunk_causal_ssd_ret_diff_fox_peer'


def rms_norm(x, g):
    xf = x.astype(jnp.float32)
    y = xf * lax.rsqrt(jnp.mean(xf * xf, axis=-1, keepdims=True) + NORM_EPS)
    return (y * g.astype(jnp.float32)).astype(x.dtype)


def to_chunks(t, L):
    B, S = t.shape[:2]
    return jnp.moveaxis(t.reshape(B, S // L, L, *t.shape[2:]), 1, 0)


def from_chunks(t):
    n, B, L = t.shape[:3]
    return jnp.moveaxis(t, 0, 1).reshape(B, n * L, *t.shape[3:])


def rotary(x):
    S, d = x.shape[1], x.shape[-1]
    inv = 1.0 / (ROPE_BASE ** (jnp.arange(0, d, 2, dtype=jnp.float32) / d))
    ang = jnp.arange(S, dtype=jnp.float32)[:, None] * inv[None, :]
    cos = jnp.cos(ang)[None, :, None, :]
    sin = jnp.sin(ang)[None, :, None, :]
    xf = x.astype(jnp.float32)
    x1, x2 = xf[..., : d // 2], xf[..., d // 2:]
    return jnp.concatenate([x1 * cos - x2 * sin, x1 * sin + x2 * cos], axis=-1).astype(x.dtype)


def t5_bucket(rel):
    nb = REL_BUCKETS // 2
    max_exact = nb // 2
    ret = (rel > 0).astype(jnp.int32) * nb
    n = jnp.abs(rel)
    nf = jnp.maximum(n, 1).astype(jnp.float32)
    large = max_exact + (jnp.log(nf / max_exact) / math.log(REL_MAX_DIST / max_exact)
                         * (nb - max_exact)).astype(jnp.int32)
    large = jnp.minimum(large, nb - 1)
    return ret + jnp.where(n < max_exact, n, large)


def causal_depthwise_conv(x, w, b):
    K = w.shape[0]
    y = lax.conv_general_dilated(x, w, window_strides=(1,), padding=[(K - 1, 0)],
                                 dimension_numbers=('NWC', 'WIO', 'NWC'),
                                 feature_group_count=x.shape[-1])
    return y + b


def ssd_chunk_step(state, inp):
    xc, ac, dtc, bc, cc = inp
    L = xc.shape[1]
    cum = jnp.cumsum(ac, axis=1)
    causal = jnp.tril(jnp.ones((L, L), dtype=bool))[None, :, :, None, None]
    seg = jnp.where(causal, cum[:, :, None] - cum[:, None, :], -jnp.inf)
    cb = jnp.einsum('btgn,bsgn->btsg', cc, bc)
    w = jnp.exp(seg) * cb[..., None] * dtc[:, None]
    y = jnp.einsum('btsgr,bsgrp->btgrp', w, xc)
    y = y + jnp.einsum('btgn,bgrpn->btgrp', cc, state) * jnp.exp(cum)[..., None]
    last = cum[:, -1]
    wst = jnp.exp(last[:, None] - cum) * dtc
    state = state * jnp.exp(last)[..., None, None] + jnp.einsum('bsgr,bsgrp,bsgn->bgrpn', wst, xc, bc)
    return state, y


def ssd_mixer(h, w_in, conv_w, conv_b, dt_bias, a_log, d_skip, norm_g, w_out):
    B, S, _ = h.shape
    G, R, P, N = SSD_GROUPS, SSD_HEADS_PER_GROUP, SSD_HEAD_DIM, SSD_STATE
    DI = SSD_D_INNER
    proj = h @ w_in
    z = proj[..., :DI]
    xbc = jax.nn.silu(causal_depthwise_conv(proj[..., DI:DI + SSD_CONV_DIM], conv_w, conv_b))
    dt = jax.nn.softplus((proj[..., DI + SSD_CONV_DIM:] + dt_bias).astype(jnp.float32))
    xs = xbc[..., :DI].reshape(B, S, G, R, P)
    bm = xbc[..., DI:DI + G * N].reshape(B, S, G, N)
    cm = xbc[..., DI + G * N:].reshape(B, S, G, N)
    a = (dt * -jnp.exp(a_log.astype(jnp.float32))).reshape(B, S, G, R)
    dt = dt.reshape(B, S, G, R)
    state0 = jnp.zeros((B, G, R, P, N), jnp.float32)
    _, y = lax.scan(ssd_chunk_step, state0,
                    (to_chunks(xs, CHUNK), to_chunks(a, CHUNK), to_chunks(dt, CHUNK),
                     to_chunks(bm, CHUNK), to_chunks(cm, CHUNK)))
    y = from_chunks(y) + d_skip.astype(jnp.float32).reshape(G, R, 1) * xs
    y = y.reshape(B, S, DI).astype(h.dtype) * jax.nn.silu(z)
    y = rms_norm(y.reshape(B, S, G, DI // G), norm_g.reshape(G, DI // G))
    return y.reshape(B, S, DI) @ w_out


def retention(h, w_in, norm_g, w_out):
    B, S, _ = h.shape
    H, dk, dv, L = RET_HEADS, RET_QK_DIM, RET_V_DIM, CHUNK
    proj = h @ w_in
    q = rotary(proj[..., :D_MODEL].reshape(B, S, H, dk))
    k = rotary(proj[..., D_MODEL:2 * D_MODEL].reshape(B, S, H, dk)) * (dk ** -0.5)
    v = proj[..., 2 * D_MODEL:2 * D_MODEL + RET_V_WIDTH].reshape(B, S, H, dv)
    gate = proj[..., 2 * D_MODEL + RET_V_WIDTH:]
    log_gamma = jnp.log1p(-jnp.exp2(-5.0 - jnp.arange(H, dtype=jnp.float32)))
    idx = jnp.arange(L, dtype=jnp.float32)
    intra = jnp.exp(log_gamma[:, None, None] * jnp.abs(idx[:, None] - idx[None, :]))
    q_decay = jnp.exp(log_gamma[None, :] * (idx[:, None] + 1.0))
    k_decay = jnp.exp(log_gamma[None, :] * (L - 1.0 - idx[:, None]))
    chunk_decay = jnp.exp(log_gamma * L)

    def step(state, inp):
        qc, kc, vc = inp
        s = jnp.einsum('blhd,bmhd->bhlm', qc, kc) * intra[None]
        o = jnp.einsum('bhlm,bmhe->blhe', s, vc)
        o = o + jnp.einsum('blhd,bhde->blhe', qc, state) * q_decay[None, :, :, None]
        state = state * chunk_decay[None, :, None, None] + jnp.einsum(
            'blhd,blhe->bhde', kc * k_decay[None, :, :, None], vc)
        return state, o

    state0 = jnp.zeros((B, H, dk, dv), jnp.float32)
    _, o = lax.scan(step, state0, (to_chunks(q, L), to_chunks(k, L), to_chunks(v, L)))
    o = rms_norm(from_chunks(o).astype(h.dtype), norm_g.reshape(H, dv))
    o = o.reshape(B, S, RET_V_WIDTH) * jax.nn.silu(gate)
    return o @ w_out


def diff_attention(h, w_in, lam_vecs, norm_g, w_out, rel_bias, lam_init):
    B, S, _ = h.shape
    H, dh = DIFF_HEADS, DIFF_HEAD_DIM
    proj = h @ w_in
    q = proj[..., :D_MODEL].reshape(B, S, H, 2, dh)
    k = proj[..., D_MODEL:2 * D_MODEL].reshape(B, S, H, 2, dh)
    v = proj[..., 2 * D_MODEL:].reshape(B, S, H, 2 * dh)
    lv = lam_vecs.astype(jnp.float32)
    lam = jnp.exp(jnp.sum(lv[0] * lv[1])) - jnp.exp(jnp.sum(lv[2] * lv[3])) + lam_init
    scale = dh ** -0.5
    pos = jnp.arange(S)
    nb = S // Q_BLOCK

    def block(args):
        qb, t0 = args
        tq = t0 + jnp.arange(Q_BLOCK)
        s = jnp.einsum('bqhmd,bkhmd->bhmqk', qb, k).astype(jnp.float32) * scale
        bias = rel_bias[t5_bucket(pos[None, :] - tq[:, None])].astype(jnp.float32)
        s = s + jnp.transpose(bias, (2, 0, 1))[None, :, None]
        mask = (pos[None, :] // CHUNK) <= (tq[:, None] // CHUNK)
        a = jax.nn.softmax(jnp.where(mask, s, -jnp.inf), axis=-1)
        a = a[:, :, 0] - lam * a[:, :, 1]
        return jnp.einsum('bhqk,bkhe->bqhe', a.astype(v.dtype), v)

    o = lax.map(block, (to_chunks(q, Q_BLOCK), jnp.arange(nb) * Q_BLOCK))
    o = rms_norm(from_chunks(o), norm_g) * (1.0 - lam_init)
    return o.reshape(B, S, D_MODEL) @ w_out


def forgetting_attention(h, w_in, b_f, w_out):
    B, S, _ = h.shape
    H, dh = FOX_HEADS, FOX_HEAD_DIM
    proj = h @ w_in
    q = proj[..., :D_MODEL].reshape(B, S, H, dh)
    k = proj[..., D_MODEL:2 * D_MODEL].reshape(B, S, H, dh)
    v = proj[..., 2 * D_MODEL:3 * D_MODEL].reshape(B, S, H, dh)
    log_f = jax.nn.log_sigmoid((proj[..., 3 * D_MODEL:] + b_f).astype(jnp.float32))
    c = jnp.cumsum(log_f, axis=1)
    c_key = jnp.transpose(c, (0, 2, 1))
    scale = dh ** -0.5
    pos = jnp.arange(S)
    nb = S // Q_BLOCK

    def block(args):
        qb, cq, t0 = args
        tq = t0 + jnp.arange(Q_BLOCK)
        s = jnp.einsum('bqhd,bkhd->bhqk', qb, k).astype(jnp.float32) * scale
        s = s + jnp.transpose(cq, (0, 2, 1))[..., :, None] - c_key[..., None, :]
        mask = pos[None, :] <= tq[:, None]
        a = jax.nn.softmax(jnp.where(mask, s, -jnp.inf), axis=-1)
        return jnp.einsum('bhqk,bkhd->bqhd', a.astype(v.dtype), v)

    o = lax.map(block, (to_chunks(q, Q_BLOCK), to_chunks(c, Q_BLOCK), jnp.arange(nb) * Q_BLOCK))
    return from_chunks(o).reshape(B, S, D_MODEL) @ w_out


def peer_ffn(h, w_q, sub_keys, u, v):
    B, S, D = h.shape
    T = B * S
    K, NK = PEER_TOPK, PEER_KEYS
    xt = h.reshape(T, D)
    q = (xt @ w_q).reshape(T, PEER_HEADS, 2, PEER_HALF)
    sc = jnp.einsum('thcd,hckd->thck', q, sub_keys).astype(jnp.float32)
    s1, i1 = lax.top_k(sc[:, :, 0], K)
    s2, i2 = lax.top_k(sc[:, :, 1], K)
    cand = (s1[..., :, None] + s2[..., None, :]).reshape(T, PEER_HEADS, K * K)
    cand_idx = (i1[..., :, None] * NK + i2[..., None, :]).reshape(T, PEER_HEADS, K * K)
    top_s, top_pos = lax.top_k(cand, K)
    idx = jnp.take_along_axis(cand_idx, top_pos, axis=-1)
    gate = jax.nn.softmax(top_s, axis=-1)
    nb = T // PEER_BLOCK

    def block(args):
        xb, ib, gb = args
        hid = jax.nn.gelu(jnp.einsum('tkd,td->tk', u[ib], xb), approximate=False)
        return jnp.einsum('tk,tkd->td', (gb * hid).astype(xb.dtype), v[ib])

    out = lax.map(block, (xt.reshape(nb, PEER_BLOCK, D),
                          idx.reshape(nb, PEER_BLOCK, PEER_HEADS * K),
                          gate.reshape(nb, PEER_BLOCK, PEER_HEADS * K)))
    return out.reshape(B, S, D)


def setup_inputs(seed: int = 0) -> dict:
    key = jax.random.key(seed)
    ks = iter(jax.random.split(key, 64))
    f32 = jnp.float32
    D = D_MODEL

    def nrm(shape, scale):
        return jax.random.normal(next(ks), shape, f32) * scale

    def gain(shape):
        return 1.0 + 0.05 * jax.random.normal(next(ks), shape, f32)

    def unif(shape, lo, hi):
        return jax.random.uniform(next(ks), shape, f32, lo, hi)

    nA, nB, nC, nD = [len(range(m, DEPTH, N_MIXERS)) for m in range(N_MIXERS)]
    dt0 = jnp.exp(unif((nA, SSD_HEADS), math.log(1e-3), math.log(1e-1)))
    return {
        'x': nrm((BATCH, SEQ, D), 1.0),
        'p': nrm((DEPTH, BATCH, SEQ, PLE_DIM), 1.0),
        'mix_norm': gain((DEPTH, D)),
        'ffn_norm': gain((DEPTH, D)),
        'ple_norm': gain((DEPTH, D)),
        'final_norm': gain((D,)),
        'ssd_w_in': nrm((nA, D, SSD_IN_DIM), D ** -0.5),
        'ssd_conv_w': nrm((nA, SSD_CONV, 1, SSD_CONV_DIM), SSD_CONV ** -0.5),
        'ssd_conv_b': nrm((nA, SSD_CONV_DIM), 0.02),
        'ssd_dt_bias': dt0 + jnp.log(-jnp.expm1(-dt0)),
        'ssd_a_log': jnp.log(unif((nA, SSD_HEADS), 1.0, 16.0)),
        'ssd_d': gain((nA, SSD_HEADS)),
        'ssd_norm': gain((nA, SSD_D_INNER)),
        'ssd_w_out': nrm((nA, SSD_D_INNER, D), SSD_D_INNER ** -0.5),
        'ret_w_in': nrm((nB, D, RET_IN_DIM), D ** -0.5),
        'ret_norm': gain((nB, RET_V_WIDTH)),
        'ret_w_out': nrm((nB, RET_V_WIDTH, D), RET_V_WIDTH ** -0.5),
        'diff_w_in': nrm((nC, D, DIFF_IN_DIM), D ** -0.5),
        'diff_lambda': nrm((nC, 4, DIFF_HEAD_DIM), 0.1),
        'diff_norm': gain((nC, 2 * DIFF_HEAD_DIM)),
        'diff_w_out': nrm((nC, D, D), D ** -0.5),
        'fox_w_in': nrm((nD, D, FOX_IN_DIM), D ** -0.5),
        'fox_b_f': unif((nD, FOX_HEADS), 1.0, 4.0),
        'fox_w_out': nrm((nD, D, D), D ** -0.5),
        'rel_bias': nrm((REL_BUCKETS, DIFF_HEADS), 0.3),
        'peer_w_q': nrm((DEPTH, D, PEER_HEADS * PEER_QUERY_DIM), D ** -0.5),
        'peer_keys': nrm((DEPTH, PEER_HEADS, 2, PEER_KEYS, PEER_HALF), PEER_HALF ** -0.5),
        'peer_u': nrm((DEPTH, PEER_EXPERTS, D), D ** -0.5),
        'peer_v': nrm((DEPTH, PEER_EXPERTS, D), (PEER_HEADS * PEER_TOPK) ** -0.5),
        'ple_proj': nrm((DEPTH, PLE_DIM, D), PLE_DIM ** -0.5),
        'ple_gate': nrm((DEPTH, D, D), D ** -0.5),
    }


def reference(x, p, mix_norm, ffn_norm, ple_norm, final_norm,
              ssd_w_in, ssd_conv_w, ssd_conv_b, ssd_dt_bias, ssd_a_log, ssd_d, ssd_norm, ssd_w_out,
              ret_w_in, ret_norm, ret_w_out,
              diff_w_in, diff_lambda, diff_norm, diff_w_out,
              fox_w_in, fox_b_f, fox_w_out,
              rel_bias,
              peer_w_q, peer_keys, peer_u, peer_v,
              ple_proj, ple_gate):
    h = x
    for i in range(DEPTH):
        m, j = i % N_MIXERS, i // N_MIXERS
        hn = rms_norm(h, mix_norm[i])
        if m == 0:
            y = ssd_mixer(hn, ssd_w_in[j], ssd_conv_w[j], ssd_conv_b[j], ssd_dt_bias[j],
                          ssd_a_log[j], ssd_d[j], ssd_norm[j], ssd_w_out[j])
        elif m == 1:
            y = retention(hn, ret_w_in[j], ret_norm[j], ret_w_out[j])
        elif m == 2:
            lam_init = 0.8 - 0.6 * math.exp(-0.3 * i)
            y = diff_attention(hn, diff_w_in[j], diff_lambda[j], diff_norm[j], diff_w_out[j],
                               rel_bias, lam_init)
        else:
            y = forgetting_attention(hn, fox_w_in[j], fox_b_f[j], fox_w_out[j])
        h = h + y
        h = h + peer_ffn(rms_norm(h, ffn_norm[i]), peer_w_q[i], peer_keys[i], peer_u[i], peer_v[i])
        g = jax.nn.sigmoid(rms_norm(h, ple_norm[i]) @ ple_gate[i])
        h = h + g * (p[i] @ ple_proj[i])
    return rms_norm(h, final_norm)
```

```python
from contextlib import ExitStack
import math
import numpy as np
import concourse.bass as bass
import concourse.mybir as mybir
from concourse.bass_utils import run_bass_kernel_spmd

F32 = mybir.dt.float32
BF16 = mybir.dt.bfloat16
AF = mybir.ActivationFunctionType
ALU = mybir.AluOpType
AX = mybir.AxisListType

NCORES = 8
D = 1024
NEG = -30000.0


class KB:
    NDMA = 24

    def __init__(self, nc, es):
        self.nc = nc
        self.es = es
        self.eng = dict(pe=nc.tensor, act=nc.scalar, dve=nc.vector, pool=nc.gpsimd, sp=nc.sync)
        es.enter_context(nc.allow_non_contiguous_dma(reason="small strided param loads"))
        self.sem = {}
        self.cnt = {}
        for e in ("pe", "act", "dve", "pool"):
            self.sem[e] = es.enter_context(nc.semaphore("s_" + e))
            self.cnt[e] = 0
        self.dsem = []
        for i in range(self.NDMA):
            nm = "d%d" % i
            self.sem[nm] = es.enter_context(nc.semaphore("s_" + nm))
            self.cnt[nm] = 0
            self.dsem.append(nm)
        self.dnext = 0
        self.known = {e: {} for e in self.eng}
        self.lastw = {}
        self.readers = {}
        self.n_ins = 0
        self.uid = 0

    def scope(self):
        kb = self

        class _S:
            def __enter__(self_):
                self_.old = kb.es
                self_.st = ExitStack()
                self_.st.__enter__()
                kb.es = self_.st
                return self_

            def __exit__(self_, *a):
                kb.es = self_.old
                return self_.st.__exit__(*a)

        return _S()

    def sb(self, name, shape, dtype=F32):
        self.uid += 1
        return self.es.enter_context(self.nc.sbuf_tensor("%s_%d" % (name, self.uid), list(shape), dtype))

    def ps(self, name, shape, dtype=F32):
        self.uid += 1
        return self.es.enter_context(self.nc.psum_tensor("%s_%d" % (name, self.uid), list(shape), dtype))

    @staticmethod
    def _key(a):
        return a if isinstance(a, (str, tuple)) else a.name

    def _wait(self, e, s, v):
        if v <= 0:
            return
        if self.known[e].get(s, 0) >= v:
            return
        self.eng[e].wait_ge(self.sem[s], v)
        self.known[e][s] = v
        self.n_ins += 1

    def _deps(self, e, R, W, pe_acc=False):
        deps = {}

        def add(tok, same_ok):
            if tok is None:
                return
            s, v = tok
            if same_ok and s == e:
                return
            if deps.get(s, 0) < v:
                deps[s] = v

        for k in R:
            add(self.lastw.get(k), False)
        for k in W:
            add(self.lastw.get(k), True)
            for s, v in self.readers.get(k, {}).items():
                add((s, v), True)
        for s, v in deps.items():
            self._wait(e, s, v)

    def _commit(self, tok, R, W):
        s, v = tok
        for k in W:
            self.lastw[k] = tok
            self.readers[k] = {}
        for k in R:
            d = self.readers.setdefault(k, {})
            if d.get(s, 0) < v:
                d[s] = v

    def op(self, e, ins_fn, R, W):
        R = [self._key(a) for a in R if a is not None and not isinstance(a, (int, float))]
        W = [self._key(a) for a in W]
        self._deps(e, R, W)
        ins = ins_fn()
        self.cnt[e] += 1
        ins.then_inc(self.sem[e], 1)
        self.n_ins += 1
        self._commit((e, self.cnt[e]), R, W)
        return ins

    def dma(self, out, in_, q="sp", rk=None, wk=None):
        R = [rk if rk is not None else self._key(in_)]
        W = [wk if wk is not None else self._key(out)]
        self._deps(q, R, W)
        s = self.dsem[self.dnext]
        self.dnext = (self.dnext + 1) % self.NDMA
        self._wait(q, s, self.cnt[s])
        self.eng[q].dma_start(out=out, in_=in_).then_inc(self.sem[s], 16)
        self.cnt[s] += 16
        self.n_ins += 1
        self._commit((s, self.cnt[s]), R, W)

    def barrier(self):
        for e in self.eng:
            for s in self.sem:
                if s != e:
                    self._wait(e, s, self.cnt[s])
        self.lastw = {}
        self.readers = {}

    def mm(self, out, lhsT, rhs, start=True, stop=True, extra_r=()):
        return self.op("pe", lambda: self.nc.tensor.matmul(out, lhsT, rhs, start=start, stop=stop),
                       [lhsT, rhs, *extra_r], [out])

    def tr(self, out, in_, ident):
        return self.op("pe", lambda: self.nc.tensor.transpose(out, in_, ident), [in_, ident], [out])

    def act(self, out, in_, func, bias=None, scale=None, accum_out=None, extra_r=()):
        kw = {}
        if bias is not None:
            kw["bias"] = bias
        if scale is not None:
            kw["scale"] = scale
        if accum_out is not None:
            kw["accum_out"] = accum_out
        W = [out] + ([accum_out] if accum_out is not None else [])
        return self.op("act", lambda: self.nc.scalar.activation(out, in_, func, **kw),
                       [in_, bias, scale, *extra_r], W)

    def tt(self, out, in0, in1, op, e="dve"):
        return self.op(e, lambda: self.eng[e].tensor_tensor(out, in0, in1, op), [in0, in1], [out])

    def ts(self, out, in0, s1, s2=None, op0=ALU.mult, op1=None, e="dve", accum_out=None):
        kw = {}
        if op1 is not None:
            kw["op1"] = op1
        if accum_out is not None:
            kw["accum_out"] = accum_out
        W = [out] + ([accum_out] if accum_out is not None else [])
        return self.op(e, lambda: self.eng[e].tensor_scalar(out, in0, s1, s2, op0, **kw), [in0, s1, s2], W)

    def stt(self, out, in0, scalar, in1, op0, op1, e="dve"):
        return self.op(e, lambda: self.eng[e].scalar_tensor_tensor(out, in0, scalar, in1, op0, op1),
                       [in0, scalar, in1], [out])

    def cp(self, out, in_, e="dve"):
        return self.op(e, lambda: self.eng[e].tensor_copy(out, in_), [in_], [out])

    def memset(self, out, val, e="dve"):
        return self.op(e, lambda: self.eng[e].memset(out, val), [], [out])

    def recip(self, out, in_):
        return self.op("dve", lambda: self.nc.vector.reciprocal(out, in_), [in_], [out])

    def red(self, out, in_, op=ALU.add, e="dve"):
        return self.op(e, lambda: self.eng[e].tensor_reduce(out, in_, AX.X, op), [in_], [out])


class GemmRes:
    def __init__(self, kb):
        self.kb = kb
        self.ident_f = kb.sb("identf", [128, 128], F32)
        self.ident = kb.sb("ident", [128, 128], BF16)
        self.xin = [kb.sb("xin%d" % i, [128, 1024], F32) for i in range(3)]
        self.xsq = kb.sb("xsq", [128, 1024], F32)
        self.ssq = [kb.sb("ssq%d" % i, [128, 1], F32) for i in range(3)]
        self.xn = [kb.sb("xn%d" % i, [128, 2048], BF16) for i in range(2)]
        self.gcol = kb.sb("gcol", [128, 8], F32)
        self.epsc = kb.sb("epsc", [128, 1], F32)
        self.pmi = 0

    def next_pm(self):
        p = self.pm[self.pmi % len(self.pm)]
        self.pmi += 1
        return p

    def init(self, ident_dram):
        kb = self.kb
        kb.dma(self.ident_f[:], ident_dram)
        kb.cp(self.ident[:], self.ident_f[:])
        kb.memset(self.epsc[:], 1e-6)


def load_weights(kb, gr, W_dram, Kin, N, gcol_dram=None):
    KC = Kin // 128
    Wb = kb.sb("Wb", [128, KC * N], BF16)
    Wv = Wb[:].rearrange("p (c n) -> p c n", c=KC)
    with kb.scope():
        wst = [kb.sb("wst%d" % i, [128, 2048], F32) for i in range(2)]
        if gcol_dram is not None:
            kb.dma(gr.gcol[:, 0:KC], gcol_dram.rearrange("(c p) -> p c", p=128))
        i = 0
        for kc in range(KC):
            for n0 in range(0, N, 2048):
                n1 = min(N, n0 + 2048)
                st = wst[i % 2]
                kb.dma(st[:, 0:n1 - n0], W_dram[kc * 128:(kc + 1) * 128, n0:n1], q="sp")
                if gcol_dram is not None:
                    if i % 2 == 0:
                        kb.ts(Wv[:, kc, n0:n1], st[:, 0:n1 - n0], gr.gcol[:, kc:kc + 1], None, op0=ALU.mult)
                    else:
                        kb.act(Wv[:, kc, n0:n1], st[:, 0:n1 - n0], AF.Copy, scale=gr.gcol[:, kc:kc + 1])
                else:
                    kb.cp(Wv[:, kc, n0:n1], st[:, 0:n1 - n0], e=("dve", "pool")[i % 2])
                i += 1
        kb.barrier()
    return Wv


def gemm_stage(kb, gr, x_dram, T, Kin, Wv, segs, norm=False, x_bf16=False, n_psT=2, n_pm=4):
    KC = Kin // 128
    nblk = T // 512
    gr.xT = [kb.sb("xT%d" % i, [128, KC * 512], BF16) for i in range(2)]
    gr.psT = [kb.ps("psT%d" % i, [128, 1024], BF16) for i in range(n_psT)]
    gr.pm = [kb.ps("pm%d" % i, [128, 512], F32) for i in range(n_pm)]
    for blk in range(nblk):
        xT = gr.xT[blk % 2]
        xTv = xT[:, 0:KC * 512].rearrange("p (c t) -> p c t", c=KC)
        xins = []
        for sub in range(4):
            tok0 = blk * 512 + sub * 128
            it = blk * 4 + sub
            if x_bf16:
                xt = gr.xn[it % 2]
                kb.dma(xt[:, 0:Kin], x_dram[tok0:tok0 + 128, :])
                xn = xt
            else:
                xt = gr.xin[it % 3]
                kb.dma(xt[:, 0:Kin], x_dram[tok0:tok0 + 128, :])
                xn = gr.xn[it % 2]
                if norm:
                    ssq = gr.ssq[it % 3]
                    kb.act(gr.xsq[:, 0:Kin], xt[:, 0:Kin], AF.Square)
                    kb.red(ssq[:], gr.xsq[:, 0:Kin])
                    kb.act(ssq[:], ssq[:], AF.Sqrt, bias=gr.epsc[:, 0:1], scale=1.0 / Kin)
                    kb.recip(ssq[:], ssq[:])
                    kb.act(xn[:, 0:Kin], xt[:, 0:Kin], AF.Copy, scale=ssq[:, 0:1])
                else:
                    kb.cp(xn[:, 0:Kin], xt[:, 0:Kin], e="pool")
            xins.append(xt)
            for half in range((KC + 7) // 8):
                pst = gr.psT[(it * 2 + half) % n_psT]
                nk = min(8, KC - half * 8)
                for j in range(nk):
                    kc = half * 8 + j
                    kb.tr(pst[:, j * 128:(j + 1) * 128], xn[:, kc * 128:(kc + 1) * 128], gr.ident[:])
                src = pst[:, 0:nk * 128].rearrange("p (c t) -> p c t", c=nk)
                dst = xTv[:, half * 8:half * 8 + nk, sub * 128:(sub + 1) * 128]
                if (it + half) % 2 == 0:
                    kb.cp(dst, src, e="dve")
                else:
                    kb.act(dst, src, AF.Copy)
        for seg in segs:
            if seg["kind"] == "custom":
                seg["fn"](blk, xTv)
                continue
            n0, n1 = seg["n0"], seg["n1"]
            if seg["kind"] == "FM":
                grp = seg.get("group", 1)
                nch = (n1 - n0 + 127) // 128
                for c0 in range(0, nch, grp):
                    pss = []
                    for ci in range(c0, min(nch, c0 + grp)):
                        a = n0 + ci * 128
                        b = min(n1, a + 128)
                        ps = gr.next_pm()
                        for kc in range(KC):
                            kb.mm(ps[0:b - a, :], Wv[:, kc, a:b], xTv[:, kc, :], start=(kc == 0), stop=(kc == KC - 1))
                        pss.append(ps)
                    seg["epi"](blk, blk * 512, c0, pss)
            else:
                for sub in range(4):
                    for a in range(n0, n1, 512):
                        b = min(n1, a + 512)
                        ps = gr.next_pm()
                        for kc in range(KC):
                            kb.mm(ps[:, 0:b - a], xTv[:, kc, sub * 128:(sub + 1) * 128], Wv[:, kc, a:b],
                                  start=(kc == 0), stop=(kc == KC - 1))
                        seg["epi"](blk, blk * 512 + sub * 128, sub, a - n0, b - a, ps, xins[sub])


def attn_stage(kb, spec, S, nseq):
    nsup = S // 512
    pst = [kb.ps("ast%d" % i, [128, 512], F32) for i in range(2)]
    outs = [kb.ps("aout%d" % i, [128, 512], F32) for i in range(4)]
    Ps = [kb.sb("aP%d" % i, [128, 512], BF16) for i in range(3)]
    jobs = [(seq, g) for seq in range(nseq) for g in spec.groups(seq)]
    ti = 0
    nbuf = getattr(spec, "nbuf", 2)
    for n, (seq, g) in enumerate(jobs):
        buf = n % nbuf
        if nbuf == 1:
            spec.load(g, 0)
        else:
            if n == 0:
                spec.load(g, buf)
            if n + 1 < len(jobs):
                spec.load(jobs[n + 1][1], (n + 1) % 2)
        shs = spec.subheads(g)
        for j in range(nsup):
            for i in range(4 * j + 4):
                if hasattr(spec, "pre"):
                    spec.pre(g, j, i, buf)
                for sh in shs:
                    ps = pst[ti % 2]
                    P = Ps[ti % 3]
                    ti += 1
                    spec.qk(g, sh, j, i, ps, buf)
                    spec.evac(g, sh, j, i, ps, P, buf)
                    c0, dvp = spec.ocols(g, sh)
                    vb = spec.vblk(g, sh, i, buf)
                    for qb in range(4):
                        if i <= 4 * j + qb:
                            kb.mm(outs[qb][:, c0:c0 + dvp], P[:, qb * 128:(qb + 1) * 128], vb,
                                  start=(i == 0 and sh == shs[0]), stop=(i == 4 * j + qb))
            spec.post(g, j, outs, buf)


class FoxSpec:
    def __init__(self, kb, gr, S, QT, KT, V, CQK, O, cmask_bf):
        self.kb, self.gr, self.S = kb, gr, S
        self.QT, self.KT, self.V, self.CQK, self.O = QT, KT, V, CQK, O
        self.cmask = cmask_bf
        self.Qa = [kb.sb("fQa%d" % i, [70, S], BF16) for i in range(2)]
        self.Ka = [kb.sb("fKa%d" % i, [70, S], BF16) for i in range(2)]
        self.Vt = [kb.sb("fV%d" % i, [128, (S // 128) * 65], BF16) for i in range(2)]
        self.rz = [kb.sb("frz%d" % i, [128, 1], F32) for i in range(4)]
        self.ost = [kb.sb("fost%d" % i, [128, 64], BF16) for i in range(4)]
        self.pi = 0
        for i in range(2):
            kb.memset(self.Qa[i][64:70, :], 1.0)
            kb.memset(self.Ka[i][64:70, :], 1.0, e="pool")
            kb.memset(self.Vt[i][:], 1.0, e="pool")

    def groups(self, seq):
        return [(seq, h) for h in range(16)]

    def subheads(self, g):
        return [0]

    def load(self, g, buf):
        kb, S = self.kb, self.S
        seq, h = g
        t0 = seq * S
        kb.dma(self.Qa[buf][0:64, :], self.QT[64 * h:64 * h + 64, t0:t0 + S])
        kb.dma(self.Qa[buf][64:67, :], self.CQK[h, 3:6, t0:t0 + S])
        kb.dma(self.Ka[buf][0:64, :], self.KT[64 * h:64 * h + 64, t0:t0 + S])
        kb.dma(self.Ka[buf][67:70, :], self.CQK[h, 0:3, t0:t0 + S])
        vt = self.Vt[buf][:].rearrange("p (b d) -> p b d", d=65)
        kb.dma(vt[:, :, 0:64], self.V[t0:t0 + S, 64 * h:64 * h + 64].rearrange("(b p) d -> p b d", p=128))

    def qk(self, g, sh, j, i, ps, buf):
        kb = self.kb
        r = i - 4 * j
        kb.mm(ps[:], self.Ka[buf][:, i * 128:(i + 1) * 128], self.Qa[buf][:, j * 512:(j + 1) * 512],
              start=True, stop=(r < 0))
        if r >= 0:
            kb.mm(ps[:], self.gr.ident[:], self.cmask[:, r * 512:(r + 1) * 512], start=False, stop=True)

    def evac(self, g, sh, j, i, ps, P, buf):
        self.kb.act(P[:], ps[:], AF.Exp)

    def ocols(self, g, sh):
        return 0, 65

    def vblk(self, g, sh, i, buf):
        return self.Vt[buf][:, i * 65:(i + 1) * 65]

    def post(self, g, j, outs, buf):
        kb, S = self.kb, self.S
        seq, h = g
        for qb in range(4):
            rz = self.rz[self.pi % 4]
            st = self.ost[self.pi % 4]
            self.pi += 1
            kb.recip(rz[:], outs[qb][:, 64:65])
            kb.ts(st[:], outs[qb][:, 0:64], rz[:, 0:1], None, op0=ALU.mult)
            tok = seq * S + (4 * j + qb) * 128
            kb.dma(self.O[tok:tok + 128, 64 * h:64 * h + 64], st[:], q="pool")


class Epi:
    def __init__(self, kb):
        self.kb = kb
        self.fm = [kb.sb("efm%d" % i, [128, 512], BF16) for i in range(3)]
        self.tmb = [kb.sb("etmb%d" % i, [128, 512], BF16) for i in range(3)]
        self.tmf = [kb.sb("etmf%d" % i, [128, 512], F32) for i in range(3)]
        self.n = 0

    def fm_store(self, dst, scale=1.0):
        kb = self.kb

        def epi(blk, tok0, c0, pss):
            ps = pss[0]
            st = self.fm[self.n % 3]
            self.n += 1
            rows = min(128, dst.shape[0] - c0 * 128)
            if self.n % 2 == 0:
                kb.act(st[0:rows, :], ps[0:rows, :], AF.Copy, scale=float(scale))
            else:
                kb.ts(st[0:rows, :], ps[0:rows, :], float(scale), None, op0=ALU.mult)
            kb.dma(dst[c0 * 128:c0 * 128 + rows, tok0:tok0 + 512], st[0:rows, :], q="pool")
        return epi

    def tm_store(self, dst, func=AF.Copy, col0=0):
        kb = self.kb

        def epi(blk, tok0, sub, n0c, ncols, ps, xt):
            st = self.tmb[self.n % 3]
            self.n += 1
            if func == AF.Copy and self.n % 2 == 0:
                kb.cp(st[:, 0:ncols], ps[:, 0:ncols])
            else:
                kb.act(st[:, 0:ncols], ps[:, 0:ncols], func)
            kb.dma(dst[tok0:tok0 + 128, col0 + n0c:col0 + n0c + ncols], st[:, 0:ncols], q="pool")
        return epi

    def tm_resid(self, h):
        kb = self.kb

        def epi(blk, tok0, sub, n0c, ncols, ps, xt):
            st = self.tmf[self.n % 3]
            self.n += 1
            key = ("h", tok0, n0c)
            kb.dma(st[:, 0:ncols], h[tok0:tok0 + 128, n0c:n0c + ncols], rk=key)
            kb.tt(st[:, 0:ncols], ps[:, 0:ncols], st[:, 0:ncols], ALU.add)
            kb.dma(h[tok0:tok0 + 128, n0c:n0c + ncols], st[:, 0:ncols], q="pool", wk=key)
        return epi


def split3(kb, src, dst6, tmp_r, tmp_b, rows, n):
    r = tmp_r
    kb.cp(dst6[0:rows, 0, :], src)
    kb.tt(r[0:rows, 0:n], src, dst6[0:rows, 0, :], ALU.subtract)
    kb.cp(dst6[0:rows, 1, :], r[0:rows, 0:n])
    kb.tt(r[0:rows, 0:n], r[0:rows, 0:n], dst6[0:rows, 1, :], ALU.subtract)
    kb.cp(dst6[0:rows, 2, :], r[0:rows, 0:n])
    kb.ts(dst6[0:rows, 3:6, :], dst6[0:rows, 0:3, :], -1.0, None, op0=ALU.mult, e="pool")


class Model:
    def __init__(self, nc, es, nseq, S):
        self.nc, self.nseq, self.S = nc, nseq, S
        self.T = nseq * S
        self.kb = KB(nc, es)
        self.din = {}
        self.scratch = {}

    def inp(self, name, shape, dtype=F32):
        self.din[name] = self.nc.dram_tensor(name, list(shape), dtype, kind="ExternalInput").ap()
        return self.din[name]

    def dscr(self, name, shape, dtype):
        if name not in self.scratch:
            self.scratch[name] = self.nc.dram_tensor(name, list(shape), dtype, kind="Internal").ap()
        return self.scratch[name]

    def setup(self):
        kb = self.kb
        self.gr = GemmRes(kb)
        self.gr.init(self.din["ident"])
        self.epi = Epi(kb)
        self.one = kb.sb("onec", [128, 1], F32)
        kb.memset(self.one[:], 1.0)
        self.zeros = kb.sb("zeros", [128, 512], F32)
        kb.memset(self.zeros[:], 0.0)
        self.cmask = kb.sb("cmask", [128, 2048], BF16)
        with kb.scope():
            tmp = kb.sb("cmtmp", [128, 2048], F32)
            kb.dma(tmp[:], self.din["cmask"])
            kb.cp(self.cmask[:], tmp[:])
            kb.barrier()

    def fox_mixer(self, h, w_in, gcol, b_f, w_out):
        kb, gr, T, S, nseq = self.kb, self.gr, self.T, self.S, self.nseq
        QT = self.dscr("QT", [1024, T], BF16)
        KT = self.dscr("KT", [1024, T], BF16)
        V = self.dscr("Vtm", [T, 2048], BF16)
        CQK = self.dscr("CQK", [32, 6, T], BF16)
        O = self.dscr("Otm", [T, 2048], BF16)
        with kb.scope():
            Wv = load_weights(kb, gr, w_in, 1024, 3088, gcol)
            nbf = kb.sb("nbf", [16, 1], F32)
            kb.dma(nbf[:], b_f.rearrange("(p o) -> p o", o=1))
            kb.ts(nbf[:], nbf[:], -1.0, None, op0=ALU.mult)
            carry = kb.sb("carry", [16, 1], F32)
            t1 = kb.sb("ft1", [16, 512], F32)
            cum = kb.sb("fcum", [16, 512], F32)
            rr = kb.sb("frr", [16, 512], F32)
            spl = kb.sb("fspl", [16, 6 * 512], BF16)
            splv = spl[:].rearrange("p (s n) -> p s n", s=6)
            bps = S // 512

            def f_epi(blk, tok0, c0, pss):
                ps = pss[0]
                if blk % bps == 0:
                    kb.memset(carry[:], 0.0)
                kb.act(t1[:], ps[0:16, :], AF.Exp, bias=nbf[:, 0:1], scale=-1.0)
                kb.act(t1[:], t1[:], AF.Ln, bias=self.one[0:16, 0:1])
                kb.op("dve", lambda: self.nc.vector.tensor_tensor_scan(cum[:], t1[:], self.zeros[0:16, :], carry[:, 0:1], ALU.add, ALU.add),
                      [t1, self.zeros, carry], [cum])
                kb.cp(carry[:], cum[:, 511:512])
                split3(kb, cum[:], splv, rr, None, 16, 512)
                kb.dma(CQK[0:16, :, tok0:tok0 + 512], splv, q="pool")

            segs = [dict(kind="FM", n0=0, n1=1024, epi=self.epi.fm_store(QT, 0.125)),
                    dict(kind="FM", n0=1024, n1=2048, epi=self.epi.fm_store(KT, 1.0)),
                    dict(kind="TM", n0=2048, n1=3072, epi=self.epi.tm_store(V)),
                    dict(kind="FM", n0=3072, n1=3088, epi=f_epi)]
            gemm_stage(kb, gr, h, T, 1024, Wv, segs, norm=True)
            kb.barrier()
        with kb.scope():
            spec = FoxSpec(kb, gr, S, QT, KT, V, CQK, O, self.cmask)
            attn_stage(kb, spec, S, nseq)
            kb.barrier()
        with kb.scope():
            Wv = load_weights(kb, gr, w_out, 1024, 1024, None)
            segs = [dict(kind="TM", n0=0, n1=1024, epi=self.epi.tm_resid(h))]
            gemm_stage(kb, gr, O[:, 0:1024], T, 1024, Wv, segs, x_bf16=True)
            kb.barrier()

    def peer(self, h, gcol, w_q, keysT, uT, v):
        kb, gr, T = self.kb, self.gr, self.T
        nc = self.nc
        UTb = self.dscr("UTb", [1024, 16384], BF16)
        Vb = self.dscr("Vb", [16384, 1024], BF16)
        Gd = self.dscr("Gd", [T, 16384], BF16)
        with kb.scope():
            kb.dma(gr.gcol[:, 0:8], gcol.rearrange("(c p) -> p c", p=128))
            st = [kb.sb("pst%d" % i, [128, 2048], F32) for i in range(3)]
            sb = [kb.sb("psb%d" % i, [128, 2048], BF16) for i in range(3)]
            n = 0
            for kc in range(8):
                for e0 in range(0, 16384, 2048):
                    s_, b_ = st[n % 3], sb[n % 3]
                    kb.dma(s_[:], uT[kc * 128:(kc + 1) * 128, e0:e0 + 2048])
                    if n % 2 == 0:
                        kb.ts(b_[:], s_[:], gr.gcol[:, kc:kc + 1], None, op0=ALU.mult)
                    else:
                        kb.act(b_[:], s_[:], AF.Copy, scale=gr.gcol[:, kc:kc + 1])
                    kb.dma(UTb[kc * 128:(kc + 1) * 128, e0:e0 + 2048], b_[:], q="pool")
                    n += 1
            for r0 in range(0, 16384, 256):
                s_, b_ = st[n % 3], sb[n % 3]
                kb.dma(s_[:].rearrange("p (c n) -> p c n", c=2), v[r0:r0 + 256, :].rearrange("(c p) n -> p c n", p=128))
                if n % 3 == 0:
                    kb.cp(b_[:], s_[:], e="dve")
                elif n % 3 == 1:
                    kb.act(b_[:], s_[:], AF.Copy)
                else:
                    kb.cp(b_[:], s_[:], e="pool")
                kb.dma(Vb[r0:r0 + 256, :].rearrange("(c p) n -> p c n", p=128), b_[:].rearrange("p (c n) -> p c n", c=2), q="pool")
                n += 1
            kb.barrier()
        with kb.scope():
            Wv = load_weights(kb, gr, w_q, 1024, 2048, gcol)
            kT = kb.sb("keysT", [128, 16 * 128], BF16)
            with kb.scope():
                ktmp = kb.sb("ktmp", [128, 16 * 128], F32)
                kb.dma(ktmp[:].rearrange("p (c k) -> p c k", c=16), keysT.rearrange("c d k -> d c k"))
                kb.cp(kT[:], ktmp[:])
                kb.barrier()
            kTv = kT[:].rearrange("p (c k) -> p c k", c=16)
            qT = [kb.sb("pqT%d" % i, [128, 512], BF16) for i in range(2)]
            psc = [kb.ps("psc%d" % i, [128, 512], F32) for i in range(2)]
            sc_all = kb.sb("sc_all", [128, 4 * 16 * 128], F32)
            scv = sc_all[:].rearrange("p (s c k) -> p s c k", s=4, c=16)
            a16 = kb.sb("a16", [128, 16], F32)
            b16 = kb.sb("b16", [128, 16], F32)
            c16 = kb.sb("c16", [128, 16], F32)
            e16 = kb.sb("e16", [128, 16], F32)
            t128 = kb.sb("t128", [128, 128], F32)
            cand = kb.sb("cand", [128, 256], F32)
            cand2 = kb.sb("cand2", [128, 256], F32)
            tau = kb.sb("tau", [128, 8], F32)
            nb = kb.sb("nb", [128, 8], F32)
            zz = kb.sb("zz", [128, 1], F32)
            Sc = [kb.sb("Sc%d" % i, [128, 2048], F32) for i in range(2)]
            Ec = [kb.sb("Ec%d" % i, [128, 2048], F32) for i in range(2)]
            Gc = [kb.sb("Gc%d" % i, [128, 2048], F32) for i in range(2)]
            Gb = [kb.sb("Gb%d" % i, [128, 2048], BF16) for i in range(2)]
            cn = [0, 0, 0]
            V = nc.vector

            def top16(dst, src, tmp):
                kb.op("dve", lambda: V.max(out=dst[:, 0:8], in_=src), [src], [dst])
                kb.op("dve", lambda: V.match_replace(out=tmp, in_to_replace=dst[:, 0:8], in_values=src, imm_value=-1e30),
                      [dst, src], [tmp])
                kb.op("dve", lambda: V.max(out=dst[:, 8:16], in_=tmp), [tmp], [dst])

            def gates(tokb, sub):
                for hh in range(8):
                    s1 = scv[:, sub, 2 * hh, :]
                    s2 = scv[:, sub, 2 * hh + 1, :]
                    top16(a16, s1, t128[:])
                    top16(b16, s2, t128[:])
                    kb.tt(cand[:].rearrange("p (a b) -> p a b", a=16),
                          a16[:].unsqueeze(2).to_broadcast([128, 16, 16]),
                          b16[:].unsqueeze(1).to_broadcast([128, 16, 16]), ALU.add)
                    top16(c16, cand[:], cand2[:])
                    kb.cp(tau[:, hh:hh + 1], c16[:, 15:16])
                    kb.ts(nb[:, hh:hh + 1], c16[:, 0:1], -1.0, None, op0=ALU.mult)
                    kb.act(e16[:], c16[:], AF.Exp, bias=nb[:, hh:hh + 1])
                    kb.red(zz[:], e16[:])
                    kb.act(zz[:], zz[:], AF.Ln)
                    kb.tt(nb[:, hh:hh + 1], nb[:, hh:hh + 1], zz[:], ALU.subtract)
                for c in range(8):
                    G = Gc[cn[2] % 2]
                    Gbb = Gb[cn[2] % 2]
                    cn[2] += 1
                    for hh in range(8):
                        s1 = scv[:, sub, 2 * hh, 16 * c:16 * c + 16]
                        s2 = scv[:, sub, 2 * hh + 1, :]
                        S_ = Sc[cn[0] % 2]
                        cn[0] += 1
                        E_ = Ec[cn[1] % 2]
                        cn[1] += 1
                        S3 = S_[:].rearrange("p (a b) -> p a b", a=16)
                        kb.tt(S3, s1.unsqueeze(2).to_broadcast([128, 16, 128]),
                              s2.unsqueeze(1).to_broadcast([128, 16, 128]), ALU.add)
                        kb.act(E_[:], S_[:], AF.Exp, bias=nb[:, hh:hh + 1])
                        dst = G if hh == 0 else E_
                        kb.stt(dst[:], S_[:], tau[:, hh:hh + 1], E_[:], ALU.is_ge, ALU.mult)
                        if 0 < hh < 7:
                            kb.tt(G[:], G[:], E_[:], ALU.add, e="pool")
                        elif hh == 7:
                            kb.tt(Gbb[:], G[:], E_[:], ALU.add, e="pool")
                    kb.dma(Gd[tokb:tokb + 128, c * 2048:(c + 1) * 2048], Gbb[:], q="sp")

            def q_epi(blk, tok0, c0, pss):
                qt = qT[c0 % 2]
                if c0 % 2 == 0:
                    kb.act(qt[:], pss[0][:], AF.Copy)
                else:
                    kb.cp(qt[:], pss[0][:])
                pc = psc[c0 % 2]
                for sub in range(4):
                    kb.mm(pc[:, sub * 128:(sub + 1) * 128], qt[:, sub * 128:(sub + 1) * 128], kTv[:, c0, :])
                src = pc[:].rearrange("p (s k) -> p s k", s=4)
                if c0 % 2 == 0:
                    kb.cp(scv[:, :, c0, :], src)
                else:
                    kb.act(scv[:, :, c0, :], src, AF.Copy)
                if c0 == 15:
                    for sub in range(4):
                        gates(tok0 + sub * 128, sub)

            segs = [dict(kind="FM", n0=0, n1=2048, epi=q_epi)]
            gemm_stage(kb, gr, h, T, 1024, Wv, segs, norm=True)
            kb.barrier()
        if getattr(self, 'skip_dense', False):
            return
        with kb.scope():
            ut = [kb.sb("ut%d" % i, [128, 8 * 512], BF16) for i in range(2)]
            vt = [kb.sb("vt%d" % i, [128, 4 * 1024], BF16) for i in range(2)]
            gt = [kb.sb("gt%d" % i, [128, 512], BF16) for i in range(3)]
            gel = [kb.sb("gel%d" % i, [128, 512], F32) for i in range(2)]
            gh = [kb.sb("gh%d" % i, [128, 512], BF16) for i in range(2)]
            ghT = [kb.sb("ghT%d" % i, [128, 512], BF16) for i in range(2)]
            acc = [kb.sb("acc%d" % i, [128, 1024], F32) for i in range(4)]
            php = [kb.ps("php%d" % i, [128, 512], F32) for i in range(2)]
            ptp = [kb.ps("ptp%d" % i, [128, 1024], BF16) for i in range(1)]
            pop = [kb.ps("pop%d" % i, [128, 512], F32) for i in range(4)]
            cn2 = [0]

            def dense(blk, xTv):
                tokb = blk * 512
                for ec in range(32):
                    u_ = ut[ec % 2]
                    v_ = vt[ec % 2]
                    uv = u_[:].rearrange("p (c e) -> p c e", c=8)
                    vv = v_[:].rearrange("p (c n) -> p c n", c=4)
                    kb.dma(uv, UTb[:, ec * 512:(ec + 1) * 512].rearrange("(c p) e -> p c e", p=128))
                    kb.dma(vv, Vb[ec * 512:(ec + 1) * 512, :].rearrange("(c p) n -> p c n", p=128))
                    for sub in range(4):
                        k = cn2[0]
                        cn2[0] += 1
                        g_ = gt[k % 3]
                        kb.dma(g_[:], Gd[tokb + sub * 128:tokb + sub * 128 + 128, ec * 512:(ec + 1) * 512])
                        ph = php[k % 2]
                        for kc in range(8):
                            kb.mm(ph[:], xTv[:, kc, sub * 128:(sub + 1) * 128], uv[:, kc, :], start=(kc == 0), stop=(kc == 7))
                        ge = gel[k % 2]
                        kb.act(ge[:], ph[:], AF.Gelu)
                        gh_ = gh[k % 2]
                        kb.tt(gh_[:], ge[:], g_[:], ALU.mult, e="pool")
                        pt = ptp[0]
                        for c4 in range(4):
                            kb.tr(pt[:, c4 * 128:(c4 + 1) * 128], gh_[:, c4 * 128:(c4 + 1) * 128], gr.ident[:])
                        gT = ghT[k % 2]
                        if k % 2 == 0:
                            kb.cp(gT[:], pt[:, 0:512])
                        else:
                            kb.act(gT[:], pt[:, 0:512], AF.Copy)
                        for nh in range(2):
                            po = pop[(k % 2) * 2 + nh]
                            for c4 in range(4):
                                kb.mm(po[:], gT[:, c4 * 128:(c4 + 1) * 128], vv[:, c4, nh * 512:(nh + 1) * 512],
                                      start=(c4 == 0), stop=(c4 == 3))
                            a_ = acc[sub][:, nh * 512:(nh + 1) * 512]
                            if ec == 0:
                                kb.cp(a_, po[:])
                            else:
                                kb.tt(a_, po[:], a_, ALU.add)
                for sub in range(4):
                    t0 = tokb + sub * 128
                    hs = gr.xin[sub % 3]
                    key = ("h", t0)
                    kb.dma(hs[:], h[t0:t0 + 128, :], rk=key)
                    kb.tt(acc[sub][:], acc[sub][:], hs[:], ALU.add, e="pool")
                    kb.dma(h[t0:t0 + 128, :], acc[sub][:], q="pool", wk=key)

            segs = [dict(kind="custom", fn=dense)]
            gemm_stage(kb, gr, h, T, 1024, None, segs, norm=True, n_psT=1, n_pm=0)
            kb.barrier()

    def ple(self, h, gcol, w_gate, p_i, w_proj):
        kb, gr, T = self.kb, self.gr, self.T
        PP = self.dscr("PP", [T, 1024], F32)
        with kb.scope():
            Wv = load_weights(kb, gr, w_proj, 256, 1024, None)
            stf = [kb.sb("ppst%d" % i, [128, 512], F32) for i in range(3)]
            cn = [0]

            def pp_epi(blk, tok0, sub, n0c, ncols, ps, xt):
                st = stf[cn[0] % 3]
                cn[0] += 1
                if cn[0] % 2 == 0:
                    kb.cp(st[:, 0:ncols], ps[:, 0:ncols])
                else:
                    kb.act(st[:, 0:ncols], ps[:, 0:ncols], AF.Copy)
                kb.dma(PP[tok0:tok0 + 128, n0c:n0c + ncols], st[:, 0:ncols], q="pool")
            gemm_stage(kb, gr, p_i, T, 256, Wv, [dict(kind="TM", n0=0, n1=1024, epi=pp_epi)])
            kb.barrier()
        with kb.scope():
            Wv = load_weights(kb, gr, w_gate, 1024, 1024, gcol)
            sg = [kb.sb("plg%d" % i, [128, 512], F32) for i in range(3)]
            sp_ = [kb.sb("plp%d" % i, [128, 512], F32) for i in range(3)]
            sh_ = [kb.sb("plh%d" % i, [128, 512], F32) for i in range(3)]
            cn = [0]

            def g_epi(blk, tok0, sub, n0c, ncols, ps, xt):
                k = cn[0] % 3
                cn[0] += 1
                key = ("h", tok0, n0c)
                kb.dma(sp_[k][:, 0:ncols], PP[tok0:tok0 + 128, n0c:n0c + ncols])
                kb.dma(sh_[k][:, 0:ncols], h[tok0:tok0 + 128, n0c:n0c + ncols], rk=key)
                kb.act(sg[k][:, 0:ncols], ps[:, 0:ncols], AF.Sigmoid)
                kb.tt(sg[k][:, 0:ncols], sg[k][:, 0:ncols], sp_[k][:, 0:ncols], ALU.mult, e="pool")
                kb.tt(sh_[k][:, 0:ncols], sh_[k][:, 0:ncols], sg[k][:, 0:ncols], ALU.add)
                kb.dma(h[tok0:tok0 + 128, n0c:n0c + ncols], sh_[k][:, 0:ncols], q="pool", wk=key)
            gemm_stage(kb, gr, h, T, 1024, Wv, [dict(kind="TM", n0=0, n1=1024, epi=g_epi)], norm=True)
            kb.barrier()

    def final(self, h, g, out):
        kb, gr, T = self.kb, self.gr, self.T
        with kb.scope():
            gb = kb.sb("fgb", [128, 1024], F32)
            kb.dma(gb[:], g.partition_broadcast(128))
            ot = [kb.sb("fot%d" % i, [128, 1024], F32) for i in range(2)]
            for it in range(T // 128):
                xt = gr.xin[it % 3]
                ssq = gr.ssq[it % 3]
                o_ = ot[it % 2]
                kb.dma(xt[:], h[it * 128:(it + 1) * 128, :])
                kb.act(gr.xsq[:], xt[:], AF.Square)
                kb.red(ssq[:], gr.xsq[:])
                kb.act(ssq[:], ssq[:], AF.Sqrt, bias=gr.epsc[:, 0:1], scale=1.0 / 1024)
                kb.recip(ssq[:], ssq[:])
                kb.stt(o_[:], xt[:], ssq[:, 0:1], gb[:], ALU.mult, ALU.mult)
                kb.dma(out[it * 128:(it + 1) * 128, :], o_[:], q="pool")
            kb.barrier()

    def diff_mixer(self, h, w_in, gcol, lam_vecs, lam_init, norm_g, w_out, BT, bfar, dmask):
        kb, gr, T, S, nseq = self.kb, self.gr, self.T, self.S, self.nseq
        QT = self.dscr("QT", [1024, T], BF16)
        KT = self.dscr("KT", [1024, T], BF16)
        V = self.dscr("Vtm", [T, 2048], BF16)
        O = self.dscr("Otm", [T, 2048], BF16)
        with kb.scope():
            Wv = load_weights(kb, gr, w_in, 1024, 3072, gcol)
            segs = [dict(kind="FM", n0=0, n1=1024, epi=self.epi.fm_store(QT, 0.125)),
                    dict(kind="FM", n0=1024, n1=2048, epi=self.epi.fm_store(KT, 1.0)),
                    dict(kind="TM", n0=2048, n1=3072, epi=self.epi.tm_store(V))]
            gemm_stage(kb, gr, h, T, 1024, Wv, segs, norm=True)
            kb.barrier()
        with kb.scope():
            spec = DiffSpec(self, S, QT, KT, V, O, lam_vecs, lam_init, norm_g, BT, bfar, dmask)
            attn_stage(kb, spec, S, nseq)
            kb.barrier()
        with kb.scope():
            Wv = load_weights(kb, gr, w_out, 1024, 1024, None)
            segs = [dict(kind="TM", n0=0, n1=1024, epi=self.epi.tm_resid(h))]
            gemm_stage(kb, gr, O[:, 0:1024], T, 1024, Wv, segs, x_bf16=True)
            kb.barrier()


class DiffSpec:
    def __init__(self, m, S, QT, KT, V, O, lam_vecs, lam_init, norm_g, BT, bfar, dmask):
        kb = m.kb
        self.kb, self.gr, self.S = kb, m.gr, S
        self.QT, self.KT, self.V, self.O = QT, KT, V, O
        self.Qa = [[kb.sb("dQ%d_%d" % (i, mm), [64, S], BF16) for mm in range(2)] for i in range(2)]
        self.Ka = [[kb.sb("dK%d_%d" % (i, mm), [64, S], BF16) for mm in range(2)] for i in range(2)]
        self.Vt = [kb.sb("dV%d" % i, [128, (S // 128) * 129], BF16) for i in range(2)]
        for i in range(2):
            kb.memset(self.Vt[i][:], 1.0, e="pool")
        lv = kb.sb("dlv", [128, 256], F32)
        kb.dma(lv[:], lam_vecs.rearrange("a d -> (a d)").partition_broadcast(128))
        pr = kb.sb("dpr", [128, 128], F32)
        lvv = lv[:].rearrange("p (a d) -> p a d", a=4)
        prv = pr[:].rearrange("p (a d) -> p a d", a=2)
        kb.tt(prv[:, 0, :], lvv[:, 0, :], lvv[:, 1, :], ALU.mult)
        kb.tt(prv[:, 1, :], lvv[:, 2, :], lvv[:, 3, :], ALU.mult)
        l2 = kb.sb("dl2", [128, 2], F32)
        kb.red(l2[:, 0:1], prv[:, 0, :])
        kb.red(l2[:, 1:2], prv[:, 1, :])
        kb.act(l2[:], l2[:], AF.Exp)
        self.nlam = kb.sb("dnlam", [128, 1], F32)
        kb.tt(self.nlam[:], l2[:, 1:2], l2[:, 0:1], ALU.subtract)
        kb.ts(self.nlam[:], self.nlam[:], -float(lam_init), None, op0=ALU.add)
        self.gsc = kb.sb("dgsc", [128, 128], F32)
        kb.dma(self.gsc[:], norm_g.partition_broadcast(128))
        kb.ts(self.gsc[:], self.gsc[:], 1.0 - float(lam_init), None, op0=ALU.mult)
        self.Bhl = kb.sb("dBhl", [128, 8 * 2 * 2 * 128], BF16)
        self.Bv = self.Bhl[:].rearrange("p (h d s q) -> p h d s q", h=8, d=2, s=2)
        self.bfar = kb.sb("dbfar", [128, 8], F32)
        kb.dma(self.bfar[:], bfar.partition_broadcast(128))
        self.ones2 = kb.sb("dones2", [2, 128], BF16)
        kb.memset(self.ones2[:], 1.0)
        self.cfar = kb.sb("dcfar", [2, 8 * 128], BF16)
        with kb.scope():
            bt = kb.sb("dbt", [128, 8 * 2 * 128], F32)
            btv = bt[:].rearrange("p (h d q) -> p h d q", h=8, d=2)
            kb.dma(btv, BT.rearrange("h d k q -> k h d q"))
            dm = kb.sb("ddm", [128, 128], F32)
            kb.dma(dm[:], dmask)
            rr = kb.sb("drr", [128, 128], F32)
            for hh in range(8):
                kb.tt(btv[:, hh, 0, :], btv[:, hh, 0, :], dm[:], ALU.add)
                for d in range(2):
                    kb.cp(self.Bv[:, hh, d, 0, :], btv[:, hh, d, :])
                    kb.tt(rr[:], btv[:, hh, d, :], self.Bv[:, hh, d, 0, :], ALU.subtract)
                    kb.cp(self.Bv[:, hh, d, 1, :], rr[:])
            cf = kb.sb("dcf", [2, 8], F32)
            kb.dma(cf[0:1, :], bfar.rearrange("(o h) -> o h", o=1))
            kb.dma(cf[1:2, :], bfar.rearrange("(o h) -> o h", o=1))
            cfb = kb.sb("dcfb", [2, 8], BF16)
            kb.cp(cfb[:], cf[:])
            cr = kb.sb("dcr", [2, 8], F32)
            kb.tt(cr[:], cf[:], cfb[:], ALU.subtract)
            cfl = kb.sb("dcfl", [2, 8], BF16)
            kb.cp(cfl[:], cr[:])
            kb.dma(cfb[1:2, :], cfl[1:2, :])
            cfv = self.cfar[:].rearrange("p (h q) -> p h q", h=8)
            kb.cp(cfv, cfb[:].unsqueeze(2).to_broadcast([2, 8, 128]))
            kb.barrier()
        self.rz = [kb.sb("drz%d" % i, [128, 2], F32) for i in range(3)]
        self.o = [kb.sb("do%d" % i, [128, 128], F32) for i in range(3)]
        self.sq = kb.sb("dsq", [128, 128], F32)
        self.ss = [kb.sb("dss%d" % i, [128, 1], F32) for i in range(3)]
        self.ost = [kb.sb("dost%d" % i, [128, 128], BF16) for i in range(3)]
        self.pi = 0

    def groups(self, seq):
        return [(seq, h) for h in range(8)]

    def subheads(self, g):
        return [0, 1]

    def load(self, g, buf):
        kb, S = self.kb, self.S
        seq, h = g
        t0 = seq * S
        for mm in range(2):
            r0 = 128 * h + 64 * mm
            kb.dma(self.Qa[buf][mm][:], self.QT[r0:r0 + 64, t0:t0 + S])
            kb.dma(self.Ka[buf][mm][:], self.KT[r0:r0 + 64, t0:t0 + S])
        vt = self.Vt[buf][:].rearrange("p (b d) -> p b d", d=129)
        kb.dma(vt[:, :, 0:128], self.V[t0:t0 + S, 128 * h:128 * h + 128].rearrange("(b p) d -> p b d", p=128))

    def qk(self, g, sh, j, i, ps, buf):
        kb = self.kb
        seq, h = g
        r = i - 4 * j
        kb.mm(ps[:], self.Ka[buf][sh][:, i * 128:(i + 1) * 128], self.Qa[buf][sh][:, j * 512:(j + 1) * 512],
              start=True, stop=(r < -1))
        if r >= -1:
            for qb in range(4):
                d = r - qb
                o_ = ps[:, qb * 128:(qb + 1) * 128]
                if d <= -2:
                    kb.mm(o_, self.ones2[:], self.cfar[:, h * 128:(h + 1) * 128], start=False, stop=False)
                elif d <= 0:
                    kb.mm(o_, self.gr.ident[:], self.Bv[:, h, -d, 0, :], start=False, stop=False)
                    kb.mm(o_, self.gr.ident[:], self.Bv[:, h, -d, 1, :], start=False, stop=False)

    def evac(self, g, sh, j, i, ps, P, buf):
        seq, h = g
        if i - 4 * j < -1:
            self.kb.act(P[:], ps[:], AF.Exp, bias=self.bfar[:, h:h + 1])
        else:
            self.kb.act(P[:], ps[:], AF.Exp)

    def ocols(self, g, sh):
        return 256 * sh, 129

    def vblk(self, g, sh, i, buf):
        return self.Vt[buf][:, i * 129:(i + 1) * 129]

    def post(self, g, j, outs, buf):
        kb, S = self.kb, self.S
        seq, h = g
        for qb in range(4):
            k = self.pi % 3
            self.pi += 1
            rz, o, ss, st = self.rz[k], self.o[k], self.ss[k], self.ost[k]
            ob = outs[qb]
            kb.recip(rz[:, 0:1], ob[:, 128:129])
            kb.recip(rz[:, 1:2], ob[:, 384:385])
            kb.tt(rz[:, 1:2], rz[:, 1:2], self.nlam[:], ALU.mult)
            kb.ts(o[:], ob[:, 0:128], rz[:, 0:1], None, op0=ALU.mult)
            kb.stt(o[:], ob[:, 256:384], rz[:, 1:2], o[:], ALU.mult, ALU.add)
            kb.tt(self.sq[:], o[:], o[:], ALU.mult, e="pool")
            kb.red(ss[:], self.sq[:])
            kb.act(ss[:], ss[:], AF.Sqrt, bias=self.gr.epsc[:, 0:1], scale=1.0 / 128)
            kb.recip(ss[:], ss[:])
            kb.stt(st[:], o[:], ss[:, 0:1], self.gsc[:], ALU.mult, ALU.mult)
            tok = seq * S + (4 * j + qb) * 128
            kb.dma(self.O[tok:tok + 128, 128 * h:128 * h + 128], st[:], q="pool")


def _t5_bucket_np(rel):
    nb = 16
    max_exact = 8
    ret = (rel > 0).astype(np.int64) * nb
    n = np.abs(rel)
    nf = np.maximum(n, 1).astype(np.float32)
    large = max_exact + (np.log(nf / max_exact) / math.log(128 / max_exact) * (nb - max_exact)).astype(np.int32)
    large = np.minimum(large, nb - 1)
    return ret + np.where(n < max_exact, n, large)


def diff_tables(rel_bias):
    kl = np.arange(128)[:, None]
    ql = np.arange(128)[None, :]
    idx0 = _t5_bucket_np(kl - ql)
    idx1 = _t5_bucket_np(kl - ql - 128)
    rb = np.asarray(rel_bias, dtype=np.float32)
    BT = np.stack([rb[idx0], rb[idx1]], axis=0)
    BT = np.ascontiguousarray(BT.transpose(3, 0, 1, 2))
    bfar = np.ascontiguousarray(rb[15, :])
    dmask = np.where((kl >= 64) & (ql < 64), NEG, 0.0).astype(np.float32)
    return BT, bfar, dmask


def cmask_np():
    k = np.arange(128)[:, None]
    q = np.arange(512)[None, :]
    return np.concatenate([np.where(128 * r + k <= q, 0.0, NEG) for r in range(4)], axis=1).astype(np.float32)


RET_LG = [math.log1p(-2.0 ** (-5.0 - h)) for h in range(4)]


def ret_tables(S):
    d = np.arange(0, 256, 2, dtype=np.float32) / np.float32(256.0)
    inv = (1.0 / (np.float32(10000.0) ** d)).astype(np.float32)
    ang = np.arange(S, dtype=np.float32)[None, :] * inv[:, None]
    cosT = np.cos(ang).astype(np.float32)
    sinT = np.sin(ang).astype(np.float32)
    t = np.arange(512)
    dq = np.stack([np.exp(RET_LG[h] * t) for h in range(4)]).astype(np.float32)
    dk = np.stack([np.exp(-RET_LG[h] * (t % 128)) / 16.0 for h in range(4)]).astype(np.float32)
    kl = np.arange(128)[:, None]
    q = np.arange(512)[None, :]
    mret = np.zeros((4, 4, 128, 512), np.float32)
    for h in range(4):
        for r in range(4):
            k = 128 * r + kl
            ck, cq = k // 64, q // 64
            f = np.where(ck < cq, 1.0, np.where(ck > cq, 0.0, np.where(k <= q, 1.0, np.exp(2.0 * RET_LG[h] * (k - q)))))
            mret[h, r] = f * np.exp(-RET_LG[h] * 128.0 * r)
    return cosT, sinT, dq, dk, mret


class RetSpec:
    nbuf = 1

    def __init__(self, m, S, QT, KT, V, GS, O, norm_g, mret):
        kb = m.kb
        self.kb, self.gr, self.S = kb, m.gr, S
        self.QT, self.KT, self.V, self.GS, self.O = QT, KT, V, GS, O
        self.Q = [kb.sb("rQ%d" % c, [128, S], BF16) for c in range(2)]
        self.K = [kb.sb("rK%d" % c, [128, S], BF16) for c in range(2)]
        self.Vt = kb.sb("rV", [128, (S // 128) * 512], BF16)
        self.M = kb.sb("rM", [128, 16 * 512], F32)
        kb.dma(self.M[:].rearrange("p (a q) -> p a q", a=16), mret.rearrange("h r k q -> k (h r) q"))
        self.gb = kb.sb("rgb", [128, 2048], F32)
        kb.dma(self.gb[:], norm_g.partition_broadcast(128))
        self.sq = kb.sb("rsq", [128, 512], F32)
        self.ss = [kb.sb("rss%d" % i, [128, 1], F32) for i in range(2)]
        self.on = [kb.sb("ron%d" % i, [128, 512], F32) for i in range(2)]
        self.gs = [kb.sb("rgs%d" % i, [128, 512], BF16) for i in range(2)]
        self.ost = [kb.sb("rost%d" % i, [128, 512], BF16) for i in range(2)]
        self.pi = 0

    def groups(self, seq):
        return [(seq, h) for h in range(4)]

    def subheads(self, g):
        return [0]

    def load(self, g, buf):
        kb, S = self.kb, self.S
        seq, h = g
        t0 = seq * S
        for c in range(2):
            r0 = 256 * h + 128 * c
            kb.dma(self.Q[c][:], self.QT[r0:r0 + 128, t0:t0 + S])
            kb.dma(self.K[c][:], self.KT[r0:r0 + 128, t0:t0 + S])
        kb.dma(self.Vt[:].rearrange("p (b d) -> p b d", d=512),
               self.V[t0:t0 + S, 512 * h:512 * h + 512].rearrange("(b p) d -> p b d", p=128))

    def qk(self, g, sh, j, i, ps, buf):
        kb = self.kb
        for c in range(2):
            kb.mm(ps[:], self.K[c][:, i * 128:(i + 1) * 128], self.Q[c][:, j * 512:(j + 1) * 512],
                  start=(c == 0), stop=(c == 1))

    def evac(self, g, sh, j, i, ps, P, buf):
        seq, h = g
        r = i - 4 * j
        if r < 0:
            self.kb.act(P[:], ps[:], AF.Copy, scale=float(math.exp(RET_LG[h] * (512 * j - 128 * i))))
        else:
            a = h * 4 + r
            self.kb.tt(P[:], ps[:], self.M[:, a * 512:(a + 1) * 512], ALU.mult)

    def ocols(self, g, sh):
        return 0, 512

    def vblk(self, g, sh, i, buf):
        return self.Vt[:, i * 512:(i + 1) * 512]

    def post(self, g, j, outs, buf):
        kb, S = self.kb, self.S
        seq, h = g
        for qb in range(4):
            k = self.pi % 2
            self.pi += 1
            ob = outs[qb]
            ss, on, gs, st = self.ss[k], self.on[k], self.gs[k], self.ost[k]
            tok = seq * S + (4 * j + qb) * 128
            kb.dma(gs[:], self.GS[tok:tok + 128, 512 * h:512 * h + 512])
            kb.act(self.sq[:], ob[:], AF.Square)
            kb.red(ss[:], self.sq[:])
            kb.act(ss[:], ss[:], AF.Sqrt, bias=self.gr.epsc[:, 0:1], scale=1.0 / 512)
            kb.recip(ss[:], ss[:])
            kb.stt(on[:], ob[:], ss[:, 0:1], self.gb[:, 512 * h:512 * h + 512], ALU.mult, ALU.mult)
            kb.tt(st[:], on[:], gs[:], ALU.mult, e="pool")
            kb.dma(self.O[tok:tok + 128, 512 * h:512 * h + 512], st[:], q="pool")


def ret_mixer(self, h, w_in, gcol, norm_g, w_out, cosT, sinT, dq, dk, mret):
    kb, gr, T, S, nseq = self.kb, self.gr, self.T, self.S, self.nseq
    QT = self.dscr("QT", [1024, T], BF16)
    KT = self.dscr("KT", [1024, T], BF16)
    V = self.dscr("Vtm", [T, 2048], BF16)
    GS = self.dscr("GStm", [T, 2048], BF16)
    O = self.dscr("Otm", [T, 2048], BF16)
    with kb.scope():
        Wv = load_weights(kb, gr, w_in, 1024, 6144, gcol)
        cs = [kb.sb("rcs%d" % i, [128, 1024], F32) for i in range(2)]
        dtab = kb.sb("rdtab", [128, 2 * 4 * 512], F32)
        dtv = dtab[:].rearrange("p (a h t) -> p a h t", a=2, h=4)
        kb.dma(dtv[:, 0], dq.partition_broadcast(128))
        kb.dma(dtv[:, 1], dk.partition_broadcast(128))
        tmp = [kb.sb("rtmp%d" % i, [128, 512], F32) for i in range(4)]
        yst = [kb.sb("ryst%d" % i, [128, 512], BF16) for i in range(4)]
        state = dict(blk=-1, n=0)

        def rot_epi(which, dst):
            def epi(blk, tok0, c0, pss):
                if state["blk"] != blk:
                    state["blk"] = blk
                    p0 = tok0 % S
                    c_ = cs[blk % 2]
                    kb.dma(c_[:, 0:512], cosT[:, p0:p0 + 512])
                    kb.dma(c_[:, 512:1024], sinT[:, p0:p0 + 512])
                c_ = cs[blk % 2]
                cosb, sinb = c_[:, 0:512], c_[:, 512:1024]
                hh = c0 // 2
                d_ = dtv[:, which, hh, :]
                x1, x2 = pss[0], pss[1]
                n = state["n"]
                state["n"] += 1
                ta, tb = tmp[(n % 2) * 2], tmp[(n % 2) * 2 + 1]
                y1, y2 = yst[(n % 2) * 2], yst[(n % 2) * 2 + 1]
                kb.tt(ta[:], x1[:], cosb, ALU.mult)
                kb.tt(tb[:], x2[:], sinb, ALU.mult)
                kb.tt(ta[:], ta[:], tb[:], ALU.subtract, e="pool")
                kb.tt(y1[:], ta[:], d_, ALU.mult, e="pool")
                kb.dma(dst[256 * hh:256 * hh + 128, tok0:tok0 + 512], y1[:], q="sp")
                kb.tt(ta[:], x1[:], sinb, ALU.mult)
                kb.tt(tb[:], x2[:], cosb, ALU.mult)
                kb.tt(ta[:], ta[:], tb[:], ALU.add, e="pool")
                kb.tt(y2[:], ta[:], d_, ALU.mult, e="pool")
                kb.dma(dst[256 * hh + 128:256 * hh + 256, tok0:tok0 + 512], y2[:], q="sp")
            return epi

        segs = [dict(kind="FM", n0=0, n1=1024, group=2, epi=rot_epi(0, QT)),
                dict(kind="FM", n0=1024, n1=2048, group=2, epi=rot_epi(1, KT)),
                dict(kind="TM", n0=2048, n1=4096, epi=self.epi.tm_store(V)),
                dict(kind="TM", n0=4096, n1=6144, epi=self.epi.tm_store(GS, func=AF.Silu))]
        gemm_stage(kb, gr, h, T, 1024, Wv, segs, norm=True)
        kb.barrier()
    with kb.scope():
        spec = RetSpec(self, S, QT, KT, V, GS, O, norm_g, mret)
        attn_stage(kb, spec, S, nseq)
        kb.barrier()
    with kb.scope():
        Wv = load_weights(kb, gr, w_out, 2048, 1024, None)
        segs = [dict(kind="TM", n0=0, n1=1024, epi=self.epi.tm_resid(h))]
        gemm_stage(kb, gr, O, T, 2048, Wv, segs, x_bf16=True)
        kb.barrier()


Model.ret_mixer = ret_mixer


def ssd_tables():
    sa = np.zeros((48, 1024), np.float32)
    for hh in range(8):
        sa[6 * hh:6 * hh + 3, hh * 128:(hh + 1) * 128] = 1.0
    return sa


class SsdSpec:
    nbuf = 1

    def __init__(self, m, S, AQK, BTd, CTd, XSd, DTd, ZS, O, dsk_rep, norm_g, sa_init):
        kb = m.kb
        self.kb, self.gr, self.S, self.m = kb, m.gr, S, m
        self.AQK, self.BTd, self.CTd, self.XSd, self.DTd, self.ZS, self.O = AQK, BTd, CTd, XSd, DTd, ZS, O
        nb = S // 128
        self.nb = nb
        self.TA = kb.sb("sTA", [48, S], BF16)
        kb.memset(self.TA[:], 1.0)
        self.SA = [kb.sb("sSA%d" % i, [48, 1024], BF16) for i in range(3)]
        with kb.scope():
            t = kb.sb("ssat", [48, 1024], F32)
            kb.dma(t[:], sa_init)
            for i in range(3):
                kb.cp(self.SA[i][:], t[:])
            kb.barrier()
        self.Bt = kb.sb("sBt", [128, S], BF16)
        self.Ct = kb.sb("sCt", [128, S], BF16)
        self.XS = kb.sb("sXS", [128, nb * 512], BF16)
        self.XP = kb.sb("sXP", [128, nb * 512], BF16)
        self.DTt = kb.sb("sDT", [128, nb * 8], F32)
        self.E = [kb.sb("sE%d" % i, [128, 512], F32) for i in range(2)]
        self.cbp = [kb.ps("scb%d" % i, [128, 512], F32) for i in range(2)]
        self.dsk = kb.sb("sdsk", [128, 2048], F32)
        kb.dma(self.dsk[:], dsk_rep.partition_broadcast(128))
        self.gn = kb.sb("sgn", [128, 2048], F32)
        kb.dma(self.gn[:], norm_g.partition_broadcast(128))
        self.y1 = [kb.sb("sy1%d" % i, [128, 512], F32) for i in range(2)]
        self.y3 = [kb.sb("sy3%d" % i, [128, 512], F32) for i in range(2)]
        self.zs = [kb.sb("szs%d" % i, [128, 512], BF16) for i in range(2)]
        self.ost = [kb.sb("sost%d" % i, [128, 512], BF16) for i in range(2)]
        self.sq = kb.sb("ssq", [128, 512], F32)
        self.ss = [kb.sb("sss%d" % i, [128, 1], F32) for i in range(2)]
        self.pi = 0
        self.ni = 0
        self.ne = 0

    def groups(self, seq):
        return [(seq, g) for g in range(4)]

    def subheads(self, g):
        return list(range(8))

    def load(self, g, buf):
        kb, S, nb = self.kb, self.S, self.nb
        seq, gg = g
        t0 = seq * S
        for hh in range(8):
            kb.dma(self.TA[6 * hh:6 * hh + 3, :], self.AQK[gg * 8 + hh, 0:3, t0:t0 + S])
        kb.dma(self.Bt[:], self.BTd[gg * 128:(gg + 1) * 128, t0:t0 + S])
        kb.dma(self.Ct[:], self.CTd[gg * 128:(gg + 1) * 128, t0:t0 + S])
        kb.dma(self.XS[:].rearrange("p (b d) -> p b d", d=512),
               self.XSd[t0:t0 + S, gg * 512:(gg + 1) * 512].rearrange("(b p) d -> p b d", p=128))
        kb.dma(self.DTt[:].rearrange("p (b h) -> p b h", h=8),
               self.DTd[t0:t0 + S, gg * 8:(gg + 1) * 8].rearrange("(b p) h -> p b h", p=128))
        kb.tt(self.XP[:].rearrange("p (a d) -> p a d", d=64), self.XS[:].rearrange("p (a d) -> p a d", d=64),
              self.DTt[:].unsqueeze(2).to_broadcast([128, nb * 8, 64]), ALU.mult)

    def pre(self, g, j, i, buf):
        kb, S = self.kb, self.S
        seq, gg = g
        t0 = seq * S
        sa = self.SA[self.ni % 3]
        cb = self.cbp[self.ni % 2]
        self.ni += 1
        for hh in range(8):
            kb.dma(sa[6 * hh + 3:6 * hh + 6, hh * 128:(hh + 1) * 128],
                   self.AQK[gg * 8 + hh, 3:6, t0 + i * 128:t0 + (i + 1) * 128])
        kb.mm(cb[:], self.Bt[:, i * 128:(i + 1) * 128], self.Ct[:, j * 512:(j + 1) * 512])
        self.cur = (sa, cb)

    def qk(self, g, sh, j, i, ps, buf):
        kb = self.kb
        sa, cb = self.cur
        r = i - 4 * j
        kb.mm(ps[:], sa[:, sh * 128:(sh + 1) * 128], self.TA[:, j * 512:(j + 1) * 512], start=True, stop=(r < 0))
        if r >= 0:
            kb.mm(ps[:], self.gr.ident[:], self.m.cmask[:, r * 512:(r + 1) * 512], start=False, stop=True)

    def evac(self, g, sh, j, i, ps, P, buf):
        kb = self.kb
        sa, cb = self.cur
        E = self.E[self.ne % 2]
        self.ne += 1
        kb.act(E[:], ps[:], AF.Exp)
        kb.tt(P[:], E[:], cb[:], ALU.mult)

    def ocols(self, g, sh):
        return sh * 64, 64

    def vblk(self, g, sh, i, buf):
        return self.XP[:, i * 512 + sh * 64:i * 512 + sh * 64 + 64]

    def post(self, g, j, outs, buf):
        kb, S = self.kb, self.S
        seq, gg = g
        for qb in range(4):
            k = self.pi % 2
            self.pi += 1
            b = 4 * j + qb
            tok = seq * S + b * 128
            y1, y3, zs, st, ss = self.y1[k], self.y3[k], self.zs[k], self.ost[k], self.ss[k]
            kb.dma(zs[:], self.ZS[tok:tok + 128, gg * 512:(gg + 1) * 512])
            kb.tt(y1[:], self.XS[:, b * 512:(b + 1) * 512], self.dsk[:, gg * 512:(gg + 1) * 512], ALU.mult, e="pool")
            kb.tt(y1[:], outs[qb][:], y1[:], ALU.add)
            kb.tt(y3[:], y1[:], zs[:], ALU.mult, e="pool")
            kb.act(self.sq[:], y3[:], AF.Square)
            kb.red(ss[:], self.sq[:])
            kb.act(ss[:], ss[:], AF.Sqrt, bias=self.gr.epsc[:, 0:1], scale=1.0 / 512)
            kb.recip(ss[:], ss[:])
            kb.stt(st[:], y3[:], ss[:, 0:1], self.gn[:, gg * 512:(gg + 1) * 512], ALU.mult, ALU.mult)
            kb.dma(self.O[tok:tok + 128, gg * 512:(gg + 1) * 512], st[:], q="pool")


def ssd_mixer(self, h, w_in, gcol, conv_wT, conv_b, dt_bias, a_log, dsk_rep, norm_g, w_out, sa_init):
    kb, gr, T, S, nseq = self.kb, self.gr, self.T, self.S, self.nseq
    nc = self.nc
    ZS = self.dscr("GStm", [T, 2048], BF16)
    XSd = self.dscr("Vtm", [T, 2048], BF16)
    BTd = self.dscr("QT", [1024, T], BF16)
    CTd = self.dscr("KT", [1024, T], BF16)
    AQK = self.dscr("CQK", [32, 6, T], BF16)
    DTd = self.dscr("DTd", [T, 32], F32)
    O = self.dscr("Otm", [T, 2048], BF16)
    bps = S // 512
    with kb.scope():
        Wv = load_weights(kb, gr, w_in, 1024, 5152, gcol)
        cw = kb.sb("scw", [128, 24 * 4], F32)
        cwv = cw[:].rearrange("p (c k) -> p c k", k=4)
        kb.dma(cwv, conv_wT.rearrange("(c p) k -> p c k", p=128))
        cbias = kb.sb("scbias", [128, 24], F32)
        kb.dma(cbias[:], conv_b.rearrange("(c p) -> p c", p=128))
        hal = kb.sb("shal", [128, 24 * 3], F32)
        cbuf = [kb.sb("scbuf%d" % i, [128, 515], F32) for i in range(2)]
        accb = [kb.sb("saccb%d" % i, [128, 512], F32) for i in range(2)]
        sil = [kb.sb("ssil%d" % i, [128, 512], BF16) for i in range(2)]
        xst = [kb.sb("sxst%d" % i, [128, 512], BF16) for i in range(2)]
        ptx = kb.ps("sptx", [128, 1024], BF16)
        pdt = kb.ps("spdt", [128, 512], F32)
        dtb = kb.sb("sdtb", [32, 1], F32)
        kb.dma(dtb[:], dt_bias.rearrange("(p o) -> p o", o=1))
        nA = kb.sb("snA", [32, 1], F32)
        kb.dma(nA[:], a_log.rearrange("(p o) -> p o", o=1))
        kb.act(nA[:], nA[:], AF.Exp)
        kb.ts(nA[:], nA[:], -1.0, None, op0=ALU.mult)
        carry = kb.sb("scarry", [32, 1], F32)
        d_ = [kb.sb("sd%d" % i, [32, 512], F32) for i in range(5)]
        cum = kb.sb("scum", [32, 512], F32)
        rr = kb.sb("srr", [32, 512], F32)
        spl = kb.sb("sspl", [32, 6 * 512], BF16)
        splv = spl[:].rearrange("p (s n) -> p s n", s=6)
        dtm = kb.sb("sdtm", [128, 4 * 32], F32)
        cn = [0]

        def conv_epi(blk, tok0, c, pss):
            ps = pss[0]
            n = cn[0]
            cn[0] += 1
            cb, acc, sl = cbuf[n % 2], accb[n % 2], sil[n % 2]
            if blk % bps == 0:
                kb.memset(cb[:, 0:3], 0.0)
            else:
                kb.cp(cb[:, 0:3], hal[:, 3 * c:3 * c + 3])
            kb.act(cb[:, 3:515], ps[:], AF.Copy)
            kb.cp(hal[:, 3 * c:3 * c + 3], cb[:, 512:515])
            kb.ts(acc[:], cb[:, 0:512], cwv[:, c, 0:1], cbias[:, c:c + 1], op0=ALU.mult, op1=ALU.add)
            for k in range(1, 4):
                kb.stt(acc[:], cb[:, k:k + 512], cwv[:, c, k:k + 1], acc[:], ALU.mult, ALU.add)
            kb.act(sl[:], acc[:], AF.Silu)
            if c < 16:
                for s_ in range(4):
                    kb.tr(ptx[:, s_ * 128:(s_ + 1) * 128], sl[:, s_ * 128:(s_ + 1) * 128], gr.ident[:])
                xs_ = xst[n % 2]
                kb.cp(xs_[:], ptx[:, 0:512])
                kb.dma(XSd[tok0:tok0 + 512, c * 128:(c + 1) * 128].rearrange("(s p) ch -> p s ch", p=128),
                       xs_[:].rearrange("p (s ch) -> p s ch", s=4), q="pool")
            elif c < 20:
                kb.dma(BTd[(c - 16) * 128:(c - 15) * 128, tok0:tok0 + 512], sl[:], q="pool")
            else:
                kb.dma(CTd[(c - 20) * 128:(c - 19) * 128, tok0:tok0 + 512], sl[:], q="pool")

        def dt_epi(blk, tok0, c0, pss):
            ps = pss[0]
            xb, ab, e_, r_, dtt = d_
            if blk % bps == 0:
                kb.memset(carry[:], 0.0)
            kb.act(xb[:], ps[0:32, :], AF.Identity, bias=dtb[:, 0:1])
            kb.act(ab[:], xb[:], AF.Abs)
            kb.act(e_[:], ab[:], AF.Exp, scale=-1.0)
            kb.act(e_[:], e_[:], AF.Ln, bias=self.one[0:32, 0:1])
            kb.ts(r_[:], xb[:], 0.0, None, op0=ALU.max)
            kb.tt(dtt[:], r_[:], e_[:], ALU.add)
            kb.ts(ab[:], dtt[:], nA[:, 0:1], None, op0=ALU.mult)
            kb.op("dve", lambda: nc.vector.tensor_tensor_scan(cum[:], ab[:], self.zeros[0:32, :], carry[:, 0:1], ALU.add, ALU.add),
                  [ab, self.zeros, carry], [cum])
            kb.cp(carry[:], cum[:, 511:512])
            split3(kb, cum[:], splv, rr, None, 32, 512)
            kb.dma(AQK[0:32, :, tok0:tok0 + 512], splv, q="pool")
            for s_ in range(4):
                kb.tr(pdt[:, s_ * 32:(s_ + 1) * 32], dtt[:, s_ * 128:(s_ + 1) * 128], gr.ident_f[0:32, 0:32])
            kb.cp(dtm[:], pdt[:, 0:128])
            kb.dma(DTd[tok0:tok0 + 512, :].rearrange("(s p) h -> p s h", p=128), dtm[:].rearrange("p (s h) -> p s h", s=4), q="pool")

        segs = [dict(kind="TM", n0=0, n1=2048, epi=self.epi.tm_store(ZS, func=AF.Silu)),
                dict(kind="FM", n0=2048, n1=5120, epi=conv_epi),
                dict(kind="FM", n0=5120, n1=5152, epi=dt_epi)]
        gemm_stage(kb, gr, h, T, 1024, Wv, segs, norm=True)
        kb.barrier()
    with kb.scope():
        spec = SsdSpec(self, S, AQK, BTd, CTd, XSd, DTd, ZS, O, dsk_rep, norm_g, sa_init)
        attn_stage(kb, spec, S, nseq)
        kb.barrier()
    with kb.scope():
        Wv = load_weights(kb, gr, w_out, 2048, 1024, None)
        segs = [dict(kind="TM", n0=0, n1=1024, epi=self.epi.tm_resid(h))]
        gemm_stage(kb, gr, O, T, 2048, Wv, segs, x_bf16=True)
        kb.barrier()


Model.ssd_mixer = ssd_mixer


DEPTH = 4
SEQ = 4096
NSEQ = 2


def build_program(nseq=NSEQ, S=SEQ):
    T = nseq * S
    nc = bass.Bass("TRN2", target_bir_lowering=False)
    es = ExitStack()
    m = Model(nc, es, nseq, S)
    I = m.inp
    I("ident", [128, 128]); I("cmask", [128, 2048])
    x = I("x", [T, 1024]); p = I("p", [DEPTH, T, 256])
    mix_norm = I("mix_norm", [DEPTH, 1024]); ffn_norm = I("ffn_norm", [DEPTH, 1024]); ple_norm = I("ple_norm", [DEPTH, 1024])
    final_norm = I("final_norm", [1024])
    ssd_w_in = I("ssd_w_in", [1024, 5152]); ssd_cwT = I("ssd_cwT", [3072, 4]); ssd_cb = I("ssd_cb", [3072])
    ssd_dtb = I("ssd_dtb", [32]); ssd_alog = I("ssd_alog", [32]); ssd_dsk = I("ssd_dsk", [2048]); ssd_norm = I("ssd_norm", [2048])
    ssd_w_out = I("ssd_w_out", [2048, 1024]); ssd_sa = I("ssd_sa", [48, 1024])
    ret_w_in = I("ret_w_in", [1024, 6144]); ret_norm = I("ret_norm", [2048]); ret_w_out = I("ret_w_out", [2048, 1024])
    ret_cos = I("ret_cos", [128, S]); ret_sin = I("ret_sin", [128, S]); ret_dq = I("ret_dq", [4, 512]); ret_dk = I("ret_dk", [4, 512])
    ret_m = I("ret_m", [4, 4, 128, 512])
    diff_w_in = I("diff_w_in", [1024, 3072]); diff_lam = I("diff_lam", [4, 64]); diff_norm = I("diff_norm", [128])
    diff_w_out = I("diff_w_out", [1024, 1024]); diff_BT = I("diff_BT", [8, 2, 128, 128]); diff_bfar = I("diff_bfar", [8])
    diff_mask = I("diff_mask", [128, 128])
    fox_w_in = I("fox_w_in", [1024, 3088]); fox_bf = I("fox_bf", [16]); fox_w_out = I("fox_w_out", [1024, 1024])
    peer_wq = I("peer_wq", [DEPTH, 1024, 2048]); peer_kT = I("peer_kT", [DEPTH, 16, 128, 128])
    peer_uT = I("peer_uT", [DEPTH, 1024, 16384]); peer_v = I("peer_v", [DEPTH, 16384, 1024])
    ple_proj = I("ple_proj", [DEPTH, 256, 1024]); ple_gate = I("ple_gate", [DEPTH, 1024, 1024])
    out = nc.dram_tensor("out", [T, 1024], F32, kind="ExternalOutput").ap()
    hb = m.dscr("hbuf", [T, 1024], F32)
    m.setup()
    kb = m.kb
    for r0 in range(0, T, 2048):
        kb.dma(hb[r0:r0 + 2048, :], x[r0:r0 + 2048, :])
    kb.barrier()
    for i in range(DEPTH):
        if i == 0:
            m.ssd_mixer(hb, ssd_w_in, mix_norm[i], ssd_cwT, ssd_cb, ssd_dtb, ssd_alog, ssd_dsk, ssd_norm, ssd_w_out, ssd_sa)
        elif i == 1:
            m.ret_mixer(hb, ret_w_in, mix_norm[i], ret_norm, ret_w_out, ret_cos, ret_sin, ret_dq, ret_dk, ret_m)
        elif i == 2:
            lam_init = 0.8 - 0.6 * math.exp(-0.3 * i)
            m.diff_mixer(hb, diff_w_in, mix_norm[i], diff_lam, lam_init, diff_norm, diff_w_out, diff_BT, diff_bfar, diff_mask)
        else:
            m.fox_mixer(hb, fox_w_in, mix_norm[i], fox_bf, fox_w_out)
        m.peer(hb, ffn_norm[i], peer_wq[i], peer_kT[i], peer_uT[i], peer_v[i])
        m.ple(hb, ple_norm[i], ple_gate[i], p[i], ple_proj[i])
    m.final(hb, final_norm, out)
    es.close()
    return nc, m


def host_inputs(inputs, nseq=NSEQ, S=SEQ, ncores=NCORES):
    f = lambda a: np.ascontiguousarray(np.asarray(a, dtype=np.float32))
    g = {k: np.asarray(v) for k, v in inputs.items()}
    BT, bfar, dmask = diff_tables(g["rel_bias"])
    cosT, sinT, dq, dk, mret = ret_tables(S)
    shared = {
        "ident": np.eye(128, dtype=np.float32), "cmask": cmask_np(),
        "mix_norm": f(g["mix_norm"]), "ffn_norm": f(g["ffn_norm"]), "ple_norm": f(g["ple_norm"]), "final_norm": f(g["final_norm"]),
        "ssd_w_in": f(g["ssd_w_in"][0]), "ssd_cwT": f(g["ssd_conv_w"][0][:, 0, :].T), "ssd_cb": f(g["ssd_conv_b"][0]),
        "ssd_dtb": f(g["ssd_dt_bias"][0]), "ssd_alog": f(g["ssd_a_log"][0]), "ssd_dsk": f(np.repeat(g["ssd_d"][0], 64)),
        "ssd_norm": f(g["ssd_norm"][0]), "ssd_w_out": f(g["ssd_w_out"][0]), "ssd_sa": ssd_tables(),
        "ret_w_in": f(g["ret_w_in"][0]), "ret_norm": f(g["ret_norm"][0]), "ret_w_out": f(g["ret_w_out"][0]),
        "ret_cos": cosT, "ret_sin": sinT, "ret_dq": dq, "ret_dk": dk, "ret_m": mret,
        "diff_w_in": f(g["diff_w_in"][0]), "diff_lam": f(g["diff_lambda"][0]), "diff_norm": f(g["diff_norm"][0]),
        "diff_w_out": f(g["diff_w_out"][0]), "diff_BT": f(BT), "diff_bfar": f(bfar), "diff_mask": dmask,
        "fox_w_in": f(g["fox_w_in"][0]), "fox_bf": f(g["fox_b_f"][0]), "fox_w_out": f(g["fox_w_out"][0]),
        "peer_wq": f(g["peer_w_q"]),
        "peer_kT": f(g["peer_keys"].reshape(DEPTH, 16, 128, 128).transpose(0, 1, 3, 2)),
        "peer_uT": f(g["peer_u"].transpose(0, 2, 1)), "peer_v": f(g["peer_v"]),
        "ple_proj": f(g["ple_proj"]), "ple_gate": f(g["ple_gate"]),
    }
    T = nseq * S
    maps = []
    for c in range(ncores):
        d = dict(shared)
        d["x"] = f(g["x"][c * nseq:(c + 1) * nseq].reshape(T, 1024))
        d["p"] = f(g["p"][:, c * nseq:(c + 1) * nseq].reshape(DEPTH, T, 256))
        maps.append(d)
    return maps


def kernel(**inputs):
    nc, m = build_program()
    maps = host_inputs(inputs)
    res = run_bass_kernel_spmd(nc, maps, core_ids=list(range(NCORES)))
    outs = [np.asarray(r["out"]).reshape(NSEQ, SEQ, 1024) for r in res.results]
    return np.concatenate(outs, axis=0).astype(np.float32)
```

```python
from contextlib import ExitStack
import math
import numpy as np
import concourse.bass as bass
import concourse.mybir as mybir
from concourse.bass_utils import run_bass_kernel_spmd

F32 = mybir.dt.float32
BF16 = mybir.dt.bfloat16
AF = mybir.ActivationFunctionType
ALU = mybir.AluOpType
AX = mybir.AxisListType

NCORES = 8
D = 1024
NEG = -30000.0
DBG = set()


class KB:
    NDMA = 24

    def __init__(self, nc, es):
        self.nc = nc
        self.es = es
        self.eng = dict(pe=nc.tensor, act=nc.scalar, dve=nc.vector, pool=nc.gpsimd, sp=nc.sync)
        es.enter_context(nc.allow_non_contiguous_dma(reason="small strided param loads"))
        self.sem = {}
        self.cnt = {}
        for e in ("pe", "act", "dve", "pool"):
            self.sem[e] = es.enter_context(nc.semaphore("s_" + e))
            self.cnt[e] = 0
        self.dsem = []
        for i in range(self.NDMA):
            nm = "d%d" % i
            self.sem[nm] = es.enter_context(nc.semaphore("s_" + nm))
            self.cnt[nm] = 0
            self.dsem.append(nm)
        self.dnext = 0
        self.known = {e: {} for e in self.eng}
        self.lastw = {}
        self.readers = {}
        self.n_ins = 0
        self.uid = 0

    def scope(self):
        kb = self

        class _S:
            def __enter__(self_):
                self_.old = kb.es
                self_.st = ExitStack()
                self_.st.__enter__()
                kb.es = self_.st
                return self_

            def __exit__(self_, *a):
                kb.es = self_.old
                return self_.st.__exit__(*a)

        return _S()

    def sb(self, name, shape, dtype=F32):
        self.uid += 1
        return self.es.enter_context(self.nc.sbuf_tensor("%s_%d" % (name, self.uid), list(shape), dtype))

    def ps(self, name, shape, dtype=F32):
        self.uid += 1
        return self.es.enter_context(self.nc.psum_tensor("%s_%d" % (name, self.uid), list(shape), dtype))

    @staticmethod
    def _key(a):
        return a if isinstance(a, (str, tuple)) else a.name

    def _wait(self, e, s, v):
        if v <= 0:
            return
        if self.known[e].get(s, 0) >= v:
            return
        self.eng[e].wait_ge(self.sem[s], v)
        self.known[e][s] = v
        self.n_ins += 1

    def _deps(self, e, R, W, pe_acc=False):
        deps = {}

        def add(tok, same_ok):
            if tok is None:
                return
            s, v = tok
            if same_ok and s == e:
                return
            if deps.get(s, 0) < v:
                deps[s] = v

        for k in R:
            add(self.lastw.get(k), False)
        for k in W:
            add(self.lastw.get(k), True)
            for s, v in self.readers.get(k, {}).items():
                add((s, v), True)
        for s, v in deps.items():
            self._wait(e, s, v)

    def _commit(self, tok, R, W):
        s, v = tok
        for k in W:
            self.lastw[k] = tok
            self.readers[k] = {}
        for k in R:
            d = self.readers.setdefault(k, {})
            if d.get(s, 0) < v:
                d[s] = v

    def op(self, e, ins_fn, R, W):
        R = [self._key(a) for a in R if a is not None and not isinstance(a, (int, float))]
        W = [self._key(a) for a in W]
        self._deps(e, R, W)
        ins = ins_fn()
        self.cnt[e] += 1
        ins.then_inc(self.sem[e], 1)
        self.n_ins += 1
        self._commit((e, self.cnt[e]), R, W)
        return ins

    def dma(self, out, in_, q="sp", rk=None, wk=None):
        R = [rk if rk is not None else self._key(in_)]
        W = [wk if wk is not None else self._key(out)]
        self._deps(q, R, W)
        s = self.dsem[self.dnext]
        self.dnext = (self.dnext + 1) % self.NDMA
        self._wait(q, s, self.cnt[s])
        self.eng[q].dma_start(out=out, in_=in_).then_inc(self.sem[s], 16)
        self.cnt[s] += 16
        self.n_ins += 1
        self._commit((s, self.cnt[s]), R, W)

    def barrier(self):
        for e in self.eng:
            for s in self.sem:
                if s != e:
                    self._wait(e, s, self.cnt[s])
        self.lastw = {}
        self.readers = {}

    def mm(self, out, lhsT, rhs, start=True, stop=True, extra_r=()):
        return self.op("pe", lambda: self.nc.tensor.matmul(out, lhsT, rhs, start=start, stop=stop),
                       [lhsT, rhs, *extra_r], [out])

    def tr(self, out, in_, ident):
        return self.op("pe", lambda: self.nc.tensor.transpose(out, in_, ident), [in_, ident], [out])

    def act(self, out, in_, func, bias=None, scale=None, accum_out=None, extra_r=()):
        kw = {}
        if bias is not None:
            kw["bias"] = bias
        if scale is not None:
            kw["scale"] = scale
        if accum_out is not None:
            kw["accum_out"] = accum_out
        W = [out] + ([accum_out] if accum_out is not None else [])
        return self.op("act", lambda: self.nc.scalar.activation(out, in_, func, **kw),
                       [in_, bias, scale, *extra_r], W)

    def tt(self, out, in0, in1, op, e="dve"):
        return self.op(e, lambda: self.eng[e].tensor_tensor(out, in0, in1, op), [in0, in1], [out])

    def ts(self, out, in0, s1, s2=None, op0=ALU.mult, op1=None, e="dve", accum_out=None):
        kw = {}
        if op1 is not None:
            kw["op1"] = op1
        if accum_out is not None:
            kw["accum_out"] = accum_out
        W = [out] + ([accum_out] if accum_out is not None else [])
        return self.op(e, lambda: self.eng[e].tensor_scalar(out, in0, s1, s2, op0, **kw), [in0, s1, s2], W)

    def stt(self, out, in0, scalar, in1, op0, op1, e="dve"):
        return self.op(e, lambda: self.eng[e].scalar_tensor_tensor(out, in0, scalar, in1, op0, op1),
                       [in0, scalar, in1], [out])

    def cp(self, out, in_, e="dve"):
        return self.op(e, lambda: self.eng[e].tensor_copy(out, in_), [in_], [out])

    def memset(self, out, val, e="dve"):
        return self.op(e, lambda: self.eng[e].memset(out, val), [], [out])

    def recip(self, out, in_):
        return self.op("dve", lambda: self.nc.vector.reciprocal(out, in_), [in_], [out])

    def red(self, out, in_, op=ALU.add, e="dve"):
        return self.op(e, lambda: self.eng[e].tensor_reduce(out, in_, AX.X, op), [in_], [out])


class GemmRes:
    def __init__(self, kb):
        self.kb = kb
        self.ident_f = kb.sb("identf", [128, 128], F32)
        self.ident = kb.sb("ident", [128, 128], BF16)
        self.xin = [kb.sb("xin%d" % i, [128, 1024], F32) for i in range(3)]
        self.xsq = kb.sb("xsq", [128, 1024], F32)
        self.ssq = [kb.sb("ssq%d" % i, [128, 1], F32) for i in range(3)]
        self.xn = [kb.sb("xn%d" % i, [128, 2048], BF16) for i in range(2)]
        self.gcol = kb.sb("gcol", [128, 8], F32)
        self.epsc = kb.sb("epsc", [128, 1], F32)
        self.pmi = 0

    def next_pm(self):
        p = self.pm[self.pmi % len(self.pm)]
        self.pmi += 1
        return p

    def init(self, ident_dram):
        kb = self.kb
        kb.dma(self.ident_f[:], ident_dram)
        kb.cp(self.ident[:], self.ident_f[:])
        kb.memset(self.epsc[:], 1e-6)


def load_weights(kb, gr, W_dram, Kin, N, gcol_dram=None):
    KC = Kin // 128
    Wb = kb.sb("Wb", [128, KC * N], BF16)
    Wv = Wb[:].rearrange("p (c n) -> p c n", c=KC)
    with kb.scope():
        wst = [kb.sb("wst%d" % i, [128, 2048], F32) for i in range(2)]
        if gcol_dram is not None:
            kb.dma(gr.gcol[:, 0:KC], gcol_dram.rearrange("(c p) -> p c", p=128))
        i = 0
        for kc in range(KC):
            for n0 in range(0, N, 2048):
                n1 = min(N, n0 + 2048)
                st = wst[i % 2]
                kb.dma(st[:, 0:n1 - n0], W_dram[kc * 128:(kc + 1) * 128, n0:n1], q="sp")
                if gcol_dram is not None:
                    if i % 2 == 0:
                        kb.ts(Wv[:, kc, n0:n1], st[:, 0:n1 - n0], gr.gcol[:, kc:kc + 1], None, op0=ALU.mult)
                    else:
                        kb.act(Wv[:, kc, n0:n1], st[:, 0:n1 - n0], AF.Copy, scale=gr.gcol[:, kc:kc + 1])
                else:
                    kb.cp(Wv[:, kc, n0:n1], st[:, 0:n1 - n0], e=("dve", "pool")[i % 2])
                i += 1
        kb.barrier()
    return Wv


def gemm_stage(kb, gr, x_dram, T, Kin, Wv, segs, norm=False, x_bf16=False, n_psT=2, n_pm=4):
    KC = Kin // 128
    nblk = T // 512
    gr.xT = [kb.sb("xT%d" % i, [128, KC * 512], BF16) for i in range(2)]
    gr.psT = [kb.ps("psT%d" % i, [128, 1024], BF16) for i in range(n_psT)]
    gr.pm = [kb.ps("pm%d" % i, [128, 512], F32) for i in range(n_pm)]
    for blk in range(nblk):
        xT = gr.xT[blk % 2]
        xTv = xT[:, 0:KC * 512].rearrange("p (c t) -> p c t", c=KC)
        xins = []
        for sub in range(4):
            tok0 = blk * 512 + sub * 128
            it = blk * 4 + sub
            if x_bf16:
                xt = gr.xn[it % 2]
                kb.dma(xt[:, 0:Kin], x_dram[tok0:tok0 + 128, :])
                xn = xt
            else:
                xt = gr.xin[it % 3]
                kb.dma(xt[:, 0:Kin], x_dram[tok0:tok0 + 128, :])
                xn = gr.xn[it % 2]
                if norm:
                    ssq = gr.ssq[it % 3]
                    kb.act(gr.xsq[:, 0:Kin], xt[:, 0:Kin], AF.Square)
                    kb.red(ssq[:], gr.xsq[:, 0:Kin])
                    kb.act(ssq[:], ssq[:], AF.Sqrt, bias=gr.epsc[:, 0:1], scale=1.0 / Kin)
                    kb.recip(ssq[:], ssq[:])
                    kb.act(xn[:, 0:Kin], xt[:, 0:Kin], AF.Copy, scale=ssq[:, 0:1])
                else:
                    kb.cp(xn[:, 0:Kin], xt[:, 0:Kin], e="pool")
            xins.append(xt)
            for half in range((KC + 7) // 8):
                pst = gr.psT[(it * 2 + half) % n_psT]
                nk = min(8, KC - half * 8)
                for j in range(nk):
                    kc = half * 8 + j
                    kb.tr(pst[:, j * 128:(j + 1) * 128], xn[:, kc * 128:(kc + 1) * 128], gr.ident[:])
                src = pst[:, 0:nk * 128].rearrange("p (c t) -> p c t", c=nk)
                dst = xTv[:, half * 8:half * 8 + nk, sub * 128:(sub + 1) * 128]
                if (it + half) % 2 == 0:
                    kb.cp(dst, src, e="dve")
                else:
                    kb.act(dst, src, AF.Copy)
        for seg in segs:
            if seg["kind"] == "custom":
                seg["fn"](blk, xTv)
                continue
            n0, n1 = seg["n0"], seg["n1"]
            if seg["kind"] == "FM":
                grp = seg.get("group", 1)
                nch = (n1 - n0 + 127) // 128
                for c0 in range(0, nch, grp):
                    pss = []
                    for ci in range(c0, min(nch, c0 + grp)):
                        a = n0 + ci * 128
                        b = min(n1, a + 128)
                        ps = gr.next_pm()
                        for kc in range(KC):
                            kb.mm(ps[0:b - a, :], Wv[:, kc, a:b], xTv[:, kc, :], start=(kc == 0), stop=(kc == KC - 1))
                        pss.append(ps)
                    seg["epi"](blk, blk * 512, c0, pss)
            else:
                for sub in range(4):
                    for a in range(n0, n1, 512):
                        b = min(n1, a + 512)
                        ps = gr.next_pm()
                        for kc in range(KC):
                            kb.mm(ps[:, 0:b - a], xTv[:, kc, sub * 128:(sub + 1) * 128], Wv[:, kc, a:b],
                                  start=(kc == 0), stop=(kc == KC - 1))
                        seg["epi"](blk, blk * 512 + sub * 128, sub, a - n0, b - a, ps, xins[sub])


def attn_stage(kb, spec, S, nseq):
    nsup = S // 512
    pst = [kb.ps("ast%d" % i, [128, 512], F32) for i in range(2)]
    outs = [kb.ps("aout%d" % i, [128, 512], F32) for i in range(4)]
    Ps = [kb.sb("aP%d" % i, [128, 512], BF16) for i in range(3)]
    jobs = [(seq, g) for seq in range(nseq) for g in spec.groups(seq)]
    ti = 0
    nbuf = getattr(spec, "nbuf", 2)
    for n, (seq, g) in enumerate(jobs):
        buf = n % nbuf
        if nbuf == 1:
            spec.load(g, 0)
        else:
            if n == 0:
                spec.load(g, buf)
            if n + 1 < len(jobs):
                spec.load(jobs[n + 1][1], (n + 1) % 2)
        shs = spec.subheads(g)
        for j in range(nsup):
            for i in range(4 * j + 4):
                if hasattr(spec, "pre"):
                    spec.pre(g, j, i, buf)
                for sh in shs:
                    ps = pst[ti % 2]
                    P = Ps[ti % 3]
                    ti += 1
                    spec.qk(g, sh, j, i, ps, buf)
                    spec.evac(g, sh, j, i, ps, P, buf)
                    c0, dvp = spec.ocols(g, sh)
                    vb = spec.vblk(g, sh, i, buf)
                    for qb in range(4):
                        if i <= 4 * j + qb:
                            kb.mm(outs[qb][:, c0:c0 + dvp], P[:, qb * 128:(qb + 1) * 128], vb,
                                  start=(i == 0 and sh == shs[0]), stop=(i == 4 * j + qb))
            spec.post(g, j, outs, buf)


class FoxSpec:
    def __init__(self, kb, gr, S, QT, KT, V, CQK, O, cmask_bf):
        self.kb, self.gr, self.S = kb, gr, S
        self.QT, self.KT, self.V, self.CQK, self.O = QT, KT, V, CQK, O
        self.cmask = cmask_bf
        self.Qa = [kb.sb("fQa%d" % i, [70, S], BF16) for i in range(2)]
        self.Ka = [kb.sb("fKa%d" % i, [70, S], BF16) for i in range(2)]
        self.Vt = [kb.sb("fV%d" % i, [128, (S // 128) * 65], BF16) for i in range(2)]
        self.rz = [kb.sb("frz%d" % i, [128, 1], F32) for i in range(4)]
        self.ost = [kb.sb("fost%d" % i, [128, 64], BF16) for i in range(4)]
        self.pi = 0
        for i in range(2):
            kb.memset(self.Qa[i][64:70, :], 1.0)
            kb.memset(self.Ka[i][64:70, :], 1.0, e="pool")
            kb.memset(self.Vt[i][:], 1.0, e="pool")

    def groups(self, seq):
        return [(seq, h) for h in range(16)]

    def subheads(self, g):
        return [0]

    def load(self, g, buf):
        kb, S = self.kb, self.S
        seq, h = g
        t0 = seq * S
        kb.dma(self.Qa[buf][0:64, :], self.QT[64 * h:64 * h + 64, t0:t0 + S])
        kb.dma(self.Qa[buf][64:67, :], self.CQK[h, 3:6, t0:t0 + S])
        kb.dma(self.Ka[buf][0:64, :], self.KT[64 * h:64 * h + 64, t0:t0 + S])
        kb.dma(self.Ka[buf][67:70, :], self.CQK[h, 0:3, t0:t0 + S])
        vt = self.Vt[buf][:].rearrange("p (b d) -> p b d", d=65)
        kb.dma(vt[:, :, 0:64], self.V[t0:t0 + S, 64 * h:64 * h + 64].rearrange("(b p) d -> p b d", p=128))

    def qk(self, g, sh, j, i, ps, buf):
        kb = self.kb
        r = i - 4 * j
        kb.mm(ps[:], self.Ka[buf][:, i * 128:(i + 1) * 128], self.Qa[buf][:, j * 512:(j + 1) * 512],
              start=True, stop=(r < 0))
        if r >= 0:
            kb.mm(ps[:], self.gr.ident[:], self.cmask[:, r * 512:(r + 1) * 512], start=False, stop=True)

    def evac(self, g, sh, j, i, ps, P, buf):
        self.kb.act(P[:], ps[:], AF.Exp)

    def ocols(self, g, sh):
        return 0, 65

    def vblk(self, g, sh, i, buf):
        return self.Vt[buf][:, i * 65:(i + 1) * 65]

    def post(self, g, j, outs, buf):
        kb, S = self.kb, self.S
        seq, h = g
        for qb in range(4):
            rz = self.rz[self.pi % 4]
            st = self.ost[self.pi % 4]
            self.pi += 1
            kb.recip(rz[:], outs[qb][:, 64:65])
            kb.ts(st[:], outs[qb][:, 0:64], rz[:, 0:1], None, op0=ALU.mult)
            tok = seq * S + (4 * j + qb) * 128
            kb.dma(self.O[tok:tok + 128, 64 * h:64 * h + 64], st[:], q="pool")


class Epi:
    def __init__(self, kb):
        self.kb = kb
        self.fm = [kb.sb("efm%d" % i, [128, 512], BF16) for i in range(3)]
        self.tmb = [kb.sb("etmb%d" % i, [128, 512], BF16) for i in range(3)]
        self.tmf = [kb.sb("etmf%d" % i, [128, 512], F32) for i in range(3)]
        self.n = 0

    def fm_store(self, dst, scale=1.0):
        kb = self.kb

        def epi(blk, tok0, c0, pss):
            ps = pss[0]
            st = self.fm[self.n % 3]
            self.n += 1
            rows = min(128, dst.shape[0] - c0 * 128)
            if self.n % 2 == 0:
                kb.act(st[0:rows, :], ps[0:rows, :], AF.Copy, scale=float(scale))
            else:
                kb.ts(st[0:rows, :], ps[0:rows, :], float(scale), None, op0=ALU.mult)
            kb.dma(dst[c0 * 128:c0 * 128 + rows, tok0:tok0 + 512], st[0:rows, :], q="pool")
        return epi

    def tm_store(self, dst, func=AF.Copy, col0=0):
        kb = self.kb

        def epi(blk, tok0, sub, n0c, ncols, ps, xt):
            st = self.tmb[self.n % 3]
            self.n += 1
            if func == AF.Copy and self.n % 2 == 0:
                kb.cp(st[:, 0:ncols], ps[:, 0:ncols])
            else:
                kb.act(st[:, 0:ncols], ps[:, 0:ncols], func)
            kb.dma(dst[tok0:tok0 + 128, col0 + n0c:col0 + n0c + ncols], st[:, 0:ncols], q="pool")
        return epi

    def tm_resid(self, h):
        kb = self.kb

        def epi(blk, tok0, sub, n0c, ncols, ps, xt):
            st = self.tmf[self.n % 3]
            self.n += 1
            key = ("h", tok0, n0c)
            kb.dma(st[:, 0:ncols], h[tok0:tok0 + 128, n0c:n0c + ncols], rk=key)
            kb.tt(st[:, 0:ncols], ps[:, 0:ncols], st[:, 0:ncols], ALU.add)
            kb.dma(h[tok0:tok0 + 128, n0c:n0c + ncols], st[:, 0:ncols], q="pool", wk=key)
        return epi


def split3(kb, src, dst6, tmp_r, tmp_b, rows, n):
    r = tmp_r
    kb.cp(dst6[0:rows, 0, :], src)
    kb.tt(r[0:rows, 0:n], src, dst6[0:rows, 0, :], ALU.subtract)
    kb.cp(dst6[0:rows, 1, :], r[0:rows, 0:n])
    kb.tt(r[0:rows, 0:n], r[0:rows, 0:n], dst6[0:rows, 1, :], ALU.subtract)
    kb.cp(dst6[0:rows, 2, :], r[0:rows, 0:n])
    kb.ts(dst6[0:rows, 3:6, :], dst6[0:rows, 0:3, :], -1.0, None, op0=ALU.mult, e="pool")


class Model:
    _epi_cache = (None, None)

    @property
    def epi(self):
        if self._epi_cache[0] is not self.kb.es:
            self._epi_cache = (self.kb.es, Epi(self.kb))
        return self._epi_cache[1]

    def __init__(self, nc, es, nseq, S):
        self.nc, self.nseq, self.S = nc, nseq, S
        self.T = nseq * S
        self.kb = KB(nc, es)
        self.din = {}
        self.scratch = {}

    def inp(self, name, shape, dtype=F32):
        self.din[name] = self.nc.dram_tensor(name, list(shape), dtype, kind="ExternalInput").ap()
        return self.din[name]

    def dscr(self, name, shape, dtype):
        if name not in self.scratch:
            self.scratch[name] = self.nc.dram_tensor(name, list(shape), dtype, kind="Internal").ap()
        return self.scratch[name]

    def setup(self):
        kb = self.kb
        self.gr = GemmRes(kb)
        self.gr.init(self.din["ident"])
        self.one = kb.sb("onec", [128, 1], F32)
        kb.memset(self.one[:], 1.0)
        self.zeros = kb.sb("zeros", [128, 512], F32)
        kb.memset(self.zeros[:], 0.0)
        self.cmask = kb.sb("cmask", [128, 2048], BF16)
        with kb.scope():
            tmp = kb.sb("cmtmp", [128, 2048], F32)
            kb.dma(tmp[:], self.din["cmask"])
            kb.cp(self.cmask[:], tmp[:])
            kb.barrier()

    def fox_mixer(self, h, w_in, gcol, b_f, w_out):
        kb, gr, T, S, nseq = self.kb, self.gr, self.T, self.S, self.nseq
        QT = self.dscr("QT", [1024, T], BF16)
        KT = self.dscr("KT", [1024, T], BF16)
        V = self.dscr("Vtm", [T, 2048], BF16)
        CQK = self.dscr("CQK", [32, 6, T], BF16)
        O = self.dscr("Otm", [T, 2048], BF16)
        with kb.scope():
            Wv = load_weights(kb, gr, w_in, 1024, 3088, gcol)
            nbf = kb.sb("nbf", [16, 1], F32)
            kb.dma(nbf[:], b_f.rearrange("(p o) -> p o", o=1))
            kb.ts(nbf[:], nbf[:], -1.0, None, op0=ALU.mult)
            carry = kb.sb("carry", [16, 1], F32)
            t1 = kb.sb("ft1", [16, 512], F32)
            cum = kb.sb("fcum", [16, 512], F32)
            rr = kb.sb("frr", [16, 512], F32)
            spl = kb.sb("fspl", [16, 6 * 512], BF16)
            splv = spl[:].rearrange("p (s n) -> p s n", s=6)
            bps = S // 512

            def f_epi(blk, tok0, c0, pss):
                ps = pss[0]
                if blk % bps == 0:
                    kb.memset(carry[:], 0.0)
                kb.act(t1[:], ps[0:16, :], AF.Exp, bias=nbf[:, 0:1], scale=-1.0)
                kb.act(t1[:], t1[:], AF.Ln, bias=self.one[0:16, 0:1])
                kb.op("dve", lambda: self.nc.vector.tensor_tensor_scan(cum[:], t1[:], self.zeros[0:16, :], carry[:, 0:1], ALU.add, ALU.add),
                      [t1, self.zeros, carry], [cum])
                kb.cp(carry[:], cum[:, 511:512])
                split3(kb, cum[:], splv, rr, None, 16, 512)
                kb.dma(CQK[0:16, :, tok0:tok0 + 512], splv, q="pool")

            segs = [dict(kind="FM", n0=0, n1=1024, epi=self.epi.fm_store(QT, 0.125)),
                    dict(kind="FM", n0=1024, n1=2048, epi=self.epi.fm_store(KT, 1.0)),
                    dict(kind="TM", n0=2048, n1=3072, epi=self.epi.tm_store(V)),
                    dict(kind="FM", n0=3072, n1=3088, epi=f_epi)]
            gemm_stage(kb, gr, h, T, 1024, Wv, segs, norm=True)
            kb.barrier()
        with kb.scope():
            spec = FoxSpec(kb, gr, S, QT, KT, V, CQK, O, self.cmask)
            attn_stage(kb, spec, S, nseq)
            kb.barrier()
        with kb.scope():
            Wv = load_weights(kb, gr, w_out, 1024, 1024, None)
            segs = [dict(kind="TM", n0=0, n1=1024, epi=self.epi.tm_resid(h))]
            gemm_stage(kb, gr, O[:, 0:1024], T, 1024, Wv, segs, x_bf16=True)
            kb.barrier()

    def peer_old(self, h, gcol, w_q, keysT, uT, v):
        kb, gr, T = self.kb, self.gr, self.T
        nc = self.nc
        UTb = self.dscr("UTb", [1024, 16384], BF16)
        Vb = self.dscr("Vb", [16384, 1024], BF16)
        Gd = self.dscr("Gd", [T, 16384], BF16)
        with kb.scope():
            kb.dma(gr.gcol[:, 0:8], gcol.rearrange("(c p) -> p c", p=128))
            st = [kb.sb("pst%d" % i, [128, 2048], F32) for i in range(3)]
            sb = [kb.sb("psb%d" % i, [128, 2048], BF16) for i in range(3)]
            n = 0
            for kc in range(8):
                for e0 in range(0, 16384, 2048):
                    s_, b_ = st[n % 3], sb[n % 3]
                    kb.dma(s_[:], uT[kc * 128:(kc + 1) * 128, e0:e0 + 2048])
                    if n % 2 == 0:
                        kb.ts(b_[:], s_[:], gr.gcol[:, kc:kc + 1], None, op0=ALU.mult)
                    else:
                        kb.act(b_[:], s_[:], AF.Copy, scale=gr.gcol[:, kc:kc + 1])
                    kb.dma(UTb[kc * 128:(kc + 1) * 128, e0:e0 + 2048], b_[:], q="pool")
                    n += 1
            for r0 in range(0, 16384, 256):
                s_, b_ = st[n % 3], sb[n % 3]
                kb.dma(s_[:].rearrange("p (c n) -> p c n", c=2), v[r0:r0 + 256, :].rearrange("(c p) n -> p c n", p=128))
                if n % 3 == 0:
                    kb.cp(b_[:], s_[:], e="dve")
                elif n % 3 == 1:
                    kb.act(b_[:], s_[:], AF.Copy)
                else:
                    kb.cp(b_[:], s_[:], e="pool")
                kb.dma(Vb[r0:r0 + 256, :].rearrange("(c p) n -> p c n", p=128), b_[:].rearrange("p (c n) -> p c n", c=2), q="pool")
                n += 1
            kb.barrier()
        with kb.scope():
            Wv = load_weights(kb, gr, w_q, 1024, 2048, gcol)
            kT = kb.sb("keysT", [128, 16 * 128], BF16)
            with kb.scope():
                ktmp = kb.sb("ktmp", [128, 16 * 128], F32)
                kb.dma(ktmp[:].rearrange("p (c k) -> p c k", c=16), keysT.rearrange("c d k -> d c k"))
                kb.cp(kT[:], ktmp[:])
                kb.barrier()
            kTv = kT[:].rearrange("p (c k) -> p c k", c=16)
            qT = [kb.sb("pqT%d" % i, [128, 512], BF16) for i in range(2)]
            psc = [kb.ps("psc%d" % i, [128, 512], F32) for i in range(2)]
            sc_all = kb.sb("sc_all", [128, 4 * 16 * 128], F32)
            scv = sc_all[:].rearrange("p (s c k) -> p s c k", s=4, c=16)
            a16 = kb.sb("a16", [128, 16], F32)
            b16 = kb.sb("b16", [128, 16], F32)
            c16 = kb.sb("c16", [128, 16], F32)
            e16 = kb.sb("e16", [128, 16], F32)
            t128 = kb.sb("t128", [128, 128], F32)
            cand = kb.sb("cand", [128, 256], F32)
            cand2 = kb.sb("cand2", [128, 256], F32)
            tau = kb.sb("tau", [128, 8], F32)
            nb = kb.sb("nb", [128, 8], F32)
            nb2 = kb.sb("nb2", [128, 8], F32)
            zz = kb.sb("zz", [128, 1], F32)
            Sc = [kb.sb("Sc%d" % i, [128, 2048], F32) for i in range(3)]
            Ec = [kb.sb("Ec%d" % i, [128, 2048], BF16) for i in range(3)]
            gcnt = [0]
            Gb = [kb.sb("Gb%d" % i, [128, 2048], BF16) for i in range(2)]
            cn = [0, 0, 0]
            V = nc.vector

            def top16(dst, src, tmp):
                kb.op("dve", lambda: V.max(out=dst[:, 0:8], in_=src), [src], [dst])
                kb.op("dve", lambda: V.match_replace(out=tmp, in_to_replace=dst[:, 0:8], in_values=src, imm_value=-1e30),
                      [dst, src], [tmp])
                kb.op("dve", lambda: V.max(out=dst[:, 8:16], in_=tmp), [tmp], [dst])

            def gates(tokb, sub):
                for hh in range(8):
                    s1 = scv[:, sub, 2 * hh, :]
                    s2 = scv[:, sub, 2 * hh + 1, :]
                    top16(a16, s1, t128[:])
                    top16(b16, s2, t128[:])
                    kb.tt(cand[:].rearrange("p (a b) -> p a b", a=16),
                          a16[:].unsqueeze(2).to_broadcast([128, 16, 16]),
                          b16[:].unsqueeze(1).to_broadcast([128, 16, 16]), ALU.add)
                    top16(c16, cand[:], cand2[:])
                    kb.cp(tau[:, hh:hh + 1], c16[:, 15:16])
                    kb.ts(nb[:, hh:hh + 1], c16[:, 0:1], -1.0, None, op0=ALU.mult)
                    kb.act(e16[:], c16[:], AF.Exp, bias=nb[:, hh:hh + 1])
                    kb.red(zz[:], e16[:])
                    kb.act(zz[:], zz[:], AF.Ln)
                    kb.tt(nb[:, hh:hh + 1], nb[:, hh:hh + 1], zz[:], ALU.subtract)
                items = [(c, hh) for c in range(8) for hh in range(8)]

                def stage1(n):
                    c, hh = items[n]
                    S_, E_ = Sc[n % 3], Ec[n % 3]
                    s1 = scv[:, sub, 2 * hh, 16 * c:16 * c + 16]
                    s2 = scv[:, sub, 2 * hh + 1, :]
                    S3 = S_[:].rearrange("p (a b) -> p a b", a=16)
                    kb.tt(S3, s1.unsqueeze(2).to_broadcast([128, 16, 128]),
                          s2.unsqueeze(1).to_broadcast([128, 16, 128]), ALU.add)
                    kb.act(E_[:], S_[:], AF.Exp, bias=nb[:, hh:hh + 1])

                def stage2(n):
                    c, hh = items[n]
                    S_, E_ = Sc[n % 3], Ec[n % 3]
                    G = Gb[(gcnt[0] + c) % 2]
                    dst = G if hh == 0 else E_
                    kb.stt(dst[:], S_[:], tau[:, hh:hh + 1], E_[:], ALU.is_ge, ALU.mult)
                    if hh > 0:
                        kb.tt(G[:], G[:], E_[:], ALU.add)
                    if hh == 7:
                        kb.dma(Gd[tokb:tokb + 128, c * 2048:(c + 1) * 2048], G[:], q="sp")

                stage1(0)
                for n in range(64):
                    if n + 1 < 64:
                        stage1(n + 1)
                    stage2(n)

            def q_epi(blk, tok0, c0, pss):
                qt = qT[c0 % 2]
                if c0 % 2 == 0:
                    kb.act(qt[:], pss[0][:], AF.Copy)
                else:
                    kb.cp(qt[:], pss[0][:])
                pc = psc[c0 % 2]
                for sub in range(4):
                    kb.mm(pc[:, sub * 128:(sub + 1) * 128], qt[:, sub * 128:(sub + 1) * 128], kTv[:, c0, :])
                src = pc[:].rearrange("p (s k) -> p s k", s=4)
                if c0 % 2 == 0:
                    kb.cp(scv[:, :, c0, :], src)
                else:
                    kb.act(scv[:, :, c0, :], src, AF.Copy)
                if c0 == 15:
                    for sub in range(4):
                        gates(tok0 + sub * 128, sub)

            segs = [dict(kind="FM", n0=0, n1=2048, epi=q_epi)]
            gemm_stage(kb, gr, h, T, 1024, Wv, segs, norm=True)
            kb.barrier()
        if getattr(self, 'skip_dense', False):
            return
        with kb.scope():
            ut = [kb.sb("ut%d" % i, [128, 8 * 512], BF16) for i in range(2)]
            vt = [kb.sb("vt%d" % i, [128, 4 * 1024], BF16) for i in range(2)]
            gt = [kb.sb("gt%d" % i, [128, 512], BF16) for i in range(3)]
            gel = [kb.sb("gel%d" % i, [128, 512], F32) for i in range(2)]
            gh = [kb.sb("gh%d" % i, [128, 512], BF16) for i in range(2)]
            ghT = [kb.sb("ghT%d" % i, [128, 512], BF16) for i in range(2)]
            acc = [kb.sb("acc%d" % i, [128, 1024], F32) for i in range(4)]
            php = [kb.ps("php%d" % i, [128, 512], F32) for i in range(2)]
            ptp = [kb.ps("ptp%d" % i, [128, 1024], BF16) for i in range(1)]
            pop = [kb.ps("pop%d" % i, [128, 512], F32) for i in range(4)]
            cn2 = [0]

            def dense(blk, xTv):
                tokb = blk * 512
                for ec in range(32):
                    u_ = ut[ec % 2]
                    v_ = vt[ec % 2]
                    uv = u_[:].rearrange("p (c e) -> p c e", c=8)
                    vv = v_[:].rearrange("p (c n) -> p c n", c=4)
                    kb.dma(uv, UTb[:, ec * 512:(ec + 1) * 512].rearrange("(c p) e -> p c e", p=128))
                    kb.dma(vv, Vb[ec * 512:(ec + 1) * 512, :].rearrange("(c p) n -> p c n", p=128))
                    for sub in range(4):
                        k = cn2[0]
                        cn2[0] += 1
                        g_ = gt[k % 3]
                        kb.dma(g_[:], Gd[tokb + sub * 128:tokb + sub * 128 + 128, ec * 512:(ec + 1) * 512])
                        ph = php[k % 2]
                        for kc in range(8):
                            kb.mm(ph[:], xTv[:, kc, sub * 128:(sub + 1) * 128], uv[:, kc, :], start=(kc == 0), stop=(kc == 7))
                        ge = gel[k % 2]
                        kb.act(ge[:], ph[:], AF.Gelu)
                        gh_ = gh[k % 2]
                        kb.tt(gh_[:], ge[:], g_[:], ALU.mult, e="pool")
                        pt = ptp[0]
                        for c4 in range(4):
                            kb.tr(pt[:, c4 * 128:(c4 + 1) * 128], gh_[:, c4 * 128:(c4 + 1) * 128], gr.ident[:])
                        gT = ghT[k % 2]
                        if k % 2 == 0:
                            kb.cp(gT[:], pt[:, 0:512])
                        else:
                            kb.act(gT[:], pt[:, 0:512], AF.Copy)
                        for nh in range(2):
                            po = pop[(k % 2) * 2 + nh]
                            for c4 in range(4):
                                kb.mm(po[:], gT[:, c4 * 128:(c4 + 1) * 128], vv[:, c4, nh * 512:(nh + 1) * 512],
                                      start=(c4 == 0), stop=(c4 == 3))
                            a_ = acc[sub][:, nh * 512:(nh + 1) * 512]
                            if ec == 0:
                                kb.cp(a_, po[:])
                            else:
                                kb.tt(a_, po[:], a_, ALU.add)
                for sub in range(4):
                    t0 = tokb + sub * 128
                    hs = gr.xin[sub % 3]
                    key = ("h", t0)
                    kb.dma(hs[:], h[t0:t0 + 128, :], rk=key)
                    kb.tt(acc[sub][:], acc[sub][:], hs[:], ALU.add, e="pool")
                    kb.dma(h[t0:t0 + 128, :], acc[sub][:], q="pool", wk=key)

            segs = [dict(kind="custom", fn=dense)]
            gemm_stage(kb, gr, h, T, 1024, None, segs, norm=True, n_psT=1, n_pm=0)
            kb.barrier()

    def ple(self, h, gcol, w_gate, p_i, w_proj):
        kb, gr, T = self.kb, self.gr, self.T
        PP = self.dscr("PP", [T, 1024], F32)
        with kb.scope():
            Wv = load_weights(kb, gr, w_proj, 256, 1024, None)
            stf = [kb.sb("ppst%d" % i, [128, 512], F32) for i in range(3)]
            cn = [0]

            def pp_epi(blk, tok0, sub, n0c, ncols, ps, xt):
                st = stf[cn[0] % 3]
                cn[0] += 1
                if cn[0] % 2 == 0:
                    kb.cp(st[:, 0:ncols], ps[:, 0:ncols])
                else:
                    kb.act(st[:, 0:ncols], ps[:, 0:ncols], AF.Copy)
                kb.dma(PP[tok0:tok0 + 128, n0c:n0c + ncols], st[:, 0:ncols], q="pool")
            gemm_stage(kb, gr, p_i, T, 256, Wv, [dict(kind="TM", n0=0, n1=1024, epi=pp_epi)])
            kb.barrier()
        with kb.scope():
            Wv = load_weights(kb, gr, w_gate, 1024, 1024, gcol)
            sg = [kb.sb("plg%d" % i, [128, 512], F32) for i in range(3)]
            sp_ = [kb.sb("plp%d" % i, [128, 512], F32) for i in range(3)]
            sh_ = [kb.sb("plh%d" % i, [128, 512], F32) for i in range(3)]
            cn = [0]

            def g_epi(blk, tok0, sub, n0c, ncols, ps, xt):
                k = cn[0] % 3
                cn[0] += 1
                key = ("h", tok0, n0c)
                kb.dma(sp_[k][:, 0:ncols], PP[tok0:tok0 + 128, n0c:n0c + ncols])
                kb.dma(sh_[k][:, 0:ncols], h[tok0:tok0 + 128, n0c:n0c + ncols], rk=key)
                kb.act(sg[k][:, 0:ncols], ps[:, 0:ncols], AF.Sigmoid)
                kb.tt(sg[k][:, 0:ncols], sg[k][:, 0:ncols], sp_[k][:, 0:ncols], ALU.mult, e="pool")
                kb.tt(sh_[k][:, 0:ncols], sh_[k][:, 0:ncols], sg[k][:, 0:ncols], ALU.add)
                kb.dma(h[tok0:tok0 + 128, n0c:n0c + ncols], sh_[k][:, 0:ncols], q="pool", wk=key)
            gemm_stage(kb, gr, h, T, 1024, Wv, [dict(kind="TM", n0=0, n1=1024, epi=g_epi)], norm=True)
            kb.barrier()

    def final(self, h, g, out):
        kb, gr, T = self.kb, self.gr, self.T
        with kb.scope():
            gb = kb.sb("fgb", [128, 1024], F32)
            kb.dma(gb[:], g.partition_broadcast(128))
            ot = [kb.sb("fot%d" % i, [128, 1024], F32) for i in range(2)]
            for it in range(T // 128):
                xt = gr.xin[it % 3]
                ssq = gr.ssq[it % 3]
                o_ = ot[it % 2]
                kb.dma(xt[:], h[it * 128:(it + 1) * 128, :])
                kb.act(gr.xsq[:], xt[:], AF.Square)
                kb.red(ssq[:], gr.xsq[:])
                kb.act(ssq[:], ssq[:], AF.Sqrt, bias=gr.epsc[:, 0:1], scale=1.0 / 1024)
                kb.recip(ssq[:], ssq[:])
                kb.stt(o_[:], xt[:], ssq[:, 0:1], gb[:], ALU.mult, ALU.mult)
                kb.dma(out[it * 128:(it + 1) * 128, :], o_[:], q="pool")
            kb.barrier()

    def diff_mixer(self, h, w_in, gcol, lam_vecs, lam_init, norm_g, w_out, BT, bfar, dmask):
        kb, gr, T, S, nseq = self.kb, self.gr, self.T, self.S, self.nseq
        QT = self.dscr("QT", [1024, T], BF16)
        KT = self.dscr("KT", [1024, T], BF16)
        V = self.dscr("Vtm", [T, 2048], BF16)
        O = self.dscr("Otm", [T, 2048], BF16)
        with kb.scope():
            Wv = load_weights(kb, gr, w_in, 1024, 3072, gcol)
            segs = [dict(kind="FM", n0=0, n1=1024, epi=self.epi.fm_store(QT, 0.125)),
                    dict(kind="FM", n0=1024, n1=2048, epi=self.epi.fm_store(KT, 1.0)),
                    dict(kind="TM", n0=2048, n1=3072, epi=self.epi.tm_store(V))]
            gemm_stage(kb, gr, h, T, 1024, Wv, segs, norm=True)
            kb.barrier()
        with kb.scope():
            spec = DiffSpec(self, S, QT, KT, V, O, lam_vecs, lam_init, norm_g, BT, bfar, dmask)
            attn_stage(kb, spec, S, nseq)
            kb.barrier()
        with kb.scope():
            Wv = load_weights(kb, gr, w_out, 1024, 1024, None)
            segs = [dict(kind="TM", n0=0, n1=1024, epi=self.epi.tm_resid(h))]
            gemm_stage(kb, gr, O[:, 0:1024], T, 1024, Wv, segs, x_bf16=True)
            kb.barrier()


class DiffSpec:
    def __init__(self, m, S, QT, KT, V, O, lam_vecs, lam_init, norm_g, BT, bfar, dmask):
        kb = m.kb
        self.kb, self.gr, self.S = kb, m.gr, S
        self.QT, self.KT, self.V, self.O = QT, KT, V, O
        self.Qa = [[kb.sb("dQ%d_%d" % (i, mm), [64, S], BF16) for mm in range(2)] for i in range(2)]
        self.Ka = [[kb.sb("dK%d_%d" % (i, mm), [64, S], BF16) for mm in range(2)] for i in range(2)]
        self.Vt = [kb.sb("dV%d" % i, [128, (S // 128) * 129], BF16) for i in range(2)]
        for i in range(2):
            kb.memset(self.Vt[i][:], 1.0, e="pool")
        lv = kb.sb("dlv", [128, 256], F32)
        kb.dma(lv[:], lam_vecs.rearrange("a d -> (a d)").partition_broadcast(128))
        pr = kb.sb("dpr", [128, 128], F32)
        lvv = lv[:].rearrange("p (a d) -> p a d", a=4)
        prv = pr[:].rearrange("p (a d) -> p a d", a=2)
        kb.tt(prv[:, 0, :], lvv[:, 0, :], lvv[:, 1, :], ALU.mult)
        kb.tt(prv[:, 1, :], lvv[:, 2, :], lvv[:, 3, :], ALU.mult)
        l2 = kb.sb("dl2", [128, 2], F32)
        kb.red(l2[:, 0:1], prv[:, 0, :])
        kb.red(l2[:, 1:2], prv[:, 1, :])
        kb.act(l2[:], l2[:], AF.Exp)
        self.nlam = kb.sb("dnlam", [128, 1], F32)
        kb.tt(self.nlam[:], l2[:, 1:2], l2[:, 0:1], ALU.subtract)
        kb.ts(self.nlam[:], self.nlam[:], -float(lam_init), None, op0=ALU.add)
        self.gsc = kb.sb("dgsc", [128, 128], F32)
        kb.dma(self.gsc[:], norm_g.partition_broadcast(128))
        kb.ts(self.gsc[:], self.gsc[:], 1.0 - float(lam_init), None, op0=ALU.mult)
        self.Bhl = kb.sb("dBhl", [128, 8 * 2 * 2 * 128], BF16)
        self.Bv = self.Bhl[:].rearrange("p (h d s q) -> p h d s q", h=8, d=2, s=2)
        self.bfar = kb.sb("dbfar", [128, 8], F32)
        kb.dma(self.bfar[:], bfar.partition_broadcast(128))
        self.ones2 = kb.sb("dones2", [2, 128], BF16)
        kb.memset(self.ones2[:], 1.0)
        self.cfar = kb.sb("dcfar", [2, 8 * 128], BF16)
        with kb.scope():
            bt = kb.sb("dbt", [128, 8 * 2 * 128], F32)
            btv = bt[:].rearrange("p (h d q) -> p h d q", h=8, d=2)
            kb.dma(btv, BT.rearrange("h d k q -> k h d q"))
            dm = kb.sb("ddm", [128, 128], F32)
            kb.dma(dm[:], dmask)
            rr = kb.sb("drr", [128, 128], F32)
            for hh in range(8):
                kb.tt(btv[:, hh, 0, :], btv[:, hh, 0, :], dm[:], ALU.add)
                for d in range(2):
                    kb.cp(self.Bv[:, hh, d, 0, :], btv[:, hh, d, :])
                    kb.tt(rr[:], btv[:, hh, d, :], self.Bv[:, hh, d, 0, :], ALU.subtract)
                    kb.cp(self.Bv[:, hh, d, 1, :], rr[:])
            cf = kb.sb("dcf", [2, 8], F32)
            kb.dma(cf[0:1, :], bfar.rearrange("(o h) -> o h", o=1))
            kb.dma(cf[1:2, :], bfar.rearrange("(o h) -> o h", o=1))
            cfb = kb.sb("dcfb", [2, 8], BF16)
            kb.cp(cfb[:], cf[:])
            cr = kb.sb("dcr", [2, 8], F32)
            kb.tt(cr[:], cf[:], cfb[:], ALU.subtract)
            cfl = kb.sb("dcfl", [2, 8], BF16)
            kb.cp(cfl[:], cr[:])
            kb.dma(cfb[1:2, :], cfl[1:2, :])
            cfv = self.cfar[:].rearrange("p (h q) -> p h q", h=8)
            kb.cp(cfv, cfb[:].unsqueeze(2).to_broadcast([2, 8, 128]))
            kb.barrier()
        self.rz = [kb.sb("drz%d" % i, [128, 2], F32) for i in range(3)]
        self.o = [kb.sb("do%d" % i, [128, 128], F32) for i in range(3)]
        self.sq = kb.sb("dsq", [128, 128], F32)
        self.ss = [kb.sb("dss%d" % i, [128, 1], F32) for i in range(3)]
        self.ost = [kb.sb("dost%d" % i, [128, 128], BF16) for i in range(3)]
        self.pi = 0

    def groups(self, seq):
        return [(seq, h) for h in range(8)]

    def subheads(self, g):
        return [0, 1]

    def load(self, g, buf):
        kb, S = self.kb, self.S
        seq, h = g
        t0 = seq * S
        for mm in range(2):
            r0 = 128 * h + 64 * mm
            kb.dma(self.Qa[buf][mm][:], self.QT[r0:r0 + 64, t0:t0 + S])
            kb.dma(self.Ka[buf][mm][:], self.KT[r0:r0 + 64, t0:t0 + S])
        vt = self.Vt[buf][:].rearrange("p (b d) -> p b d", d=129)
        kb.dma(vt[:, :, 0:128], self.V[t0:t0 + S, 128 * h:128 * h + 128].rearrange("(b p) d -> p b d", p=128))

    def qk(self, g, sh, j, i, ps, buf):
        kb = self.kb
        seq, h = g
        r = i - 4 * j
        kb.mm(ps[:], self.Ka[buf][sh][:, i * 128:(i + 1) * 128], self.Qa[buf][sh][:, j * 512:(j + 1) * 512],
              start=True, stop=(r < -1))
        if r >= -1:
            for qb in range(4):
                d = r - qb
                o_ = ps[:, qb * 128:(qb + 1) * 128]
                if d <= -2:
                    kb.mm(o_, self.ones2[:], self.cfar[:, h * 128:(h + 1) * 128], start=False, stop=False)
                elif d <= 0:
                    kb.mm(o_, self.gr.ident[:], self.Bv[:, h, -d, 0, :], start=False, stop=False)
                    kb.mm(o_, self.gr.ident[:], self.Bv[:, h, -d, 1, :], start=False, stop=False)

    def evac(self, g, sh, j, i, ps, P, buf):
        seq, h = g
        if i - 4 * j < -1:
            self.kb.act(P[:], ps[:], AF.Exp, bias=self.bfar[:, h:h + 1])
        else:
            self.kb.act(P[:], ps[:], AF.Exp)

    def ocols(self, g, sh):
        return 256 * sh, 129

    def vblk(self, g, sh, i, buf):
        return self.Vt[buf][:, i * 129:(i + 1) * 129]

    def post(self, g, j, outs, buf):
        kb, S = self.kb, self.S
        seq, h = g
        for qb in range(4):
            k = self.pi % 3
            self.pi += 1
            rz, o, ss, st = self.rz[k], self.o[k], self.ss[k], self.ost[k]
            ob = outs[qb]
            kb.recip(rz[:, 0:1], ob[:, 128:129])
            kb.recip(rz[:, 1:2], ob[:, 384:385])
            kb.tt(rz[:, 1:2], rz[:, 1:2], self.nlam[:], ALU.mult)
            kb.ts(o[:], ob[:, 0:128], rz[:, 0:1], None, op0=ALU.mult)
            kb.stt(o[:], ob[:, 256:384], rz[:, 1:2], o[:], ALU.mult, ALU.add)
            kb.tt(self.sq[:], o[:], o[:], ALU.mult, e="pool")
            kb.red(ss[:], self.sq[:])
            kb.act(ss[:], ss[:], AF.Sqrt, bias=self.gr.epsc[:, 0:1], scale=1.0 / 128)
            kb.recip(ss[:], ss[:])
            kb.stt(st[:], o[:], ss[:, 0:1], self.gsc[:], ALU.mult, ALU.mult)
            tok = seq * S + (4 * j + qb) * 128
            kb.dma(self.O[tok:tok + 128, 128 * h:128 * h + 128], st[:], q="pool")


def _t5_bucket_np(rel):
    nb = 16
    max_exact = 8
    ret = (rel > 0).astype(np.int64) * nb
    n = np.abs(rel)
    nf = np.maximum(n, 1).astype(np.float32)
    large = max_exact + (np.log(nf / max_exact) / math.log(128 / max_exact) * (nb - max_exact)).astype(np.int32)
    large = np.minimum(large, nb - 1)
    return ret + np.where(n < max_exact, n, large)


def diff_tables(rel_bias):
    kl = np.arange(128)[:, None]
    ql = np.arange(128)[None, :]
    idx0 = _t5_bucket_np(kl - ql)
    idx1 = _t5_bucket_np(kl - ql - 128)
    rb = np.asarray(rel_bias, dtype=np.float32)
    BT = np.stack([rb[idx0], rb[idx1]], axis=0)
    BT = np.ascontiguousarray(BT.transpose(3, 0, 1, 2))
    bfar = np.ascontiguousarray(rb[15, :])
    dmask = np.where((kl >= 64) & (ql < 64), NEG, 0.0).astype(np.float32)
    return BT, bfar, dmask


def cmask_np():
    k = np.arange(128)[:, None]
    q = np.arange(512)[None, :]
    return np.concatenate([np.where(128 * r + k <= q, 0.0, NEG) for r in range(4)], axis=1).astype(np.float32)


RET_LG = [math.log1p(-2.0 ** (-5.0 - h)) for h in range(4)]


def ret_tables(S):
    d = np.arange(0, 256, 2, dtype=np.float32) / np.float32(256.0)
    inv = (1.0 / (np.float32(10000.0) ** d)).astype(np.float32)
    ang = np.arange(S, dtype=np.float32)[None, :] * inv[:, None]
    cosT = np.cos(ang).astype(np.float32)
    sinT = np.sin(ang).astype(np.float32)
    t = np.arange(512)
    dq = np.stack([np.exp(RET_LG[h] * t) for h in range(4)]).astype(np.float32)
    dk = np.stack([np.exp(-RET_LG[h] * (t % 128)) / 16.0 for h in range(4)]).astype(np.float32)
    kl = np.arange(128)[:, None]
    q = np.arange(512)[None, :]
    mret = np.zeros((4, 4, 128, 512), np.float32)
    for h in range(4):
        for r in range(4):
            k = 128 * r + kl
            ck, cq = k // 64, q // 64
            f = np.where(ck < cq, 1.0, np.where(ck > cq, 0.0, np.where(k <= q, 1.0, np.exp(2.0 * RET_LG[h] * (k - q)))))
            mret[h, r] = f * np.exp(-RET_LG[h] * 128.0 * r)
    return cosT, sinT, dq, dk, mret


class RetSpec:
    nbuf = 1

    def __init__(self, m, S, QT, KT, V, GS, O, norm_g, mret):
        kb = m.kb
        self.kb, self.gr, self.S = kb, m.gr, S
        self.QT, self.KT, self.V, self.GS, self.O = QT, KT, V, GS, O
        self.Q = [kb.sb("rQ%d" % c, [128, S], BF16) for c in range(2)]
        self.K = [kb.sb("rK%d" % c, [128, S], BF16) for c in range(2)]
        self.Vt = kb.sb("rV", [128, (S // 128) * 512], BF16)
        self.M = kb.sb("rM", [128, 16 * 512], F32)
        kb.dma(self.M[:].rearrange("p (a q) -> p a q", a=16), mret.rearrange("h r k q -> k (h r) q"))
        self.gb = kb.sb("rgb", [128, 2048], F32)
        kb.dma(self.gb[:], norm_g.partition_broadcast(128))
        self.sq = kb.sb("rsq", [128, 512], F32)
        self.ss = [kb.sb("rss%d" % i, [128, 1], F32) for i in range(2)]
        self.on = [kb.sb("ron%d" % i, [128, 512], F32) for i in range(2)]
        self.gs = [kb.sb("rgs%d" % i, [128, 512], BF16) for i in range(2)]
        self.ost = [kb.sb("rost%d" % i, [128, 512], BF16) for i in range(2)]
        self.pi = 0

    def groups(self, seq):
        return [(seq, h) for h in range(4)]

    def subheads(self, g):
        return [0]

    def load(self, g, buf):
        kb, S = self.kb, self.S
        seq, h = g
        t0 = seq * S
        for c in range(2):
            r0 = 256 * h + 128 * c
            kb.dma(self.Q[c][:], self.QT[r0:r0 + 128, t0:t0 + S])
            kb.dma(self.K[c][:], self.KT[r0:r0 + 128, t0:t0 + S])
        kb.dma(self.Vt[:].rearrange("p (b d) -> p b d", d=512),
               self.V[t0:t0 + S, 512 * h:512 * h + 512].rearrange("(b p) d -> p b d", p=128))

    def qk(self, g, sh, j, i, ps, buf):
        kb = self.kb
        for c in range(2):
            kb.mm(ps[:], self.K[c][:, i * 128:(i + 1) * 128], self.Q[c][:, j * 512:(j + 1) * 512],
                  start=(c == 0), stop=(c == 1))

    def evac(self, g, sh, j, i, ps, P, buf):
        seq, h = g
        r = i - 4 * j
        if r < 0:
            self.kb.act(P[:], ps[:], AF.Copy, scale=float(math.exp(RET_LG[h] * (512 * j - 128 * i))))
        else:
            a = h * 4 + r
            self.kb.tt(P[:], ps[:], self.M[:, a * 512:(a + 1) * 512], ALU.mult)

    def ocols(self, g, sh):
        return 0, 512

    def vblk(self, g, sh, i, buf):
        return self.Vt[:, i * 512:(i + 1) * 512]

    def post(self, g, j, outs, buf):
        kb, S = self.kb, self.S
        seq, h = g
        for qb in range(4):
            k = self.pi % 2
            self.pi += 1
            ob = outs[qb]
            ss, on, gs, st = self.ss[k], self.on[k], self.gs[k], self.ost[k]
            tok = seq * S + (4 * j + qb) * 128
            kb.dma(gs[:], self.GS[tok:tok + 128, 512 * h:512 * h + 512])
            kb.act(self.sq[:], ob[:], AF.Square)
            kb.red(ss[:], self.sq[:])
            kb.act(ss[:], ss[:], AF.Sqrt, bias=self.gr.epsc[:, 0:1], scale=1.0 / 512)
            kb.recip(ss[:], ss[:])
            kb.stt(on[:], ob[:], ss[:, 0:1], self.gb[:, 512 * h:512 * h + 512], ALU.mult, ALU.mult)
            kb.tt(st[:], on[:], gs[:], ALU.mult, e="pool")
            kb.dma(self.O[tok:tok + 128, 512 * h:512 * h + 512], st[:], q="pool")


def ret_mixer(self, h, w_in, gcol, norm_g, w_out, cosT, sinT, dq, dk, mret):
    kb, gr, T, S, nseq = self.kb, self.gr, self.T, self.S, self.nseq
    QT = self.dscr("QT", [1024, T], BF16)
    KT = self.dscr("KT", [1024, T], BF16)
    V = self.dscr("Vtm", [T, 2048], BF16)
    GS = self.dscr("GStm", [T, 2048], BF16)
    O = self.dscr("Otm", [T, 2048], BF16)
    with kb.scope():
        Wv = load_weights(kb, gr, w_in, 1024, 6144, gcol)
        cs = [kb.sb("rcs%d" % i, [128, 1024], F32) for i in range(2)]
        dtab = kb.sb("rdtab", [128, 2 * 4 * 512], F32)
        dtv = dtab[:].rearrange("p (a h t) -> p a h t", a=2, h=4)
        kb.dma(dtv[:, 0], dq.partition_broadcast(128))
        kb.dma(dtv[:, 1], dk.partition_broadcast(128))
        tmp = [kb.sb("rtmp%d" % i, [128, 512], F32) for i in range(4)]
        yst = [kb.sb("ryst%d" % i, [128, 512], BF16) for i in range(4)]
        state = dict(blk=-1, n=0)

        def rot_epi(which, dst):
            def epi(blk, tok0, c0, pss):
                if state["blk"] != blk:
                    state["blk"] = blk
                    p0 = tok0 % S
                    c_ = cs[blk % 2]
                    kb.dma(c_[:, 0:512], cosT[:, p0:p0 + 512])
                    kb.dma(c_[:, 512:1024], sinT[:, p0:p0 + 512])
                c_ = cs[blk % 2]
                cosb, sinb = c_[:, 0:512], c_[:, 512:1024]
                hh = c0 // 2
                d_ = dtv[:, which, hh, :]
                x1, x2 = pss[0], pss[1]
                n = state["n"]
                state["n"] += 1
                ta, tb = tmp[(n % 2) * 2], tmp[(n % 2) * 2 + 1]
                y1, y2 = yst[(n % 2) * 2], yst[(n % 2) * 2 + 1]
                kb.tt(ta[:], x1[:], cosb, ALU.mult)
                kb.tt(tb[:], x2[:], sinb, ALU.mult)
                kb.tt(ta[:], ta[:], tb[:], ALU.subtract, e="pool")
                kb.tt(y1[:], ta[:], d_, ALU.mult, e="pool")
                kb.dma(dst[256 * hh:256 * hh + 128, tok0:tok0 + 512], y1[:], q="sp")
                kb.tt(ta[:], x1[:], sinb, ALU.mult)
                kb.tt(tb[:], x2[:], cosb, ALU.mult)
                kb.tt(ta[:], ta[:], tb[:], ALU.add, e="pool")
                kb.tt(y2[:], ta[:], d_, ALU.mult, e="pool")
                kb.dma(dst[256 * hh + 128:256 * hh + 256, tok0:tok0 + 512], y2[:], q="sp")
            return epi

        segs = [dict(kind="FM", n0=0, n1=1024, group=2, epi=rot_epi(0, QT)),
                dict(kind="FM", n0=1024, n1=2048, group=2, epi=rot_epi(1, KT)),
                dict(kind="TM", n0=2048, n1=4096, epi=self.epi.tm_store(V)),
                dict(kind="TM", n0=4096, n1=6144, epi=self.epi.tm_store(GS, func=AF.Silu))]
        gemm_stage(kb, gr, h, T, 1024, Wv, segs, norm=True)
        kb.barrier()
    with kb.scope():
        spec = RetSpec(self, S, QT, KT, V, GS, O, norm_g, mret)
        attn_stage(kb, spec, S, nseq)
        kb.barrier()
    with kb.scope():
        Wv = load_weights(kb, gr, w_out, 2048, 1024, None)
        segs = [dict(kind="TM", n0=0, n1=1024, epi=self.epi.tm_resid(h))]
        gemm_stage(kb, gr, O, T, 2048, Wv, segs, x_bf16=True)
        kb.barrier()


Model.ret_mixer = ret_mixer


def ssd_tables():
    sa = np.zeros((48, 1024), np.float32)
    for hh in range(8):
        sa[6 * hh:6 * hh + 3, hh * 128:(hh + 1) * 128] = 1.0
    return sa


class SsdSpec:
    nbuf = 1

    def __init__(self, m, S, AQK, BTd, CTd, XSd, DTd, ZS, O, dsk_rep, norm_g, sa_init):
        kb = m.kb
        self.kb, self.gr, self.S, self.m = kb, m.gr, S, m
        self.AQK, self.BTd, self.CTd, self.XSd, self.DTd, self.ZS, self.O = AQK, BTd, CTd, XSd, DTd, ZS, O
        nb = S // 128
        self.nb = nb
        self.TA = kb.sb("sTA", [48, S], BF16)
        kb.memset(self.TA[:], 1.0)
        self.SA = [kb.sb("sSA%d" % i, [48, 1024], BF16) for i in range(3)]
        with kb.scope():
            t = kb.sb("ssat", [48, 1024], F32)
            kb.dma(t[:], sa_init)
            for i in range(3):
                kb.cp(self.SA[i][:], t[:])
            kb.barrier()
        self.Bt = kb.sb("sBt", [128, S], BF16)
        self.Ct = kb.sb("sCt", [128, S], BF16)
        self.XS = kb.sb("sXS", [128, nb * 512], BF16)
        self.XP = kb.sb("sXP", [128, nb * 512], BF16)
        self.DTt = kb.sb("sDT", [128, nb * 8], F32)
        self.E = [kb.sb("sE%d" % i, [128, 512], F32) for i in range(2)]
        self.cbp = [kb.ps("scb%d" % i, [128, 512], F32) for i in range(2)]
        self.dsk = kb.sb("sdsk", [128, 2048], F32)
        kb.dma(self.dsk[:], dsk_rep.partition_broadcast(128))
        self.gn = kb.sb("sgn", [128, 2048], F32)
        kb.dma(self.gn[:], norm_g.partition_broadcast(128))
        self.y1 = [kb.sb("sy1%d" % i, [128, 512], F32) for i in range(2)]
        self.y3 = [kb.sb("sy3%d" % i, [128, 512], F32) for i in range(2)]
        self.zs = [kb.sb("szs%d" % i, [128, 512], BF16) for i in range(2)]
        self.ost = [kb.sb("sost%d" % i, [128, 512], BF16) for i in range(2)]
        self.sq = kb.sb("ssq", [128, 512], F32)
        self.ss = [kb.sb("sss%d" % i, [128, 1], F32) for i in range(2)]
        self.pi = 0
        self.ni = 0
        self.ne = 0

    def groups(self, seq):
        return [(seq, g) for g in range(4)]

    def subheads(self, g):
        return list(range(8))

    def load(self, g, buf):
        kb, S, nb = self.kb, self.S, self.nb
        seq, gg = g
        t0 = seq * S
        for hh in range(8):
            kb.dma(self.TA[6 * hh:6 * hh + 3, :], self.AQK[gg * 8 + hh, 0:3, t0:t0 + S])
        kb.dma(self.Bt[:], self.BTd[gg * 128:(gg + 1) * 128, t0:t0 + S])
        kb.dma(self.Ct[:], self.CTd[gg * 128:(gg + 1) * 128, t0:t0 + S])
        kb.dma(self.XS[:].rearrange("p (b d) -> p b d", d=512),
               self.XSd[t0:t0 + S, gg * 512:(gg + 1) * 512].rearrange("(b p) d -> p b d", p=128))
        kb.dma(self.DTt[:].rearrange("p (b h) -> p b h", h=8),
               self.DTd[t0:t0 + S, gg * 8:(gg + 1) * 8].rearrange("(b p) h -> p b h", p=128))
        kb.tt(self.XP[:].rearrange("p (a d) -> p a d", d=64), self.XS[:].rearrange("p (a d) -> p a d", d=64),
              self.DTt[:].unsqueeze(2).to_broadcast([128, nb * 8, 64]), ALU.mult)

    def pre(self, g, j, i, buf):
        kb, S = self.kb, self.S
        seq, gg = g
        t0 = seq * S
        sa = self.SA[self.ni % 3]
        cb = self.cbp[self.ni % 2]
        self.ni += 1
        for hh in range(8):
            kb.dma(sa[6 * hh + 3:6 * hh + 6, hh * 128:(hh + 1) * 128],
                   self.AQK[gg * 8 + hh, 3:6, t0 + i * 128:t0 + (i + 1) * 128])
        kb.mm(cb[:], self.Bt[:, i * 128:(i + 1) * 128], self.Ct[:, j * 512:(j + 1) * 512])
        self.cur = (sa, cb)

    def qk(self, g, sh, j, i, ps, buf):
        kb = self.kb
        sa, cb = self.cur
        r = i - 4 * j
        kb.mm(ps[:], sa[:, sh * 128:(sh + 1) * 128], self.TA[:, j * 512:(j + 1) * 512], start=True, stop=(r < 0))
        if r >= 0:
            kb.mm(ps[:], self.gr.ident[:], self.m.cmask[:, r * 512:(r + 1) * 512], start=False, stop=True)

    def evac(self, g, sh, j, i, ps, P, buf):
        kb = self.kb
        sa, cb = self.cur
        E = self.E[self.ne % 2]
        self.ne += 1
        kb.act(E[:], ps[:], AF.Exp)
        kb.tt(P[:], E[:], cb[:], ALU.mult)

    def ocols(self, g, sh):
        return sh * 64, 64

    def vblk(self, g, sh, i, buf):
        return self.XP[:, i * 512 + sh * 64:i * 512 + sh * 64 + 64]

    def post(self, g, j, outs, buf):
        kb, S = self.kb, self.S
        seq, gg = g
        for qb in range(4):
            k = self.pi % 2
            self.pi += 1
            b = 4 * j + qb
            tok = seq * S + b * 128
            y1, y3, zs, st, ss = self.y1[k], self.y3[k], self.zs[k], self.ost[k], self.ss[k]
            kb.dma(zs[:], self.ZS[tok:tok + 128, gg * 512:(gg + 1) * 512])
            kb.tt(y1[:], self.XS[:, b * 512:(b + 1) * 512], self.dsk[:, gg * 512:(gg + 1) * 512], ALU.mult, e="pool")
            kb.tt(y1[:], outs[qb][:], y1[:], ALU.add)
            kb.tt(y3[:], y1[:], zs[:], ALU.mult, e="pool")
            kb.act(self.sq[:], y3[:], AF.Square)
            kb.red(ss[:], self.sq[:])
            kb.act(ss[:], ss[:], AF.Sqrt, bias=self.gr.epsc[:, 0:1], scale=1.0 / 512)
            kb.recip(ss[:], ss[:])
            kb.stt(st[:], y3[:], ss[:, 0:1], self.gn[:, gg * 512:(gg + 1) * 512], ALU.mult, ALU.mult)
            kb.dma(self.O[tok:tok + 128, gg * 512:(gg + 1) * 512], st[:], q="pool")


def ssd_mixer(self, h, w_in, gcol, conv_wT, conv_b, dt_bias, a_log, dsk_rep, norm_g, w_out, sa_init):
    kb, gr, T, S, nseq = self.kb, self.gr, self.T, self.S, self.nseq
    nc = self.nc
    ZS = self.dscr("GStm", [T, 2048], BF16)
    XSd = self.dscr("Vtm", [T, 2048], BF16)
    BTd = self.dscr("QT", [1024, T], BF16)
    CTd = self.dscr("KT", [1024, T], BF16)
    AQK = self.dscr("CQK", [32, 6, T], BF16)
    DTd = self.dscr("DTd", [T, 32], F32)
    O = self.dscr("Otm", [T, 2048], BF16)
    bps = S // 512
    with kb.scope():
        Wv = load_weights(kb, gr, w_in, 1024, 5152, gcol)
        cw = kb.sb("scw", [128, 24 * 4], F32)
        cwv = cw[:].rearrange("p (c k) -> p c k", k=4)
        kb.dma(cwv, conv_wT.rearrange("(c p) k -> p c k", p=128))
        cbias = kb.sb("scbias", [128, 24], F32)
        kb.dma(cbias[:], conv_b.rearrange("(c p) -> p c", p=128))
        hal = kb.sb("shal", [128, 24 * 3], F32)
        cbuf = [kb.sb("scbuf%d" % i, [128, 515], F32) for i in range(2)]
        accb = [kb.sb("saccb%d" % i, [128, 512], F32) for i in range(2)]
        sil = [kb.sb("ssil%d" % i, [128, 512], BF16) for i in range(2)]
        xst = [kb.sb("sxst%d" % i, [128, 512], BF16) for i in range(2)]
        ptx = kb.ps("sptx", [128, 1024], BF16)
        pdt = kb.ps("spdt", [128, 512], F32)
        dtb = kb.sb("sdtb", [32, 1], F32)
        kb.dma(dtb[:], dt_bias.rearrange("(p o) -> p o", o=1))
        nA = kb.sb("snA", [32, 1], F32)
        kb.dma(nA[:], a_log.rearrange("(p o) -> p o", o=1))
        kb.act(nA[:], nA[:], AF.Exp)
        kb.ts(nA[:], nA[:], -1.0, None, op0=ALU.mult)
        carry = kb.sb("scarry", [32, 1], F32)
        d_ = [kb.sb("sd%d" % i, [32, 512], F32) for i in range(5)]
        cum = kb.sb("scum", [32, 512], F32)
        rr = kb.sb("srr", [32, 512], F32)
        spl = kb.sb("sspl", [32, 6 * 512], BF16)
        splv = spl[:].rearrange("p (s n) -> p s n", s=6)
        dtm = kb.sb("sdtm", [128, 4 * 32], F32)
        cn = [0]

        def conv_epi(blk, tok0, c, pss):
            ps = pss[0]
            n = cn[0]
            cn[0] += 1
            cb, acc, sl = cbuf[n % 2], accb[n % 2], sil[n % 2]
            if blk % bps == 0:
                kb.memset(cb[:, 0:3], 0.0)
            else:
                kb.cp(cb[:, 0:3], hal[:, 3 * c:3 * c + 3])
            kb.act(cb[:, 3:515], ps[:], AF.Copy)
            kb.cp(hal[:, 3 * c:3 * c + 3], cb[:, 512:515])
            kb.ts(acc[:], cb[:, 0:512], cwv[:, c, 0:1], cbias[:, c:c + 1], op0=ALU.mult, op1=ALU.add)
            for k in range(1, 4):
                kb.stt(acc[:], cb[:, k:k + 512], cwv[:, c, k:k + 1], acc[:], ALU.mult, ALU.add)
            kb.act(sl[:], acc[:], AF.Silu)
            if c < 16:
                for s_ in range(4):
                    kb.tr(ptx[:, s_ * 128:(s_ + 1) * 128], sl[:, s_ * 128:(s_ + 1) * 128], gr.ident[:])
                xs_ = xst[n % 2]
                kb.cp(xs_[:], ptx[:, 0:512])
                kb.dma(XSd[tok0:tok0 + 512, c * 128:(c + 1) * 128].rearrange("(s p) ch -> p s ch", p=128),
                       xs_[:].rearrange("p (s ch) -> p s ch", s=4), q="pool")
            elif c < 20:
                kb.dma(BTd[(c - 16) * 128:(c - 15) * 128, tok0:tok0 + 512], sl[:], q="pool")
            else:
                kb.dma(CTd[(c - 20) * 128:(c - 19) * 128, tok0:tok0 + 512], sl[:], q="pool")

        def dt_epi(blk, tok0, c0, pss):
            ps = pss[0]
            xb, ab, e_, r_, dtt = d_
            if blk % bps == 0:
                kb.memset(carry[:], 0.0)
            kb.act(xb[:], ps[0:32, :], AF.Identity, bias=dtb[:, 0:1])
            kb.act(ab[:], xb[:], AF.Abs)
            kb.act(e_[:], ab[:], AF.Exp, scale=-1.0)
            kb.act(e_[:], e_[:], AF.Ln, bias=self.one[0:32, 0:1])
            kb.ts(r_[:], xb[:], 0.0, None, op0=ALU.max)
            kb.tt(dtt[:], r_[:], e_[:], ALU.add)
            kb.ts(ab[:], dtt[:], nA[:, 0:1], None, op0=ALU.mult)
            kb.op("dve", lambda: nc.vector.tensor_tensor_scan(cum[:], ab[:], self.zeros[0:32, :], carry[:, 0:1], ALU.add, ALU.add),
                  [ab, self.zeros, carry], [cum])
            kb.cp(carry[:], cum[:, 511:512])
            split3(kb, cum[:], splv, rr, None, 32, 512)
            kb.dma(AQK[0:32, :, tok0:tok0 + 512], splv, q="pool")
            for s_ in range(4):
                kb.tr(pdt[:, s_ * 32:(s_ + 1) * 32], dtt[:, s_ * 128:(s_ + 1) * 128], gr.ident_f[0:32, 0:32])
            kb.cp(dtm[:], pdt[:, 0:128])
            kb.dma(DTd[tok0:tok0 + 512, :].rearrange("(s p) h -> p s h", p=128), dtm[:].rearrange("p (s h) -> p s h", s=4), q="pool")

        segs = [dict(kind="TM", n0=0, n1=2048, epi=self.epi.tm_store(ZS, func=AF.Silu)),
                dict(kind="FM", n0=2048, n1=5120, epi=conv_epi),
                dict(kind="FM", n0=5120, n1=5152, epi=dt_epi)]
        gemm_stage(kb, gr, h, T, 1024, Wv, segs, norm=True)
        kb.barrier()
    with kb.scope():
        spec = SsdSpec(self, S, AQK, BTd, CTd, XSd, DTd, ZS, O, dsk_rep, norm_g, sa_init)
        attn_stage(kb, spec, S, nseq)
        kb.barrier()
    with kb.scope():
        Wv = load_weights(kb, gr, w_out, 2048, 1024, None)
        segs = [dict(kind="TM", n0=0, n1=1024, epi=self.epi.tm_resid(h))]
        gemm_stage(kb, gr, O, T, 2048, Wv, segs, x_bf16=True)
        kb.barrier()


Model.ssd_mixer = ssd_mixer


DEPTH = 4
SEQ = 4096
NSEQ = 2


def build_program(nseq=NSEQ, S=SEQ):
    T = nseq * S
    nc = bass.Bass("TRN2", target_bir_lowering=False)
    es = ExitStack()
    m = Model(nc, es, nseq, S)
    I = m.inp
    I("ident", [128, 128]); I("cmask", [128, 2048])
    x = I("x", [T, 1024]); p = I("p", [DEPTH, T, 256])
    mix_norm = I("mix_norm", [DEPTH, 1024]); ffn_norm = I("ffn_norm", [DEPTH, 1024]); ple_norm = I("ple_norm", [DEPTH, 1024])
    final_norm = I("final_norm", [1024])
    ssd_w_in = I("ssd_w_in", [1024, 5152]); ssd_cwT = I("ssd_cwT", [3072, 4]); ssd_cb = I("ssd_cb", [3072])
    ssd_dtb = I("ssd_dtb", [32]); ssd_alog = I("ssd_alog", [32]); ssd_dsk = I("ssd_dsk", [2048]); ssd_norm = I("ssd_norm", [2048])
    ssd_w_out = I("ssd_w_out", [2048, 1024]); ssd_sa = I("ssd_sa", [48, 1024])
    ret_w_in = I("ret_w_in", [1024, 6144]); ret_norm = I("ret_norm", [2048]); ret_w_out = I("ret_w_out", [2048, 1024])
    ret_cos = I("ret_cos", [128, S]); ret_sin = I("ret_sin", [128, S]); ret_dq = I("ret_dq", [4, 512]); ret_dk = I("ret_dk", [4, 512])
    ret_m = I("ret_m", [4, 4, 128, 512])
    diff_w_in = I("diff_w_in", [1024, 3072]); diff_lam = I("diff_lam", [4, 64]); diff_norm = I("diff_norm", [128])
    diff_w_out = I("diff_w_out", [1024, 1024]); diff_BT = I("diff_BT", [8, 2, 128, 128]); diff_bfar = I("diff_bfar", [8])
    diff_mask = I("diff_mask", [128, 128])
    fox_w_in = I("fox_w_in", [1024, 3088]); fox_bf = I("fox_bf", [16]); fox_w_out = I("fox_w_out", [1024, 1024])
    peer_wq = I("peer_wq", [DEPTH, 1024, 2048]); peer_kT = I("peer_kT", [DEPTH, 16, 128, 128])
    peer_uT = I("peer_uT", [DEPTH, 1024, 16384]); peer_v = I("peer_v", [DEPTH, 16384, 1024])
    ple_proj = I("ple_proj", [DEPTH, 256, 1024]); ple_gate = I("ple_gate", [DEPTH, 1024, 1024])
    out = nc.dram_tensor("out", [T, 1024], F32, kind="ExternalOutput").ap()
    hb = m.dscr("hbuf", [T, 1024], F32)
    m.setup()
    kb = m.kb
    for r0 in range(0, T, 512):
        kb.dma(hb[r0:r0 + 512, :], x[r0:r0 + 512, :])
    kb.barrier()
    for i in range(DEPTH):
        if i == 0:
            m.ssd_mixer(hb, ssd_w_in, mix_norm[i], ssd_cwT, ssd_cb, ssd_dtb, ssd_alog, ssd_dsk, ssd_norm, ssd_w_out, ssd_sa)
        elif i == 1:
            m.ret_mixer(hb, ret_w_in, mix_norm[i], ret_norm, ret_w_out, ret_cos, ret_sin, ret_dq, ret_dk, ret_m)
        elif i == 2:
            lam_init = 0.8 - 0.6 * math.exp(-0.3 * i)
            m.diff_mixer(hb, diff_w_in, mix_norm[i], diff_lam, lam_init, diff_norm, diff_w_out, diff_BT, diff_bfar, diff_mask)
        else:
            m.fox_mixer(hb, fox_w_in, mix_norm[i], fox_bf, fox_w_out)
        m.peer(hb, ffn_norm[i], peer_wq[i], peer_kT[i], peer_uT[i], peer_v[i])
        m.ple(hb, ple_norm[i], ple_gate[i], p[i], ple_proj[i])
    m.final(hb, final_norm, out)
    es.close()
    return nc, m


def host_inputs(inputs, nseq=NSEQ, S=SEQ, ncores=NCORES):
    f = lambda a: np.ascontiguousarray(np.asarray(a, dtype=np.float32))
    g = {k: np.asarray(v) for k, v in inputs.items()}
    BT, bfar, dmask = diff_tables(g["rel_bias"])
    cosT, sinT, dq, dk, mret = ret_tables(S)
    shared = {
        "ident": np.eye(128, dtype=np.float32), "cmask": cmask_np(),
        "mix_norm": f(g["mix_norm"]), "ffn_norm": f(g["ffn_norm"]), "ple_norm": f(g["ple_norm"]), "final_norm": f(g["final_norm"]),
        "ssd_w_in": f(g["ssd_w_in"][0]), "ssd_cwT": f(g["ssd_conv_w"][0][:, 0, :].T), "ssd_cb": f(g["ssd_conv_b"][0]),
        "ssd_dtb": f(g["ssd_dt_bias"][0]), "ssd_alog": f(g["ssd_a_log"][0]), "ssd_dsk": f(np.repeat(g["ssd_d"][0], 64)),
        "ssd_norm": f(g["ssd_norm"][0]), "ssd_w_out": f(g["ssd_w_out"][0]), "ssd_sa": ssd_tables(),
        "ret_w_in": f(g["ret_w_in"][0]), "ret_norm": f(g["ret_norm"][0]), "ret_w_out": f(g["ret_w_out"][0]),
        "ret_cos": cosT, "ret_sin": sinT, "ret_dq": dq, "ret_dk": dk, "ret_m": mret,
        "diff_w_in": f(g["diff_w_in"][0]), "diff_lam": f(g["diff_lambda"][0]), "diff_norm": f(g["diff_norm"][0]),
        "diff_w_out": f(g["diff_w_out"][0]), "diff_BT": f(BT), "diff_bfar": f(bfar), "diff_mask": dmask,
        "fox_w_in": f(g["fox_w_in"][0]), "fox_bf": f(g["fox_b_f"][0]), "fox_w_out": f(g["fox_w_out"][0]),
        "peer_wq": f(g["peer_w_q"]),
        "peer_kT": f(g["peer_keys"].reshape(DEPTH, 16, 128, 128).transpose(0, 1, 3, 2)),
        "peer_uT": f(g["peer_u"].transpose(0, 2, 1)), "peer_v": f(g["peer_v"]),
        "ple_proj": f(g["ple_proj"]), "ple_gate": f(g["ple_gate"]),
    }
    T = nseq * S
    maps = []
    for c in range(ncores):
        d = dict(shared)
        d["x"] = f(g["x"][c * nseq:(c + 1) * nseq].reshape(T, 1024))
        d["p"] = f(g["p"][:, c * nseq:(c + 1) * nseq].reshape(DEPTH, T, 256))
        maps.append(d)
    return maps


def kernel(**inputs):
    nc, m = build_program()
    maps = host_inputs(inputs)
    res = run_bass_kernel_spmd(nc, maps, core_ids=list(range(NCORES)))
    outs = [np.asarray(r["out"]).reshape(NSEQ, SEQ, 1024) for r in res.results]
    return np.concatenate(outs, axis=0).astype(np.float32)


def peer_merged(self, h, gcol, w_q, keysT, uT, v):
    kb, gr, T = self.kb, self.gr, self.T
    nc = self.nc
    V = nc.vector
    UTb = self.dscr("UTb", [1024, 16384], BF16)
    Vb = self.dscr("Vb", [16384, 1024], BF16)
    Gd = self.dscr("Gd", [T, 16384], BF16)
    with kb.scope():
        kb.dma(gr.gcol[:, 0:8], gcol.rearrange("(c p) -> p c", p=128))
        st = [kb.sb("pst%d" % i, [128, 2048], F32) for i in range(3)]
        sb = [kb.sb("psb%d" % i, [128, 2048], BF16) for i in range(3)]
        n = 0
        for kc in range(8):
            for e0 in range(0, 16384, 2048):
                s_, b_ = st[n % 3], sb[n % 3]
                kb.dma(s_[:], uT[kc * 128:(kc + 1) * 128, e0:e0 + 2048])
                if n % 2 == 0:
                    kb.ts(b_[:], s_[:], gr.gcol[:, kc:kc + 1], None, op0=ALU.mult)
                else:
                    kb.act(b_[:], s_[:], AF.Copy, scale=gr.gcol[:, kc:kc + 1])
                kb.dma(UTb[kc * 128:(kc + 1) * 128, e0:e0 + 2048], b_[:], q="pool")
                n += 1
        for r0 in range(0, 16384, 256):
            s_, b_ = st[n % 3], sb[n % 3]
            kb.dma(s_[:].rearrange("p (c n) -> p c n", c=2), v[r0:r0 + 256, :].rearrange("(c p) n -> p c n", p=128))
            if n % 2 == 0:
                kb.cp(b_[:], s_[:], e="dve")
            else:
                kb.act(b_[:], s_[:], AF.Copy)
            kb.dma(Vb[r0:r0 + 256, :].rearrange("(c p) n -> p c n", p=128), b_[:].rearrange("p (c n) -> p c n", c=2), q="pool")
            n += 1
        kb.barrier()
    with kb.scope():
        Wv = load_weights(kb, gr, w_q, 1024, 2048, gcol)
        kT = kb.sb("keysT", [128, 16 * 128], BF16)
        with kb.scope():
            ktmp = kb.sb("ktmp", [128, 16 * 128], F32)
            kb.dma(ktmp[:].rearrange("p (c k) -> p c k", c=16), keysT.rearrange("c d k -> d c k"))
            kb.cp(kT[:], ktmp[:])
            kb.barrier()
        kTv = kT[:].rearrange("p (c k) -> p c k", c=16)
        qT = [kb.sb("pqT%d" % i, [128, 512], BF16) for i in range(2)]
        psc = kb.ps("psc", [128, 512], F32)
        sc_all = kb.sb("sc_all", [128, 4 * 16 * 128], F32)
        scv = sc_all[:].rearrange("p (s c k) -> p s c k", s=4, c=16)
        a16 = kb.sb("a16", [128, 16], F32)
        b16 = kb.sb("b16", [128, 16], F32)
        c16 = kb.sb("c16", [128, 16], F32)
        e16 = kb.sb("e16", [128, 16], F32)
        t128 = kb.sb("t128", [128, 128], F32)
        cand = kb.sb("cand", [128, 256], F32)
        cand2 = kb.sb("cand2", [128, 256], F32)
        tau = kb.sb("tau", [128, 8], F32)
        nb = kb.sb("nb", [128, 8], F32)
        zz = kb.sb("zz", [128, 1], F32)
        Sc = [kb.sb("Sc%d" % i, [128, 2048], F32) for i in range(2)]
        Ec = [kb.sb("Ec%d" % i, [128, 2048], BF16) for i in range(2)]
        Gb = [kb.sb("Gb%d" % i, [128, 2048], BF16) for i in range(2)]
        ut = [kb.sb("ut%d" % i, [128, 8 * 512], BF16) for i in range(2)]
        vt = [kb.sb("vt%d" % i, [128, 4 * 1024], BF16) for i in range(2)]
        gt = [kb.sb("gt%d" % i, [128, 512], BF16) for i in range(2)]
        gel = [kb.sb("gel%d" % i, [128, 512], F32) for i in range(1)]
        gh = [kb.sb("gh%d" % i, [128, 512], BF16) for i in range(2)]
        ghT = [kb.sb("ghT%d" % i, [128, 512], BF16) for i in range(1)]
        acc = [kb.sb("acc%d" % i, [128, 1024], F32) for i in range(4)]
        hst = gr.xin
        php = kb.ps("php", [128, 512], F32)
        ptp = kb.ps("ptp", [128, 1024], BF16)
        pop = [kb.ps("pop%d" % i, [128, 512], F32) for i in range(2)]
        state = dict(pending=None, nq=0, nd=0, gchunk=0)

        def top16(dst, src, tmp):
            kb.op("dve", lambda: V.max(out=dst[:, 0:8], in_=src), [src], [dst])
            kb.op("dve", lambda: V.match_replace(out=tmp, in_to_replace=dst[:, 0:8], in_values=src, imm_value=-1e30),
                  [dst, src], [tmp])
            kb.op("dve", lambda: V.max(out=dst[:, 8:16], in_=tmp), [tmp], [dst])

        def gates_gen(blk, tokb, sub):
            gkey = ("Gd", blk)
            for hh in range(8):
                s1 = scv[:, sub, 2 * hh, :]
                s2 = scv[:, sub, 2 * hh + 1, :]
                top16(a16, s1, t128[:])
                top16(b16, s2, t128[:])
                kb.tt(cand[:].rearrange("p (a b) -> p a b", a=16),
                      a16[:].unsqueeze(2).to_broadcast([128, 16, 16]),
                      b16[:].unsqueeze(1).to_broadcast([128, 16, 16]), ALU.add)
                top16(c16, cand[:], cand2[:])
                kb.cp(tau[:, hh:hh + 1], c16[:, 15:16])
                kb.ts(nb[:, hh:hh + 1], c16[:, 0:1], -1.0, None, op0=ALU.mult)
                kb.act(e16[:], c16[:], AF.Exp, bias=nb[:, hh:hh + 1])
                kb.red(zz[:], e16[:])
                kb.act(zz[:], zz[:], AF.Ln)
                kb.tt(nb[:, hh:hh + 1], nb[:, hh:hh + 1], zz[:], ALU.subtract)
                yield
            items = [(c, hh) for c in range(8) for hh in range(8)]

            def stage1(n):
                c, hh = items[n]
                S_, E_ = Sc[n % 2], Ec[n % 2]
                s1 = scv[:, sub, 2 * hh, 16 * c:16 * c + 16]
                s2 = scv[:, sub, 2 * hh + 1, :]
                S3 = S_[:].rearrange("p (a b) -> p a b", a=16)
                kb.tt(S3, s1.unsqueeze(2).to_broadcast([128, 16, 128]),
                      s2.unsqueeze(1).to_broadcast([128, 16, 128]), ALU.add)
                kb.act(E_[:], S_[:], AF.Exp, bias=nb[:, hh:hh + 1])

            def stage2(n):
                c, hh = items[n]
                S_, E_ = Sc[n % 2], Ec[n % 2]
                G = Gb[c % 2]
                dst = G if hh == 0 else E_
                kb.stt(dst[:], S_[:], tau[:, hh:hh + 1], E_[:], ALU.is_ge, ALU.mult)
                if hh > 0:
                    kb.tt(G[:], G[:], E_[:], ALU.add)
                if hh == 7:
                    kb.dma(Gd[tokb:tokb + 128, c * 2048:(c + 1) * 2048], G[:], q="sp", wk=gkey)

            stage1(0)
            for n in range(64):
                if n + 1 < 64:
                    stage1(n + 1)
                stage2(n)
                yield

        def dense_gen(blk, xTv):
            tokb = blk * 512
            gkey = ("Gd", blk)
            for ec in range(32):
                u_ = ut[ec % 2]
                v_ = vt[ec % 2]
                uv = u_[:].rearrange("p (c e) -> p c e", c=8)
                vv = v_[:].rearrange("p (c n) -> p c n", c=4)
                kb.dma(uv, UTb[:, ec * 512:(ec + 1) * 512].rearrange("(c p) e -> p c e", p=128))
                kb.dma(vv, Vb[ec * 512:(ec + 1) * 512, :].rearrange("(c p) n -> p c n", p=128))
                for sub in range(4):
                    k = state["nd"]
                    state["nd"] += 1
                    g_ = gt[k % 2]
                    kb.dma(g_[:], Gd[tokb + sub * 128:tokb + sub * 128 + 128, ec * 512:(ec + 1) * 512], rk=gkey)
                    for kc in range(8):
                        kb.mm(php[:], xTv[:, kc, sub * 128:(sub + 1) * 128], uv[:, kc, :], start=(kc == 0), stop=(kc == 7))
                    ge = gel[0]
                    kb.act(ge[:], php[:], AF.Gelu)
                    gh_ = gh[k % 2]
                    kb.tt(gh_[:], ge[:], g_[:], ALU.mult, e="pool")
                    for c4 in range(4):
                        kb.tr(ptp[:, c4 * 128:(c4 + 1) * 128], gh_[:, c4 * 128:(c4 + 1) * 128], gr.ident[:])
                    yield
                    gT = ghT[0]
                    kb.act(gT[:], ptp[:, 0:512], AF.Copy)
                    pop4 = [pop[0], pop[1], gr.pm[0], gr.pm[1]]
                    pos = []
                    for nh in range(2):
                        po = pop4[(k % 2) * 2 + nh]
                        pos.append(po)
                        for c4 in range(4):
                            kb.mm(po[:], gT[:, c4 * 128:(c4 + 1) * 128], vv[:, c4, nh * 512:(nh + 1) * 512],
                                  start=(c4 == 0), stop=(c4 == 3))
                    yield
                    for nh in range(2):
                        a_ = acc[sub][:, nh * 512:(nh + 1) * 512]
                        if ec == 0:
                            kb.act(a_, pos[nh][:], AF.Copy)
                        else:
                            kb.tt(a_, pos[nh][:], a_, ALU.add)
                    yield
            for sub in range(4):
                t0 = tokb + sub * 128
                hs = hst[sub % 3]
                key = ("h", t0)
                kb.dma(hs[:], h[t0:t0 + 128, :], rk=key)
                kb.tt(hs[:], acc[sub][:], hs[:], ALU.add, e="pool")
                kb.dma(h[t0:t0 + 128, :], hs[:], q="pool", wk=key)
            yield

        def run_interleaved(gg, dg, ng=3, nd=4):
            alive_g, alive_d = gg is not None, dg is not None
            while alive_g or alive_d:
                if alive_g:
                    for _ in range(ng):
                        try:
                            next(gg)
                        except StopIteration:
                            alive_g = False
                            break
                if alive_d:
                    for _ in range(nd):
                        try:
                            next(dg)
                        except StopIteration:
                            alive_d = False
                            break

        def q_epi(blk, tok0, c0, pss):
            qt = qT[c0 % 2]
            kb.act(qt[:], pss[0][:], AF.Copy)
            for sub in range(4):
                kb.mm(psc[:, sub * 128:(sub + 1) * 128], qt[:, sub * 128:(sub + 1) * 128], kTv[:, c0, :])
            kb.act(scv[:, :, c0, :], psc[:].rearrange("p (s k) -> p s k", s=4), AF.Copy)

        def chain(blk):
            for sub in range(4):
                for _ in gates_gen(blk, blk * 512 + sub * 128, sub):
                    yield

        def merged(blk, xTv):
            pend = state["pending"]
            dg = dense_gen(*pend) if pend is not None else None
            run_interleaved(chain(blk), dg)
            state["pending"] = (blk, xTv)

        segs = [dict(kind="FM", n0=0, n1=2048, epi=q_epi), dict(kind="custom", fn=merged)]
        gemm_stage(kb, gr, h, T, 1024, Wv, segs, norm=True, n_psT=1, n_pm=2)
        run_interleaved(None, dense_gen(*state["pending"]))
        kb.barrier()


Model.peer = peer_merged
```

```python
from contextlib import ExitStack
import math
import numpy as np
import concourse.bass as bass
import concourse.mybir as mybir
from concourse.bass_utils import run_bass_kernel_spmd

F32 = mybir.dt.float32
BF16 = mybir.dt.bfloat16
AF = mybir.ActivationFunctionType
ALU = mybir.AluOpType
AX = mybir.AxisListType

NCORES = 8
D = 1024
NEG = -30000.0
DBG = set()


class KB:
    NDMA = 24

    def __init__(self, nc, es):
        self.nc = nc
        self.es = es
        self.eng = dict(pe=nc.tensor, act=nc.scalar, dve=nc.vector, pool=nc.gpsimd, sp=nc.sync)
        es.enter_context(nc.allow_non_contiguous_dma(reason="small strided param loads"))
        self.sem = {}
        self.cnt = {}
        for e in ("pe", "act", "dve", "pool"):
            self.sem[e] = es.enter_context(nc.semaphore("s_" + e))
            self.cnt[e] = 0
        self.dsem = []
        for i in range(self.NDMA):
            nm = "d%d" % i
            self.sem[nm] = es.enter_context(nc.semaphore("s_" + nm))
            self.cnt[nm] = 0
            self.dsem.append(nm)
        self.dnext = 0
        self.known = {e: {} for e in self.eng}
        self.lastw = {}
        self.readers = {}
        self.n_ins = 0
        self.uid = 0

    def scope(self):
        kb = self

        class _S:
            def __enter__(self_):
                self_.old = kb.es
                self_.st = ExitStack()
                self_.st.__enter__()
                kb.es = self_.st
                return self_

            def __exit__(self_, *a):
                kb.es = self_.old
                return self_.st.__exit__(*a)

        return _S()

    def sb(self, name, shape, dtype=F32):
        self.uid += 1
        return self.es.enter_context(self.nc.sbuf_tensor("%s_%d" % (name, self.uid), list(shape), dtype))

    def ps(self, name, shape, dtype=F32):
        self.uid += 1
        return self.es.enter_context(self.nc.psum_tensor("%s_%d" % (name, self.uid), list(shape), dtype))

    @staticmethod
    def _key(a):
        return a if isinstance(a, (str, tuple)) else a.name

    def _wait(self, e, s, v):
        if v <= 0:
            return
        if self.known[e].get(s, 0) >= v:
            return
        self.eng[e].wait_ge(self.sem[s], v)
        self.known[e][s] = v
        self.n_ins += 1

    def _deps(self, e, R, W, pe_acc=False):
        deps = {}

        def add(tok, same_ok):
            if tok is None:
                return
            s, v = tok
            if same_ok and s == e:
                return
            if deps.get(s, 0) < v:
                deps[s] = v

        for k in R:
            add(self.lastw.get(k), False)
        for k in W:
            add(self.lastw.get(k), True)
            for s, v in self.readers.get(k, {}).items():
                add((s, v), True)
        for s, v in deps.items():
            self._wait(e, s, v)

    def _commit(self, tok, R, W):
        s, v = tok
        for k in W:
            self.lastw[k] = tok
            self.readers[k] = {}
        for k in R:
            d = self.readers.setdefault(k, {})
            if d.get(s, 0) < v:
                d[s] = v

    def op(self, e, ins_fn, R, W):
        R = [self._key(a) for a in R if a is not None and not isinstance(a, (int, float))]
        W = [self._key(a) for a in W]
        self._deps(e, R, W)
        ins = ins_fn()
        self.cnt[e] += 1
        ins.then_inc(self.sem[e], 1)
        self.n_ins += 1
        self._commit((e, self.cnt[e]), R, W)
        return ins

    def dma(self, out, in_, q="sp", rk=None, wk=None):
        R = [rk if rk is not None else self._key(in_)]
        W = [wk if wk is not None else self._key(out)]
        self._deps(q, R, W)
        s = self.dsem[self.dnext]
        self.dnext = (self.dnext + 1) % self.NDMA
        self._wait(q, s, self.cnt[s])
        self.eng[q].dma_start(out=out, in_=in_).then_inc(self.sem[s], 16)
        self.cnt[s] += 16
        self.n_ins += 1
        self._commit((s, self.cnt[s]), R, W)

    def barrier(self):
        for e in self.eng:
            for s in self.sem:
                if s != e:
                    self._wait(e, s, self.cnt[s])
        self.lastw = {}
        self.readers = {}

    def mm(self, out, lhsT, rhs, start=True, stop=True, extra_r=()):
        return self.op("pe", lambda: self.nc.tensor.matmul(out, lhsT, rhs, start=start, stop=stop),
                       [lhsT, rhs, *extra_r], [out])

    def tr(self, out, in_, ident):
        return self.op("pe", lambda: self.nc.tensor.transpose(out, in_, ident), [in_, ident], [out])

    def act(self, out, in_, func, bias=None, scale=None, accum_out=None, extra_r=()):
        kw = {}
        if bias is not None:
            kw["bias"] = bias
        if scale is not None:
            kw["scale"] = scale
        if accum_out is not None:
            kw["accum_out"] = accum_out
        W = [out] + ([accum_out] if accum_out is not None else [])
        return self.op("act", lambda: self.nc.scalar.activation(out, in_, func, **kw),
                       [in_, bias, scale, *extra_r], W)

    def tt(self, out, in0, in1, op, e="dve"):
        return self.op(e, lambda: self.eng[e].tensor_tensor(out, in0, in1, op), [in0, in1], [out])

    def ts(self, out, in0, s1, s2=None, op0=ALU.mult, op1=None, e="dve", accum_out=None):
        kw = {}
        if op1 is not None:
            kw["op1"] = op1
        if accum_out is not None:
            kw["accum_out"] = accum_out
        W = [out] + ([accum_out] if accum_out is not None else [])
        return self.op(e, lambda: self.eng[e].tensor_scalar(out, in0, s1, s2, op0, **kw), [in0, s1, s2], W)

    def stt(self, out, in0, scalar, in1, op0, op1, e="dve"):
        return self.op(e, lambda: self.eng[e].scalar_tensor_tensor(out, in0, scalar, in1, op0, op1),
                       [in0, scalar, in1], [out])

    def cp(self, out, in_, e="dve"):
        return self.op(e, lambda: self.eng[e].tensor_copy(out, in_), [in_], [out])

    def memset(self, out, val, e="dve"):
        return self.op(e, lambda: self.eng[e].memset(out, val), [], [out])

    def recip(self, out, in_):
        return self.op("dve", lambda: self.nc.vector.reciprocal(out, in_), [in_], [out])

    def red(self, out, in_, op=ALU.add, e="dve"):
        return self.op(e, lambda: self.eng[e].tensor_reduce(out, in_, AX.X, op), [in_], [out])


class GemmRes:
    def __init__(self, kb):
        self.kb = kb
        self.ident_f = kb.sb("identf", [128, 128], F32)
        self.ident = kb.sb("ident", [128, 128], BF16)
        self.xin = [kb.sb("xin%d" % i, [128, 1024], F32) for i in range(3)]
        self.xsq = kb.sb("xsq", [128, 1024], F32)
        self.ssq = [kb.sb("ssq%d" % i, [128, 1], F32) for i in range(3)]
        self.xn = [kb.sb("xn%d" % i, [128, 2048], BF16) for i in range(2)]
        self.gcol = kb.sb("gcol", [128, 8], F32)
        self.epsc = kb.sb("epsc", [128, 1], F32)
        self.pmi = 0

    def next_pm(self):
        p = self.pm[self.pmi % len(self.pm)]
        self.pmi += 1
        return p

    def init(self, ident_dram):
        kb = self.kb
        kb.dma(self.ident_f[:], ident_dram)
        kb.cp(self.ident[:], self.ident_f[:])
        kb.memset(self.epsc[:], 1e-6)


def load_weights(kb, gr, W_dram, Kin, N, gcol_dram=None):
    KC = Kin // 128
    Wb = kb.sb("Wb", [128, KC * N], BF16)
    Wv = Wb[:].rearrange("p (c n) -> p c n", c=KC)
    with kb.scope():
        wst = [kb.sb("wst%d" % i, [128, 2048], F32) for i in range(2)]
        if gcol_dram is not None:
            kb.dma(gr.gcol[:, 0:KC], gcol_dram.rearrange("(c p) -> p c", p=128))
        i = 0
        for kc in range(KC):
            for n0 in range(0, N, 2048):
                n1 = min(N, n0 + 2048)
                st = wst[i % 2]
                kb.dma(st[:, 0:n1 - n0], W_dram[kc * 128:(kc + 1) * 128, n0:n1], q="sp")
                if gcol_dram is not None:
                    if i % 2 == 0:
                        kb.ts(Wv[:, kc, n0:n1], st[:, 0:n1 - n0], gr.gcol[:, kc:kc + 1], None, op0=ALU.mult)
                    else:
                        kb.act(Wv[:, kc, n0:n1], st[:, 0:n1 - n0], AF.Copy, scale=gr.gcol[:, kc:kc + 1])
                else:
                    kb.cp(Wv[:, kc, n0:n1], st[:, 0:n1 - n0], e=("dve", "pool")[i % 2])
                i += 1
        kb.barrier()
    return Wv


def gemm_stage(kb, gr, x_dram, T, Kin, Wv, segs, norm=False, x_bf16=False, n_psT=2, n_pm=4):
    KC = Kin // 128
    nblk = T // 512
    gr.xT = [kb.sb("xT%d" % i, [128, KC * 512], BF16) for i in range(2)]
    gr.psT = [kb.ps("psT%d" % i, [128, 1024], BF16) for i in range(n_psT)]
    gr.pm = [kb.ps("pm%d" % i, [128, 512], F32) for i in range(n_pm)]
    for blk in range(nblk):
        xT = gr.xT[blk % 2]
        xTv = xT[:, 0:KC * 512].rearrange("p (c t) -> p c t", c=KC)
        xins = []
        for sub in range(4):
            tok0 = blk * 512 + sub * 128
            it = blk * 4 + sub
            if x_bf16:
                xt = gr.xn[it % 2]
                kb.dma(xt[:, 0:Kin], x_dram[tok0:tok0 + 128, :])
                xn = xt
            else:
                xt = gr.xin[it % 3]
                kb.dma(xt[:, 0:Kin], x_dram[tok0:tok0 + 128, :])
                xn = gr.xn[it % 2]
                if norm:
                    ssq = gr.ssq[it % 3]
                    kb.act(gr.xsq[:, 0:Kin], xt[:, 0:Kin], AF.Square)
                    kb.red(ssq[:], gr.xsq[:, 0:Kin])
                    kb.act(ssq[:], ssq[:], AF.Sqrt, bias=gr.epsc[:, 0:1], scale=1.0 / Kin)
                    kb.recip(ssq[:], ssq[:])
                    kb.act(xn[:, 0:Kin], xt[:, 0:Kin], AF.Copy, scale=ssq[:, 0:1])
                else:
                    kb.cp(xn[:, 0:Kin], xt[:, 0:Kin], e="pool")
            xins.append(xt)
            for half in range((KC + 7) // 8):
                pst = gr.psT[(it * 2 + half) % n_psT]
                nk = min(8, KC - half * 8)
                for j in range(nk):
                    kc = half * 8 + j
                    kb.tr(pst[:, j * 128:(j + 1) * 128], xn[:, kc * 128:(kc + 1) * 128], gr.ident[:])
                src = pst[:, 0:nk * 128].rearrange("p (c t) -> p c t", c=nk)
                dst = xTv[:, half * 8:half * 8 + nk, sub * 128:(sub + 1) * 128]
                if (it + half) % 2 == 0:
                    kb.cp(dst, src, e="dve")
                else:
                    kb.act(dst, src, AF.Copy)
        for seg in segs:
            if seg["kind"] == "custom":
                seg["fn"](blk, xTv)
                continue
            n0, n1 = seg["n0"], seg["n1"]
            if seg["kind"] == "FM":
                grp = seg.get("group", 1)
                nch = (n1 - n0 + 127) // 128
                for c0 in range(0, nch, grp):
                    pss = []
                    for ci in range(c0, min(nch, c0 + grp)):
                        a = n0 + ci * 128
                        b = min(n1, a + 128)
                        ps = gr.next_pm()
                        for kc in range(KC):
                            kb.mm(ps[0:b - a, :], Wv[:, kc, a:b], xTv[:, kc, :], start=(kc == 0), stop=(kc == KC - 1))
                        pss.append(ps)
                    seg["epi"](blk, blk * 512, c0, pss)
            else:
                for sub in range(4):
                    for a in range(n0, n1, 512):
                        b = min(n1, a + 512)
                        ps = gr.next_pm()
                        for kc in range(KC):
                            kb.mm(ps[:, 0:b - a], xTv[:, kc, sub * 128:(sub + 1) * 128], Wv[:, kc, a:b],
                                  start=(kc == 0), stop=(kc == KC - 1))
                        seg["epi"](blk, blk * 512 + sub * 128, sub, a - n0, b - a, ps, xins[sub])


def attn_stage(kb, spec, S, nseq):
    nsup = S // 512
    pst = [kb.ps("ast%d" % i, [128, 512], F32) for i in range(2)]
    outs = [kb.ps("aout%d" % i, [128, 512], F32) for i in range(4)]
    Ps = [kb.sb("aP%d" % i, [128, 512], BF16) for i in range(3)]
    jobs = [(seq, g) for seq in range(nseq) for g in spec.groups(seq)]
    ti = 0
    nbuf = getattr(spec, "nbuf", 2)
    for n, (seq, g) in enumerate(jobs):
        buf = n % nbuf
        if nbuf == 1:
            spec.load(g, 0)
        else:
            if n == 0:
                spec.load(g, buf)
            if n + 1 < len(jobs):
                spec.load(jobs[n + 1][1], (n + 1) % 2)
        shs = spec.subheads(g)
        items = [(j, i, sh) for j in range(nsup) for i in range(4 * j + 4) for sh in shs]

        def emit_qk(it):
            j, i, sh = it
            ctx = None
            if hasattr(spec, "pre"):
                if sh == shs[0]:
                    state["ctx"] = spec.pre(g, j, i, buf)
                ctx = state["ctx"]
            ps = pst[state["ti"] % 2]
            P = Ps[state["ti"] % 3]
            state["ti"] += 1
            if ctx is not None:
                spec.cur = ctx
            spec.qk(g, sh, j, i, ps, buf)
            return (ps, P, ctx)

        state = dict(ti=ti, ctx=None)
        nxt = emit_qk(items[0])
        for n, (j, i, sh) in enumerate(items):
            ps, P, ctx = nxt
            if n + 1 < len(items):
                nxt = emit_qk(items[n + 1])
            if ctx is not None:
                spec.cur = ctx
            spec.evac(g, sh, j, i, ps, P, buf)
            c0, dvp = spec.ocols(g, sh)
            vb = spec.vblk(g, sh, i, buf)
            for qb in range(4):
                if i <= 4 * j + qb:
                    kb.mm(outs[qb][:, c0:c0 + dvp], P[:, qb * 128:(qb + 1) * 128], vb,
                          start=(i == 0 and sh == shs[0]), stop=(i == 4 * j + qb))
            if i == 4 * j + 3 and sh == shs[-1]:
                spec.post(g, j, outs, buf)
        ti = state["ti"]


class FoxSpec:
    def __init__(self, kb, gr, S, QT, KT, V, CQK, O, cmask_bf):
        self.kb, self.gr, self.S = kb, gr, S
        self.QT, self.KT, self.V, self.CQK, self.O = QT, KT, V, CQK, O
        self.cmask = cmask_bf
        self.Qa = [kb.sb("fQa%d" % i, [70, S], BF16) for i in range(2)]
        self.Ka = [kb.sb("fKa%d" % i, [70, S], BF16) for i in range(2)]
        self.Vt = [kb.sb("fV%d" % i, [128, (S // 128) * 65], BF16) for i in range(2)]
        self.rz = [kb.sb("frz%d" % i, [128, 1], F32) for i in range(4)]
        self.ost = [kb.sb("fost%d" % i, [128, 64], BF16) for i in range(4)]
        self.pi = 0
        for i in range(2):
            kb.memset(self.Qa[i][64:70, :], 1.0)
            kb.memset(self.Ka[i][64:70, :], 1.0, e="pool")
            kb.memset(self.Vt[i][:], 1.0, e="pool")

    def groups(self, seq):
        return [(seq, h) for h in range(16)]

    def subheads(self, g):
        return [0]

    def load(self, g, buf):
        kb, S = self.kb, self.S
        seq, h = g
        t0 = seq * S
        kb.dma(self.Qa[buf][0:64, :], self.QT[64 * h:64 * h + 64, t0:t0 + S])
        kb.dma(self.Qa[buf][64:67, :], self.CQK[h, 3:6, t0:t0 + S])
        kb.dma(self.Ka[buf][0:64, :], self.KT[64 * h:64 * h + 64, t0:t0 + S])
        kb.dma(self.Ka[buf][67:70, :], self.CQK[h, 0:3, t0:t0 + S])
        vt = self.Vt[buf][:].rearrange("p (b d) -> p b d", d=65)
        kb.dma(vt[:, :, 0:64], self.V[t0:t0 + S, 64 * h:64 * h + 64].rearrange("(b p) d -> p b d", p=128))

    def qk(self, g, sh, j, i, ps, buf):
        kb = self.kb
        r = i - 4 * j
        kb.mm(ps[:], self.Ka[buf][:, i * 128:(i + 1) * 128], self.Qa[buf][:, j * 512:(j + 1) * 512],
              start=True, stop=(r < 0))
        if r >= 0:
            kb.mm(ps[:], self.gr.ident[:], self.cmask[:, r * 512:(r + 1) * 512], start=False, stop=True)

    def evac(self, g, sh, j, i, ps, P, buf):
        self.kb.act(P[:], ps[:], AF.Exp)

    def ocols(self, g, sh):
        return 0, 65

    def vblk(self, g, sh, i, buf):
        return self.Vt[buf][:, i * 65:(i + 1) * 65]

    def post(self, g, j, outs, buf):
        kb, S = self.kb, self.S
        seq, h = g
        for qb in range(4):
            rz = self.rz[self.pi % 4]
            st = self.ost[self.pi % 4]
            self.pi += 1
            kb.recip(rz[:], outs[qb][:, 64:65])
            kb.ts(st[:], outs[qb][:, 0:64], rz[:, 0:1], None, op0=ALU.mult)
            tok = seq * S + (4 * j + qb) * 128
            kb.dma(self.O[tok:tok + 128, 64 * h:64 * h + 64], st[:], q="pool")


class Epi:
    def __init__(self, kb):
        self.kb = kb
        self.fm = [kb.sb("efm%d" % i, [128, 512], BF16) for i in range(3)]
        self.tmb = [kb.sb("etmb%d" % i, [128, 512], BF16) for i in range(3)]
        self.tmf = [kb.sb("etmf%d" % i, [128, 512], F32) for i in range(3)]
        self.n = 0

    def fm_store(self, dst, scale=1.0):
        kb = self.kb

        def epi(blk, tok0, c0, pss):
            ps = pss[0]
            st = self.fm[self.n % 3]
            self.n += 1
            rows = min(128, dst.shape[0] - c0 * 128)
            if self.n % 2 == 0:
                kb.act(st[0:rows, :], ps[0:rows, :], AF.Copy, scale=float(scale))
            else:
                kb.ts(st[0:rows, :], ps[0:rows, :], float(scale), None, op0=ALU.mult)
            kb.dma(dst[c0 * 128:c0 * 128 + rows, tok0:tok0 + 512], st[0:rows, :], q="pool")
        return epi

    def tm_store(self, dst, func=AF.Copy, col0=0):
        kb = self.kb

        def epi(blk, tok0, sub, n0c, ncols, ps, xt):
            st = self.tmb[self.n % 3]
            self.n += 1
            if func == AF.Copy and self.n % 2 == 0:
                kb.cp(st[:, 0:ncols], ps[:, 0:ncols])
            else:
                kb.act(st[:, 0:ncols], ps[:, 0:ncols], func)
            kb.dma(dst[tok0:tok0 + 128, col0 + n0c:col0 + n0c + ncols], st[:, 0:ncols], q="pool")
        return epi

    def tm_resid(self, h):
        kb = self.kb

        def epi(blk, tok0, sub, n0c, ncols, ps, xt):
            st = self.tmf[self.n % 3]
            self.n += 1
            key = ("h", tok0, n0c)
            kb.dma(st[:, 0:ncols], h[tok0:tok0 + 128, n0c:n0c + ncols], rk=key)
            kb.tt(st[:, 0:ncols], ps[:, 0:ncols], st[:, 0:ncols], ALU.add)
            kb.dma(h[tok0:tok0 + 128, n0c:n0c + ncols], st[:, 0:ncols], q="pool", wk=key)
        return epi


def split3(kb, src, dst6, tmp_r, tmp_b, rows, n):
    r = tmp_r
    kb.cp(dst6[0:rows, 0, :], src)
    kb.tt(r[0:rows, 0:n], src, dst6[0:rows, 0, :], ALU.subtract)
    kb.cp(dst6[0:rows, 1, :], r[0:rows, 0:n])
    kb.tt(r[0:rows, 0:n], r[0:rows, 0:n], dst6[0:rows, 1, :], ALU.subtract)
    kb.cp(dst6[0:rows, 2, :], r[0:rows, 0:n])
    kb.ts(dst6[0:rows, 3:6, :], dst6[0:rows, 0:3, :], -1.0, None, op0=ALU.mult, e="pool")


class Model:
    _epi_cache = (None, None)

    @property
    def epi(self):
        if self._epi_cache[0] is not self.kb.es:
            self._epi_cache = (self.kb.es, Epi(self.kb))
        return self._epi_cache[1]

    def __init__(self, nc, es, nseq, S):
        self.nc, self.nseq, self.S = nc, nseq, S
        self.T = nseq * S
        self.kb = KB(nc, es)
        self.din = {}
        self.scratch = {}

    def inp(self, name, shape, dtype=F32):
        self.din[name] = self.nc.dram_tensor(name, list(shape), dtype, kind="ExternalInput").ap()
        return self.din[name]

    def dscr(self, name, shape, dtype):
        if name not in self.scratch:
            self.scratch[name] = self.nc.dram_tensor(name, list(shape), dtype, kind="Internal").ap()
        return self.scratch[name]

    def setup(self):
        kb = self.kb
        self.gr = GemmRes(kb)
        self.gr.init(self.din["ident"])
        self.one = kb.sb("onec", [128, 1], F32)
        kb.memset(self.one[:], 1.0)
        self.zeros = kb.sb("zeros", [128, 512], F32)
        kb.memset(self.zeros[:], 0.0)
        self.cmask = kb.sb("cmask", [128, 2048], BF16)
        with kb.scope():
            tmp = kb.sb("cmtmp", [128, 2048], F32)
            kb.dma(tmp[:], self.din["cmask"])
            kb.cp(self.cmask[:], tmp[:])
            kb.barrier()

    def fox_mixer(self, h, w_in, gcol, b_f, w_out):
        kb, gr, T, S, nseq = self.kb, self.gr, self.T, self.S, self.nseq
        QT = self.dscr("QT", [1024, T], BF16)
        KT = self.dscr("KT", [1024, T], BF16)
        V = self.dscr("Vtm", [T, 2048], BF16)
        CQK = self.dscr("CQK", [32, 6, T], BF16)
        O = self.dscr("Otm", [T, 2048], BF16)
        with kb.scope():
            Wv = load_weights(kb, gr, w_in, 1024, 3088, gcol)
            nbf = kb.sb("nbf", [16, 1], F32)
            kb.dma(nbf[:], b_f.rearrange("(p o) -> p o", o=1))
            kb.ts(nbf[:], nbf[:], -1.0, None, op0=ALU.mult)
            carry = kb.sb("carry", [16, 1], F32)
            t1 = kb.sb("ft1", [16, 512], F32)
            cum = kb.sb("fcum", [16, 512], F32)
            rr = kb.sb("frr", [16, 512], F32)
            spl = kb.sb("fspl", [16, 6 * 512], BF16)
            splv = spl[:].rearrange("p (s n) -> p s n", s=6)
            bps = S // 512

            def f_epi(blk, tok0, c0, pss):
                ps = pss[0]
                if blk % bps == 0:
                    kb.memset(carry[:], 0.0)
                kb.act(t1[:], ps[0:16, :], AF.Exp, bias=nbf[:, 0:1], scale=-1.0)
                kb.act(t1[:], t1[:], AF.Ln, bias=self.one[0:16, 0:1])
                kb.op("dve", lambda: self.nc.vector.tensor_tensor_scan(cum[:], t1[:], self.zeros[0:16, :], carry[:, 0:1], ALU.add, ALU.add),
                      [t1, self.zeros, carry], [cum])
                kb.cp(carry[:], cum[:, 511:512])
                split3(kb, cum[:], splv, rr, None, 16, 512)
                kb.dma(CQK[0:16, :, tok0:tok0 + 512], splv, q="pool")

            segs = [dict(kind="FM", n0=0, n1=1024, epi=self.epi.fm_store(QT, 0.125)),
                    dict(kind="FM", n0=1024, n1=2048, epi=self.epi.fm_store(KT, 1.0)),
                    dict(kind="TM", n0=2048, n1=3072, epi=self.epi.tm_store(V)),
                    dict(kind="FM", n0=3072, n1=3088, epi=f_epi)]
            gemm_stage(kb, gr, h, T, 1024, Wv, segs, norm=True)
            kb.barrier()
        with kb.scope():
            spec = FoxSpec(kb, gr, S, QT, KT, V, CQK, O, self.cmask)
            attn_stage(kb, spec, S, nseq)
            kb.barrier()
        with kb.scope():
            Wv = load_weights(kb, gr, w_out, 1024, 1024, None)
            segs = [dict(kind="TM", n0=0, n1=1024, epi=self.epi.tm_resid(h))]
            gemm_stage(kb, gr, O[:, 0:1024], T, 1024, Wv, segs, x_bf16=True)
            kb.barrier()

    def peer_old(self, h, gcol, w_q, keysT, uT, v):
        kb, gr, T = self.kb, self.gr, self.T
        nc = self.nc
        UTb = self.dscr("UTb", [1024, 16384], BF16)
        Vb = self.dscr("Vb", [16384, 1024], BF16)
        Gd = self.dscr("Gd", [T, 16384], BF16)
        with kb.scope():
            kb.dma(gr.gcol[:, 0:8], gcol.rearrange("(c p) -> p c", p=128))
            st = [kb.sb("pst%d" % i, [128, 2048], F32) for i in range(3)]
            sb = [kb.sb("psb%d" % i, [128, 2048], BF16) for i in range(3)]
            n = 0
            for kc in range(8):
                for e0 in range(0, 16384, 2048):
                    s_, b_ = st[n % 3], sb[n % 3]
                    kb.dma(s_[:], uT[kc * 128:(kc + 1) * 128, e0:e0 + 2048])
                    if n % 2 == 0:
                        kb.ts(b_[:], s_[:], gr.gcol[:, kc:kc + 1], None, op0=ALU.mult)
                    else:
                        kb.act(b_[:], s_[:], AF.Copy, scale=gr.gcol[:, kc:kc + 1])
                    kb.dma(UTb[kc * 128:(kc + 1) * 128, e0:e0 + 2048], b_[:], q="pool")
                    n += 1
            for r0 in range(0, 16384, 256):
                s_, b_ = st[n % 3], sb[n % 3]
                kb.dma(s_[:].rearrange("p (c n) -> p c n", c=2), v[r0:r0 + 256, :].rearrange("(c p) n -> p c n", p=128))
                if n % 3 == 0:
                    kb.cp(b_[:], s_[:], e="dve")
                elif n % 3 == 1:
                    kb.act(b_[:], s_[:], AF.Copy)
                else:
                    kb.cp(b_[:], s_[:], e="pool")
                kb.dma(Vb[r0:r0 + 256, :].rearrange("(c p) n -> p c n", p=128), b_[:].rearrange("p (c n) -> p c n", c=2), q="pool")
                n += 1
            kb.barrier()
        with kb.scope():
            Wv = load_weights(kb, gr, w_q, 1024, 2048, gcol)
            kT = kb.sb("keysT", [128, 16 * 128], BF16)
            with kb.scope():
                ktmp = kb.sb("ktmp", [128, 16 * 128], F32)
                kb.dma(ktmp[:].rearrange("p (c k) -> p c k", c=16), keysT.rearrange("c d k -> d c k"))
                kb.cp(kT[:], ktmp[:])
                kb.barrier()
            kTv = kT[:].rearrange("p (c k) -> p c k", c=16)
            qT = [kb.sb("pqT%d" % i, [128, 512], BF16) for i in range(2)]
            psc = [kb.ps("psc%d" % i, [128, 512], F32) for i in range(2)]
            sc_all = kb.sb("sc_all", [128, 4 * 16 * 128], F32)
            scv = sc_all[:].rearrange("p (s c k) -> p s c k", s=4, c=16)
            a16 = kb.sb("a16", [128, 16], F32)
            b16 = kb.sb("b16", [128, 16], F32)
            c16 = kb.sb("c16", [128, 16], F32)
            e16 = kb.sb("e16", [128, 16], F32)
            t128 = kb.sb("t128", [128, 128], F32)
            cand = kb.sb("cand", [128, 256], F32)
            cand2 = kb.sb("cand2", [128, 256], F32)
            tau = kb.sb("tau", [128, 8], F32)
            nb = kb.sb("nb", [128, 8], F32)
            nb2 = kb.sb("nb2", [128, 8], F32)
            zz = kb.sb("zz", [128, 1], F32)
            Sc = [kb.sb("Sc%d" % i, [128, 2048], F32) for i in range(3)]
            Ec = [kb.sb("Ec%d" % i, [128, 2048], BF16) for i in range(3)]
            gcnt = [0]
            Gb = [kb.sb("Gb%d" % i, [128, 2048], BF16) for i in range(2)]
            cn = [0, 0, 0]
            V = nc.vector

            def top16(dst, src, tmp):
                kb.op("dve", lambda: V.max(out=dst[:, 0:8], in_=src), [src], [dst])
                kb.op("dve", lambda: V.match_replace(out=tmp, in_to_replace=dst[:, 0:8], in_values=src, imm_value=-1e30),
                      [dst, src], [tmp])
                kb.op("dve", lambda: V.max(out=dst[:, 8:16], in_=tmp), [tmp], [dst])

            def gates(tokb, sub):
                for hh in range(8):
                    s1 = scv[:, sub, 2 * hh, :]
                    s2 = scv[:, sub, 2 * hh + 1, :]
                    top16(a16, s1, t128[:])
                    top16(b16, s2, t128[:])
                    kb.tt(cand[:].rearrange("p (a b) -> p a b", a=16),
                          a16[:].unsqueeze(2).to_broadcast([128, 16, 16]),
                          b16[:].unsqueeze(1).to_broadcast([128, 16, 16]), ALU.add)
                    top16(c16, cand[:], cand2[:])
                    kb.cp(tau[:, hh:hh + 1], c16[:, 15:16])
                    kb.ts(nb[:, hh:hh + 1], c16[:, 0:1], -1.0, None, op0=ALU.mult)
                    kb.act(e16[:], c16[:], AF.Exp, bias=nb[:, hh:hh + 1])
                    kb.red(zz[:], e16[:])
                    kb.act(zz[:], zz[:], AF.Ln)
                    kb.tt(nb[:, hh:hh + 1], nb[:, hh:hh + 1], zz[:], ALU.subtract)
                items = [(c, hh) for c in range(8) for hh in range(8)]

                def stage1(n):
                    c, hh = items[n]
                    S_, E_ = Sc[n % 3], Ec[n % 3]
                    s1 = scv[:, sub, 2 * hh, 16 * c:16 * c + 16]
                    s2 = scv[:, sub, 2 * hh + 1, :]
                    S3 = S_[:].rearrange("p (a b) -> p a b", a=16)
                    kb.tt(S3, s1.unsqueeze(2).to_broadcast([128, 16, 128]),
                          s2.unsqueeze(1).to_broadcast([128, 16, 128]), ALU.add)
                    kb.act(E_[:], S_[:], AF.Exp, bias=nb[:, hh:hh + 1])

                def stage2(n):
                    c, hh = items[n]
                    S_, E_ = Sc[n % 3], Ec[n % 3]
                    G = Gb[(gcnt[0] + c) % 2]
                    dst = G if hh == 0 else E_
                    kb.stt(dst[:], S_[:], tau[:, hh:hh + 1], E_[:], ALU.is_ge, ALU.mult)
                    if hh > 0:
                        kb.tt(G[:], G[:], E_[:], ALU.add)
                    if hh == 7:
                        kb.dma(Gd[tokb:tokb + 128, c * 2048:(c + 1) * 2048], G[:], q="sp")

                stage1(0)
                for n in range(64):
                    if n + 1 < 64:
                        stage1(n + 1)
                    stage2(n)

            def q_epi(blk, tok0, c0, pss):
                qt = qT[c0 % 2]
                if c0 % 2 == 0:
                    kb.act(qt[:], pss[0][:], AF.Copy)
                else:
                    kb.cp(qt[:], pss[0][:])
                pc = psc[c0 % 2]
                for sub in range(4):
                    kb.mm(pc[:, sub * 128:(sub + 1) * 128], qt[:, sub * 128:(sub + 1) * 128], kTv[:, c0, :])
                src = pc[:].rearrange("p (s k) -> p s k", s=4)
                if c0 % 2 == 0:
                    kb.cp(scv[:, :, c0, :], src)
                else:
                    kb.act(scv[:, :, c0, :], src, AF.Copy)
                if c0 == 15:
                    for sub in range(4):
                        gates(tok0 + sub * 128, sub)

            segs = [dict(kind="FM", n0=0, n1=2048, epi=q_epi)]
            gemm_stage(kb, gr, h, T, 1024, Wv, segs, norm=True)
            kb.barrier()
        if getattr(self, 'skip_dense', False):
            return
        with kb.scope():
            ut = [kb.sb("ut%d" % i, [128, 8 * 512], BF16) for i in range(2)]
            vt = [kb.sb("vt%d" % i, [128, 4 * 1024], BF16) for i in range(2)]
            gt = [kb.sb("gt%d" % i, [128, 512], BF16) for i in range(3)]
            gel = [kb.sb("gel%d" % i, [128, 512], F32) for i in range(2)]
            gh = [kb.sb("gh%d" % i, [128, 512], BF16) for i in range(2)]
            ghT = [kb.sb("ghT%d" % i, [128, 512], BF16) for i in range(2)]
            acc = [kb.sb("acc%d" % i, [128, 1024], F32) for i in range(4)]
            php = [kb.ps("php%d" % i, [128, 512], F32) for i in range(2)]
            ptp = [kb.ps("ptp%d" % i, [128, 1024], BF16) for i in range(1)]
            pop = [kb.ps("pop%d" % i, [128, 512], F32) for i in range(4)]
            cn2 = [0]

            def dense(blk, xTv):
                tokb = blk * 512
                for ec in range(32):
                    u_ = ut[ec % 2]
                    v_ = vt[ec % 2]
                    uv = u_[:].rearrange("p (c e) -> p c e", c=8)
                    vv = v_[:].rearrange("p (c n) -> p c n", c=4)
                    kb.dma(uv, UTb[:, ec * 512:(ec + 1) * 512].rearrange("(c p) e -> p c e", p=128))
                    kb.dma(vv, Vb[ec * 512:(ec + 1) * 512, :].rearrange("(c p) n -> p c n", p=128))
                    for sub in range(4):
                        k = cn2[0]
                        cn2[0] += 1
                        g_ = gt[k % 3]
                        kb.dma(g_[:], Gd[tokb + sub * 128:tokb + sub * 128 + 128, ec * 512:(ec + 1) * 512])
                        ph = php[k % 2]
                        for kc in range(8):
                            kb.mm(ph[:], xTv[:, kc, sub * 128:(sub + 1) * 128], uv[:, kc, :], start=(kc == 0), stop=(kc == 7))
                        ge = gel[k % 2]
                        kb.act(ge[:], ph[:], AF.Gelu)
                        gh_ = gh[k % 2]
                        kb.tt(gh_[:], ge[:], g_[:], ALU.mult, e="pool")
                        pt = ptp[0]
                        for c4 in range(4):
                            kb.tr(pt[:, c4 * 128:(c4 + 1) * 128], gh_[:, c4 * 128:(c4 + 1) * 128], gr.ident[:])
                        gT = ghT[k % 2]
                        if k % 2 == 0:
                            kb.cp(gT[:], pt[:, 0:512])
                        else:
                            kb.act(gT[:], pt[:, 0:512], AF.Copy)
                        for nh in range(2):
                            po = pop[(k % 2) * 2 + nh]
                            for c4 in range(4):
                                kb.mm(po[:], gT[:, c4 * 128:(c4 + 1) * 128], vv[:, c4, nh * 512:(nh + 1) * 512],
                                      start=(c4 == 0), stop=(c4 == 3))
                            a_ = acc[sub][:, nh * 512:(nh + 1) * 512]
                            if ec == 0:
                                kb.cp(a_, po[:])
                            else:
                                kb.tt(a_, po[:], a_, ALU.add)
                for sub in range(4):
                    t0 = tokb + sub * 128
                    hs = gr.xin[sub % 3]
                    key = ("h", t0)
                    kb.dma(hs[:], h[t0:t0 + 128, :], rk=key)
                    kb.tt(acc[sub][:], acc[sub][:], hs[:], ALU.add, e="pool")
                    kb.dma(h[t0:t0 + 128, :], acc[sub][:], q="pool", wk=key)

            segs = [dict(kind="custom", fn=dense)]
            gemm_stage(kb, gr, h, T, 1024, None, segs, norm=True, n_psT=1, n_pm=0)
            kb.barrier()

    def ple(self, h, gcol, w_gate, p_i, w_proj):
        kb, gr, T = self.kb, self.gr, self.T
        PP = self.dscr("PP", [T, 1024], F32)
        with kb.scope():
            Wv = load_weights(kb, gr, w_proj, 256, 1024, None)
            stf = [kb.sb("ppst%d" % i, [128, 512], F32) for i in range(3)]
            cn = [0]

            def pp_epi(blk, tok0, sub, n0c, ncols, ps, xt):
                st = stf[cn[0] % 3]
                cn[0] += 1
                if cn[0] % 2 == 0:
                    kb.cp(st[:, 0:ncols], ps[:, 0:ncols])
                else:
                    kb.act(st[:, 0:ncols], ps[:, 0:ncols], AF.Copy)
                kb.dma(PP[tok0:tok0 + 128, n0c:n0c + ncols], st[:, 0:ncols], q="pool")
            gemm_stage(kb, gr, p_i, T, 256, Wv, [dict(kind="TM", n0=0, n1=1024, epi=pp_epi)])
            kb.barrier()
        with kb.scope():
            Wv = load_weights(kb, gr, w_gate, 1024, 1024, gcol)
            sg = [kb.sb("plg%d" % i, [128, 512], F32) for i in range(3)]
            sp_ = [kb.sb("plp%d" % i, [128, 512], F32) for i in range(3)]
            sh_ = [kb.sb("plh%d" % i, [128, 512], F32) for i in range(3)]
            cn = [0]

            def g_epi(blk, tok0, sub, n0c, ncols, ps, xt):
                k = cn[0] % 3
                cn[0] += 1
                key = ("h", tok0, n0c)
                kb.dma(sp_[k][:, 0:ncols], PP[tok0:tok0 + 128, n0c:n0c + ncols])
                kb.dma(sh_[k][:, 0:ncols], h[tok0:tok0 + 128, n0c:n0c + ncols], rk=key)
                kb.act(sg[k][:, 0:ncols], ps[:, 0:ncols], AF.Sigmoid)
                kb.tt(sg[k][:, 0:ncols], sg[k][:, 0:ncols], sp_[k][:, 0:ncols], ALU.mult, e="pool")
                kb.tt(sh_[k][:, 0:ncols], sh_[k][:, 0:ncols], sg[k][:, 0:ncols], ALU.add)
                kb.dma(h[tok0:tok0 + 128, n0c:n0c + ncols], sh_[k][:, 0:ncols], q="pool", wk=key)
            gemm_stage(kb, gr, h, T, 1024, Wv, [dict(kind="TM", n0=0, n1=1024, epi=g_epi)], norm=True)
            kb.barrier()

    def final(self, h, g, out):
        kb, gr, T = self.kb, self.gr, self.T
        with kb.scope():
            gb = kb.sb("fgb", [128, 1024], F32)
            kb.dma(gb[:], g.partition_broadcast(128))
            ot = [kb.sb("fot%d" % i, [128, 1024], F32) for i in range(2)]
            for it in range(T // 128):
                xt = gr.xin[it % 3]
                ssq = gr.ssq[it % 3]
                o_ = ot[it % 2]
                kb.dma(xt[:], h[it * 128:(it + 1) * 128, :])
                kb.act(gr.xsq[:], xt[:], AF.Square)
                kb.red(ssq[:], gr.xsq[:])
                kb.act(ssq[:], ssq[:], AF.Sqrt, bias=gr.epsc[:, 0:1], scale=1.0 / 1024)
                kb.recip(ssq[:], ssq[:])
                kb.stt(o_[:], xt[:], ssq[:, 0:1], gb[:], ALU.mult, ALU.mult)
                kb.dma(out[it * 128:(it + 1) * 128, :], o_[:], q="pool")
            kb.barrier()

    def diff_mixer(self, h, w_in, gcol, lam_vecs, lam_init, norm_g, w_out, BT, bfar, dmask):
        kb, gr, T, S, nseq = self.kb, self.gr, self.T, self.S, self.nseq
        QT = self.dscr("QT", [1024, T], BF16)
        KT = self.dscr("KT", [1024, T], BF16)
        V = self.dscr("Vtm", [T, 2048], BF16)
        O = self.dscr("Otm", [T, 2048], BF16)
        with kb.scope():
            Wv = load_weights(kb, gr, w_in, 1024, 3072, gcol)
            segs = [dict(kind="FM", n0=0, n1=1024, epi=self.epi.fm_store(QT, 0.125)),
                    dict(kind="FM", n0=1024, n1=2048, epi=self.epi.fm_store(KT, 1.0)),
                    dict(kind="TM", n0=2048, n1=3072, epi=self.epi.tm_store(V))]
            gemm_stage(kb, gr, h, T, 1024, Wv, segs, norm=True)
            kb.barrier()
        with kb.scope():
            spec = DiffSpec(self, S, QT, KT, V, O, lam_vecs, lam_init, norm_g, BT, bfar, dmask)
            attn_stage(kb, spec, S, nseq)
            kb.barrier()
        with kb.scope():
            Wv = load_weights(kb, gr, w_out, 1024, 1024, None)
            segs = [dict(kind="TM", n0=0, n1=1024, epi=self.epi.tm_resid(h))]
            gemm_stage(kb, gr, O[:, 0:1024], T, 1024, Wv, segs, x_bf16=True)
            kb.barrier()


class DiffSpec:
    def __init__(self, m, S, QT, KT, V, O, lam_vecs, lam_init, norm_g, BT, bfar, dmask):
        kb = m.kb
        self.kb, self.gr, self.S = kb, m.gr, S
        self.QT, self.KT, self.V, self.O = QT, KT, V, O
        self.Qa = [[kb.sb("dQ%d_%d" % (i, mm), [64, S], BF16) for mm in range(2)] for i in range(2)]
        self.Ka = [[kb.sb("dK%d_%d" % (i, mm), [64, S], BF16) for mm in range(2)] for i in range(2)]
        self.Vt = [kb.sb("dV%d" % i, [128, (S // 128) * 129], BF16) for i in range(2)]
        for i in range(2):
            kb.memset(self.Vt[i][:], 1.0, e="pool")
        lv = kb.sb("dlv", [128, 256], F32)
        kb.dma(lv[:], lam_vecs.rearrange("a d -> (a d)").partition_broadcast(128))
        pr = kb.sb("dpr", [128, 128], F32)
        lvv = lv[:].rearrange("p (a d) -> p a d", a=4)
        prv = pr[:].rearrange("p (a d) -> p a d", a=2)
        kb.tt(prv[:, 0, :], lvv[:, 0, :], lvv[:, 1, :], ALU.mult)
        kb.tt(prv[:, 1, :], lvv[:, 2, :], lvv[:, 3, :], ALU.mult)
        l2 = kb.sb("dl2", [128, 2], F32)
        kb.red(l2[:, 0:1], prv[:, 0, :])
        kb.red(l2[:, 1:2], prv[:, 1, :])
        kb.act(l2[:], l2[:], AF.Exp)
        self.nlam = kb.sb("dnlam", [128, 1], F32)
        kb.tt(self.nlam[:], l2[:, 1:2], l2[:, 0:1], ALU.subtract)
        kb.ts(self.nlam[:], self.nlam[:], -float(lam_init), None, op0=ALU.add)
        self.gsc = kb.sb("dgsc", [128, 128], F32)
        kb.dma(self.gsc[:], norm_g.partition_broadcast(128))
        kb.ts(self.gsc[:], self.gsc[:], 1.0 - float(lam_init), None, op0=ALU.mult)
        self.Bhl = kb.sb("dBhl", [128, 8 * 2 * 2 * 128], BF16)
        self.Bv = self.Bhl[:].rearrange("p (h d s q) -> p h d s q", h=8, d=2, s=2)
        self.bfar = kb.sb("dbfar", [128, 8], F32)
        kb.dma(self.bfar[:], bfar.partition_broadcast(128))
        self.ones2 = kb.sb("dones2", [2, 128], BF16)
        kb.memset(self.ones2[:], 1.0)
        self.cfar = kb.sb("dcfar", [2, 8 * 128], BF16)
        with kb.scope():
            bt = kb.sb("dbt", [128, 8 * 2 * 128], F32)
            btv = bt[:].rearrange("p (h d q) -> p h d q", h=8, d=2)
            kb.dma(btv, BT.rearrange("h d k q -> k h d q"))
            dm = kb.sb("ddm", [128, 128], F32)
            kb.dma(dm[:], dmask)
            rr = kb.sb("drr", [128, 128], F32)
            for hh in range(8):
                kb.tt(btv[:, hh, 0, :], btv[:, hh, 0, :], dm[:], ALU.add)
                for d in range(2):
                    kb.cp(self.Bv[:, hh, d, 0, :], btv[:, hh, d, :])
                    kb.tt(rr[:], btv[:, hh, d, :], self.Bv[:, hh, d, 0, :], ALU.subtract)
                    kb.cp(self.Bv[:, hh, d, 1, :], rr[:])
            cf = kb.sb("dcf", [2, 8], F32)
            kb.dma(cf[0:1, :], bfar.rearrange("(o h) -> o h", o=1))
            kb.dma(cf[1:2, :], bfar.rearrange("(o h) -> o h", o=1))
            cfb = kb.sb("dcfb", [2, 8], BF16)
            kb.cp(cfb[:], cf[:])
            cr = kb.sb("dcr", [2, 8], F32)
            kb.tt(cr[:], cf[:], cfb[:], ALU.subtract)
            cfl = kb.sb("dcfl", [2, 8], BF16)
            kb.cp(cfl[:], cr[:])
            kb.dma(cfb[1:2, :], cfl[1:2, :])
            cfv = self.cfar[:].rearrange("p (h q) -> p h q", h=8)
            kb.cp(cfv, cfb[:].unsqueeze(2).to_broadcast([2, 8, 128]))
            kb.barrier()
        self.rz = [kb.sb("drz%d" % i, [128, 2], F32) for i in range(3)]
        self.o = [kb.sb("do%d" % i, [128, 128], F32) for i in range(3)]
        self.sq = kb.sb("dsq", [128, 128], F32)
        self.ss = [kb.sb("dss%d" % i, [128, 1], F32) for i in range(3)]
        self.ost = [kb.sb("dost%d" % i, [128, 128], BF16) for i in range(3)]
        self.pi = 0

    def groups(self, seq):
        return [(seq, h) for h in range(8)]

    def subheads(self, g):
        return [0, 1]

    def load(self, g, buf):
        kb, S = self.kb, self.S
        seq, h = g
        t0 = seq * S
        for mm in range(2):
            r0 = 128 * h + 64 * mm
            kb.dma(self.Qa[buf][mm][:], self.QT[r0:r0 + 64, t0:t0 + S])
            kb.dma(self.Ka[buf][mm][:], self.KT[r0:r0 + 64, t0:t0 + S])
        vt = self.Vt[buf][:].rearrange("p (b d) -> p b d", d=129)
        kb.dma(vt[:, :, 0:128], self.V[t0:t0 + S, 128 * h:128 * h + 128].rearrange("(b p) d -> p b d", p=128))

    def qk(self, g, sh, j, i, ps, buf):
        kb = self.kb
        seq, h = g
        r = i - 4 * j
        kb.mm(ps[:], self.Ka[buf][sh][:, i * 128:(i + 1) * 128], self.Qa[buf][sh][:, j * 512:(j + 1) * 512],
              start=True, stop=(r < -1))
        if r >= -1:
            for qb in range(4):
                d = r - qb
                o_ = ps[:, qb * 128:(qb + 1) * 128]
                if d <= -2:
                    kb.mm(o_, self.ones2[:], self.cfar[:, h * 128:(h + 1) * 128], start=False, stop=False)
                elif d <= 0:
                    kb.mm(o_, self.gr.ident[:], self.Bv[:, h, -d, 0, :], start=False, stop=False)
                    kb.mm(o_, self.gr.ident[:], self.Bv[:, h, -d, 1, :], start=False, stop=False)

    def evac(self, g, sh, j, i, ps, P, buf):
        seq, h = g
        if i - 4 * j < -1:
            self.kb.act(P[:], ps[:], AF.Exp, bias=self.bfar[:, h:h + 1])
        else:
            self.kb.act(P[:], ps[:], AF.Exp)

    def ocols(self, g, sh):
        return 256 * sh, 129

    def vblk(self, g, sh, i, buf):
        return self.Vt[buf][:, i * 129:(i + 1) * 129]

    def post(self, g, j, outs, buf):
        kb, S = self.kb, self.S
        seq, h = g
        for qb in range(4):
            k = self.pi % 3
            self.pi += 1
            rz, o, ss, st = self.rz[k], self.o[k], self.ss[k], self.ost[k]
            ob = outs[qb]
            kb.recip(rz[:, 0:1], ob[:, 128:129])
            kb.recip(rz[:, 1:2], ob[:, 384:385])
            kb.tt(rz[:, 1:2], rz[:, 1:2], self.nlam[:], ALU.mult)
            kb.ts(o[:], ob[:, 0:128], rz[:, 0:1], None, op0=ALU.mult)
            kb.stt(o[:], ob[:, 256:384], rz[:, 1:2], o[:], ALU.mult, ALU.add)
            kb.tt(self.sq[:], o[:], o[:], ALU.mult, e="pool")
            kb.red(ss[:], self.sq[:])
            kb.act(ss[:], ss[:], AF.Sqrt, bias=self.gr.epsc[:, 0:1], scale=1.0 / 128)
            kb.recip(ss[:], ss[:])
            kb.stt(st[:], o[:], ss[:, 0:1], self.gsc[:], ALU.mult, ALU.mult)
            tok = seq * S + (4 * j + qb) * 128
            kb.dma(self.O[tok:tok + 128, 128 * h:128 * h + 128], st[:], q="pool")


def _t5_bucket_np(rel):
    nb = 16
    max_exact = 8
    ret = (rel > 0).astype(np.int64) * nb
    n = np.abs(rel)
    nf = np.maximum(n, 1).astype(np.float32)
    large = max_exact + (np.log(nf / max_exact) / math.log(128 / max_exact) * (nb - max_exact)).astype(np.int32)
    large = np.minimum(large, nb - 1)
    return ret + np.where(n < max_exact, n, large)


def diff_tables(rel_bias):
    kl = np.arange(128)[:, None]
    ql = np.arange(128)[None, :]
    idx0 = _t5_bucket_np(kl - ql)
    idx1 = _t5_bucket_np(kl - ql - 128)
    rb = np.asarray(rel_bias, dtype=np.float32)
    BT = np.stack([rb[idx0], rb[idx1]], axis=0)
    BT = np.ascontiguousarray(BT.transpose(3, 0, 1, 2))
    bfar = np.ascontiguousarray(rb[15, :])
    dmask = np.where((kl >= 64) & (ql < 64), NEG, 0.0).astype(np.float32)
    return BT, bfar, dmask


def cmask_np():
    k = np.arange(128)[:, None]
    q = np.arange(512)[None, :]
    return np.concatenate([np.where(128 * r + k <= q, 0.0, NEG) for r in range(4)], axis=1).astype(np.float32)


RET_LG = [math.log1p(-2.0 ** (-5.0 - h)) for h in range(4)]


def ret_tables(S):
    d = np.arange(0, 256, 2, dtype=np.float32) / np.float32(256.0)
    inv = (1.0 / (np.float32(10000.0) ** d)).astype(np.float32)
    ang = np.arange(S, dtype=np.float32)[None, :] * inv[:, None]
    cosT = np.cos(ang).astype(np.float32)
    sinT = np.sin(ang).astype(np.float32)
    t = np.arange(512)
    dq = np.stack([np.exp(RET_LG[h] * t) for h in range(4)]).astype(np.float32)
    dk = np.stack([np.exp(-RET_LG[h] * (t % 128)) / 16.0 for h in range(4)]).astype(np.float32)
    kl = np.arange(128)[:, None]
    q = np.arange(512)[None, :]
    mret = np.zeros((4, 4, 128, 512), np.float32)
    for h in range(4):
        for r in range(4):
            k = 128 * r + kl
            ck, cq = k // 64, q // 64
            f = np.where(ck < cq, 1.0, np.where(ck > cq, 0.0, np.where(k <= q, 1.0, np.exp(2.0 * RET_LG[h] * (k - q)))))
            mret[h, r] = f * np.exp(-RET_LG[h] * 128.0 * r)
    return cosT, sinT, dq, dk, mret


class RetSpec:
    nbuf = 1

    def __init__(self, m, S, QT, KT, V, GS, O, norm_g, mret):
        kb = m.kb
        self.kb, self.gr, self.S = kb, m.gr, S
        self.QT, self.KT, self.V, self.GS, self.O = QT, KT, V, GS, O
        self.Q = [kb.sb("rQ%d" % c, [128, S], BF16) for c in range(2)]
        self.K = [kb.sb("rK%d" % c, [128, S], BF16) for c in range(2)]
        self.Vt = kb.sb("rV", [128, (S // 128) * 512], BF16)
        self.M = kb.sb("rM", [128, 16 * 512], F32)
        kb.dma(self.M[:].rearrange("p (a q) -> p a q", a=16), mret.rearrange("h r k q -> k (h r) q"))
        self.gb = kb.sb("rgb", [128, 2048], F32)
        kb.dma(self.gb[:], norm_g.partition_broadcast(128))
        self.sq = kb.sb("rsq", [128, 512], F32)
        self.ss = [kb.sb("rss%d" % i, [128, 1], F32) for i in range(2)]
        self.on = [kb.sb("ron%d" % i, [128, 512], F32) for i in range(2)]
        self.gs = [kb.sb("rgs%d" % i, [128, 512], BF16) for i in range(2)]
        self.ost = [kb.sb("rost%d" % i, [128, 512], BF16) for i in range(2)]
        self.pi = 0

    def groups(self, seq):
        return [(seq, h) for h in range(4)]

    def subheads(self, g):
        return [0]

    def load(self, g, buf):
        kb, S = self.kb, self.S
        seq, h = g
        t0 = seq * S
        for c in range(2):
            r0 = 256 * h + 128 * c
            kb.dma(self.Q[c][:], self.QT[r0:r0 + 128, t0:t0 + S])
            kb.dma(self.K[c][:], self.KT[r0:r0 + 128, t0:t0 + S])
        kb.dma(self.Vt[:].rearrange("p (b d) -> p b d", d=512),
               self.V[t0:t0 + S, 512 * h:512 * h + 512].rearrange("(b p) d -> p b d", p=128))

    def qk(self, g, sh, j, i, ps, buf):
        kb = self.kb
        for c in range(2):
            kb.mm(ps[:], self.K[c][:, i * 128:(i + 1) * 128], self.Q[c][:, j * 512:(j + 1) * 512],
                  start=(c == 0), stop=(c == 1))

    def evac(self, g, sh, j, i, ps, P, buf):
        seq, h = g
        r = i - 4 * j
        if r < 0:
            self.kb.act(P[:], ps[:], AF.Copy, scale=float(math.exp(RET_LG[h] * (512 * j - 128 * i))))
        else:
            a = h * 4 + r
            self.kb.tt(P[:], ps[:], self.M[:, a * 512:(a + 1) * 512], ALU.mult)

    def ocols(self, g, sh):
        return 0, 512

    def vblk(self, g, sh, i, buf):
        return self.Vt[:, i * 512:(i + 1) * 512]

    def post(self, g, j, outs, buf):
        kb, S = self.kb, self.S
        seq, h = g
        for qb in range(4):
            k = self.pi % 2
            self.pi += 1
            ob = outs[qb]
            ss, on, gs, st = self.ss[k], self.on[k], self.gs[k], self.ost[k]
            tok = seq * S + (4 * j + qb) * 128
            kb.dma(gs[:], self.GS[tok:tok + 128, 512 * h:512 * h + 512])
            kb.act(self.sq[:], ob[:], AF.Square)
            kb.red(ss[:], self.sq[:])
            kb.act(ss[:], ss[:], AF.Sqrt, bias=self.gr.epsc[:, 0:1], scale=1.0 / 512)
            kb.recip(ss[:], ss[:])
            kb.stt(on[:], ob[:], ss[:, 0:1], self.gb[:, 512 * h:512 * h + 512], ALU.mult, ALU.mult)
            kb.tt(st[:], on[:], gs[:], ALU.mult, e="pool")
            kb.dma(self.O[tok:tok + 128, 512 * h:512 * h + 512], st[:], q="pool")


def ret_mixer(self, h, w_in, gcol, norm_g, w_out, cosT, sinT, dq, dk, mret):
    kb, gr, T, S, nseq = self.kb, self.gr, self.T, self.S, self.nseq
    QT = self.dscr("QT", [1024, T], BF16)
    KT = self.dscr("KT", [1024, T], BF16)
    V = self.dscr("Vtm", [T, 2048], BF16)
    GS = self.dscr("GStm", [T, 2048], BF16)
    O = self.dscr("Otm", [T, 2048], BF16)
    with kb.scope():
        Wv = load_weights(kb, gr, w_in, 1024, 6144, gcol)
        cs = [kb.sb("rcs%d" % i, [128, 1024], F32) for i in range(2)]
        dtab = kb.sb("rdtab", [128, 2 * 4 * 512], F32)
        dtv = dtab[:].rearrange("p (a h t) -> p a h t", a=2, h=4)
        kb.dma(dtv[:, 0], dq.partition_broadcast(128))
        kb.dma(dtv[:, 1], dk.partition_broadcast(128))
        tmp = [kb.sb("rtmp%d" % i, [128, 512], F32) for i in range(4)]
        yst = [kb.sb("ryst%d" % i, [128, 512], BF16) for i in range(4)]
        state = dict(blk=-1, n=0)

        def rot_epi(which, dst):
            def epi(blk, tok0, c0, pss):
                if state["blk"] != blk:
                    state["blk"] = blk
                    p0 = tok0 % S
                    c_ = cs[blk % 2]
                    kb.dma(c_[:, 0:512], cosT[:, p0:p0 + 512])
                    kb.dma(c_[:, 512:1024], sinT[:, p0:p0 + 512])
                c_ = cs[blk % 2]
                cosb, sinb = c_[:, 0:512], c_[:, 512:1024]
                hh = c0 // 2
                d_ = dtv[:, which, hh, :]
                x1, x2 = pss[0], pss[1]
                n = state["n"]
                state["n"] += 1
                ta, tb = tmp[(n % 2) * 2], tmp[(n % 2) * 2 + 1]
                y1, y2 = yst[(n % 2) * 2], yst[(n % 2) * 2 + 1]
                kb.tt(ta[:], x1[:], cosb, ALU.mult)
                kb.tt(tb[:], x2[:], sinb, ALU.mult)
                kb.tt(ta[:], ta[:], tb[:], ALU.subtract, e="pool")
                kb.tt(y1[:], ta[:], d_, ALU.mult, e="pool")
                kb.dma(dst[256 * hh:256 * hh + 128, tok0:tok0 + 512], y1[:], q="sp")
                kb.tt(ta[:], x1[:], sinb, ALU.mult)
                kb.tt(tb[:], x2[:], cosb, ALU.mult)
                kb.tt(ta[:], ta[:], tb[:], ALU.add, e="pool")
                kb.tt(y2[:], ta[:], d_, ALU.mult, e="pool")
                kb.dma(dst[256 * hh + 128:256 * hh + 256, tok0:tok0 + 512], y2[:], q="sp")
            return epi

        segs = [dict(kind="FM", n0=0, n1=1024, group=2, epi=rot_epi(0, QT)),
                dict(kind="FM", n0=1024, n1=2048, group=2, epi=rot_epi(1, KT)),
                dict(kind="TM", n0=2048, n1=4096, epi=self.epi.tm_store(V)),
                dict(kind="TM", n0=4096, n1=6144, epi=self.epi.tm_store(GS, func=AF.Silu))]
        gemm_stage(kb, gr, h, T, 1024, Wv, segs, norm=True)
        kb.barrier()
    with kb.scope():
        spec = RetSpec(self, S, QT, KT, V, GS, O, norm_g, mret)
        attn_stage(kb, spec, S, nseq)
        kb.barrier()
    with kb.scope():
        Wv = load_weights(kb, gr, w_out, 2048, 1024, None)
        segs = [dict(kind="TM", n0=0, n1=1024, epi=self.epi.tm_resid(h))]
        gemm_stage(kb, gr, O, T, 2048, Wv, segs, x_bf16=True)
        kb.barrier()


Model.ret_mixer = ret_mixer


def ssd_tables():
    sa = np.zeros((48, 1024), np.float32)
    for hh in range(8):
        sa[6 * hh:6 * hh + 3, hh * 128:(hh + 1) * 128] = 1.0
    return sa


class SsdSpec:
    nbuf = 1

    def __init__(self, m, S, AQK, BTd, CTd, XSd, DTd, ZS, O, dsk_rep, norm_g, sa_init):
        kb = m.kb
        self.kb, self.gr, self.S, self.m = kb, m.gr, S, m
        self.AQK, self.BTd, self.CTd, self.XSd, self.DTd, self.ZS, self.O = AQK, BTd, CTd, XSd, DTd, ZS, O
        nb = S // 128
        self.nb = nb
        self.TA = kb.sb("sTA", [48, S], BF16)
        kb.memset(self.TA[:], 1.0)
        self.SA = [kb.sb("sSA%d" % i, [48, 1024], BF16) for i in range(3)]
        with kb.scope():
            t = kb.sb("ssat", [48, 1024], F32)
            kb.dma(t[:], sa_init)
            for i in range(3):
                kb.cp(self.SA[i][:], t[:])
            kb.barrier()
        self.Bt = kb.sb("sBt", [128, S], BF16)
        self.Ct = kb.sb("sCt", [128, S], BF16)
        self.XS = kb.sb("sXS", [128, nb * 512], BF16)
        self.XP = kb.sb("sXP", [128, nb * 512], BF16)
        self.DTt = kb.sb("sDT", [128, nb * 8], F32)
        self.E = [kb.sb("sE%d" % i, [128, 512], F32) for i in range(2)]
        self.cbp = [kb.ps("scb%d" % i, [128, 512], F32) for i in range(2)]
        self.dsk = kb.sb("sdsk", [128, 2048], F32)
        kb.dma(self.dsk[:], dsk_rep.partition_broadcast(128))
        self.gn = kb.sb("sgn", [128, 2048], F32)
        kb.dma(self.gn[:], norm_g.partition_broadcast(128))
        self.y1 = [kb.sb("sy1%d" % i, [128, 512], F32) for i in range(2)]
        self.y3 = [kb.sb("sy3%d" % i, [128, 512], F32) for i in range(2)]
        self.zs = [kb.sb("szs%d" % i, [128, 512], BF16) for i in range(2)]
        self.ost = [kb.sb("sost%d" % i, [128, 512], BF16) for i in range(2)]
        self.sq = kb.sb("ssq", [128, 512], F32)
        self.ss = [kb.sb("sss%d" % i, [128, 1], F32) for i in range(2)]
        self.pi = 0
        self.ni = 0
        self.ne = 0

    def groups(self, seq):
        return [(seq, g) for g in range(4)]

    def subheads(self, g):
        return list(range(8))

    def load(self, g, buf):
        kb, S, nb = self.kb, self.S, self.nb
        seq, gg = g
        t0 = seq * S
        for hh in range(8):
            kb.dma(self.TA[6 * hh:6 * hh + 3, :], self.AQK[gg * 8 + hh, 0:3, t0:t0 + S])
        kb.dma(self.Bt[:], self.BTd[gg * 128:(gg + 1) * 128, t0:t0 + S])
        kb.dma(self.Ct[:], self.CTd[gg * 128:(gg + 1) * 128, t0:t0 + S])
        kb.dma(self.XS[:].rearrange("p (b d) -> p b d", d=512),
               self.XSd[t0:t0 + S, gg * 512:(gg + 1) * 512].rearrange("(b p) d -> p b d", p=128))
        kb.dma(self.DTt[:].rearrange("p (b h) -> p b h", h=8),
               self.DTd[t0:t0 + S, gg * 8:(gg + 1) * 8].rearrange("(b p) h -> p b h", p=128))
        kb.tt(self.XP[:].rearrange("p (a d) -> p a d", d=64), self.XS[:].rearrange("p (a d) -> p a d", d=64),
              self.DTt[:].unsqueeze(2).to_broadcast([128, nb * 8, 64]), ALU.mult)

    def pre(self, g, j, i, buf):
        kb, S = self.kb, self.S
        seq, gg = g
        t0 = seq * S
        sa = self.SA[self.ni % 3]
        cb = self.cbp[self.ni % 2]
        self.ni += 1
        for hh in range(8):
            kb.dma(sa[6 * hh + 3:6 * hh + 6, hh * 128:(hh + 1) * 128],
                   self.AQK[gg * 8 + hh, 3:6, t0 + i * 128:t0 + (i + 1) * 128])
        kb.mm(cb[:], self.Bt[:, i * 128:(i + 1) * 128], self.Ct[:, j * 512:(j + 1) * 512])
        return (sa, cb)

    def qk(self, g, sh, j, i, ps, buf):
        kb = self.kb
        sa, cb = self.cur
        r = i - 4 * j
        kb.mm(ps[:], sa[:, sh * 128:(sh + 1) * 128], self.TA[:, j * 512:(j + 1) * 512], start=True, stop=(r < 0))
        if r >= 0:
            kb.mm(ps[:], self.gr.ident[:], self.m.cmask[:, r * 512:(r + 1) * 512], start=False, stop=True)

    def evac(self, g, sh, j, i, ps, P, buf):
        kb = self.kb
        sa, cb = self.cur
        E = self.E[self.ne % 2]
        self.ne += 1
        kb.act(E[:], ps[:], AF.Exp)
        kb.tt(P[:], E[:], cb[:], ALU.mult)

    def ocols(self, g, sh):
        return sh * 64, 64

    def vblk(self, g, sh, i, buf):
        return self.XP[:, i * 512 + sh * 64:i * 512 + sh * 64 + 64]

    def post(self, g, j, outs, buf):
        kb, S = self.kb, self.S
        seq, gg = g
        for qb in range(4):
            k = self.pi % 2
            self.pi += 1
            b = 4 * j + qb
            tok = seq * S + b * 128
            y1, y3, zs, st, ss = self.y1[k], self.y3[k], self.zs[k], self.ost[k], self.ss[k]
            kb.dma(zs[:], self.ZS[tok:tok + 128, gg * 512:(gg + 1) * 512])
            kb.tt(y1[:], self.XS[:, b * 512:(b + 1) * 512], self.dsk[:, gg * 512:(gg + 1) * 512], ALU.mult, e="pool")
            kb.tt(y1[:], outs[qb][:], y1[:], ALU.add)
            kb.tt(y3[:], y1[:], zs[:], ALU.mult, e="pool")
            kb.act(self.sq[:], y3[:], AF.Square)
            kb.red(ss[:], self.sq[:])
            kb.act(ss[:], ss[:], AF.Sqrt, bias=self.gr.epsc[:, 0:1], scale=1.0 / 512)
            kb.recip(ss[:], ss[:])
            kb.stt(st[:], y3[:], ss[:, 0:1], self.gn[:, gg * 512:(gg + 1) * 512], ALU.mult, ALU.mult)
            kb.dma(self.O[tok:tok + 128, gg * 512:(gg + 1) * 512], st[:], q="pool")


def ssd_mixer(self, h, w_in, gcol, conv_wT, conv_b, dt_bias, a_log, dsk_rep, norm_g, w_out, sa_init):
    kb, gr, T, S, nseq = self.kb, self.gr, self.T, self.S, self.nseq
    nc = self.nc
    ZS = self.dscr("GStm", [T, 2048], BF16)
    XSd = self.dscr("Vtm", [T, 2048], BF16)
    BTd = self.dscr("QT", [1024, T], BF16)
    CTd = self.dscr("KT", [1024, T], BF16)
    AQK = self.dscr("CQK", [32, 6, T], BF16)
    DTd = self.dscr("DTd", [T, 32], F32)
    O = self.dscr("Otm", [T, 2048], BF16)
    bps = S // 512
    with kb.scope():
        Wv = load_weights(kb, gr, w_in, 1024, 5152, gcol)
        cw = kb.sb("scw", [128, 24 * 4], F32)
        cwv = cw[:].rearrange("p (c k) -> p c k", k=4)
        kb.dma(cwv, conv_wT.rearrange("(c p) k -> p c k", p=128))
        cbias = kb.sb("scbias", [128, 24], F32)
        kb.dma(cbias[:], conv_b.rearrange("(c p) -> p c", p=128))
        hal = kb.sb("shal", [128, 24 * 3], F32)
        cbuf = [kb.sb("scbuf%d" % i, [128, 515], F32) for i in range(2)]
        accb = [kb.sb("saccb%d" % i, [128, 512], F32) for i in range(2)]
        sil = [kb.sb("ssil%d" % i, [128, 512], BF16) for i in range(2)]
        xst = [kb.sb("sxst%d" % i, [128, 512], BF16) for i in range(2)]
        ptx = kb.ps("sptx", [128, 1024], BF16)
        pdt = kb.ps("spdt", [128, 512], F32)
        dtb = kb.sb("sdtb", [32, 1], F32)
        kb.dma(dtb[:], dt_bias.rearrange("(p o) -> p o", o=1))
        nA = kb.sb("snA", [32, 1], F32)
        kb.dma(nA[:], a_log.rearrange("(p o) -> p o", o=1))
        kb.act(nA[:], nA[:], AF.Exp)
        kb.ts(nA[:], nA[:], -1.0, None, op0=ALU.mult)
        carry = kb.sb("scarry", [32, 1], F32)
        d_ = [kb.sb("sd%d" % i, [32, 512], F32) for i in range(5)]
        cum = kb.sb("scum", [32, 512], F32)
        rr = kb.sb("srr", [32, 512], F32)
        spl = kb.sb("sspl", [32, 6 * 512], BF16)
        splv = spl[:].rearrange("p (s n) -> p s n", s=6)
        dtm = kb.sb("sdtm", [128, 4 * 32], F32)
        cn = [0]

        def conv_epi(blk, tok0, c, pss):
            ps = pss[0]
            n = cn[0]
            cn[0] += 1
            cb, acc, sl = cbuf[n % 2], accb[n % 2], sil[n % 2]
            if blk % bps == 0:
                kb.memset(cb[:, 0:3], 0.0)
            else:
                kb.cp(cb[:, 0:3], hal[:, 3 * c:3 * c + 3])
            kb.act(cb[:, 3:515], ps[:], AF.Copy)
            kb.cp(hal[:, 3 * c:3 * c + 3], cb[:, 512:515])
            kb.ts(acc[:], cb[:, 0:512], cwv[:, c, 0:1], cbias[:, c:c + 1], op0=ALU.mult, op1=ALU.add)
            for k in range(1, 4):
                kb.stt(acc[:], cb[:, k:k + 512], cwv[:, c, k:k + 1], acc[:], ALU.mult, ALU.add)
            kb.act(sl[:], acc[:], AF.Silu)
            if c < 16:
                for s_ in range(4):
                    kb.tr(ptx[:, s_ * 128:(s_ + 1) * 128], sl[:, s_ * 128:(s_ + 1) * 128], gr.ident[:])
                xs_ = xst[n % 2]
                kb.cp(xs_[:], ptx[:, 0:512])
                kb.dma(XSd[tok0:tok0 + 512, c * 128:(c + 1) * 128].rearrange("(s p) ch -> p s ch", p=128),
                       xs_[:].rearrange("p (s ch) -> p s ch", s=4), q="pool")
            elif c < 20:
                kb.dma(BTd[(c - 16) * 128:(c - 15) * 128, tok0:tok0 + 512], sl[:], q="pool")
            else:
                kb.dma(CTd[(c - 20) * 128:(c - 19) * 128, tok0:tok0 + 512], sl[:], q="pool")

        def dt_epi(blk, tok0, c0, pss):
            ps = pss[0]
            xb, ab, e_, r_, dtt = d_
            if blk % bps == 0:
                kb.memset(carry[:], 0.0)
            kb.act(xb[:], ps[0:32, :], AF.Identity, bias=dtb[:, 0:1])
            kb.act(ab[:], xb[:], AF.Abs)
            kb.act(e_[:], ab[:], AF.Exp, scale=-1.0)
            kb.act(e_[:], e_[:], AF.Ln, bias=self.one[0:32, 0:1])
            kb.ts(r_[:], xb[:], 0.0, None, op0=ALU.max)
            kb.tt(dtt[:], r_[:], e_[:], ALU.add)
            kb.ts(ab[:], dtt[:], nA[:, 0:1], None, op0=ALU.mult)
            kb.op("dve", lambda: nc.vector.tensor_tensor_scan(cum[:], ab[:], self.zeros[0:32, :], carry[:, 0:1], ALU.add, ALU.add),
                  [ab, self.zeros, carry], [cum])
            kb.cp(carry[:], cum[:, 511:512])
            split3(kb, cum[:], splv, rr, None, 32, 512)
            kb.dma(AQK[0:32, :, tok0:tok0 + 512], splv, q="pool")
            for s_ in range(4):
                kb.tr(pdt[:, s_ * 32:(s_ + 1) * 32], dtt[:, s_ * 128:(s_ + 1) * 128], gr.ident_f[0:32, 0:32])
            kb.cp(dtm[:], pdt[:, 0:128])
            kb.dma(DTd[tok0:tok0 + 512, :].rearrange("(s p) h -> p s h", p=128), dtm[:].rearrange("p (s h) -> p s h", s=4), q="pool")

        segs = [dict(kind="TM", n0=0, n1=2048, epi=self.epi.tm_store(ZS, func=AF.Silu)),
                dict(kind="FM", n0=2048, n1=5120, epi=conv_epi),
                dict(kind="FM", n0=5120, n1=5152, epi=dt_epi)]
        gemm_stage(kb, gr, h, T, 1024, Wv, segs, norm=True)
        kb.barrier()
    with kb.scope():
        spec = SsdSpec(self, S, AQK, BTd, CTd, XSd, DTd, ZS, O, dsk_rep, norm_g, sa_init)
        attn_stage(kb, spec, S, nseq)
        kb.barrier()
    with kb.scope():
        Wv = load_weights(kb, gr, w_out, 2048, 1024, None)
        segs = [dict(kind="TM", n0=0, n1=1024, epi=self.epi.tm_resid(h))]
        gemm_stage(kb, gr, O, T, 2048, Wv, segs, x_bf16=True)
        kb.barrier()


Model.ssd_mixer = ssd_mixer


DEPTH = 4
SEQ = 4096
NSEQ = 2


def build_program(nseq=NSEQ, S=SEQ):
    T = nseq * S
    nc = bass.Bass("TRN2", target_bir_lowering=False)
    es = ExitStack()
    m = Model(nc, es, nseq, S)
    I = m.inp
    I("ident", [128, 128]); I("cmask", [128, 2048])
    x = I("x", [T, 1024]); p = I("p", [DEPTH, T, 256])
    mix_norm = I("mix_norm", [DEPTH, 1024]); ffn_norm = I("ffn_norm", [DEPTH, 1024]); ple_norm = I("ple_norm", [DEPTH, 1024])
    final_norm = I("final_norm", [1024])
    ssd_w_in = I("ssd_w_in", [1024, 5152]); ssd_cwT = I("ssd_cwT", [3072, 4]); ssd_cb = I("ssd_cb", [3072])
    ssd_dtb = I("ssd_dtb", [32]); ssd_alog = I("ssd_alog", [32]); ssd_dsk = I("ssd_dsk", [2048]); ssd_norm = I("ssd_norm", [2048])
    ssd_w_out = I("ssd_w_out", [2048, 1024]); ssd_sa = I("ssd_sa", [48, 1024])
    ret_w_in = I("ret_w_in", [1024, 6144]); ret_norm = I("ret_norm", [2048]); ret_w_out = I("ret_w_out", [2048, 1024])
    ret_cos = I("ret_cos", [128, S]); ret_sin = I("ret_sin", [128, S]); ret_dq = I("ret_dq", [4, 512]); ret_dk = I("ret_dk", [4, 512])
    ret_m = I("ret_m", [4, 4, 128, 512])
    diff_w_in = I("diff_w_in", [1024, 3072]); diff_lam = I("diff_lam", [4, 64]); diff_norm = I("diff_norm", [128])
    diff_w_out = I("diff_w_out", [1024, 1024]); diff_BT = I("diff_BT", [8, 2, 128, 128]); diff_bfar = I("diff_bfar", [8])
    diff_mask = I("diff_mask", [128, 128])
    fox_w_in = I("fox_w_in", [1024, 3088]); fox_bf = I("fox_bf", [16]); fox_w_out = I("fox_w_out", [1024, 1024])
    peer_wq = I("peer_wq", [DEPTH, 1024, 2048]); peer_kT = I("peer_kT", [DEPTH, 16, 128, 128])
    peer_uT = I("peer_uT", [DEPTH, 1024, 16384]); peer_v = I("peer_v", [DEPTH, 16384, 1024])
    ple_proj = I("ple_proj", [DEPTH, 256, 1024]); ple_gate = I("ple_gate", [DEPTH, 1024, 1024])
    out = nc.dram_tensor("out", [T, 1024], F32, kind="ExternalOutput").ap()
    hb = m.dscr("hbuf", [T, 1024], F32)
    m.setup()
    kb = m.kb
    for r0 in range(0, T, 512):
        kb.dma(hb[r0:r0 + 512, :], x[r0:r0 + 512, :])
    kb.barrier()
    for i in range(DEPTH):
        if i == 0:
            m.ssd_mixer(hb, ssd_w_in, mix_norm[i], ssd_cwT, ssd_cb, ssd_dtb, ssd_alog, ssd_dsk, ssd_norm, ssd_w_out, ssd_sa)
        elif i == 1:
            m.ret_mixer(hb, ret_w_in, mix_norm[i], ret_norm, ret_w_out, ret_cos, ret_sin, ret_dq, ret_dk, ret_m)
        elif i == 2:
            lam_init = 0.8 - 0.6 * math.exp(-0.3 * i)
            m.diff_mixer(hb, diff_w_in, mix_norm[i], diff_lam, lam_init, diff_norm, diff_w_out, diff_BT, diff_bfar, diff_mask)
        else:
            m.fox_mixer(hb, fox_w_in, mix_norm[i], fox_bf, fox_w_out)
        m.peer(hb, ffn_norm[i], peer_wq[i], peer_kT[i], peer_uT[i], peer_v[i])
        m.ple(hb, ple_norm[i], ple_gate[i], p[i], ple_proj[i])
    m.final(hb, final_norm, out)
    es.close()
    return nc, m


def host_inputs(inputs, nseq=NSEQ, S=SEQ, ncores=NCORES):
    f = lambda a: np.ascontiguousarray(np.asarray(a, dtype=np.float32))
    g = {k: np.asarray(v) for k, v in inputs.items()}
    BT, bfar, dmask = diff_tables(g["rel_bias"])
    cosT, sinT, dq, dk, mret = ret_tables(S)
    shared = {
        "ident": np.eye(128, dtype=np.float32), "cmask": cmask_np(),
        "mix_norm": f(g["mix_norm"]), "ffn_norm": f(g["ffn_norm"]), "ple_norm": f(g["ple_norm"]), "final_norm": f(g["final_norm"]),
        "ssd_w_in": f(g["ssd_w_in"][0]), "ssd_cwT": f(g["ssd_conv_w"][0][:, 0, :].T), "ssd_cb": f(g["ssd_conv_b"][0]),
        "ssd_dtb": f(g["ssd_dt_bias"][0]), "ssd_alog": f(g["ssd_a_log"][0]), "ssd_dsk": f(np.repeat(g["ssd_d"][0], 64)),
        "ssd_norm": f(g["ssd_norm"][0]), "ssd_w_out": f(g["ssd_w_out"][0]), "ssd_sa": ssd_tables(),
        "ret_w_in": f(g["ret_w_in"][0]), "ret_norm": f(g["ret_norm"][0]), "ret_w_out": f(g["ret_w_out"][0]),
        "ret_cos": cosT, "ret_sin": sinT, "ret_dq": dq, "ret_dk": dk, "ret_m": mret,
        "diff_w_in": f(g["diff_w_in"][0]), "diff_lam": f(g["diff_lambda"][0]), "diff_norm": f(g["diff_norm"][0]),
        "diff_w_out": f(g["diff_w_out"][0]), "diff_BT": f(BT), "diff_bfar": f(bfar), "diff_mask": dmask,
        "fox_w_in": f(g["fox_w_in"][0]), "fox_bf": f(g["fox_b_f"][0]), "fox_w_out": f(g["fox_w_out"][0]),
        "peer_wq": f(g["peer_w_q"]),
        "peer_kT": f(g["peer_keys"].reshape(DEPTH, 16, 128, 128).transpose(0, 1, 3, 2)),
        "peer_uT": f(g["peer_u"].transpose(0, 2, 1)), "peer_v": f(g["peer_v"]),
        "ple_proj": f(g["ple_proj"]), "ple_gate": f(g["ple_gate"]),
    }
    T = nseq * S
    maps = []
    for c in range(ncores):
        d = dict(shared)
        d["x"] = f(g["x"][c * nseq:(c + 1) * nseq].reshape(T, 1024))
        d["p"] = f(g["p"][:, c * nseq:(c + 1) * nseq].reshape(DEPTH, T, 256))
        maps.append(d)
    return maps


def kernel(**inputs):
    nc, m = build_program()
    maps = host_inputs(inputs)
    res = run_bass_kernel_spmd(nc, maps, core_ids=list(range(NCORES)))
    outs = [np.asarray(r["out"]).reshape(NSEQ, SEQ, 1024) for r in res.results]
    return np.concatenate(outs, axis=0).astype(np.float32)


def peer_merged(self, h, gcol, w_q, keysT, uT, v):
    kb, gr, T = self.kb, self.gr, self.T
    nc = self.nc
    V = nc.vector
    UTb = self.dscr("UTb", [1024, 16384], BF16)
    Vb = self.dscr("Vb", [16384, 1024], BF16)
    Gd = self.dscr("Gd", [T, 16384], BF16)
    with kb.scope():
        kb.dma(gr.gcol[:, 0:8], gcol.rearrange("(c p) -> p c", p=128))
        st = [kb.sb("pst%d" % i, [128, 2048], F32) for i in range(3)]
        sb = [kb.sb("psb%d" % i, [128, 2048], BF16) for i in range(3)]
        n = 0
        for kc in range(8):
            for e0 in range(0, 16384, 2048):
                s_, b_ = st[n % 3], sb[n % 3]
                kb.dma(s_[:], uT[kc * 128:(kc + 1) * 128, e0:e0 + 2048])
                if n % 2 == 0:
                    kb.ts(b_[:], s_[:], gr.gcol[:, kc:kc + 1], None, op0=ALU.mult)
                else:
                    kb.act(b_[:], s_[:], AF.Copy, scale=gr.gcol[:, kc:kc + 1])
                kb.dma(UTb[kc * 128:(kc + 1) * 128, e0:e0 + 2048], b_[:], q="pool")
                n += 1
        for r0 in range(0, 16384, 256):
            s_, b_ = st[n % 3], sb[n % 3]
            kb.dma(s_[:].rearrange("p (c n) -> p c n", c=2), v[r0:r0 + 256, :].rearrange("(c p) n -> p c n", p=128))
            if n % 2 == 0:
                kb.cp(b_[:], s_[:], e="dve")
            else:
                kb.act(b_[:], s_[:], AF.Copy)
            kb.dma(Vb[r0:r0 + 256, :].rearrange("(c p) n -> p c n", p=128), b_[:].rearrange("p (c n) -> p c n", c=2), q="pool")
            n += 1
        kb.barrier()
    with kb.scope():
        Wv = load_weights(kb, gr, w_q, 1024, 2048, gcol)
        kT = kb.sb("keysT", [128, 16 * 128], BF16)
        with kb.scope():
            ktmp = kb.sb("ktmp", [128, 16 * 128], F32)
            kb.dma(ktmp[:].rearrange("p (c k) -> p c k", c=16), keysT.rearrange("c d k -> d c k"))
            kb.cp(kT[:], ktmp[:])
            kb.barrier()
        kTv = kT[:].rearrange("p (c k) -> p c k", c=16)
        qT = [kb.sb("pqT%d" % i, [128, 512], BF16) for i in range(2)]
        psc = kb.ps("psc", [128, 512], F32)
        sc_all = kb.sb("sc_all", [128, 4 * 16 * 128], F32)
        scv = sc_all[:].rearrange("p (s c k) -> p s c k", s=4, c=16)
        a16 = kb.sb("a16", [128, 16], F32)
        b16 = kb.sb("b16", [128, 16], F32)
        c16 = kb.sb("c16", [128, 16], F32)
        e16 = kb.sb("e16", [128, 16], F32)
        t128 = kb.sb("t128", [128, 128], F32)
        cand = kb.sb("cand", [128, 256], F32)
        cand2 = kb.sb("cand2", [128, 256], F32)
        tau = kb.sb("tau", [128, 8], F32)
        nb = kb.sb("nb", [128, 8], F32)
        zz = kb.sb("zz", [128, 1], F32)
        Sc = [kb.sb("Sc%d" % i, [128, 2048], F32) for i in range(2)]
        Ec = [kb.sb("Ec%d" % i, [128, 2048], BF16) for i in range(2)]
        Gb = [kb.sb("Gb%d" % i, [128, 2048], BF16) for i in range(2)]
        ut = [kb.sb("ut%d" % i, [128, 8 * 512], BF16) for i in range(2)]
        vt = [kb.sb("vt%d" % i, [128, 4 * 1024], BF16) for i in range(2)]
        gt = [kb.sb("gt%d" % i, [128, 512], BF16) for i in range(2)]
        gel = [kb.sb("gel%d" % i, [128, 512], F32) for i in range(1)]
        gh = [kb.sb("gh%d" % i, [128, 512], BF16) for i in range(2)]
        ghT = [kb.sb("ghT%d" % i, [128, 512], BF16) for i in range(1)]
        acc = [kb.sb("acc%d" % i, [128, 1024], F32) for i in range(4)]
        hst = gr.xin
        php = kb.ps("php", [128, 512], F32)
        ptp = kb.ps("ptp", [128, 1024], BF16)
        pop = [kb.ps("pop%d" % i, [128, 512], F32) for i in range(2)]
        state = dict(pending=None, nq=0, nd=0, gchunk=0)

        def top16(dst, src, tmp):
            kb.op("dve", lambda: V.max(out=dst[:, 0:8], in_=src), [src], [dst])
            kb.op("dve", lambda: V.match_replace(out=tmp, in_to_replace=dst[:, 0:8], in_values=src, imm_value=-1e30),
                  [dst, src], [tmp])
            kb.op("dve", lambda: V.max(out=dst[:, 8:16], in_=tmp), [tmp], [dst])

        def gates_gen(blk, tokb, sub):
            gkey = ("Gd", blk)
            for hh in range(8):
                s1 = scv[:, sub, 2 * hh, :]
                s2 = scv[:, sub, 2 * hh + 1, :]
                top16(a16, s1, t128[:])
                top16(b16, s2, t128[:])
                kb.tt(cand[:].rearrange("p (a b) -> p a b", a=16),
                      a16[:].unsqueeze(2).to_broadcast([128, 16, 16]),
                      b16[:].unsqueeze(1).to_broadcast([128, 16, 16]), ALU.add)
                top16(c16, cand[:], cand2[:])
                kb.cp(tau[:, hh:hh + 1], c16[:, 15:16])
                kb.ts(nb[:, hh:hh + 1], c16[:, 0:1], -1.0, None, op0=ALU.mult)
                kb.act(e16[:], c16[:], AF.Exp, bias=nb[:, hh:hh + 1])
                kb.red(zz[:], e16[:])
                kb.act(zz[:], zz[:], AF.Ln)
                kb.tt(nb[:, hh:hh + 1], nb[:, hh:hh + 1], zz[:], ALU.subtract)
                yield
            items = [(c, hh) for c in range(8) for hh in range(8)]

            def stage1(n):
                c, hh = items[n]
                S_, E_ = Sc[n % 2], Ec[n % 2]
                s1 = scv[:, sub, 2 * hh, 16 * c:16 * c + 16]
                s2 = scv[:, sub, 2 * hh + 1, :]
                S3 = S_[:].rearrange("p (a b) -> p a b", a=16)
                kb.tt(S3, s1.unsqueeze(2).to_broadcast([128, 16, 128]),
                      s2.unsqueeze(1).to_broadcast([128, 16, 128]), ALU.add)
                kb.act(E_[:], S_[:], AF.Exp, bias=nb[:, hh:hh + 1])

            def stage2(n):
                c, hh = items[n]
                S_, E_ = Sc[n % 2], Ec[n % 2]
                G = Gb[(state["gchunk"] + c) % 2]
                dst = G if hh == 0 else E_
                kb.stt(dst[:], S_[:], tau[:, hh:hh + 1], E_[:], ALU.is_ge, ALU.mult)
                if hh > 0:
                    kb.tt(G[:], G[:], E_[:], ALU.add)
                if hh == 7:
                    prev = state.get("gstore")
                    if prev is not None:
                        kb.dma(prev[0], prev[1], q="act", wk=prev[2])
                    state["gstore"] = (Gd[tokb:tokb + 128, c * 2048:(c + 1) * 2048], G[:], gkey)

            stage1(0)
            for n in range(64):
                if n + 1 < 64:
                    stage1(n + 1)
                stage2(n)
                yield
            state["gchunk"] += 8
            if sub == 3:
                prev = state.get("gstore")
                kb.dma(prev[0], prev[1], q="sp", wk=prev[2])
                state["gstore"] = None

        def dense_gen(blk, xTv):
            tokb = blk * 512
            gkey = ("Gd", blk)
            for ec in range(32):
                u_ = ut[ec % 2]
                v_ = vt[ec % 2]
                uv = u_[:].rearrange("p (c e) -> p c e", c=8)
                vv = v_[:].rearrange("p (c n) -> p c n", c=4)
                kb.dma(uv, UTb[:, ec * 512:(ec + 1) * 512].rearrange("(c p) e -> p c e", p=128))
                kb.dma(vv, Vb[ec * 512:(ec + 1) * 512, :].rearrange("(c p) n -> p c n", p=128))
                for sub in range(4):
                    k = state["nd"]
                    state["nd"] += 1
                    g_ = gt[k % 2]
                    kb.dma(g_[:], Gd[tokb + sub * 128:tokb + sub * 128 + 128, ec * 512:(ec + 1) * 512], rk=gkey)
                    for kc in range(8):
                        kb.mm(php[:], xTv[:, kc, sub * 128:(sub + 1) * 128], uv[:, kc, :], start=(kc == 0), stop=(kc == 7))
                    yield
                    ge = gel[0]
                    kb.act(ge[:], php[:], AF.Gelu)
                    gh_ = gh[k % 2]
                    kb.tt(gh_[:], ge[:], g_[:], ALU.mult, e="pool")
                    yield
                    for c4 in range(4):
                        kb.tr(ptp[:, c4 * 128:(c4 + 1) * 128], gh_[:, c4 * 128:(c4 + 1) * 128], gr.ident[:])
                    if state.get("padd") is not None:
                        state["padd"]()
                        state["padd"] = None
                    yield
                    gT = ghT[0]
                    kb.act(gT[:], ptp[:, 0:512], AF.Copy)
                    pop4 = [pop[0], pop[1], gr.pm[0], gr.pm[1]]
                    pos = []
                    for nh in range(2):
                        po = pop4[(k % 2) * 2 + nh]
                        pos.append(po)
                        for c4 in range(4):
                            kb.mm(po[:], gT[:, c4 * 128:(c4 + 1) * 128], vv[:, c4, nh * 512:(nh + 1) * 512],
                                  start=(c4 == 0), stop=(c4 == 3))

                    def do_add(pos=pos, sub=sub, ec=ec):
                        for nh in range(2):
                            a_ = acc[sub][:, nh * 512:(nh + 1) * 512]
                            if ec == 0:
                                kb.act(a_, pos[nh][:], AF.Copy)
                            else:
                                kb.tt(a_, pos[nh][:], a_, ALU.add)
                    state["padd"] = do_add
                    yield
            if state.get("padd") is not None:
                state["padd"]()
                state["padd"] = None
            for sub in range(4):
                t0 = tokb + sub * 128
                hs = hst[sub % 3]
                key = ("h", t0)
                kb.dma(hs[:], h[t0:t0 + 128, :], rk=key)
                kb.tt(hs[:], acc[sub][:], hs[:], ALU.add, e="pool")
                kb.dma(h[t0:t0 + 128, :], hs[:], q="pool", wk=key)
            yield

        def run_interleaved(gg, dg, ng=3, nd=5):
            alive_g, alive_d = gg is not None, dg is not None
            while alive_g or alive_d:
                if alive_g:
                    for _ in range(ng):
                        try:
                            next(gg)
                        except StopIteration:
                            alive_g = False
                            break
                if alive_d:
                    for _ in range(nd):
                        try:
                            next(dg)
                        except StopIteration:
                            alive_d = False
                            break

        def q_epi(blk, tok0, c0, pss):
            qt = qT[c0 % 2]
            kb.act(qt[:], pss[0][:], AF.Copy)
            for sub in range(4):
                kb.mm(psc[:, sub * 128:(sub + 1) * 128], qt[:, sub * 128:(sub + 1) * 128], kTv[:, c0, :])
            kb.act(scv[:, :, c0, :], psc[:].rearrange("p (s k) -> p s k", s=4), AF.Copy)

        def chain(blk):
            for sub in range(4):
                for _ in gates_gen(blk, blk * 512 + sub * 128, sub):
                    yield

        def merged(blk, xTv):
            pend = state["pending"]
            dg = dense_gen(*pend) if pend is not None else None
            run_interleaved(chain(blk), dg)
            state["pending"] = (blk, xTv)

        segs = [dict(kind="FM", n0=0, n1=2048, epi=q_epi), dict(kind="custom", fn=merged)]
        gemm_stage(kb, gr, h, T, 1024, Wv, segs, norm=True, n_psT=1, n_pm=2)
        run_interleaved(None, dense_gen(*state["pending"]))
        kb.barrier()


Model.peer = peer_merged
```

```python
from contextlib import ExitStack
import math
import numpy as np
import concourse.bass as bass
import concourse.mybir as mybir
from concourse.bass_utils import run_bass_kernel_spmd

F32 = mybir.dt.float32
BF16 = mybir.dt.bfloat16
AF = mybir.ActivationFunctionType
ALU = mybir.AluOpType
AX = mybir.AxisListType

NCORES = 8
D = 1024
NEG = -30000.0
DBG = set()


class KB:
    NDMA = 24

    def __init__(self, nc, es):
        self.nc = nc
        self.es = es
        self.eng = dict(pe=nc.tensor, act=nc.scalar, dve=nc.vector, pool=nc.gpsimd, sp=nc.sync)
        es.enter_context(nc.allow_non_contiguous_dma(reason="small strided param loads"))
        self.sem = {}
        self.cnt = {}
        for e in ("pe", "act", "dve", "pool"):
            self.sem[e] = es.enter_context(nc.semaphore("s_" + e))
            self.cnt[e] = 0
        self.dsem = []
        for i in range(self.NDMA):
            nm = "d%d" % i
            self.sem[nm] = es.enter_context(nc.semaphore("s_" + nm))
            self.cnt[nm] = 0
            self.dsem.append(nm)
        self.dnext = 0
        self.known = {e: {} for e in self.eng}
        self.lastw = {}
        self.readers = {}
        self.n_ins = 0
        self.uid = 0

    def scope(self):
        kb = self

        class _S:
            def __enter__(self_):
                self_.old = kb.es
                self_.st = ExitStack()
                self_.st.__enter__()
                kb.es = self_.st
                return self_

            def __exit__(self_, *a):
                kb.es = self_.old
                return self_.st.__exit__(*a)

        return _S()

    def sb(self, name, shape, dtype=F32):
        self.uid += 1
        return self.es.enter_context(self.nc.sbuf_tensor("%s_%d" % (name, self.uid), list(shape), dtype))

    def ps(self, name, shape, dtype=F32):
        self.uid += 1
        return self.es.enter_context(self.nc.psum_tensor("%s_%d" % (name, self.uid), list(shape), dtype))

    @staticmethod
    def _key(a):
        return a if isinstance(a, (str, tuple)) else a.name

    def _wait(self, e, s, v):
        if v <= 0:
            return
        if self.known[e].get(s, 0) >= v:
            return
        self.eng[e].wait_ge(self.sem[s], v)
        self.known[e][s] = v
        self.n_ins += 1

    def _deps(self, e, R, W, pe_acc=False):
        deps = {}

        def add(tok, same_ok):
            if tok is None:
                return
            s, v = tok
            if same_ok and s == e:
                return
            if deps.get(s, 0) < v:
                deps[s] = v

        for k in R:
            add(self.lastw.get(k), False)
        for k in W:
            add(self.lastw.get(k), True)
            for s, v in self.readers.get(k, {}).items():
                add((s, v), True)
        for s, v in deps.items():
            self._wait(e, s, v)

    def _commit(self, tok, R, W):
        s, v = tok
        for k in W:
            self.lastw[k] = tok
            self.readers[k] = {}
        for k in R:
            d = self.readers.setdefault(k, {})
            if d.get(s, 0) < v:
                d[s] = v

    def op(self, e, ins_fn, R, W):
        R = [self._key(a) for a in R if a is not None and not isinstance(a, (int, float))]
        W = [self._key(a) for a in W]
        self._deps(e, R, W)
        ins = ins_fn()
        self.cnt[e] += 1
        ins.then_inc(self.sem[e], 1)
        self.n_ins += 1
        self._commit((e, self.cnt[e]), R, W)
        return ins

    def dma(self, out, in_, q="sp", rk=None, wk=None):
        R = [rk if rk is not None else self._key(in_)]
        W = [wk if wk is not None else self._key(out)]
        self._deps(q, R, W)
        s = self.dsem[self.dnext]
        self.dnext = (self.dnext + 1) % self.NDMA
        self._wait(q, s, self.cnt[s])
        self.eng[q].dma_start(out=out, in_=in_).then_inc(self.sem[s], 16)
        self.cnt[s] += 16
        self.n_ins += 1
        self._commit((s, self.cnt[s]), R, W)

    def barrier(self):
        for e in self.eng:
            for s in self.sem:
                if s != e:
                    self._wait(e, s, self.cnt[s])
        self.lastw = {}
        self.readers = {}

    def mm(self, out, lhsT, rhs, start=True, stop=True, extra_r=()):
        return self.op("pe", lambda: self.nc.tensor.matmul(out, lhsT, rhs, start=start, stop=stop),
                       [lhsT, rhs, *extra_r], [out])

    def tr(self, out, in_, ident):
        return self.op("pe", lambda: self.nc.tensor.transpose(out, in_, ident), [in_, ident], [out])

    def act(self, out, in_, func, bias=None, scale=None, accum_out=None, extra_r=()):
        kw = {}
        if bias is not None:
            kw["bias"] = bias
        if scale is not None:
            kw["scale"] = scale
        if accum_out is not None:
            kw["accum_out"] = accum_out
        W = [out] + ([accum_out] if accum_out is not None else [])
        return self.op("act", lambda: self.nc.scalar.activation(out, in_, func, **kw),
                       [in_, bias, scale, *extra_r], W)

    def tt(self, out, in0, in1, op, e="dve"):
        return self.op(e, lambda: self.eng[e].tensor_tensor(out, in0, in1, op), [in0, in1], [out])

    def ts(self, out, in0, s1, s2=None, op0=ALU.mult, op1=None, e="dve", accum_out=None):
        kw = {}
        if op1 is not None:
            kw["op1"] = op1
        if accum_out is not None:
            kw["accum_out"] = accum_out
        W = [out] + ([accum_out] if accum_out is not None else [])
        return self.op(e, lambda: self.eng[e].tensor_scalar(out, in0, s1, s2, op0, **kw), [in0, s1, s2], W)

    def stt(self, out, in0, scalar, in1, op0, op1, e="dve"):
        return self.op(e, lambda: self.eng[e].scalar_tensor_tensor(out, in0, scalar, in1, op0, op1),
                       [in0, scalar, in1], [out])

    def cp(self, out, in_, e="dve"):
        return self.op(e, lambda: self.eng[e].tensor_copy(out, in_), [in_], [out])

    def memset(self, out, val, e="dve"):
        return self.op(e, lambda: self.eng[e].memset(out, val), [], [out])

    def recip(self, out, in_):
        return self.op("dve", lambda: self.nc.vector.reciprocal(out, in_), [in_], [out])

    def red(self, out, in_, op=ALU.add, e="dve"):
        return self.op(e, lambda: self.eng[e].tensor_reduce(out, in_, AX.X, op), [in_], [out])


class GemmRes:
    def __init__(self, kb):
        self.kb = kb
        self.ident_f = kb.sb("identf", [128, 128], F32)
        self.ident = kb.sb("ident", [128, 128], BF16)
        self.xin = [kb.sb("xin%d" % i, [128, 1024], F32) for i in range(3)]
        self.xsq = kb.sb("xsq", [128, 1024], F32)
        self.ssq = [kb.sb("ssq%d" % i, [128, 1], F32) for i in range(3)]
        self.xn = [kb.sb("xn%d" % i, [128, 2048], BF16) for i in range(2)]
        self.gcol = kb.sb("gcol", [128, 8], F32)
        self.epsc = kb.sb("epsc", [128, 1], F32)
        self.pmi = 0

    def next_pm(self):
        p = self.pm[self.pmi % len(self.pm)]
        self.pmi += 1
        return p

    def init(self, ident_dram):
        kb = self.kb
        kb.dma(self.ident_f[:], ident_dram)
        kb.cp(self.ident[:], self.ident_f[:])
        kb.memset(self.epsc[:], 1e-6)


def load_weights(kb, gr, W_dram, Kin, N, gcol_dram=None):
    KC = Kin // 128
    Wb = kb.sb("Wb", [128, KC * N], BF16)
    Wv = Wb[:].rearrange("p (c n) -> p c n", c=KC)
    with kb.scope():
        wst = [kb.sb("wst%d" % i, [128, 2048], F32) for i in range(2)]
        if gcol_dram is not None:
            kb.dma(gr.gcol[:, 0:KC], gcol_dram.rearrange("(c p) -> p c", p=128))
        i = 0
        for kc in range(KC):
            for n0 in range(0, N, 2048):
                n1 = min(N, n0 + 2048)
                st = wst[i % 2]
                kb.dma(st[:, 0:n1 - n0], W_dram[kc * 128:(kc + 1) * 128, n0:n1], q="sp")
                if gcol_dram is not None:
                    if i % 2 == 0:
                        kb.ts(Wv[:, kc, n0:n1], st[:, 0:n1 - n0], gr.gcol[:, kc:kc + 1], None, op0=ALU.mult)
                    else:
                        kb.act(Wv[:, kc, n0:n1], st[:, 0:n1 - n0], AF.Copy, scale=gr.gcol[:, kc:kc + 1])
                else:
                    kb.cp(Wv[:, kc, n0:n1], st[:, 0:n1 - n0], e=("dve", "pool")[i % 2])
                i += 1
        kb.barrier()
    return Wv


def gemm_stage(kb, gr, x_dram, T, Kin, Wv, segs, norm=False, x_bf16=False, n_psT=2, n_pm=4):
    KC = Kin // 128
    nblk = T // 512
    gr.xT = [kb.sb("xT%d" % i, [128, KC * 512], BF16) for i in range(2)]
    gr.psT = [kb.ps("psT%d" % i, [128, 1024], BF16) for i in range(n_psT)]
    gr.pm = [kb.ps("pm%d" % i, [128, 512], F32) for i in range(n_pm)]
    for blk in range(nblk):
        xT = gr.xT[blk % 2]
        xTv = xT[:, 0:KC * 512].rearrange("p (c t) -> p c t", c=KC)
        xins = []
        for sub in range(4):
            tok0 = blk * 512 + sub * 128
            it = blk * 4 + sub
            if x_bf16:
                xt = gr.xn[it % 2]
                kb.dma(xt[:, 0:Kin], x_dram[tok0:tok0 + 128, :])
                xn = xt
            else:
                xt = gr.xin[it % 3]
                kb.dma(xt[:, 0:Kin], x_dram[tok0:tok0 + 128, :])
                xn = gr.xn[it % 2]
                if norm:
                    ssq = gr.ssq[it % 3]
                    kb.act(gr.xsq[:, 0:Kin], xt[:, 0:Kin], AF.Square)
                    kb.red(ssq[:], gr.xsq[:, 0:Kin])
                    kb.act(ssq[:], ssq[:], AF.Sqrt, bias=gr.epsc[:, 0:1], scale=1.0 / Kin)
                    kb.recip(ssq[:], ssq[:])
                    kb.act(xn[:, 0:Kin], xt[:, 0:Kin], AF.Copy, scale=ssq[:, 0:1])
                else:
                    kb.cp(xn[:, 0:Kin], xt[:, 0:Kin], e="pool")
            xins.append(xt)
            for half in range((KC + 7) // 8):
                pst = gr.psT[(it * 2 + half) % n_psT]
                nk = min(8, KC - half * 8)
                for j in range(nk):
                    kc = half * 8 + j
                    kb.tr(pst[:, j * 128:(j + 1) * 128], xn[:, kc * 128:(kc + 1) * 128], gr.ident[:])
                src = pst[:, 0:nk * 128].rearrange("p (c t) -> p c t", c=nk)
                dst = xTv[:, half * 8:half * 8 + nk, sub * 128:(sub + 1) * 128]
                if (it + half) % 2 == 0:
                    kb.cp(dst, src, e="dve")
                else:
                    kb.act(dst, src, AF.Copy)
        for seg in segs:
            if seg["kind"] == "custom":
                seg["fn"](blk, xTv)
                continue
            n0, n1 = seg["n0"], seg["n1"]
            if seg["kind"] == "FM":
                grp = seg.get("group", 1)
                nch = (n1 - n0 + 127) // 128
                for c0 in range(0, nch, grp):
                    pss = []
                    for ci in range(c0, min(nch, c0 + grp)):
                        a = n0 + ci * 128
                        b = min(n1, a + 128)
                        ps = gr.next_pm()
                        for kc in range(KC):
                            kb.mm(ps[0:b - a, :], Wv[:, kc, a:b], xTv[:, kc, :], start=(kc == 0), stop=(kc == KC - 1))
                        pss.append(ps)
                    seg["epi"](blk, blk * 512, c0, pss)
            else:
                for sub in range(4):
                    for a in range(n0, n1, 512):
                        b = min(n1, a + 512)
                        ps = gr.next_pm()
                        for kc in range(KC):
                            kb.mm(ps[:, 0:b - a], xTv[:, kc, sub * 128:(sub + 1) * 128], Wv[:, kc, a:b],
                                  start=(kc == 0), stop=(kc == KC - 1))
                        seg["epi"](blk, blk * 512 + sub * 128, sub, a - n0, b - a, ps, xins[sub])


def attn_stage(kb, spec, S, nseq):
    nsup = S // 512
    pst = [kb.ps("ast%d" % i, [128, 512], F32) for i in range(2)]
    outs = [kb.ps("aout%d" % i, [128, 512], F32) for i in range(4)]
    Ps = [kb.sb("aP%d" % i, [128, 512], BF16) for i in range(3)]
    jobs = [(seq, g) for seq in range(nseq) for g in spec.groups(seq)]
    ti = 0
    nbuf = getattr(spec, "nbuf", 2)
    for n, (seq, g) in enumerate(jobs):
        buf = n % nbuf
        if nbuf == 1:
            spec.load(g, 0)
        else:
            if n == 0:
                spec.load(g, buf)
            if n + 1 < len(jobs):
                spec.load(jobs[n + 1][1], (n + 1) % 2)
        shs = spec.subheads(g)
        items = [(j, i, sh) for j in range(nsup) for i in range(4 * j + 4) for sh in shs]

        def emit_qk(it):
            j, i, sh = it
            ctx = None
            if hasattr(spec, "pre"):
                if sh == shs[0]:
                    state["ctx"] = spec.pre(g, j, i, buf)
                ctx = state["ctx"]
            ps = pst[state["ti"] % 2]
            P = Ps[state["ti"] % 3]
            state["ti"] += 1
            if ctx is not None:
                spec.cur = ctx
            spec.qk(g, sh, j, i, ps, buf)
            return (ps, P, ctx)

        state = dict(ti=ti, ctx=None)
        nxt = emit_qk(items[0])
        for n, (j, i, sh) in enumerate(items):
            ps, P, ctx = nxt
            if n + 1 < len(items):
                nxt = emit_qk(items[n + 1])
            if ctx is not None:
                spec.cur = ctx
            spec.evac(g, sh, j, i, ps, P, buf)
            c0, dvp = spec.ocols(g, sh)
            vb = spec.vblk(g, sh, i, buf)
            for qb in range(4):
                if i <= 4 * j + qb:
                    kb.mm(outs[qb][:, c0:c0 + dvp], P[:, qb * 128:(qb + 1) * 128], vb,
                          start=(i == 0 and sh == shs[0]), stop=(i == 4 * j + qb))
            if i == 4 * j + 3 and sh == shs[-1]:
                spec.post(g, j, outs, buf)
        ti = state["ti"]


class FoxSpec:
    def __init__(self, kb, gr, S, QT, KT, V, CQK, O, cmask_bf):
        self.kb, self.gr, self.S = kb, gr, S
        self.QT, self.KT, self.V, self.CQK, self.O = QT, KT, V, CQK, O
        self.cmask = cmask_bf
        self.Qa = [kb.sb("fQa%d" % i, [70, S], BF16) for i in range(2)]
        self.Ka = [kb.sb("fKa%d" % i, [70, S], BF16) for i in range(2)]
        self.Vt = [kb.sb("fV%d" % i, [128, (S // 128) * 65], BF16) for i in range(2)]
        self.rz = [kb.sb("frz%d" % i, [128, 1], F32) for i in range(4)]
        self.ost = [kb.sb("fost%d" % i, [128, 64], BF16) for i in range(4)]
        self.pi = 0
        for i in range(2):
            kb.memset(self.Qa[i][64:70, :], 1.0)
            kb.memset(self.Ka[i][64:70, :], 1.0, e="pool")
            kb.memset(self.Vt[i][:], 1.0, e="pool")

    def groups(self, seq):
        return [(seq, h) for h in range(16)]

    def subheads(self, g):
        return [0]

    def load(self, g, buf):
        kb, S = self.kb, self.S
        seq, h = g
        t0 = seq * S
        kb.dma(self.Qa[buf][0:64, :], self.QT[64 * h:64 * h + 64, t0:t0 + S])
        kb.dma(self.Qa[buf][64:67, :], self.CQK[h, 3:6, t0:t0 + S])
        kb.dma(self.Ka[buf][0:64, :], self.KT[64 * h:64 * h + 64, t0:t0 + S])
        kb.dma(self.Ka[buf][67:70, :], self.CQK[h, 0:3, t0:t0 + S])
        vt = self.Vt[buf][:].rearrange("p (b d) -> p b d", d=65)
        kb.dma(vt[:, :, 0:64], self.V[t0:t0 + S, 64 * h:64 * h + 64].rearrange("(b p) d -> p b d", p=128))

    def qk(self, g, sh, j, i, ps, buf):
        kb = self.kb
        r = i - 4 * j
        kb.mm(ps[:], self.Ka[buf][:, i * 128:(i + 1) * 128], self.Qa[buf][:, j * 512:(j + 1) * 512],
              start=True, stop=(r < 0))
        if r >= 0:
            kb.mm(ps[:], self.gr.ident[:], self.cmask[:, r * 512:(r + 1) * 512], start=False, stop=True)

    def evac(self, g, sh, j, i, ps, P, buf):
        self.kb.act(P[:], ps[:], AF.Exp)

    def ocols(self, g, sh):
        return 0, 65

    def vblk(self, g, sh, i, buf):
        return self.Vt[buf][:, i * 65:(i + 1) * 65]

    def post(self, g, j, outs, buf):
        kb, S = self.kb, self.S
        seq, h = g
        for qb in range(4):
            rz = self.rz[self.pi % 4]
            st = self.ost[self.pi % 4]
            self.pi += 1
            kb.recip(rz[:], outs[qb][:, 64:65])
            kb.ts(st[:], outs[qb][:, 0:64], rz[:, 0:1], None, op0=ALU.mult)
            tok = seq * S + (4 * j + qb) * 128
            kb.dma(self.O[tok:tok + 128, 64 * h:64 * h + 64], st[:], q="pool")


class Epi:
    def __init__(self, kb):
        self.kb = kb
        self.fm = [kb.sb("efm%d" % i, [128, 512], BF16) for i in range(3)]
        self.tmb = [kb.sb("etmb%d" % i, [128, 512], BF16) for i in range(3)]
        self.tmf = [kb.sb("etmf%d" % i, [128, 512], F32) for i in range(3)]
        self.n = 0

    def fm_store(self, dst, scale=1.0):
        kb = self.kb

        def epi(blk, tok0, c0, pss):
            ps = pss[0]
            st = self.fm[self.n % 3]
            self.n += 1
            rows = min(128, dst.shape[0] - c0 * 128)
            if self.n % 2 == 0:
                kb.act(st[0:rows, :], ps[0:rows, :], AF.Copy, scale=float(scale))
            else:
                kb.ts(st[0:rows, :], ps[0:rows, :], float(scale), None, op0=ALU.mult)
            kb.dma(dst[c0 * 128:c0 * 128 + rows, tok0:tok0 + 512], st[0:rows, :], q="pool")
        return epi

    def tm_store(self, dst, func=AF.Copy, col0=0):
        kb = self.kb

        def epi(blk, tok0, sub, n0c, ncols, ps, xt):
            st = self.tmb[self.n % 3]
            self.n += 1
            if func == AF.Copy and self.n % 2 == 0:
                kb.cp(st[:, 0:ncols], ps[:, 0:ncols])
            else:
                kb.act(st[:, 0:ncols], ps[:, 0:ncols], func)
            kb.dma(dst[tok0:tok0 + 128, col0 + n0c:col0 + n0c + ncols], st[:, 0:ncols], q="pool")
        return epi

    def tm_resid(self, h):
        kb = self.kb

        def epi(blk, tok0, sub, n0c, ncols, ps, xt):
            st = self.tmf[self.n % 3]
            self.n += 1
            key = ("h", tok0, n0c)
            kb.dma(st[:, 0:ncols], h[tok0:tok0 + 128, n0c:n0c + ncols], rk=key)
            kb.tt(st[:, 0:ncols], ps[:, 0:ncols], st[:, 0:ncols], ALU.add)
            kb.dma(h[tok0:tok0 + 128, n0c:n0c + ncols], st[:, 0:ncols], q="pool", wk=key)
        return epi


def split3(kb, src, dst6, tmp_r, tmp_b, rows, n):
    r = tmp_r
    kb.cp(dst6[0:rows, 0, :], src)
    kb.tt(r[0:rows, 0:n], src, dst6[0:rows, 0, :], ALU.subtract)
    kb.cp(dst6[0:rows, 1, :], r[0:rows, 0:n])
    kb.tt(r[0:rows, 0:n], r[0:rows, 0:n], dst6[0:rows, 1, :], ALU.subtract)
    kb.cp(dst6[0:rows, 2, :], r[0:rows, 0:n])
    kb.ts(dst6[0:rows, 3:6, :], dst6[0:rows, 0:3, :], -1.0, None, op0=ALU.mult, e="pool")


class Model:
    _epi_cache = (None, None)

    @property
    def epi(self):
        if self._epi_cache[0] is not self.kb.es:
            self._epi_cache = (self.kb.es, Epi(self.kb))
        return self._epi_cache[1]

    def __init__(self, nc, es, nseq, S):
        self.nc, self.nseq, self.S = nc, nseq, S
        self.T = nseq * S
        self.kb = KB(nc, es)
        self.din = {}
        self.scratch = {}

    def inp(self, name, shape, dtype=F32):
        self.din[name] = self.nc.dram_tensor(name, list(shape), dtype, kind="ExternalInput").ap()
        return self.din[name]

    def dscr(self, name, shape, dtype):
        if name not in self.scratch:
            self.scratch[name] = self.nc.dram_tensor(name, list(shape), dtype, kind="Internal").ap()
        return self.scratch[name]

    def setup(self):
        kb = self.kb
        self.gr = GemmRes(kb)
        self.gr.init(self.din["ident"])
        self.one = kb.sb("onec", [128, 1], F32)
        kb.memset(self.one[:], 1.0)
        self.zeros = kb.sb("zeros", [128, 512], F32)
        kb.memset(self.zeros[:], 0.0)
        self.cmask = kb.sb("cmask", [128, 2048], BF16)
        with kb.scope():
            tmp = kb.sb("cmtmp", [128, 2048], F32)
            kb.dma(tmp[:], self.din["cmask"])
            kb.cp(self.cmask[:], tmp[:])
            kb.barrier()

    def fox_mixer(self, h, w_in, gcol, b_f, w_out):
        kb, gr, T, S, nseq = self.kb, self.gr, self.T, self.S, self.nseq
        QT = self.dscr("QT", [1024, T], BF16)
        KT = self.dscr("KT", [1024, T], BF16)
        V = self.dscr("Vtm", [T, 2048], BF16)
        CQK = self.dscr("CQK", [32, 6, T], BF16)
        O = self.dscr("Otm", [T, 2048], BF16)
        with kb.scope():
            Wv = load_weights(kb, gr, w_in, 1024, 3088, gcol)
            nbf = kb.sb("nbf", [16, 1], F32)
            kb.dma(nbf[:], b_f.rearrange("(p o) -> p o", o=1))
            kb.ts(nbf[:], nbf[:], -1.0, None, op0=ALU.mult)
            carry = kb.sb("carry", [16, 1], F32)
            t1 = kb.sb("ft1", [16, 512], F32)
            cum = kb.sb("fcum", [16, 512], F32)
            rr = kb.sb("frr", [16, 512], F32)
            spl = kb.sb("fspl", [16, 6 * 512], BF16)
            splv = spl[:].rearrange("p (s n) -> p s n", s=6)
            bps = S // 512

            def f_epi(blk, tok0, c0, pss):
                ps = pss[0]
                if blk % bps == 0:
                    kb.memset(carry[:], 0.0)
                kb.act(t1[:], ps[0:16, :], AF.Exp, bias=nbf[:, 0:1], scale=-1.0)
                kb.act(t1[:], t1[:], AF.Ln, bias=self.one[0:16, 0:1])
                kb.op("dve", lambda: self.nc.vector.tensor_tensor_scan(cum[:], t1[:], self.zeros[0:16, :], carry[:, 0:1], ALU.add, ALU.add),
                      [t1, self.zeros, carry], [cum])
                kb.cp(carry[:], cum[:, 511:512])
                split3(kb, cum[:], splv, rr, None, 16, 512)
                kb.dma(CQK[0:16, :, tok0:tok0 + 512], splv, q="pool")

            segs = [dict(kind="FM", n0=0, n1=1024, epi=self.epi.fm_store(QT, 0.125)),
                    dict(kind="FM", n0=1024, n1=2048, epi=self.epi.fm_store(KT, 1.0)),
                    dict(kind="TM", n0=2048, n1=3072, epi=self.epi.tm_store(V)),
                    dict(kind="FM", n0=3072, n1=3088, epi=f_epi)]
            gemm_stage(kb, gr, h, T, 1024, Wv, segs, norm=True)
            kb.barrier()
        with kb.scope():
            spec = FoxSpec(kb, gr, S, QT, KT, V, CQK, O, self.cmask)
            attn_stage(kb, spec, S, nseq)
            kb.barrier()
        with kb.scope():
            Wv = load_weights(kb, gr, w_out, 1024, 1024, None)
            segs = [dict(kind="TM", n0=0, n1=1024, epi=self.epi.tm_resid(h))]
            gemm_stage(kb, gr, O[:, 0:1024], T, 1024, Wv, segs, x_bf16=True)
            kb.barrier()

    def peer_old(self, h, gcol, w_q, keysT, uT, v):
        kb, gr, T = self.kb, self.gr, self.T
        nc = self.nc
        UTb = self.dscr("UTb", [1024, 16384], BF16)
        Vb = self.dscr("Vb", [16384, 1024], BF16)
        Gd = self.dscr("Gd", [T, 16384], BF16)
        with kb.scope():
            kb.dma(gr.gcol[:, 0:8], gcol.rearrange("(c p) -> p c", p=128))
            st = [kb.sb("pst%d" % i, [128, 2048], F32) for i in range(3)]
            sb = [kb.sb("psb%d" % i, [128, 2048], BF16) for i in range(3)]
            n = 0
            for kc in range(8):
                for e0 in range(0, 16384, 2048):
                    s_, b_ = st[n % 3], sb[n % 3]
                    kb.dma(s_[:], uT[kc * 128:(kc + 1) * 128, e0:e0 + 2048])
                    if n % 2 == 0:
                        kb.ts(b_[:], s_[:], gr.gcol[:, kc:kc + 1], None, op0=ALU.mult)
                    else:
                        kb.act(b_[:], s_[:], AF.Copy, scale=gr.gcol[:, kc:kc + 1])
                    kb.dma(UTb[kc * 128:(kc + 1) * 128, e0:e0 + 2048], b_[:], q="pool")
                    n += 1
            for r0 in range(0, 16384, 256):
                s_, b_ = st[n % 3], sb[n % 3]
                kb.dma(s_[:].rearrange("p (c n) -> p c n", c=2), v[r0:r0 + 256, :].rearrange("(c p) n -> p c n", p=128))
                if n % 3 == 0:
                    kb.cp(b_[:], s_[:], e="dve")
                elif n % 3 == 1:
                    kb.act(b_[:], s_[:], AF.Copy)
                else:
                    kb.cp(b_[:], s_[:], e="pool")
                kb.dma(Vb[r0:r0 + 256, :].rearrange("(c p) n -> p c n", p=128), b_[:].rearrange("p (c n) -> p c n", c=2), q="pool")
                n += 1
            kb.barrier()
        with kb.scope():
            Wv = load_weights(kb, gr, w_q, 1024, 2048, gcol)
            kT = kb.sb("keysT", [128, 16 * 128], BF16)
            with kb.scope():
                ktmp = kb.sb("ktmp", [128, 16 * 128], F32)
                kb.dma(ktmp[:].rearrange("p (c k) -> p c k", c=16), keysT.rearrange("c d k -> d c k"))
                kb.cp(kT[:], ktmp[:])
                kb.barrier()
            kTv = kT[:].rearrange("p (c k) -> p c k", c=16)
            qT = [kb.sb("pqT%d" % i, [128, 512], BF16) for i in range(2)]
            psc = [kb.ps("psc%d" % i, [128, 512], F32) for i in range(2)]
            sc_all = kb.sb("sc_all", [128, 4 * 16 * 128], F32)
            scv = sc_all[:].rearrange("p (s c k) -> p s c k", s=4, c=16)
            a16 = kb.sb("a16", [128, 16], F32)
            b16 = kb.sb("b16", [128, 16], F32)
            c16 = kb.sb("c16", [128, 16], F32)
            e16 = kb.sb("e16", [128, 16], F32)
            t128 = kb.sb("t128", [128, 128], F32)
            cand = kb.sb("cand", [128, 256], F32)
            cand2 = kb.sb("cand2", [128, 256], F32)
            tau = kb.sb("tau", [128, 8], F32)
            nb = kb.sb("nb", [128, 8], F32)
            nb2 = kb.sb("nb2", [128, 8], F32)
            zz = kb.sb("zz", [128, 1], F32)
            Sc = [kb.sb("Sc%d" % i, [128, 2048], F32) for i in range(3)]
            Ec = [kb.sb("Ec%d" % i, [128, 2048], BF16) for i in range(3)]
            gcnt = [0]
            Gb = [kb.sb("Gb%d" % i, [128, 2048], BF16) for i in range(2)]
            cn = [0, 0, 0]
            V = nc.vector

            def top16(dst, src, tmp):
                kb.op("dve", lambda: V.max(out=dst[:, 0:8], in_=src), [src], [dst])
                kb.op("dve", lambda: V.match_replace(out=tmp, in_to_replace=dst[:, 0:8], in_values=src, imm_value=-1e30),
                      [dst, src], [tmp])
                kb.op("dve", lambda: V.max(out=dst[:, 8:16], in_=tmp), [tmp], [dst])

            def gates(tokb, sub):
                for hh in range(8):
                    s1 = scv[:, sub, 2 * hh, :]
                    s2 = scv[:, sub, 2 * hh + 1, :]
                    top16(a16, s1, t128[:])
                    top16(b16, s2, t128[:])
                    kb.tt(cand[:].rearrange("p (a b) -> p a b", a=16),
                          a16[:].unsqueeze(2).to_broadcast([128, 16, 16]),
                          b16[:].unsqueeze(1).to_broadcast([128, 16, 16]), ALU.add)
                    top16(c16, cand[:], cand2[:])
                    kb.cp(tau[:, hh:hh + 1], c16[:, 15:16])
                    kb.ts(nb[:, hh:hh + 1], c16[:, 0:1], -1.0, None, op0=ALU.mult)
                    kb.act(e16[:], c16[:], AF.Exp, bias=nb[:, hh:hh + 1])
                    kb.red(zz[:], e16[:])
                    kb.act(zz[:], zz[:], AF.Ln)
                    kb.tt(nb[:, hh:hh + 1], nb[:, hh:hh + 1], zz[:], ALU.subtract)
                items = [(c, hh) for c in range(8) for hh in range(8)]

                def stage1(n):
                    c, hh = items[n]
                    S_, E_ = Sc[n % 3], Ec[n % 3]
                    s1 = scv[:, sub, 2 * hh, 16 * c:16 * c + 16]
                    s2 = scv[:, sub, 2 * hh + 1, :]
                    S3 = S_[:].rearrange("p (a b) -> p a b", a=16)
                    kb.tt(S3, s1.unsqueeze(2).to_broadcast([128, 16, 128]),
                          s2.unsqueeze(1).to_broadcast([128, 16, 128]), ALU.add)
                    kb.act(E_[:], S_[:], AF.Exp, bias=nb[:, hh:hh + 1])

                def stage2(n):
                    c, hh = items[n]
                    S_, E_ = Sc[n % 3], Ec[n % 3]
                    G = Gb[(gcnt[0] + c) % 2]
                    dst = G if hh == 0 else E_
                    kb.stt(dst[:], S_[:], tau[:, hh:hh + 1], E_[:], ALU.is_ge, ALU.mult)
                    if hh > 0:
                        kb.tt(G[:], G[:], E_[:], ALU.add)
                    if hh == 7:
                        kb.dma(Gd[tokb:tokb + 128, c * 2048:(c + 1) * 2048], G[:], q="sp")

                stage1(0)
                for n in range(64):
                    if n + 1 < 64:
                        stage1(n + 1)
                    stage2(n)

            def q_epi(blk, tok0, c0, pss):
                qt = qT[c0 % 2]
                if c0 % 2 == 0:
                    kb.act(qt[:], pss[0][:], AF.Copy)
                else:
                    kb.cp(qt[:], pss[0][:])
                pc = psc[c0 % 2]
                for sub in range(4):
                    kb.mm(pc[:, sub * 128:(sub + 1) * 128], qt[:, sub * 128:(sub + 1) * 128], kTv[:, c0, :])
                src = pc[:].rearrange("p (s k) -> p s k", s=4)
                if c0 % 2 == 0:
                    kb.cp(scv[:, :, c0, :], src)
                else:
                    kb.act(scv[:, :, c0, :], src, AF.Copy)
                if c0 == 15:
                    for sub in range(4):
                        gates(tok0 + sub * 128, sub)

            segs = [dict(kind="FM", n0=0, n1=2048, epi=q_epi)]
            gemm_stage(kb, gr, h, T, 1024, Wv, segs, norm=True)
            kb.barrier()
        if getattr(self, 'skip_dense', False):
            return
        with kb.scope():
            ut = [kb.sb("ut%d" % i, [128, 8 * 512], BF16) for i in range(2)]
            vt = [kb.sb("vt%d" % i, [128, 4 * 1024], BF16) for i in range(2)]
            gt = [kb.sb("gt%d" % i, [128, 512], BF16) for i in range(3)]
            gel = [kb.sb("gel%d" % i, [128, 512], F32) for i in range(2)]
            gh = [kb.sb("gh%d" % i, [128, 512], BF16) for i in range(2)]
            ghT = [kb.sb("ghT%d" % i, [128, 512], BF16) for i in range(2)]
            acc = [kb.sb("acc%d" % i, [128, 1024], F32) for i in range(4)]
            php = [kb.ps("php%d" % i, [128, 512], F32) for i in range(2)]
            ptp = [kb.ps("ptp%d" % i, [128, 1024], BF16) for i in range(1)]
            pop = [kb.ps("pop%d" % i, [128, 512], F32) for i in range(4)]
            cn2 = [0]

            def dense(blk, xTv):
                tokb = blk * 512
                for ec in range(32):
                    u_ = ut[ec % 2]
                    v_ = vt[ec % 2]
                    uv = u_[:].rearrange("p (c e) -> p c e", c=8)
                    vv = v_[:].rearrange("p (c n) -> p c n", c=4)
                    kb.dma(uv, UTb[:, ec * 512:(ec + 1) * 512].rearrange("(c p) e -> p c e", p=128))
                    kb.dma(vv, Vb[ec * 512:(ec + 1) * 512, :].rearrange("(c p) n -> p c n", p=128))
                    for sub in range(4):
                        k = cn2[0]
                        cn2[0] += 1
                        g_ = gt[k % 3]
                        kb.dma(g_[:], Gd[tokb + sub * 128:tokb + sub * 128 + 128, ec * 512:(ec + 1) * 512])
                        ph = php[k % 2]
                        for kc in range(8):
                            kb.mm(ph[:], xTv[:, kc, sub * 128:(sub + 1) * 128], uv[:, kc, :], start=(kc == 0), stop=(kc == 7))
                        ge = gel[k % 2]
                        kb.act(ge[:], ph[:], AF.Gelu)
                        gh_ = gh[k % 2]
                        kb.tt(gh_[:], ge[:], g_[:], ALU.mult, e="pool")
                        pt = ptp[0]
                        for c4 in range(4):
                            kb.tr(pt[:, c4 * 128:(c4 + 1) * 128], gh_[:, c4 * 128:(c4 + 1) * 128], gr.ident[:])
                        gT = ghT[k % 2]
                        if k % 2 == 0:
                            kb.cp(gT[:], pt[:, 0:512])
                        else:
                            kb.act(gT[:], pt[:, 0:512], AF.Copy)
                        for nh in range(2):
                            po = pop[(k % 2) * 2 + nh]
                            for c4 in range(4):
                                kb.mm(po[:], gT[:, c4 * 128:(c4 + 1) * 128], vv[:, c4, nh * 512:(nh + 1) * 512],
                                      start=(c4 == 0), stop=(c4 == 3))
                            a_ = acc[sub][:, nh * 512:(nh + 1) * 512]
                            if ec == 0:
                                kb.cp(a_, po[:])
                            else:
                                kb.tt(a_, po[:], a_, ALU.add)
                for sub in range(4):
                    t0 = tokb + sub * 128
                    hs = gr.xin[sub % 3]
                    key = ("h", t0)
                    kb.dma(hs[:], h[t0:t0 + 128, :], rk=key)
                    kb.tt(acc[sub][:], acc[sub][:], hs[:], ALU.add, e="pool")
                    kb.dma(h[t0:t0 + 128, :], acc[sub][:], q="pool", wk=key)

            segs = [dict(kind="custom", fn=dense)]
            gemm_stage(kb, gr, h, T, 1024, None, segs, norm=True, n_psT=1, n_pm=0)
            kb.barrier()

    def ple(self, h, gcol, w_gate, p_i, w_proj):
        kb, gr, T = self.kb, self.gr, self.T
        PP = self.dscr("PP", [T, 1024], F32)
        with kb.scope():
            Wv = load_weights(kb, gr, w_proj, 256, 1024, None)
            stf = [kb.sb("ppst%d" % i, [128, 512], F32) for i in range(3)]
            cn = [0]

            def pp_epi(blk, tok0, sub, n0c, ncols, ps, xt):
                st = stf[cn[0] % 3]
                cn[0] += 1
                if cn[0] % 2 == 0:
                    kb.cp(st[:, 0:ncols], ps[:, 0:ncols])
                else:
                    kb.act(st[:, 0:ncols], ps[:, 0:ncols], AF.Copy)
                kb.dma(PP[tok0:tok0 + 128, n0c:n0c + ncols], st[:, 0:ncols], q="pool")
            gemm_stage(kb, gr, p_i, T, 256, Wv, [dict(kind="TM", n0=0, n1=1024, epi=pp_epi)])
            kb.barrier()
        with kb.scope():
            Wv = load_weights(kb, gr, w_gate, 1024, 1024, gcol)
            sg = [kb.sb("plg%d" % i, [128, 512], F32) for i in range(3)]
            sp_ = [kb.sb("plp%d" % i, [128, 512], F32) for i in range(3)]
            sh_ = [kb.sb("plh%d" % i, [128, 512], F32) for i in range(3)]
            cn = [0]

            def g_epi(blk, tok0, sub, n0c, ncols, ps, xt):
                k = cn[0] % 3
                cn[0] += 1
                key = ("h", tok0, n0c)
                kb.dma(sp_[k][:, 0:ncols], PP[tok0:tok0 + 128, n0c:n0c + ncols])
                kb.dma(sh_[k][:, 0:ncols], h[tok0:tok0 + 128, n0c:n0c + ncols], rk=key)
                kb.act(sg[k][:, 0:ncols], ps[:, 0:ncols], AF.Sigmoid)
                kb.tt(sg[k][:, 0:ncols], sg[k][:, 0:ncols], sp_[k][:, 0:ncols], ALU.mult, e="pool")
                kb.tt(sh_[k][:, 0:ncols], sh_[k][:, 0:ncols], sg[k][:, 0:ncols], ALU.add)
                kb.dma(h[tok0:tok0 + 128, n0c:n0c + ncols], sh_[k][:, 0:ncols], q="pool", wk=key)
            gemm_stage(kb, gr, h, T, 1024, Wv, [dict(kind="TM", n0=0, n1=1024, epi=g_epi)], norm=True)
            kb.barrier()

    def final(self, h, g, out):
        kb, gr, T = self.kb, self.gr, self.T
        with kb.scope():
            gb = kb.sb("fgb", [128, 1024], F32)
            kb.dma(gb[:], g.partition_broadcast(128))
            ot = [kb.sb("fot%d" % i, [128, 1024], F32) for i in range(2)]
            for it in range(T // 128):
                xt = gr.xin[it % 3]
                ssq = gr.ssq[it % 3]
                o_ = ot[it % 2]
                kb.dma(xt[:], h[it * 128:(it + 1) * 128, :])
                kb.act(gr.xsq[:], xt[:], AF.Square)
                kb.red(ssq[:], gr.xsq[:])
                kb.act(ssq[:], ssq[:], AF.Sqrt, bias=gr.epsc[:, 0:1], scale=1.0 / 1024)
                kb.recip(ssq[:], ssq[:])
                kb.stt(o_[:], xt[:], ssq[:, 0:1], gb[:], ALU.mult, ALU.mult)
                kb.dma(out[it * 128:(it + 1) * 128, :], o_[:], q="pool")
            kb.barrier()

    def diff_mixer(self, h, w_in, gcol, lam_vecs, lam_init, norm_g, w_out, BT, bfar, dmask):
        kb, gr, T, S, nseq = self.kb, self.gr, self.T, self.S, self.nseq
        QT = self.dscr("QT", [1024, T], BF16)
        KT = self.dscr("KT", [1024, T], BF16)
        V = self.dscr("Vtm", [T, 2048], BF16)
        O = self.dscr("Otm", [T, 2048], BF16)
        with kb.scope():
            Wv = load_weights(kb, gr, w_in, 1024, 3072, gcol)
            segs = [dict(kind="FM", n0=0, n1=1024, epi=self.epi.fm_store(QT, 0.125)),
                    dict(kind="FM", n0=1024, n1=2048, epi=self.epi.fm_store(KT, 1.0)),
                    dict(kind="TM", n0=2048, n1=3072, epi=self.epi.tm_store(V))]
            gemm_stage(kb, gr, h, T, 1024, Wv, segs, norm=True)
            kb.barrier()
        with kb.scope():
            spec = DiffSpec(self, S, QT, KT, V, O, lam_vecs, lam_init, norm_g, BT, bfar, dmask)
            attn_stage(kb, spec, S, nseq)
            kb.barrier()
        with kb.scope():
            Wv = load_weights(kb, gr, w_out, 1024, 1024, None)
            segs = [dict(kind="TM", n0=0, n1=1024, epi=self.epi.tm_resid(h))]
            gemm_stage(kb, gr, O[:, 0:1024], T, 1024, Wv, segs, x_bf16=True)
            kb.barrier()


class DiffSpec:
    def __init__(self, m, S, QT, KT, V, O, lam_vecs, lam_init, norm_g, BT, bfar, dmask):
        kb = m.kb
        self.kb, self.gr, self.S = kb, m.gr, S
        self.QT, self.KT, self.V, self.O = QT, KT, V, O
        self.Qa = [[kb.sb("dQ%d_%d" % (i, mm), [64, S], BF16) for mm in range(2)] for i in range(2)]
        self.Ka = [[kb.sb("dK%d_%d" % (i, mm), [64, S], BF16) for mm in range(2)] for i in range(2)]
        self.Vt = [kb.sb("dV%d" % i, [128, (S // 128) * 129], BF16) for i in range(2)]
        for i in range(2):
            kb.memset(self.Vt[i][:], 1.0, e="pool")
        lv = kb.sb("dlv", [128, 256], F32)
        kb.dma(lv[:], lam_vecs.rearrange("a d -> (a d)").partition_broadcast(128))
        pr = kb.sb("dpr", [128, 128], F32)
        lvv = lv[:].rearrange("p (a d) -> p a d", a=4)
        prv = pr[:].rearrange("p (a d) -> p a d", a=2)
        kb.tt(prv[:, 0, :], lvv[:, 0, :], lvv[:, 1, :], ALU.mult)
        kb.tt(prv[:, 1, :], lvv[:, 2, :], lvv[:, 3, :], ALU.mult)
        l2 = kb.sb("dl2", [128, 2], F32)
        kb.red(l2[:, 0:1], prv[:, 0, :])
        kb.red(l2[:, 1:2], prv[:, 1, :])
        kb.act(l2[:], l2[:], AF.Exp)
        self.nlam = kb.sb("dnlam", [128, 1], F32)
        kb.tt(self.nlam[:], l2[:, 1:2], l2[:, 0:1], ALU.subtract)
        kb.ts(self.nlam[:], self.nlam[:], -float(lam_init), None, op0=ALU.add)
        self.gsc = kb.sb("dgsc", [128, 128], F32)
        kb.dma(self.gsc[:], norm_g.partition_broadcast(128))
        kb.ts(self.gsc[:], self.gsc[:], 1.0 - float(lam_init), None, op0=ALU.mult)
        self.Bhl = kb.sb("dBhl", [128, 8 * 2 * 2 * 128], BF16)
        self.Bv = self.Bhl[:].rearrange("p (h d s q) -> p h d s q", h=8, d=2, s=2)
        self.bfar = kb.sb("dbfar", [128, 8], F32)
        kb.dma(self.bfar[:], bfar.partition_broadcast(128))
        self.ones2 = kb.sb("dones2", [2, 128], BF16)
        kb.memset(self.ones2[:], 1.0)
        self.cfar = kb.sb("dcfar", [2, 8 * 128], BF16)
        with kb.scope():
            bt = kb.sb("dbt", [128, 8 * 2 * 128], F32)
            btv = bt[:].rearrange("p (h d q) -> p h d q", h=8, d=2)
            kb.dma(btv, BT.rearrange("h d k q -> k h d q"))
            dm = kb.sb("ddm", [128, 128], F32)
            kb.dma(dm[:], dmask)
            rr = kb.sb("drr", [128, 128], F32)
            for hh in range(8):
                kb.tt(btv[:, hh, 0, :], btv[:, hh, 0, :], dm[:], ALU.add)
                for d in range(2):
                    kb.cp(self.Bv[:, hh, d, 0, :], btv[:, hh, d, :])
                    kb.tt(rr[:], btv[:, hh, d, :], self.Bv[:, hh, d, 0, :], ALU.subtract)
                    kb.cp(self.Bv[:, hh, d, 1, :], rr[:])
            cf = kb.sb("dcf", [2, 8], F32)
            kb.dma(cf[0:1, :], bfar.rearrange("(o h) -> o h", o=1))
            kb.dma(cf[1:2, :], bfar.rearrange("(o h) -> o h", o=1))
            cfb = kb.sb("dcfb", [2, 8], BF16)
            kb.cp(cfb[:], cf[:])
            cr = kb.sb("dcr", [2, 8], F32)
            kb.tt(cr[:], cf[:], cfb[:], ALU.subtract)
            cfl = kb.sb("dcfl", [2, 8], BF16)
            kb.cp(cfl[:], cr[:])
            kb.dma(cfb[1:2, :], cfl[1:2, :])
            cfv = self.cfar[:].rearrange("p (h q) -> p h q", h=8)
            kb.cp(cfv, cfb[:].unsqueeze(2).to_broadcast([2, 8, 128]))
            kb.barrier()
        self.rz = [kb.sb("drz%d" % i, [128, 2], F32) for i in range(3)]
        self.o = [kb.sb("do%d" % i, [128, 128], F32) for i in range(3)]
        self.sq = kb.sb("dsq", [128, 128], F32)
        self.ss = [kb.sb("dss%d" % i, [128, 1], F32) for i in range(3)]
        self.ost = [kb.sb("dost%d" % i, [128, 128], BF16) for i in range(3)]
        self.pi = 0

    def groups(self, seq):
        return [(seq, h) for h in range(8)]

    def subheads(self, g):
        return [0, 1]

    def load(self, g, buf):
        kb, S = self.kb, self.S
        seq, h = g
        t0 = seq * S
        for mm in range(2):
            r0 = 128 * h + 64 * mm
            kb.dma(self.Qa[buf][mm][:], self.QT[r0:r0 + 64, t0:t0 + S])
            kb.dma(self.Ka[buf][mm][:], self.KT[r0:r0 + 64, t0:t0 + S])
        vt = self.Vt[buf][:].rearrange("p (b d) -> p b d", d=129)
        kb.dma(vt[:, :, 0:128], self.V[t0:t0 + S, 128 * h:128 * h + 128].rearrange("(b p) d -> p b d", p=128))

    def qk(self, g, sh, j, i, ps, buf):
        kb = self.kb
        seq, h = g
        r = i - 4 * j
        kb.mm(ps[:], self.Ka[buf][sh][:, i * 128:(i + 1) * 128], self.Qa[buf][sh][:, j * 512:(j + 1) * 512],
              start=True, stop=(r < -1))
        if r >= -1:
            for qb in range(4):
                d = r - qb
                o_ = ps[:, qb * 128:(qb + 1) * 128]
                if d <= -2:
                    kb.mm(o_, self.ones2[:], self.cfar[:, h * 128:(h + 1) * 128], start=False, stop=False)
                elif d <= 0:
                    kb.mm(o_, self.gr.ident[:], self.Bv[:, h, -d, 0, :], start=False, stop=False)
                    kb.mm(o_, self.gr.ident[:], self.Bv[:, h, -d, 1, :], start=False, stop=False)

    def evac(self, g, sh, j, i, ps, P, buf):
        seq, h = g
        if i - 4 * j < -1:
            self.kb.act(P[:], ps[:], AF.Exp, bias=self.bfar[:, h:h + 1])
        else:
            self.kb.act(P[:], ps[:], AF.Exp)

    def ocols(self, g, sh):
        return 256 * sh, 129

    def vblk(self, g, sh, i, buf):
        return self.Vt[buf][:, i * 129:(i + 1) * 129]

    def post(self, g, j, outs, buf):
        kb, S = self.kb, self.S
        seq, h = g
        for qb in range(4):
            k = self.pi % 3
            self.pi += 1
            rz, o, ss, st = self.rz[k], self.o[k], self.ss[k], self.ost[k]
            ob = outs[qb]
            kb.recip(rz[:, 0:1], ob[:, 128:129])
            kb.recip(rz[:, 1:2], ob[:, 384:385])
            kb.tt(rz[:, 1:2], rz[:, 1:2], self.nlam[:], ALU.mult)
            kb.ts(o[:], ob[:, 0:128], rz[:, 0:1], None, op0=ALU.mult)
            kb.stt(o[:], ob[:, 256:384], rz[:, 1:2], o[:], ALU.mult, ALU.add)
            kb.tt(self.sq[:], o[:], o[:], ALU.mult, e="pool")
            kb.red(ss[:], self.sq[:])
            kb.act(ss[:], ss[:], AF.Sqrt, bias=self.gr.epsc[:, 0:1], scale=1.0 / 128)
            kb.recip(ss[:], ss[:])
            kb.stt(st[:], o[:], ss[:, 0:1], self.gsc[:], ALU.mult, ALU.mult)
            tok = seq * S + (4 * j + qb) * 128
            kb.dma(self.O[tok:tok + 128, 128 * h:128 * h + 128], st[:], q="pool")


def _t5_bucket_np(rel):
    nb = 16
    max_exact = 8
    ret = (rel > 0).astype(np.int64) * nb
    n = np.abs(rel)
    nf = np.maximum(n, 1).astype(np.float32)
    large = max_exact + (np.log(nf / max_exact) / math.log(128 / max_exact) * (nb - max_exact)).astype(np.int32)
    large = np.minimum(large, nb - 1)
    return ret + np.where(n < max_exact, n, large)


def diff_tables(rel_bias):
    kl = np.arange(128)[:, None]
    ql = np.arange(128)[None, :]
    idx0 = _t5_bucket_np(kl - ql)
    idx1 = _t5_bucket_np(kl - ql - 128)
    rb = np.asarray(rel_bias, dtype=np.float32)
    BT = np.stack([rb[idx0], rb[idx1]], axis=0)
    BT = np.ascontiguousarray(BT.transpose(3, 0, 1, 2))
    bfar = np.ascontiguousarray(rb[15, :])
    dmask = np.where((kl >= 64) & (ql < 64), NEG, 0.0).astype(np.float32)
    return BT, bfar, dmask


def cmask_np():
    k = np.arange(128)[:, None]
    q = np.arange(512)[None, :]
    return np.concatenate([np.where(128 * r + k <= q, 0.0, NEG) for r in range(4)], axis=1).astype(np.float32)


RET_LG = [math.log1p(-2.0 ** (-5.0 - h)) for h in range(4)]


def ret_tables(S):
    d = np.arange(0, 256, 2, dtype=np.float32) / np.float32(256.0)
    inv = (1.0 / (np.float32(10000.0) ** d)).astype(np.float32)
    ang = np.arange(S, dtype=np.float32)[None, :] * inv[:, None]
    cosT = np.cos(ang).astype(np.float32)
    sinT = np.sin(ang).astype(np.float32)
    t = np.arange(512)
    dq = np.stack([np.exp(RET_LG[h] * t) for h in range(4)]).astype(np.float32)
    dk = np.stack([np.exp(-RET_LG[h] * (t % 128)) / 16.0 for h in range(4)]).astype(np.float32)
    kl = np.arange(128)[:, None]
    q = np.arange(512)[None, :]
    mret = np.zeros((4, 4, 128, 512), np.float32)
    for h in range(4):
        for r in range(4):
            k = 128 * r + kl
            ck, cq = k // 64, q // 64
            f = np.where(ck < cq, 1.0, np.where(ck > cq, 0.0, np.where(k <= q, 1.0, np.exp(2.0 * RET_LG[h] * (k - q)))))
            mret[h, r] = f * np.exp(-RET_LG[h] * 128.0 * r)
    return cosT, sinT, dq, dk, mret


class RetSpec:
    nbuf = 1

    def __init__(self, m, S, QT, KT, V, GS, O, norm_g, mret):
        kb = m.kb
        self.kb, self.gr, self.S = kb, m.gr, S
        self.QT, self.KT, self.V, self.GS, self.O = QT, KT, V, GS, O
        self.Q = [kb.sb("rQ%d" % c, [128, S], BF16) for c in range(2)]
        self.K = [kb.sb("rK%d" % c, [128, S], BF16) for c in range(2)]
        self.Vt = kb.sb("rV", [128, (S // 128) * 512], BF16)
        self.M = kb.sb("rM", [128, 16 * 512], F32)
        kb.dma(self.M[:].rearrange("p (a q) -> p a q", a=16), mret.rearrange("h r k q -> k (h r) q"))
        self.gb = kb.sb("rgb", [128, 2048], F32)
        kb.dma(self.gb[:], norm_g.partition_broadcast(128))
        self.sq = kb.sb("rsq", [128, 512], F32)
        self.ss = [kb.sb("rss%d" % i, [128, 1], F32) for i in range(2)]
        self.on = [kb.sb("ron%d" % i, [128, 512], F32) for i in range(2)]
        self.gs = [kb.sb("rgs%d" % i, [128, 512], BF16) for i in range(2)]
        self.ost = [kb.sb("rost%d" % i, [128, 512], BF16) for i in range(2)]
        self.pi = 0

    def groups(self, seq):
        return [(seq, h) for h in range(4)]

    def subheads(self, g):
        return [0]

    def load(self, g, buf):
        kb, S = self.kb, self.S
        seq, h = g
        t0 = seq * S
        for c in range(2):
            r0 = 256 * h + 128 * c
            kb.dma(self.Q[c][:], self.QT[r0:r0 + 128, t0:t0 + S])
            kb.dma(self.K[c][:], self.KT[r0:r0 + 128, t0:t0 + S])
        kb.dma(self.Vt[:].rearrange("p (b d) -> p b d", d=512),
               self.V[t0:t0 + S, 512 * h:512 * h + 512].rearrange("(b p) d -> p b d", p=128))

    def qk(self, g, sh, j, i, ps, buf):
        kb = self.kb
        for c in range(2):
            kb.mm(ps[:], self.K[c][:, i * 128:(i + 1) * 128], self.Q[c][:, j * 512:(j + 1) * 512],
                  start=(c == 0), stop=(c == 1))

    def evac(self, g, sh, j, i, ps, P, buf):
        seq, h = g
        r = i - 4 * j
        if r < 0:
            self.kb.act(P[:], ps[:], AF.Copy, scale=float(math.exp(RET_LG[h] * (512 * j - 128 * i))))
        else:
            a = h * 4 + r
            self.kb.tt(P[:], ps[:], self.M[:, a * 512:(a + 1) * 512], ALU.mult)

    def ocols(self, g, sh):
        return 0, 512

    def vblk(self, g, sh, i, buf):
        return self.Vt[:, i * 512:(i + 1) * 512]

    def post(self, g, j, outs, buf):
        kb, S = self.kb, self.S
        seq, h = g
        for qb in range(4):
            k = self.pi % 2
            self.pi += 1
            ob = outs[qb]
            ss, on, gs, st = self.ss[k], self.on[k], self.gs[k], self.ost[k]
            tok = seq * S + (4 * j + qb) * 128
            kb.dma(gs[:], self.GS[tok:tok + 128, 512 * h:512 * h + 512])
            kb.act(self.sq[:], ob[:], AF.Square)
            kb.red(ss[:], self.sq[:])
            kb.act(ss[:], ss[:], AF.Sqrt, bias=self.gr.epsc[:, 0:1], scale=1.0 / 512)
            kb.recip(ss[:], ss[:])
            kb.stt(on[:], ob[:], ss[:, 0:1], self.gb[:, 512 * h:512 * h + 512], ALU.mult, ALU.mult)
            kb.tt(st[:], on[:], gs[:], ALU.mult, e="pool")
            kb.dma(self.O[tok:tok + 128, 512 * h:512 * h + 512], st[:], q="pool")


def ret_mixer(self, h, w_in, gcol, norm_g, w_out, cosT, sinT, dq, dk, mret):
    kb, gr, T, S, nseq = self.kb, self.gr, self.T, self.S, self.nseq
    QT = self.dscr("QT", [1024, T], BF16)
    KT = self.dscr("KT", [1024, T], BF16)
    V = self.dscr("Vtm", [T, 2048], BF16)
    GS = self.dscr("GStm", [T, 2048], BF16)
    O = self.dscr("Otm", [T, 2048], BF16)
    with kb.scope():
        Wv = load_weights(kb, gr, w_in, 1024, 6144, gcol)
        cs = [kb.sb("rcs%d" % i, [128, 1024], F32) for i in range(2)]
        dtab = kb.sb("rdtab", [128, 2 * 4 * 512], F32)
        dtv = dtab[:].rearrange("p (a h t) -> p a h t", a=2, h=4)
        kb.dma(dtv[:, 0], dq.partition_broadcast(128))
        kb.dma(dtv[:, 1], dk.partition_broadcast(128))
        tmp = [kb.sb("rtmp%d" % i, [128, 512], F32) for i in range(4)]
        yst = [kb.sb("ryst%d" % i, [128, 512], BF16) for i in range(4)]
        state = dict(blk=-1, n=0)

        def rot_epi(which, dst):
            def epi(blk, tok0, c0, pss):
                if state["blk"] != blk:
                    state["blk"] = blk
                    p0 = tok0 % S
                    c_ = cs[blk % 2]
                    kb.dma(c_[:, 0:512], cosT[:, p0:p0 + 512])
                    kb.dma(c_[:, 512:1024], sinT[:, p0:p0 + 512])
                c_ = cs[blk % 2]
                cosb, sinb = c_[:, 0:512], c_[:, 512:1024]
                hh = c0 // 2
                d_ = dtv[:, which, hh, :]
                x1, x2 = pss[0], pss[1]
                n = state["n"]
                state["n"] += 1
                ta, tb = tmp[(n % 2) * 2], tmp[(n % 2) * 2 + 1]
                y1, y2 = yst[(n % 2) * 2], yst[(n % 2) * 2 + 1]
                kb.tt(ta[:], x1[:], cosb, ALU.mult)
                kb.tt(tb[:], x2[:], sinb, ALU.mult)
                kb.tt(ta[:], ta[:], tb[:], ALU.subtract, e="pool")
                kb.tt(y1[:], ta[:], d_, ALU.mult, e="pool")
                kb.dma(dst[256 * hh:256 * hh + 128, tok0:tok0 + 512], y1[:], q="sp")
                kb.tt(ta[:], x1[:], sinb, ALU.mult)
                kb.tt(tb[:], x2[:], cosb, ALU.mult)
                kb.tt(ta[:], ta[:], tb[:], ALU.add, e="pool")
                kb.tt(y2[:], ta[:], d_, ALU.mult, e="pool")
                kb.dma(dst[256 * hh + 128:256 * hh + 256, tok0:tok0 + 512], y2[:], q="sp")
            return epi

        segs = [dict(kind="FM", n0=0, n1=1024, group=2, epi=rot_epi(0, QT)),
                dict(kind="FM", n0=1024, n1=2048, group=2, epi=rot_epi(1, KT)),
                dict(kind="TM", n0=2048, n1=4096, epi=self.epi.tm_store(V)),
                dict(kind="TM", n0=4096, n1=6144, epi=self.epi.tm_store(GS, func=AF.Silu))]
        gemm_stage(kb, gr, h, T, 1024, Wv, segs, norm=True)
        kb.barrier()
    with kb.scope():
        spec = RetSpec(self, S, QT, KT, V, GS, O, norm_g, mret)
        attn_stage(kb, spec, S, nseq)
        kb.barrier()
    with kb.scope():
        Wv = load_weights(kb, gr, w_out, 2048, 1024, None)
        segs = [dict(kind="TM", n0=0, n1=1024, epi=self.epi.tm_resid(h))]
        gemm_stage(kb, gr, O, T, 2048, Wv, segs, x_bf16=True)
        kb.barrier()


Model.ret_mixer = ret_mixer


def ssd_tables():
    sa = np.zeros((48, 1024), np.float32)
    for hh in range(8):
        sa[6 * hh:6 * hh + 3, hh * 128:(hh + 1) * 128] = 1.0
    return sa


class SsdSpec:
    nbuf = 1

    def __init__(self, m, S, AQK, BTd, CTd, XSd, DTd, ZS, O, dsk_rep, norm_g, sa_init):
        kb = m.kb
        self.kb, self.gr, self.S, self.m = kb, m.gr, S, m
        self.AQK, self.BTd, self.CTd, self.XSd, self.DTd, self.ZS, self.O = AQK, BTd, CTd, XSd, DTd, ZS, O
        nb = S // 128
        self.nb = nb
        self.TA = kb.sb("sTA", [48, S], BF16)
        kb.memset(self.TA[:], 1.0)
        self.SA = [kb.sb("sSA%d" % i, [48, 1024], BF16) for i in range(3)]
        with kb.scope():
            t = kb.sb("ssat", [48, 1024], F32)
            kb.dma(t[:], sa_init)
            for i in range(3):
                kb.cp(self.SA[i][:], t[:])
            kb.barrier()
        self.Bt = kb.sb("sBt", [128, S], BF16)
        self.Ct = kb.sb("sCt", [128, S], BF16)
        self.XS = kb.sb("sXS", [128, nb * 512], BF16)
        self.XP = kb.sb("sXP", [128, nb * 512], BF16)
        self.DTt = kb.sb("sDT", [128, nb * 8], F32)
        self.E = [kb.sb("sE%d" % i, [128, 512], F32) for i in range(2)]
        self.cbp = [kb.ps("scb%d" % i, [128, 512], F32) for i in range(2)]
        self.dsk = kb.sb("sdsk", [128, 2048], F32)
        kb.dma(self.dsk[:], dsk_rep.partition_broadcast(128))
        self.gn = kb.sb("sgn", [128, 2048], F32)
        kb.dma(self.gn[:], norm_g.partition_broadcast(128))
        self.y1 = [kb.sb("sy1%d" % i, [128, 512], F32) for i in range(2)]
        self.y3 = [kb.sb("sy3%d" % i, [128, 512], F32) for i in range(2)]
        self.zs = [kb.sb("szs%d" % i, [128, 512], BF16) for i in range(2)]
        self.ost = [kb.sb("sost%d" % i, [128, 512], BF16) for i in range(2)]
        self.sq = kb.sb("ssq", [128, 512], F32)
        self.ss = [kb.sb("sss%d" % i, [128, 1], F32) for i in range(2)]
        self.pi = 0
        self.ni = 0
        self.ne = 0

    def groups(self, seq):
        return [(seq, g) for g in range(4)]

    def subheads(self, g):
        return list(range(8))

    def load(self, g, buf):
        kb, S, nb = self.kb, self.S, self.nb
        seq, gg = g
        t0 = seq * S
        for hh in range(8):
            kb.dma(self.TA[6 * hh:6 * hh + 3, :], self.AQK[gg * 8 + hh, 0:3, t0:t0 + S])
        kb.dma(self.Bt[:], self.BTd[gg * 128:(gg + 1) * 128, t0:t0 + S])
        kb.dma(self.Ct[:], self.CTd[gg * 128:(gg + 1) * 128, t0:t0 + S])
        kb.dma(self.XS[:].rearrange("p (b d) -> p b d", d=512),
               self.XSd[t0:t0 + S, gg * 512:(gg + 1) * 512].rearrange("(b p) d -> p b d", p=128))
        kb.dma(self.DTt[:].rearrange("p (b h) -> p b h", h=8),
               self.DTd[t0:t0 + S, gg * 8:(gg + 1) * 8].rearrange("(b p) h -> p b h", p=128))
        kb.tt(self.XP[:].rearrange("p (a d) -> p a d", d=64), self.XS[:].rearrange("p (a d) -> p a d", d=64),
              self.DTt[:].unsqueeze(2).to_broadcast([128, nb * 8, 64]), ALU.mult)

    def pre(self, g, j, i, buf):
        kb, S = self.kb, self.S
        seq, gg = g
        t0 = seq * S
        sa = self.SA[self.ni % 3]
        cb = self.cbp[self.ni % 2]
        self.ni += 1
        for hh in range(8):
            kb.dma(sa[6 * hh + 3:6 * hh + 6, hh * 128:(hh + 1) * 128],
                   self.AQK[gg * 8 + hh, 3:6, t0 + i * 128:t0 + (i + 1) * 128])
        kb.mm(cb[:], self.Bt[:, i * 128:(i + 1) * 128], self.Ct[:, j * 512:(j + 1) * 512])
        return (sa, cb)

    def qk(self, g, sh, j, i, ps, buf):
        kb = self.kb
        sa, cb = self.cur
        r = i - 4 * j
        kb.mm(ps[:], sa[:, sh * 128:(sh + 1) * 128], self.TA[:, j * 512:(j + 1) * 512], start=True, stop=(r < 0))
        if r >= 0:
            kb.mm(ps[:], self.gr.ident[:], self.m.cmask[:, r * 512:(r + 1) * 512], start=False, stop=True)

    def evac(self, g, sh, j, i, ps, P, buf):
        kb = self.kb
        sa, cb = self.cur
        E = self.E[self.ne % 2]
        self.ne += 1
        kb.act(E[:], ps[:], AF.Exp)
        kb.tt(P[:], E[:], cb[:], ALU.mult)

    def ocols(self, g, sh):
        return sh * 64, 64

    def vblk(self, g, sh, i, buf):
        return self.XP[:, i * 512 + sh * 64:i * 512 + sh * 64 + 64]

    def post(self, g, j, outs, buf):
        kb, S = self.kb, self.S
        seq, gg = g
        for qb in range(4):
            k = self.pi % 2
            self.pi += 1
            b = 4 * j + qb
            tok = seq * S + b * 128
            y1, y3, zs, st, ss = self.y1[k], self.y3[k], self.zs[k], self.ost[k], self.ss[k]
            kb.dma(zs[:], self.ZS[tok:tok + 128, gg * 512:(gg + 1) * 512])
            kb.tt(y1[:], self.XS[:, b * 512:(b + 1) * 512], self.dsk[:, gg * 512:(gg + 1) * 512], ALU.mult, e="pool")
            kb.tt(y1[:], outs[qb][:], y1[:], ALU.add)
            kb.tt(y3[:], y1[:], zs[:], ALU.mult, e="pool")
            kb.act(self.sq[:], y3[:], AF.Square)
            kb.red(ss[:], self.sq[:])
            kb.act(ss[:], ss[:], AF.Sqrt, bias=self.gr.epsc[:, 0:1], scale=1.0 / 512)
            kb.recip(ss[:], ss[:])
            kb.stt(st[:], y3[:], ss[:, 0:1], self.gn[:, gg * 512:(gg + 1) * 512], ALU.mult, ALU.mult)
            kb.dma(self.O[tok:tok + 128, gg * 512:(gg + 1) * 512], st[:], q="pool")


def ssd_mixer(self, h, w_in, gcol, conv_wT, conv_b, dt_bias, a_log, dsk_rep, norm_g, w_out, sa_init):
    kb, gr, T, S, nseq = self.kb, self.gr, self.T, self.S, self.nseq
    nc = self.nc
    ZS = self.dscr("GStm", [T, 2048], BF16)
    XSd = self.dscr("Vtm", [T, 2048], BF16)
    BTd = self.dscr("QT", [1024, T], BF16)
    CTd = self.dscr("KT", [1024, T], BF16)
    AQK = self.dscr("CQK", [32, 6, T], BF16)
    DTd = self.dscr("DTd", [T, 32], F32)
    O = self.dscr("Otm", [T, 2048], BF16)
    bps = S // 512
    with kb.scope():
        Wv = load_weights(kb, gr, w_in, 1024, 5152, gcol)
        cw = kb.sb("scw", [128, 24 * 4], F32)
        cwv = cw[:].rearrange("p (c k) -> p c k", k=4)
        kb.dma(cwv, conv_wT.rearrange("(c p) k -> p c k", p=128))
        cbias = kb.sb("scbias", [128, 24], F32)
        kb.dma(cbias[:], conv_b.rearrange("(c p) -> p c", p=128))
        hal = kb.sb("shal", [128, 24 * 3], F32)
        cbuf = [kb.sb("scbuf%d" % i, [128, 515], F32) for i in range(2)]
        accb = [kb.sb("saccb%d" % i, [128, 512], F32) for i in range(2)]
        sil = [kb.sb("ssil%d" % i, [128, 512], BF16) for i in range(2)]
        xst = [kb.sb("sxst%d" % i, [128, 512], BF16) for i in range(2)]
        ptx = kb.ps("sptx", [128, 1024], BF16)
        pdt = kb.ps("spdt", [128, 512], F32)
        dtb = kb.sb("sdtb", [32, 1], F32)
        kb.dma(dtb[:], dt_bias.rearrange("(p o) -> p o", o=1))
        nA = kb.sb("snA", [32, 1], F32)
        kb.dma(nA[:], a_log.rearrange("(p o) -> p o", o=1))
        kb.act(nA[:], nA[:], AF.Exp)
        kb.ts(nA[:], nA[:], -1.0, None, op0=ALU.mult)
        carry = kb.sb("scarry", [32, 1], F32)
        d_ = [kb.sb("sd%d" % i, [32, 512], F32) for i in range(5)]
        cum = kb.sb("scum", [32, 512], F32)
        rr = kb.sb("srr", [32, 512], F32)
        spl = kb.sb("sspl", [32, 6 * 512], BF16)
        splv = spl[:].rearrange("p (s n) -> p s n", s=6)
        dtm = kb.sb("sdtm", [128, 4 * 32], F32)
        cn = [0]

        def conv_epi(blk, tok0, c, pss):
            ps = pss[0]
            n = cn[0]
            cn[0] += 1
            cb, acc, sl = cbuf[n % 2], accb[n % 2], sil[n % 2]
            if blk % bps == 0:
                kb.memset(cb[:, 0:3], 0.0)
            else:
                kb.cp(cb[:, 0:3], hal[:, 3 * c:3 * c + 3])
            kb.act(cb[:, 3:515], ps[:], AF.Copy)
            kb.cp(hal[:, 3 * c:3 * c + 3], cb[:, 512:515])
            kb.ts(acc[:], cb[:, 0:512], cwv[:, c, 0:1], cbias[:, c:c + 1], op0=ALU.mult, op1=ALU.add)
            for k in range(1, 4):
                kb.stt(acc[:], cb[:, k:k + 512], cwv[:, c, k:k + 1], acc[:], ALU.mult, ALU.add)
            kb.act(sl[:], acc[:], AF.Silu)
            if c < 16:
                for s_ in range(4):
                    kb.tr(ptx[:, s_ * 128:(s_ + 1) * 128], sl[:, s_ * 128:(s_ + 1) * 128], gr.ident[:])
                xs_ = xst[n % 2]
                kb.cp(xs_[:], ptx[:, 0:512])
                kb.dma(XSd[tok0:tok0 + 512, c * 128:(c + 1) * 128].rearrange("(s p) ch -> p s ch", p=128),
                       xs_[:].rearrange("p (s ch) -> p s ch", s=4), q="pool")
            elif c < 20:
                kb.dma(BTd[(c - 16) * 128:(c - 15) * 128, tok0:tok0 + 512], sl[:], q="pool")
            else:
                kb.dma(CTd[(c - 20) * 128:(c - 19) * 128, tok0:tok0 + 512], sl[:], q="pool")

        def dt_epi(blk, tok0, c0, pss):
            ps = pss[0]
            xb, ab, e_, r_, dtt = d_
            if blk % bps == 0:
                kb.memset(carry[:], 0.0)
            kb.act(xb[:], ps[0:32, :], AF.Identity, bias=dtb[:, 0:1])
            kb.act(ab[:], xb[:], AF.Abs)
            kb.act(e_[:], ab[:], AF.Exp, scale=-1.0)
            kb.act(e_[:], e_[:], AF.Ln, bias=self.one[0:32, 0:1])
            kb.ts(r_[:], xb[:], 0.0, None, op0=ALU.max)
            kb.tt(dtt[:], r_[:], e_[:], ALU.add)
            kb.ts(ab[:], dtt[:], nA[:, 0:1], None, op0=ALU.mult)
            kb.op("dve", lambda: nc.vector.tensor_tensor_scan(cum[:], ab[:], self.zeros[0:32, :], carry[:, 0:1], ALU.add, ALU.add),
                  [ab, self.zeros, carry], [cum])
            kb.cp(carry[:], cum[:, 511:512])
            split3(kb, cum[:], splv, rr, None, 32, 512)
            kb.dma(AQK[0:32, :, tok0:tok0 + 512], splv, q="pool")
            for s_ in range(4):
                kb.tr(pdt[:, s_ * 32:(s_ + 1) * 32], dtt[:, s_ * 128:(s_ + 1) * 128], gr.ident_f[0:32, 0:32])
            kb.cp(dtm[:], pdt[:, 0:128])
            kb.dma(DTd[tok0:tok0 + 512, :].rearrange("(s p) h -> p s h", p=128), dtm[:].rearrange("p (s h) -> p s h", s=4), q="pool")

        segs = [dict(kind="TM", n0=0, n1=2048, epi=self.epi.tm_store(ZS, func=AF.Silu)),
                dict(kind="FM", n0=2048, n1=5120, epi=conv_epi),
                dict(kind="FM", n0=5120, n1=5152, epi=dt_epi)]
        gemm_stage(kb, gr, h, T, 1024, Wv, segs, norm=True)
        kb.barrier()
    with kb.scope():
        spec = SsdSpec(self, S, AQK, BTd, CTd, XSd, DTd, ZS, O, dsk_rep, norm_g, sa_init)
        attn_stage(kb, spec, S, nseq)
        kb.barrier()
    with kb.scope():
        Wv = load_weights(kb, gr, w_out, 2048, 1024, None)
        segs = [dict(kind="TM", n0=0, n1=1024, epi=self.epi.tm_resid(h))]
        gemm_stage(kb, gr, O, T, 2048, Wv, segs, x_bf16=True)
        kb.barrier()


Model.ssd_mixer = ssd_mixer


DEPTH = 4
SEQ = 4096
NSEQ = 2


def build_program(nseq=NSEQ, S=SEQ):
    T = nseq * S
    nc = bass.Bass("TRN2", target_bir_lowering=False)
    es = ExitStack()
    m = Model(nc, es, nseq, S)
    I = m.inp
    I("ident", [128, 128]); I("cmask", [128, 2048])
    x = I("x", [T, 1024]); p = I("p", [DEPTH, T, 256])
    mix_norm = I("mix_norm", [DEPTH, 1024]); ffn_norm = I("ffn_norm", [DEPTH, 1024]); ple_norm = I("ple_norm", [DEPTH, 1024])
    final_norm = I("final_norm", [1024])
    ssd_w_in = I("ssd_w_in", [1024, 5152]); ssd_cwT = I("ssd_cwT", [3072, 4]); ssd_cb = I("ssd_cb", [3072])
    ssd_dtb = I("ssd_dtb", [32]); ssd_alog = I("ssd_alog", [32]); ssd_dsk = I("ssd_dsk", [2048]); ssd_norm = I("ssd_norm", [2048])
    ssd_w_out = I("ssd_w_out", [2048, 1024]); ssd_sa = I("ssd_sa", [48, 1024])
    ret_w_in = I("ret_w_in", [1024, 6144]); ret_norm = I("ret_norm", [2048]); ret_w_out = I("ret_w_out", [2048, 1024])
    ret_cos = I("ret_cos", [128, S]); ret_sin = I("ret_sin", [128, S]); ret_dq = I("ret_dq", [4, 512]); ret_dk = I("ret_dk", [4, 512])
    ret_m = I("ret_m", [4, 4, 128, 512])
    diff_w_in = I("diff_w_in", [1024, 3072]); diff_lam = I("diff_lam", [4, 64]); diff_norm = I("diff_norm", [128])
    diff_w_out = I("diff_w_out", [1024, 1024]); diff_BT = I("diff_BT", [8, 2, 128, 128]); diff_bfar = I("diff_bfar", [8])
    diff_mask = I("diff_mask", [128, 128])
    fox_w_in = I("fox_w_in", [1024, 3088]); fox_bf = I("fox_bf", [16]); fox_w_out = I("fox_w_out", [1024, 1024])
    peer_wq = I("peer_wq", [DEPTH, 1024, 2048]); peer_kT = I("peer_kT", [DEPTH, 16, 128, 128])
    peer_uT = I("peer_uT", [DEPTH, 1024, 16384]); peer_v = I("peer_v", [DEPTH, 16384, 1024])
    ple_proj = I("ple_proj", [DEPTH, 256, 1024]); ple_gate = I("ple_gate", [DEPTH, 1024, 1024])
    out = nc.dram_tensor("out", [T, 1024], F32, kind="ExternalOutput").ap()
    hb = m.dscr("hbuf", [T, 1024], F32)
    m.setup()
    kb = m.kb
    for r0 in range(0, T, 512):
        kb.dma(hb[r0:r0 + 512, :], x[r0:r0 + 512, :])
    kb.barrier()
    for i in range(DEPTH):
        if i == 0:
            m.ssd_mixer(hb, ssd_w_in, mix_norm[i], ssd_cwT, ssd_cb, ssd_dtb, ssd_alog, ssd_dsk, ssd_norm, ssd_w_out, ssd_sa)
        elif i == 1:
            m.ret_mixer(hb, ret_w_in, mix_norm[i], ret_norm, ret_w_out, ret_cos, ret_sin, ret_dq, ret_dk, ret_m)
        elif i == 2:
            lam_init = 0.8 - 0.6 * math.exp(-0.3 * i)
            m.diff_mixer(hb, diff_w_in, mix_norm[i], diff_lam, lam_init, diff_norm, diff_w_out, diff_BT, diff_bfar, diff_mask)
        else:
            m.fox_mixer(hb, fox_w_in, mix_norm[i], fox_bf, fox_w_out)
        m.peer(hb, ffn_norm[i], peer_wq[i], peer_kT[i], peer_uT[i], peer_v[i])
        m.ple(hb, ple_norm[i], ple_gate[i], p[i], ple_proj[i])
    m.final(hb, final_norm, out)
    es.close()
    return nc, m


def host_inputs(inputs, nseq=NSEQ, S=SEQ, ncores=NCORES):
    f = lambda a: np.ascontiguousarray(np.asarray(a, dtype=np.float32))
    g = {k: np.asarray(v) for k, v in inputs.items()}
    BT, bfar, dmask = diff_tables(g["rel_bias"])
    cosT, sinT, dq, dk, mret = ret_tables(S)
    shared = {
        "ident": np.eye(128, dtype=np.float32), "cmask": cmask_np(),
        "mix_norm": f(g["mix_norm"]), "ffn_norm": f(g["ffn_norm"]), "ple_norm": f(g["ple_norm"]), "final_norm": f(g["final_norm"]),
        "ssd_w_in": f(g["ssd_w_in"][0]), "ssd_cwT": f(g["ssd_conv_w"][0][:, 0, :].T), "ssd_cb": f(g["ssd_conv_b"][0]),
        "ssd_dtb": f(g["ssd_dt_bias"][0]), "ssd_alog": f(g["ssd_a_log"][0]), "ssd_dsk": f(np.repeat(g["ssd_d"][0], 64)),
        "ssd_norm": f(g["ssd_norm"][0]), "ssd_w_out": f(g["ssd_w_out"][0]), "ssd_sa": ssd_tables(),
        "ret_w_in": f(g["ret_w_in"][0]), "ret_norm": f(g["ret_norm"][0]), "ret_w_out": f(g["ret_w_out"][0]),
        "ret_cos": cosT, "ret_sin": sinT, "ret_dq": dq, "ret_dk": dk, "ret_m": mret,
        "diff_w_in": f(g["diff_w_in"][0]), "diff_lam": f(g["diff_lambda"][0]), "diff_norm": f(g["diff_norm"][0]),
        "diff_w_out": f(g["diff_w_out"][0]), "diff_BT": f(BT), "diff_bfar": f(bfar), "diff_mask": dmask,
        "fox_w_in": f(g["fox_w_in"][0]), "fox_bf": f(g["fox_b_f"][0]), "fox_w_out": f(g["fox_w_out"][0]),
        "peer_wq": f(g["peer_w_q"]),
        "peer_kT": f(g["peer_keys"].reshape(DEPTH, 16, 128, 128).transpose(0, 1, 3, 2)),
        "peer_uT": f(g["peer_u"].transpose(0, 2, 1)), "peer_v": f(g["peer_v"]),
        "ple_proj": f(g["ple_proj"]), "ple_gate": f(g["ple_gate"]),
    }
    T = nseq * S
    maps = []
    for c in range(ncores):
        d = dict(shared)
        d["x"] = f(g["x"][c * nseq:(c + 1) * nseq].reshape(T, 1024))
        d["p"] = f(g["p"][:, c * nseq:(c + 1) * nseq].reshape(DEPTH, T, 256))
        maps.append(d)
    return maps


def kernel(**inputs):
    nc, m = build_program()
    maps = host_inputs(inputs)
    res = run_bass_kernel_spmd(nc, maps, core_ids=list(range(NCORES)))
    outs = [np.asarray(r["out"]).reshape(NSEQ, SEQ, 1024) for r in res.results]
    return np.concatenate(outs, axis=0).astype(np.float32)


def peer_merged(self, h, gcol, w_q, keysT, uT, v):
    kb, gr, T = self.kb, self.gr, self.T
    nc = self.nc
    V = nc.vector
    UTb = self.dscr("UTb", [1024, 16384], BF16)
    Vb = self.dscr("Vb", [16384, 1024], BF16)
    Gd = self.dscr("Gd", [T, 16384], BF16)
    with kb.scope():
        kb.dma(gr.gcol[:, 0:8], gcol.rearrange("(c p) -> p c", p=128))
        st = [kb.sb("pst%d" % i, [128, 2048], F32) for i in range(3)]
        sb = [kb.sb("psb%d" % i, [128, 2048], BF16) for i in range(3)]
        n = 0
        for kc in range(8):
            for e0 in range(0, 16384, 2048):
                s_, b_ = st[n % 3], sb[n % 3]
                kb.dma(s_[:], uT[kc * 128:(kc + 1) * 128, e0:e0 + 2048])
                if n % 2 == 0:
                    kb.ts(b_[:], s_[:], gr.gcol[:, kc:kc + 1], None, op0=ALU.mult)
                else:
                    kb.act(b_[:], s_[:], AF.Copy, scale=gr.gcol[:, kc:kc + 1])
                kb.dma(UTb[kc * 128:(kc + 1) * 128, e0:e0 + 2048], b_[:], q="pool")
                n += 1
        for r0 in range(0, 16384, 256):
            s_, b_ = st[n % 3], sb[n % 3]
            kb.dma(s_[:].rearrange("p (c n) -> p c n", c=2), v[r0:r0 + 256, :].rearrange("(c p) n -> p c n", p=128))
            if n % 2 == 0:
                kb.cp(b_[:], s_[:], e="dve")
            else:
                kb.act(b_[:], s_[:], AF.Copy)
            kb.dma(Vb[r0:r0 + 256, :].rearrange("(c p) n -> p c n", p=128), b_[:].rearrange("p (c n) -> p c n", c=2), q="pool")
            n += 1
        kb.barrier()
    with kb.scope():
        Wv = load_weights(kb, gr, w_q, 1024, 2048, gcol)
        kT = kb.sb("keysT", [128, 16 * 128], BF16)
        with kb.scope():
            ktmp = kb.sb("ktmp", [128, 16 * 128], F32)
            kb.dma(ktmp[:].rearrange("p (c k) -> p c k", c=16), keysT.rearrange("c d k -> d c k"))
            kb.cp(kT[:], ktmp[:])
            kb.barrier()
        kTv = kT[:].rearrange("p (c k) -> p c k", c=16)
        qT = [kb.sb("pqT%d" % i, [128, 512], BF16) for i in range(2)]
        psc = kb.ps("psc", [128, 512], F32)
        sc_all = kb.sb("sc_all", [128, 4 * 16 * 128], F32)
        scv = sc_all[:].rearrange("p (s c k) -> p s c k", s=4, c=16)
        a16 = kb.sb("a16", [128, 16], F32)
        b16 = kb.sb("b16", [128, 16], F32)
        c16 = kb.sb("c16", [128, 16], F32)
        e16 = kb.sb("e16", [128, 16], F32)
        t128 = kb.sb("t128", [128, 128], F32)
        cand = kb.sb("cand", [128, 256], F32)
        cand2 = kb.sb("cand2", [128, 256], F32)
        tau = kb.sb("tau", [128, 8], F32)
        nb = kb.sb("nb", [128, 8], F32)
        zz = kb.sb("zz", [128, 1], F32)
        Sc = [kb.sb("Sc%d" % i, [128, 2048], F32) for i in range(2)]
        Ec = [kb.sb("Ec%d" % i, [128, 2048], BF16) for i in range(2)]
        Gb = [kb.sb("Gb%d" % i, [128, 2048], BF16) for i in range(2)]
        ut = [kb.sb("ut%d" % i, [128, 8 * 512], BF16) for i in range(2)]
        vt = [kb.sb("vt%d" % i, [128, 4 * 1024], BF16) for i in range(2)]
        gt = [kb.sb("gt%d" % i, [128, 512], BF16) for i in range(2)]
        gel = [kb.sb("gel%d" % i, [128, 512], F32) for i in range(1)]
        gh = [kb.sb("gh%d" % i, [128, 512], BF16) for i in range(2)]
        ghT = [kb.sb("ghT%d" % i, [128, 512], BF16) for i in range(1)]
        acc = [kb.sb("acc%d" % i, [128, 1024], F32) for i in range(4)]
        hst = gr.xin
        php = kb.ps("php", [128, 512], F32)
        ptp = kb.ps("ptp", [128, 1024], BF16)
        pop = [kb.ps("pop%d" % i, [128, 512], F32) for i in range(2)]
        state = dict(pending=None, nq=0, nd=0, gchunk=0)

        def top16(dst, src, tmp):
            kb.op("dve", lambda: V.max(out=dst[:, 0:8], in_=src), [src], [dst])
            kb.op("dve", lambda: V.match_replace(out=tmp, in_to_replace=dst[:, 0:8], in_values=src, imm_value=-1e30),
                  [dst, src], [tmp])
            kb.op("dve", lambda: V.max(out=dst[:, 8:16], in_=tmp), [tmp], [dst])

        def gates_gen(blk, tokb, sub):
            gkey = ("Gd", blk)
            for hh in range(8):
                s1 = scv[:, sub, 2 * hh, :]
                s2 = scv[:, sub, 2 * hh + 1, :]
                top16(a16, s1, t128[:])
                top16(b16, s2, t128[:])
                kb.tt(cand[:].rearrange("p (a b) -> p a b", a=16),
                      a16[:].unsqueeze(2).to_broadcast([128, 16, 16]),
                      b16[:].unsqueeze(1).to_broadcast([128, 16, 16]), ALU.add)
                top16(c16, cand[:], cand2[:])
                kb.cp(tau[:, hh:hh + 1], c16[:, 15:16])
                kb.ts(nb[:, hh:hh + 1], c16[:, 0:1], -1.0, None, op0=ALU.mult)
                kb.act(e16[:], c16[:], AF.Exp, bias=nb[:, hh:hh + 1])
                kb.red(zz[:], e16[:])
                kb.act(zz[:], zz[:], AF.Ln)
                kb.tt(nb[:, hh:hh + 1], nb[:, hh:hh + 1], zz[:], ALU.subtract)
                yield
            items = [(c, hh) for c in range(8) for hh in range(8)]

            def stage1(n):
                c, hh = items[n]
                S_, E_ = Sc[n % 2], Ec[n % 2]
                s1 = scv[:, sub, 2 * hh, 16 * c:16 * c + 16]
                s2 = scv[:, sub, 2 * hh + 1, :]
                S3 = S_[:].rearrange("p (a b) -> p a b", a=16)
                kb.tt(S3, s1.unsqueeze(2).to_broadcast([128, 16, 128]),
                      s2.unsqueeze(1).to_broadcast([128, 16, 128]), ALU.add)
                kb.act(E_[:], S_[:], AF.Exp, bias=nb[:, hh:hh + 1])

            def stage2(n):
                c, hh = items[n]
                S_, E_ = Sc[n % 2], Ec[n % 2]
                G = Gb[(state["gchunk"] + c) % 2]
                dst = G if hh == 0 else E_
                kb.stt(dst[:], S_[:], tau[:, hh:hh + 1], E_[:], ALU.is_ge, ALU.mult)
                if hh > 0:
                    kb.tt(G[:], G[:], E_[:], ALU.add)
                if hh == 7:
                    prev = state.get("gstore")
                    if prev is not None:
                        kb.dma(prev[0], prev[1], q="act", wk=prev[2])
                    state["gstore"] = (Gd[tokb:tokb + 128, c * 2048:(c + 1) * 2048], G[:], gkey)

            stage1(0)
            for n in range(64):
                if n + 1 < 64:
                    stage1(n + 1)
                stage2(n)
                yield
            state["gchunk"] += 8
            if sub == 3:
                prev = state.get("gstore")
                kb.dma(prev[0], prev[1], q="sp", wk=prev[2])
                state["gstore"] = None

        def dense_gen(blk, xTv):
            tokb = blk * 512
            gkey = ("Gd", blk)
            for ec in range(32):
                u_ = ut[ec % 2]
                v_ = vt[ec % 2]
                uv = u_[:].rearrange("p (c e) -> p c e", c=8)
                vv = v_[:].rearrange("p (c n) -> p c n", c=4)
                kb.dma(uv, UTb[:, ec * 512:(ec + 1) * 512].rearrange("(c p) e -> p c e", p=128))
                kb.dma(vv, Vb[ec * 512:(ec + 1) * 512, :].rearrange("(c p) n -> p c n", p=128))
                for sub in range(4):
                    k = state["nd"]
                    state["nd"] += 1
                    g_ = gt[k % 2]
                    kb.dma(g_[:], Gd[tokb + sub * 128:tokb + sub * 128 + 128, ec * 512:(ec + 1) * 512], rk=gkey)
                    for kc in range(8):
                        kb.mm(php[:], xTv[:, kc, sub * 128:(sub + 1) * 128], uv[:, kc, :], start=(kc == 0), stop=(kc == 7))
                    yield
                    ge = gel[0]
                    kb.act(ge[:], php[:], AF.Gelu)
                    gh_ = gh[k % 2]
                    kb.tt(gh_[:], ge[:], g_[:], ALU.mult, e="pool")
                    yield
                    for c4 in range(4):
                        kb.tr(ptp[:, c4 * 128:(c4 + 1) * 128], gh_[:, c4 * 128:(c4 + 1) * 128], gr.ident[:])
                    if state.get("padd") is not None:
                        state["padd"]()
                        state["padd"] = None
                    yield
                    gT = ghT[0]
                    kb.act(gT[:], ptp[:, 0:512], AF.Copy)
                    pop4 = [pop[0], pop[1], gr.pm[0], gr.pm[1]]
                    pos = []
                    for nh in range(2):
                        po = pop4[(k % 2) * 2 + nh]
                        pos.append(po)
                        for c4 in range(4):
                            kb.mm(po[:], gT[:, c4 * 128:(c4 + 1) * 128], vv[:, c4, nh * 512:(nh + 1) * 512],
                                  start=(c4 == 0), stop=(c4 == 3))

                    def do_add(pos=pos, sub=sub, ec=ec):
                        for nh in range(2):
                            a_ = acc[sub][:, nh * 512:(nh + 1) * 512]
                            if ec == 0:
                                kb.act(a_, pos[nh][:], AF.Copy)
                            else:
                                kb.tt(a_, pos[nh][:], a_, ALU.add)
                    state["padd"] = do_add
                    yield
            if state.get("padd") is not None:
                state["padd"]()
                state["padd"] = None
            for sub in range(4):
                t0 = tokb + sub * 128
                hs = hst[sub % 3]
                key = ("h", t0)
                kb.dma(hs[:], h[t0:t0 + 128, :], rk=key)
                kb.tt(hs[:], acc[sub][:], hs[:], ALU.add, e="pool")
                kb.dma(h[t0:t0 + 128, :], hs[:], q="pool", wk=key)
            yield

        def run_interleaved(gg, dg, ng=2, nd=4):
            alive_g, alive_d = gg is not None, dg is not None
            while alive_g or alive_d:
                if alive_g:
                    for _ in range(ng):
                        try:
                            next(gg)
                        except StopIteration:
                            alive_g = False
                            break
                if alive_d:
                    for _ in range(nd):
                        try:
                            next(dg)
                        except StopIteration:
                            alive_d = False
                            break

        def q_epi(blk, tok0, c0, pss):
            qt = qT[c0 % 2]
            kb.act(qt[:], pss[0][:], AF.Copy)
            for sub in range(4):
                kb.mm(psc[:, sub * 128:(sub + 1) * 128], qt[:, sub * 128:(sub + 1) * 128], kTv[:, c0, :])
            kb.act(scv[:, :, c0, :], psc[:].rearrange("p (s k) -> p s k", s=4), AF.Copy)

        def chain(blk):
            for sub in range(4):
                for _ in gates_gen(blk, blk * 512 + sub * 128, sub):
                    yield

        def merged(blk, xTv):
            pend = state["pending"]
            dg = dense_gen(*pend) if pend is not None else None
            run_interleaved(chain(blk), dg)
            state["pending"] = (blk, xTv)

        segs = [dict(kind="FM", n0=0, n1=2048, epi=q_epi), dict(kind="custom", fn=merged)]
        gemm_stage(kb, gr, h, T, 1024, Wv, segs, norm=True, n_psT=1, n_pm=2)
        run_interleaved(None, dense_gen(*state["pending"]))
        kb.barrier()


Model.peer = peer_merged
```

```python
from contextlib import ExitStack
import math
import numpy as np
import concourse.bass as bass
import concourse.mybir as mybir
from concourse.bass_utils import run_bass_kernel_spmd

F32 = mybir.dt.float32
BF16 = mybir.dt.bfloat16
AF = mybir.ActivationFunctionType
ALU = mybir.AluOpType
AX = mybir.AxisListType

NCORES = 8
D = 1024
NEG = -30000.0
DBG = set()


class KB:
    NDMA = 24

    def __init__(self, nc, es):
        self.nc = nc
        self.es = es
        self.eng = dict(pe=nc.tensor, act=nc.scalar, dve=nc.vector, pool=nc.gpsimd, sp=nc.sync)
        es.enter_context(nc.allow_non_contiguous_dma(reason="small strided param loads"))
        self.sem = {}
        self.cnt = {}
        for e in ("pe", "act", "dve", "pool"):
            self.sem[e] = es.enter_context(nc.semaphore("s_" + e))
            self.cnt[e] = 0
        self.dsem = []
        for i in range(self.NDMA):
            nm = "d%d" % i
            self.sem[nm] = es.enter_context(nc.semaphore("s_" + nm))
            self.cnt[nm] = 0
            self.dsem.append(nm)
        self.dnext = 0
        self.known = {e: {} for e in self.eng}
        self.lastw = {}
        self.readers = {}
        self.n_ins = 0
        self.uid = 0

    def scope(self):
        kb = self

        class _S:
            def __enter__(self_):
                self_.old = kb.es
                self_.st = ExitStack()
                self_.st.__enter__()
                kb.es = self_.st
                return self_

            def __exit__(self_, *a):
                kb.es = self_.old
                return self_.st.__exit__(*a)

        return _S()

    def sb(self, name, shape, dtype=F32):
        self.uid += 1
        return self.es.enter_context(self.nc.sbuf_tensor("%s_%d" % (name, self.uid), list(shape), dtype))

    def ps(self, name, shape, dtype=F32):
        self.uid += 1
        return self.es.enter_context(self.nc.psum_tensor("%s_%d" % (name, self.uid), list(shape), dtype))

    @staticmethod
    def _key(a):
        return a if isinstance(a, (str, tuple)) else a.name

    def _wait(self, e, s, v):
        if v <= 0:
            return
        if self.known[e].get(s, 0) >= v:
            return
        self.eng[e].wait_ge(self.sem[s], v)
        self.known[e][s] = v
        self.n_ins += 1

    def _deps(self, e, R, W, pe_acc=False):
        deps = {}

        def add(tok, same_ok):
            if tok is None:
                return
            s, v = tok
            if same_ok and s == e:
                return
            if deps.get(s, 0) < v:
                deps[s] = v

        for k in R:
            add(self.lastw.get(k), False)
        for k in W:
            add(self.lastw.get(k), True)
            for s, v in self.readers.get(k, {}).items():
                add((s, v), True)
        for s, v in deps.items():
            self._wait(e, s, v)

    def _commit(self, tok, R, W):
        s, v = tok
        for k in W:
            self.lastw[k] = tok
            self.readers[k] = {}
        for k in R:
            d = self.readers.setdefault(k, {})
            if d.get(s, 0) < v:
                d[s] = v

    def op(self, e, ins_fn, R, W):
        R = [self._key(a) for a in R if a is not None and not isinstance(a, (int, float))]
        W = [self._key(a) for a in W]
        self._deps(e, R, W)
        ins = ins_fn()
        self.cnt[e] += 1
        ins.then_inc(self.sem[e], 1)
        self.n_ins += 1
        self._commit((e, self.cnt[e]), R, W)
        return ins

    def dma(self, out, in_, q="sp", rk=None, wk=None):
        R = [rk if rk is not None else self._key(in_)]
        W = [wk if wk is not None else self._key(out)]
        self._deps(q, R, W)
        s = self.dsem[self.dnext]
        self.dnext = (self.dnext + 1) % self.NDMA
        self._wait(q, s, self.cnt[s])
        self.eng[q].dma_start(out=out, in_=in_).then_inc(self.sem[s], 16)
        self.cnt[s] += 16
        self.n_ins += 1
        self._commit((s, self.cnt[s]), R, W)

    def barrier(self):
        for e in self.eng:
            for s in self.sem:
                if s != e:
                    self._wait(e, s, self.cnt[s])
        self.lastw = {}
        self.readers = {}

    def mm(self, out, lhsT, rhs, start=True, stop=True, extra_r=()):
        return self.op("pe", lambda: self.nc.tensor.matmul(out, lhsT, rhs, start=start, stop=stop),
                       [lhsT, rhs, *extra_r], [out])

    def tr(self, out, in_, ident):
        return self.op("pe", lambda: self.nc.tensor.transpose(out, in_, ident), [in_, ident], [out])

    def act(self, out, in_, func, bias=None, scale=None, accum_out=None, extra_r=()):
        kw = {}
        if bias is not None:
            kw["bias"] = bias
        if scale is not None:
            kw["scale"] = scale
        if accum_out is not None:
            kw["accum_out"] = accum_out
        W = [out] + ([accum_out] if accum_out is not None else [])
        return self.op("act", lambda: self.nc.scalar.activation(out, in_, func, **kw),
                       [in_, bias, scale, *extra_r], W)

    def tt(self, out, in0, in1, op, e="dve"):
        return self.op(e, lambda: self.eng[e].tensor_tensor(out, in0, in1, op), [in0, in1], [out])

    def ts(self, out, in0, s1, s2=None, op0=ALU.mult, op1=None, e="dve", accum_out=None):
        kw = {}
        if op1 is not None:
            kw["op1"] = op1
        if accum_out is not None:
            kw["accum_out"] = accum_out
        W = [out] + ([accum_out] if accum_out is not None else [])
        return self.op(e, lambda: self.eng[e].tensor_scalar(out, in0, s1, s2, op0, **kw), [in0, s1, s2], W)

    def stt(self, out, in0, scalar, in1, op0, op1, e="dve"):
        return self.op(e, lambda: self.eng[e].scalar_tensor_tensor(out, in0, scalar, in1, op0, op1),
                       [in0, scalar, in1], [out])

    def cp(self, out, in_, e="dve"):
        return self.op(e, lambda: self.eng[e].tensor_copy(out, in_), [in_], [out])

    def memset(self, out, val, e="dve"):
        return self.op(e, lambda: self.eng[e].memset(out, val), [], [out])

    def recip(self, out, in_):
        return self.op("dve", lambda: self.nc.vector.reciprocal(out, in_), [in_], [out])

    def red(self, out, in_, op=ALU.add, e="dve"):
        return self.op(e, lambda: self.eng[e].tensor_reduce(out, in_, AX.X, op), [in_], [out])


class GemmRes:
    def __init__(self, kb):
        self.kb = kb
        self.ident_f = kb.sb("identf", [128, 128], F32)
        self.ident = kb.sb("ident", [128, 128], BF16)
        self.xin = [kb.sb("xin%d" % i, [128, 1024], F32) for i in range(3)]
        self.xsq = kb.sb("xsq", [128, 1024], F32)
        self.ssq = [kb.sb("ssq%d" % i, [128, 1], F32) for i in range(3)]
        self.xn = [kb.sb("xn%d" % i, [128, 2048], BF16) for i in range(2)]
        self.gcol = kb.sb("gcol", [128, 8], F32)
        self.epsc = kb.sb("epsc", [128, 1], F32)
        self.pmi = 0

    def next_pm(self):
        p = self.pm[self.pmi % len(self.pm)]
        self.pmi += 1
        return p

    def init(self, ident_dram):
        kb = self.kb
        kb.dma(self.ident_f[:], ident_dram)
        kb.cp(self.ident[:], self.ident_f[:])
        kb.memset(self.epsc[:], 1e-6)


def load_weights(kb, gr, W_dram, Kin, N, gcol_dram=None):
    KC = Kin // 128
    Wb = kb.sb("Wb", [128, KC * N], BF16)
    Wv = Wb[:].rearrange("p (c n) -> p c n", c=KC)
    with kb.scope():
        wst = [kb.sb("wst%d" % i, [128, 2048], F32) for i in range(2)]
        if gcol_dram is not None:
            kb.dma(gr.gcol[:, 0:KC], gcol_dram.rearrange("(c p) -> p c", p=128))
        i = 0
        for kc in range(KC):
            for n0 in range(0, N, 2048):
                n1 = min(N, n0 + 2048)
                st = wst[i % 2]
                kb.dma(st[:, 0:n1 - n0], W_dram[kc * 128:(kc + 1) * 128, n0:n1], q="sp")
                if gcol_dram is not None:
                    if i % 2 == 0:
                        kb.ts(Wv[:, kc, n0:n1], st[:, 0:n1 - n0], gr.gcol[:, kc:kc + 1], None, op0=ALU.mult)
                    else:
                        kb.act(Wv[:, kc, n0:n1], st[:, 0:n1 - n0], AF.Copy, scale=gr.gcol[:, kc:kc + 1])
                else:
                    kb.cp(Wv[:, kc, n0:n1], st[:, 0:n1 - n0], e=("dve", "pool")[i % 2])
                i += 1
        kb.barrier()
    return Wv


def gemm_stage(kb, gr, x_dram, T, Kin, Wv, segs, norm=False, x_bf16=False, n_psT=2, n_pm=4):
    KC = Kin // 128
    nblk = T // 512
    gr.xT = [kb.sb("xT%d" % i, [128, KC * 512], BF16) for i in range(2)]
    gr.psT = [kb.ps("psT%d" % i, [128, 1024], BF16) for i in range(n_psT)]
    gr.pm = [kb.ps("pm%d" % i, [128, 512], F32) for i in range(n_pm)]
    for blk in range(nblk):
        xT = gr.xT[blk % 2]
        xTv = xT[:, 0:KC * 512].rearrange("p (c t) -> p c t", c=KC)
        xins = []
        for sub in range(4):
            tok0 = blk * 512 + sub * 128
            it = blk * 4 + sub
            if x_bf16:
                xt = gr.xn[it % 2]
                kb.dma(xt[:, 0:Kin], x_dram[tok0:tok0 + 128, :])
                xn = xt
            else:
                xt = gr.xin[it % 3]
                kb.dma(xt[:, 0:Kin], x_dram[tok0:tok0 + 128, :])
                xn = gr.xn[it % 2]
                if norm:
                    ssq = gr.ssq[it % 3]
                    kb.act(gr.xsq[:, 0:Kin], xt[:, 0:Kin], AF.Square)
                    kb.red(ssq[:], gr.xsq[:, 0:Kin])
                    kb.act(ssq[:], ssq[:], AF.Sqrt, bias=gr.epsc[:, 0:1], scale=1.0 / Kin)
                    kb.recip(ssq[:], ssq[:])
                    kb.act(xn[:, 0:Kin], xt[:, 0:Kin], AF.Copy, scale=ssq[:, 0:1])
                else:
                    kb.cp(xn[:, 0:Kin], xt[:, 0:Kin], e="pool")
            xins.append(xt)
            for half in range((KC + 7) // 8):
                pst = gr.psT[(it * 2 + half) % n_psT]
                nk = min(8, KC - half * 8)
                for j in range(nk):
                    kc = half * 8 + j
                    kb.tr(pst[:, j * 128:(j + 1) * 128], xn[:, kc * 128:(kc + 1) * 128], gr.ident[:])
                src = pst[:, 0:nk * 128].rearrange("p (c t) -> p c t", c=nk)
                dst = xTv[:, half * 8:half * 8 + nk, sub * 128:(sub + 1) * 128]
                if (it + half) % 2 == 0:
                    kb.cp(dst, src, e="dve")
                else:
                    kb.act(dst, src, AF.Copy)
        for seg in segs:
            if seg["kind"] == "custom":
                seg["fn"](blk, xTv)
                continue
            n0, n1 = seg["n0"], seg["n1"]
            if seg["kind"] == "FM":
                grp = seg.get("group", 1)
                nch = (n1 - n0 + 127) // 128
                for c0 in range(0, nch, grp):
                    pss = []
                    for ci in range(c0, min(nch, c0 + grp)):
                        a = n0 + ci * 128
                        b = min(n1, a + 128)
                        ps = gr.next_pm()
                        for kc in range(KC):
                            kb.mm(ps[0:b - a, :], Wv[:, kc, a:b], xTv[:, kc, :], start=(kc == 0), stop=(kc == KC - 1))
                        pss.append(ps)
                    seg["epi"](blk, blk * 512, c0, pss)
            else:
                for sub in range(4):
                    for a in range(n0, n1, 512):
                        b = min(n1, a + 512)
                        ps = gr.next_pm()
                        for kc in range(KC):
                            kb.mm(ps[:, 0:b - a], xTv[:, kc, sub * 128:(sub + 1) * 128], Wv[:, kc, a:b],
                                  start=(kc == 0), stop=(kc == KC - 1))
                        seg["epi"](blk, blk * 512 + sub * 128, sub, a - n0, b - a, ps, xins[sub])


def attn_stage(kb, spec, S, nseq):
    nsup = S // 512
    npst = getattr(spec, "npst", 2)
    pst = [kb.ps("ast%d" % i, [128, 512], F32) for i in range(npst)]
    outs = [kb.ps("aout%d" % i, [128, 512], F32) for i in range(4)]
    Ps = [kb.sb("aP%d" % i, [128, 512], BF16) for i in range(npst + 1)]
    jobs = [(seq, g) for seq in range(nseq) for g in spec.groups(seq)]
    ti = 0
    nbuf = getattr(spec, "nbuf", 2)
    for n, (seq, g) in enumerate(jobs):
        buf = n % nbuf
        if nbuf == 1:
            spec.load(g, 0)
        else:
            if n == 0:
                spec.load(g, buf)
            if n + 1 < len(jobs):
                spec.load(jobs[n + 1][1], (n + 1) % 2)
        shs = spec.subheads(g)
        items = [(j, i, sh) for j in range(nsup) for i in range(4 * j + 4) for sh in shs]

        def emit_qk(it):
            j, i, sh = it
            ctx = None
            if hasattr(spec, "pre"):
                if sh == shs[0]:
                    state["ctx"] = spec.pre(g, j, i, buf)
                ctx = state["ctx"]
            ps = pst[state["ti"] % npst]
            P = Ps[state["ti"] % (npst + 1)]
            state["ti"] += 1
            if ctx is not None:
                spec.cur = ctx
            spec.qk(g, sh, j, i, ps, buf)
            return (ps, P, ctx)

        state = dict(ti=ti, ctx=None)
        LA = npst - 1
        pend = [emit_qk(it) for it in items[:LA]]
        for n, (j, i, sh) in enumerate(items):
            if n + LA < len(items):
                pend.append(emit_qk(items[n + LA]))
            ps, P, ctx = pend.pop(0)
            if ctx is not None:
                spec.cur = ctx
            spec.evac(g, sh, j, i, ps, P, buf)
            c0, dvp = spec.ocols(g, sh)
            vb = spec.vblk(g, sh, i, buf)
            for qb in range(4):
                if i <= 4 * j + qb:
                    kb.mm(outs[qb][:, c0:c0 + dvp], P[:, qb * 128:(qb + 1) * 128], vb,
                          start=(i == 0 and sh == shs[0]), stop=(i == 4 * j + qb))
            if i == 4 * j + 3 and sh == shs[-1]:
                spec.post(g, j, outs, buf)
        ti = state["ti"]


class FoxSpec:
    npst = 4

    def __init__(self, kb, gr, S, QT, KT, V, CQK, O, cmask_bf):
        self.kb, self.gr, self.S = kb, gr, S
        self.QT, self.KT, self.V, self.CQK, self.O = QT, KT, V, CQK, O
        self.cmask = cmask_bf
        self.Qa = [kb.sb("fQa%d" % i, [70, S], BF16) for i in range(2)]
        self.Ka = [kb.sb("fKa%d" % i, [70, S], BF16) for i in range(2)]
        self.Vt = [kb.sb("fV%d" % i, [128, (S // 128) * 65], BF16) for i in range(2)]
        self.rz = [kb.sb("frz%d" % i, [128, 1], F32) for i in range(4)]
        self.ost = [kb.sb("fost%d" % i, [128, 64], BF16) for i in range(4)]
        self.pi = 0
        for i in range(2):
            kb.memset(self.Qa[i][64:70, :], 1.0)
            kb.memset(self.Ka[i][64:70, :], 1.0, e="pool")
            kb.memset(self.Vt[i][:], 1.0, e="pool")

    def groups(self, seq):
        return [(seq, h) for h in range(16)]

    def subheads(self, g):
        return [0]

    def load(self, g, buf):
        kb, S = self.kb, self.S
        seq, h = g
        t0 = seq * S
        kb.dma(self.Qa[buf][0:64, :], self.QT[64 * h:64 * h + 64, t0:t0 + S])
        kb.dma(self.Qa[buf][64:67, :], self.CQK[h, 3:6, t0:t0 + S])
        kb.dma(self.Ka[buf][0:64, :], self.KT[64 * h:64 * h + 64, t0:t0 + S])
        kb.dma(self.Ka[buf][67:70, :], self.CQK[h, 0:3, t0:t0 + S])
        vt = self.Vt[buf][:].rearrange("p (b d) -> p b d", d=65)
        kb.dma(vt[:, :, 0:64], self.V[t0:t0 + S, 64 * h:64 * h + 64].rearrange("(b p) d -> p b d", p=128))

    def qk(self, g, sh, j, i, ps, buf):
        kb = self.kb
        r = i - 4 * j
        kb.mm(ps[:], self.Ka[buf][:, i * 128:(i + 1) * 128], self.Qa[buf][:, j * 512:(j + 1) * 512],
              start=True, stop=(r < 0))
        if r >= 0:
            kb.mm(ps[:], self.gr.ident[:], self.cmask[:, r * 512:(r + 1) * 512], start=False, stop=True)

    def evac(self, g, sh, j, i, ps, P, buf):
        self.kb.act(P[:], ps[:], AF.Exp)

    def ocols(self, g, sh):
        return 0, 65

    def vblk(self, g, sh, i, buf):
        return self.Vt[buf][:, i * 65:(i + 1) * 65]

    def post(self, g, j, outs, buf):
        kb, S = self.kb, self.S
        seq, h = g
        for qb in range(4):
            rz = self.rz[self.pi % 4]
            st = self.ost[self.pi % 4]
            self.pi += 1
            kb.recip(rz[:], outs[qb][:, 64:65])
            kb.ts(st[:], outs[qb][:, 0:64], rz[:, 0:1], None, op0=ALU.mult)
            tok = seq * S + (4 * j + qb) * 128
            kb.dma(self.O[tok:tok + 128, 64 * h:64 * h + 64], st[:], q="pool")


class Epi:
    def __init__(self, kb):
        self.kb = kb
        self.fm = [kb.sb("efm%d" % i, [128, 512], BF16) for i in range(3)]
        self.tmb = [kb.sb("etmb%d" % i, [128, 512], BF16) for i in range(3)]
        self.tmf = [kb.sb("etmf%d" % i, [128, 512], F32) for i in range(3)]
        self.n = 0

    def fm_store(self, dst, scale=1.0):
        kb = self.kb

        def epi(blk, tok0, c0, pss):
            ps = pss[0]
            st = self.fm[self.n % 3]
            self.n += 1
            rows = min(128, dst.shape[0] - c0 * 128)
            if self.n % 2 == 0:
                kb.act(st[0:rows, :], ps[0:rows, :], AF.Copy, scale=float(scale))
            else:
                kb.ts(st[0:rows, :], ps[0:rows, :], float(scale), None, op0=ALU.mult)
            kb.dma(dst[c0 * 128:c0 * 128 + rows, tok0:tok0 + 512], st[0:rows, :], q="pool")
        return epi

    def tm_store(self, dst, func=AF.Copy, col0=0):
        kb = self.kb

        def epi(blk, tok0, sub, n0c, ncols, ps, xt):
            st = self.tmb[self.n % 3]
            self.n += 1
            if func == AF.Copy and self.n % 2 == 0:
                kb.cp(st[:, 0:ncols], ps[:, 0:ncols])
            else:
                kb.act(st[:, 0:ncols], ps[:, 0:ncols], func)
            kb.dma(dst[tok0:tok0 + 128, col0 + n0c:col0 + n0c + ncols], st[:, 0:ncols], q="pool")
        return epi

    def tm_resid(self, h):
        kb = self.kb

        def epi(blk, tok0, sub, n0c, ncols, ps, xt):
            st = self.tmf[self.n % 3]
            self.n += 1
            key = ("h", tok0, n0c)
            kb.dma(st[:, 0:ncols], h[tok0:tok0 + 128, n0c:n0c + ncols], rk=key)
            kb.tt(st[:, 0:ncols], ps[:, 0:ncols], st[:, 0:ncols], ALU.add)
            kb.dma(h[tok0:tok0 + 128, n0c:n0c + ncols], st[:, 0:ncols], q="pool", wk=key)
        return epi


def split3(kb, src, dst6, tmp_r, tmp_b, rows, n):
    r = tmp_r
    kb.cp(dst6[0:rows, 0, :], src)
    kb.tt(r[0:rows, 0:n], src, dst6[0:rows, 0, :], ALU.subtract)
    kb.cp(dst6[0:rows, 1, :], r[0:rows, 0:n])
    kb.tt(r[0:rows, 0:n], r[0:rows, 0:n], dst6[0:rows, 1, :], ALU.subtract)
    kb.cp(dst6[0:rows, 2, :], r[0:rows, 0:n])
    kb.ts(dst6[0:rows, 3:6, :], dst6[0:rows, 0:3, :], -1.0, None, op0=ALU.mult, e="pool")


class Model:
    _epi_cache = (None, None)

    @property
    def epi(self):
        if self._epi_cache[0] is not self.kb.es:
            self._epi_cache = (self.kb.es, Epi(self.kb))
        return self._epi_cache[1]

    def __init__(self, nc, es, nseq, S):
        self.nc, self.nseq, self.S = nc, nseq, S
        self.T = nseq * S
        self.kb = KB(nc, es)
        self.din = {}
        self.scratch = {}

    def inp(self, name, shape, dtype=F32):
        self.din[name] = self.nc.dram_tensor(name, list(shape), dtype, kind="ExternalInput").ap()
        return self.din[name]

    def dscr(self, name, shape, dtype):
        if name not in self.scratch:
            self.scratch[name] = self.nc.dram_tensor(name, list(shape), dtype, kind="Internal").ap()
        return self.scratch[name]

    def setup(self):
        kb = self.kb
        self.gr = GemmRes(kb)
        self.gr.init(self.din["ident"])
        self.one = kb.sb("onec", [128, 1], F32)
        kb.memset(self.one[:], 1.0)
        self.zeros = kb.sb("zeros", [128, 512], F32)
        kb.memset(self.zeros[:], 0.0)
        self.cmask = kb.sb("cmask", [128, 2048], BF16)
        with kb.scope():
            tmp = kb.sb("cmtmp", [128, 2048], F32)
            kb.dma(tmp[:], self.din["cmask"])
            kb.cp(self.cmask[:], tmp[:])
            kb.barrier()

    def fox_mixer(self, h, w_in, gcol, b_f, w_out):
        kb, gr, T, S, nseq = self.kb, self.gr, self.T, self.S, self.nseq
        QT = self.dscr("QT", [1024, T], BF16)
        KT = self.dscr("KT", [1024, T], BF16)
        V = self.dscr("Vtm", [T, 2048], BF16)
        CQK = self.dscr("CQK", [32, 6, T], BF16)
        O = self.dscr("Otm", [T, 2048], BF16)
        with kb.scope():
            Wv = load_weights(kb, gr, w_in, 1024, 3088, gcol)
            nbf = kb.sb("nbf", [16, 1], F32)
            kb.dma(nbf[:], b_f.rearrange("(p o) -> p o", o=1))
            kb.ts(nbf[:], nbf[:], -1.0, None, op0=ALU.mult)
            carry = kb.sb("carry", [16, 1], F32)
            t1 = kb.sb("ft1", [16, 512], F32)
            cum = kb.sb("fcum", [16, 512], F32)
            rr = kb.sb("frr", [16, 512], F32)
            spl = kb.sb("fspl", [16, 6 * 512], BF16)
            splv = spl[:].rearrange("p (s n) -> p s n", s=6)
            bps = S // 512

            def f_epi(blk, tok0, c0, pss):
                ps = pss[0]
                if blk % bps == 0:
                    kb.memset(carry[:], 0.0)
                kb.act(t1[:], ps[0:16, :], AF.Exp, bias=nbf[:, 0:1], scale=-1.0)
                kb.act(t1[:], t1[:], AF.Ln, bias=self.one[0:16, 0:1])
                kb.op("dve", lambda: self.nc.vector.tensor_tensor_scan(cum[:], t1[:], self.zeros[0:16, :], carry[:, 0:1], ALU.add, ALU.add),
                      [t1, self.zeros, carry], [cum])
                kb.cp(carry[:], cum[:, 511:512])
                split3(kb, cum[:], splv, rr, None, 16, 512)
                kb.dma(CQK[0:16, :, tok0:tok0 + 512], splv, q="pool")

            segs = [dict(kind="FM", n0=0, n1=1024, epi=self.epi.fm_store(QT, 0.125)),
                    dict(kind="FM", n0=1024, n1=2048, epi=self.epi.fm_store(KT, 1.0)),
                    dict(kind="TM", n0=2048, n1=3072, epi=self.epi.tm_store(V)),
                    dict(kind="FM", n0=3072, n1=3088, epi=f_epi)]
            gemm_stage(kb, gr, h, T, 1024, Wv, segs, norm=True)
            kb.barrier()
        with kb.scope():
            spec = FoxSpec(kb, gr, S, QT, KT, V, CQK, O, self.cmask)
            attn_stage(kb, spec, S, nseq)
            kb.barrier()
        with kb.scope():
            Wv = load_weights(kb, gr, w_out, 1024, 1024, None)
            segs = [dict(kind="TM", n0=0, n1=1024, epi=self.epi.tm_resid(h))]
            gemm_stage(kb, gr, O[:, 0:1024], T, 1024, Wv, segs, x_bf16=True)
            kb.barrier()

    def peer_old(self, h, gcol, w_q, keysT, uT, v):
        kb, gr, T = self.kb, self.gr, self.T
        nc = self.nc
        UTb = self.dscr("UTb", [1024, 16384], BF16)
        Vb = self.dscr("Vb", [16384, 1024], BF16)
        Gd = self.dscr("Gd", [T, 16384], BF16)
        with kb.scope():
            kb.dma(gr.gcol[:, 0:8], gcol.rearrange("(c p) -> p c", p=128))
            st = [kb.sb("pst%d" % i, [128, 2048], F32) for i in range(3)]
            sb = [kb.sb("psb%d" % i, [128, 2048], BF16) for i in range(3)]
            n = 0
            for kc in range(8):
                for e0 in range(0, 16384, 2048):
                    s_, b_ = st[n % 3], sb[n % 3]
                    kb.dma(s_[:], uT[kc * 128:(kc + 1) * 128, e0:e0 + 2048])
                    if n % 2 == 0:
                        kb.ts(b_[:], s_[:], gr.gcol[:, kc:kc + 1], None, op0=ALU.mult)
                    else:
                        kb.act(b_[:], s_[:], AF.Copy, scale=gr.gcol[:, kc:kc + 1])
                    kb.dma(UTb[kc * 128:(kc + 1) * 128, e0:e0 + 2048], b_[:], q="pool")
                    n += 1
            for r0 in range(0, 16384, 256):
                s_, b_ = st[n % 3], sb[n % 3]
                kb.dma(s_[:].rearrange("p (c n) -> p c n", c=2), v[r0:r0 + 256, :].rearrange("(c p) n -> p c n", p=128))
                if n % 3 == 0:
                    kb.cp(b_[:], s_[:], e="dve")
                elif n % 3 == 1:
                    kb.act(b_[:], s_[:], AF.Copy)
                else:
                    kb.cp(b_[:], s_[:], e="pool")
                kb.dma(Vb[r0:r0 + 256, :].rearrange("(c p) n -> p c n", p=128), b_[:].rearrange("p (c n) -> p c n", c=2), q="pool")
                n += 1
            kb.barrier()
        with kb.scope():
            Wv = load_weights(kb, gr, w_q, 1024, 2048, gcol)
            kT = kb.sb("keysT", [128, 16 * 128], BF16)
            with kb.scope():
                ktmp = kb.sb("ktmp", [128, 16 * 128], F32)
                kb.dma(ktmp[:].rearrange("p (c k) -> p c k", c=16), keysT.rearrange("c d k -> d c k"))
                kb.cp(kT[:], ktmp[:])
                kb.barrier()
            kTv = kT[:].rearrange("p (c k) -> p c k", c=16)
            qT = [kb.sb("pqT%d" % i, [128, 512], BF16) for i in range(2)]
            psc = [kb.ps("psc%d" % i, [128, 512], F32) for i in range(2)]
            sc_all = kb.sb("sc_all", [128, 4 * 16 * 128], F32)
            scv = sc_all[:].rearrange("p (s c k) -> p s c k", s=4, c=16)
            a16 = kb.sb("a16", [128, 16], F32)
            b16 = kb.sb("b16", [128, 16], F32)
            c16 = kb.sb("c16", [128, 16], F32)
            e16 = kb.sb("e16", [128, 16], F32)
            t128 = kb.sb("t128", [128, 128], F32)
            cand = kb.sb("cand", [128, 256], F32)
            cand2 = kb.sb("cand2", [128, 256], F32)
            tau = kb.sb("tau", [128, 8], F32)
            nb = kb.sb("nb", [128, 8], F32)
            nb2 = kb.sb("nb2", [128, 8], F32)
            zz = kb.sb("zz", [128, 1], F32)
            Sc = [kb.sb("Sc%d" % i, [128, 2048], F32) for i in range(3)]
            Ec = [kb.sb("Ec%d" % i, [128, 2048], BF16) for i in range(3)]
            gcnt = [0]
            Gb = [kb.sb("Gb%d" % i, [128, 2048], BF16) for i in range(2)]
            cn = [0, 0, 0]
            V = nc.vector

            def top16(dst, src, tmp):
                kb.op("dve", lambda: V.max(out=dst[:, 0:8], in_=src), [src], [dst])
                kb.op("dve", lambda: V.match_replace(out=tmp, in_to_replace=dst[:, 0:8], in_values=src, imm_value=-1e30),
                      [dst, src], [tmp])
                kb.op("dve", lambda: V.max(out=dst[:, 8:16], in_=tmp), [tmp], [dst])

            def gates(tokb, sub):
                for hh in range(8):
                    s1 = scv[:, sub, 2 * hh, :]
                    s2 = scv[:, sub, 2 * hh + 1, :]
                    top16(a16, s1, t128[:])
                    top16(b16, s2, t128[:])
                    kb.tt(cand[:].rearrange("p (a b) -> p a b", a=16),
                          a16[:].unsqueeze(2).to_broadcast([128, 16, 16]),
                          b16[:].unsqueeze(1).to_broadcast([128, 16, 16]), ALU.add)
                    top16(c16, cand[:], cand2[:])
                    kb.cp(tau[:, hh:hh + 1], c16[:, 15:16])
                    kb.ts(nb[:, hh:hh + 1], c16[:, 0:1], -1.0, None, op0=ALU.mult)
                    kb.act(e16[:], c16[:], AF.Exp, bias=nb[:, hh:hh + 1])
                    kb.red(zz[:], e16[:])
                    kb.act(zz[:], zz[:], AF.Ln)
                    kb.tt(nb[:, hh:hh + 1], nb[:, hh:hh + 1], zz[:], ALU.subtract)
                items = [(c, hh) for c in range(8) for hh in range(8)]

                def stage1(n):
                    c, hh = items[n]
                    S_, E_ = Sc[n % 3], Ec[n % 3]
                    s1 = scv[:, sub, 2 * hh, 16 * c:16 * c + 16]
                    s2 = scv[:, sub, 2 * hh + 1, :]
                    S3 = S_[:].rearrange("p (a b) -> p a b", a=16)
                    kb.tt(S3, s1.unsqueeze(2).to_broadcast([128, 16, 128]),
                          s2.unsqueeze(1).to_broadcast([128, 16, 128]), ALU.add)
                    kb.act(E_[:], S_[:], AF.Exp, bias=nb[:, hh:hh + 1])

                def stage2(n):
                    c, hh = items[n]
                    S_, E_ = Sc[n % 3], Ec[n % 3]
                    G = Gb[(gcnt[0] + c) % 2]
                    dst = G if hh == 0 else E_
                    kb.stt(dst[:], S_[:], tau[:, hh:hh + 1], E_[:], ALU.is_ge, ALU.mult)
                    if hh > 0:
                        kb.tt(G[:], G[:], E_[:], ALU.add)
                    if hh == 7:
                        kb.dma(Gd[tokb:tokb + 128, c * 2048:(c + 1) * 2048], G[:], q="sp")

                stage1(0)
                for n in range(64):
                    if n + 1 < 64:
                        stage1(n + 1)
                    stage2(n)

            def q_epi(blk, tok0, c0, pss):
                qt = qT[c0 % 2]
                if c0 % 2 == 0:
                    kb.act(qt[:], pss[0][:], AF.Copy)
                else:
                    kb.cp(qt[:], pss[0][:])
                pc = psc[c0 % 2]
                for sub in range(4):
                    kb.mm(pc[:, sub * 128:(sub + 1) * 128], qt[:, sub * 128:(sub + 1) * 128], kTv[:, c0, :])
                src = pc[:].rearrange("p (s k) -> p s k", s=4)
                if c0 % 2 == 0:
                    kb.cp(scv[:, :, c0, :], src)
                else:
                    kb.act(scv[:, :, c0, :], src, AF.Copy)
                if c0 == 15:
                    for sub in range(4):
                        gates(tok0 + sub * 128, sub)

            segs = [dict(kind="FM", n0=0, n1=2048, epi=q_epi)]
            gemm_stage(kb, gr, h, T, 1024, Wv, segs, norm=True)
            kb.barrier()
        if getattr(self, 'skip_dense', False):
            return
        with kb.scope():
            ut = [kb.sb("ut%d" % i, [128, 8 * 512], BF16) for i in range(2)]
            vt = [kb.sb("vt%d" % i, [128, 4 * 1024], BF16) for i in range(2)]
            gt = [kb.sb("gt%d" % i, [128, 512], BF16) for i in range(3)]
            gel = [kb.sb("gel%d" % i, [128, 512], F32) for i in range(2)]
            gh = [kb.sb("gh%d" % i, [128, 512], BF16) for i in range(2)]
            ghT = [kb.sb("ghT%d" % i, [128, 512], BF16) for i in range(2)]
            acc = [kb.sb("acc%d" % i, [128, 1024], F32) for i in range(4)]
            php = [kb.ps("php%d" % i, [128, 512], F32) for i in range(2)]
            ptp = [kb.ps("ptp%d" % i, [128, 1024], BF16) for i in range(1)]
            pop = [kb.ps("pop%d" % i, [128, 512], F32) for i in range(4)]
            cn2 = [0]

            def dense(blk, xTv):
                tokb = blk * 512
                for ec in range(32):
                    u_ = ut[ec % 2]
                    v_ = vt[ec % 2]
                    uv = u_[:].rearrange("p (c e) -> p c e", c=8)
                    vv = v_[:].rearrange("p (c n) -> p c n", c=4)
                    kb.dma(uv, UTb[:, ec * 512:(ec + 1) * 512].rearrange("(c p) e -> p c e", p=128))
                    kb.dma(vv, Vb[ec * 512:(ec + 1) * 512, :].rearrange("(c p) n -> p c n", p=128))
                    for sub in range(4):
                        k = cn2[0]
                        cn2[0] += 1
                        g_ = gt[k % 3]
                        kb.dma(g_[:], Gd[tokb + sub * 128:tokb + sub * 128 + 128, ec * 512:(ec + 1) * 512])
                        ph = php[k % 2]
                        for kc in range(8):
                            kb.mm(ph[:], xTv[:, kc, sub * 128:(sub + 1) * 128], uv[:, kc, :], start=(kc == 0), stop=(kc == 7))
                        ge = gel[k % 2]
                        kb.act(ge[:], ph[:], AF.Gelu)
                        gh_ = gh[k % 2]
                        kb.tt(gh_[:], ge[:], g_[:], ALU.mult, e="pool")
                        pt = ptp[0]
                        for c4 in range(4):
                            kb.tr(pt[:, c4 * 128:(c4 + 1) * 128], gh_[:, c4 * 128:(c4 + 1) * 128], gr.ident[:])
                        gT = ghT[k % 2]
                        if k % 2 == 0:
                            kb.cp(gT[:], pt[:, 0:512])
                        else:
                            kb.act(gT[:], pt[:, 0:512], AF.Copy)
                        for nh in range(2):
                            po = pop[(k % 2) * 2 + nh]
                            for c4 in range(4):
                                kb.mm(po[:], gT[:, c4 * 128:(c4 + 1) * 128], vv[:, c4, nh * 512:(nh + 1) * 512],
                                      start=(c4 == 0), stop=(c4 == 3))
                            a_ = acc[sub][:, nh * 512:(nh + 1) * 512]
                            if ec == 0:
                                kb.cp(a_, po[:])
                            else:
                                kb.tt(a_, po[:], a_, ALU.add)
                for sub in range(4):
                    t0 = tokb + sub * 128
                    hs = gr.xin[sub % 3]
                    key = ("h", t0)
                    kb.dma(hs[:], h[t0:t0 + 128, :], rk=key)
                    kb.tt(acc[sub][:], acc[sub][:], hs[:], ALU.add, e="pool")
                    kb.dma(h[t0:t0 + 128, :], acc[sub][:], q="pool", wk=key)

            segs = [dict(kind="custom", fn=dense)]
            gemm_stage(kb, gr, h, T, 1024, None, segs, norm=True, n_psT=1, n_pm=0)
            kb.barrier()

    def ple(self, h, gcol, w_gate, p_i, w_proj):
        kb, gr, T = self.kb, self.gr, self.T
        PP = self.dscr("PP", [T, 1024], F32)
        with kb.scope():
            Wv = load_weights(kb, gr, w_proj, 256, 1024, None)
            stf = [kb.sb("ppst%d" % i, [128, 512], F32) for i in range(3)]
            cn = [0]

            def pp_epi(blk, tok0, sub, n0c, ncols, ps, xt):
                st = stf[cn[0] % 3]
                cn[0] += 1
                if cn[0] % 2 == 0:
                    kb.cp(st[:, 0:ncols], ps[:, 0:ncols])
                else:
                    kb.act(st[:, 0:ncols], ps[:, 0:ncols], AF.Copy)
                kb.dma(PP[tok0:tok0 + 128, n0c:n0c + ncols], st[:, 0:ncols], q="pool")
            gemm_stage(kb, gr, p_i, T, 256, Wv, [dict(kind="TM", n0=0, n1=1024, epi=pp_epi)])
            kb.barrier()
        with kb.scope():
            Wv = load_weights(kb, gr, w_gate, 1024, 1024, gcol)
            sg = [kb.sb("plg%d" % i, [128, 512], F32) for i in range(3)]
            sp_ = [kb.sb("plp%d" % i, [128, 512], F32) for i in range(3)]
            sh_ = [kb.sb("plh%d" % i, [128, 512], F32) for i in range(3)]
            cn = [0]

            def g_epi(blk, tok0, sub, n0c, ncols, ps, xt):
                k = cn[0] % 3
                cn[0] += 1
                key = ("h", tok0, n0c)
                kb.dma(sp_[k][:, 0:ncols], PP[tok0:tok0 + 128, n0c:n0c + ncols])
                kb.dma(sh_[k][:, 0:ncols], h[tok0:tok0 + 128, n0c:n0c + ncols], rk=key)
                kb.act(sg[k][:, 0:ncols], ps[:, 0:ncols], AF.Sigmoid)
                kb.tt(sg[k][:, 0:ncols], sg[k][:, 0:ncols], sp_[k][:, 0:ncols], ALU.mult, e="pool")
                kb.tt(sh_[k][:, 0:ncols], sh_[k][:, 0:ncols], sg[k][:, 0:ncols], ALU.add)
                kb.dma(h[tok0:tok0 + 128, n0c:n0c + ncols], sh_[k][:, 0:ncols], q="pool", wk=key)
            gemm_stage(kb, gr, h, T, 1024, Wv, [dict(kind="TM", n0=0, n1=1024, epi=g_epi)], norm=True)
            kb.barrier()

    def final(self, h, g, out):
        kb, gr, T = self.kb, self.gr, self.T
        with kb.scope():
            gb = kb.sb("fgb", [128, 1024], F32)
            kb.dma(gb[:], g.partition_broadcast(128))
            ot = [kb.sb("fot%d" % i, [128, 1024], F32) for i in range(2)]
            for it in range(T // 128):
                xt = gr.xin[it % 3]
                ssq = gr.ssq[it % 3]
                o_ = ot[it % 2]
                kb.dma(xt[:], h[it * 128:(it + 1) * 128, :])
                kb.act(gr.xsq[:], xt[:], AF.Square)
                kb.red(ssq[:], gr.xsq[:])
                kb.act(ssq[:], ssq[:], AF.Sqrt, bias=gr.epsc[:, 0:1], scale=1.0 / 1024)
                kb.recip(ssq[:], ssq[:])
                kb.stt(o_[:], xt[:], ssq[:, 0:1], gb[:], ALU.mult, ALU.mult)
                kb.dma(out[it * 128:(it + 1) * 128, :], o_[:], q="pool")
            kb.barrier()

    def diff_mixer(self, h, w_in, gcol, lam_vecs, lam_init, norm_g, w_out, BT, bfar, dmask):
        kb, gr, T, S, nseq = self.kb, self.gr, self.T, self.S, self.nseq
        QT = self.dscr("QT", [1024, T], BF16)
        KT = self.dscr("KT", [1024, T], BF16)
        V = self.dscr("Vtm", [T, 2048], BF16)
        O = self.dscr("Otm", [T, 2048], BF16)
        with kb.scope():
            Wv = load_weights(kb, gr, w_in, 1024, 3072, gcol)
            segs = [dict(kind="FM", n0=0, n1=1024, epi=self.epi.fm_store(QT, 0.125)),
                    dict(kind="FM", n0=1024, n1=2048, epi=self.epi.fm_store(KT, 1.0)),
                    dict(kind="TM", n0=2048, n1=3072, epi=self.epi.tm_store(V))]
            gemm_stage(kb, gr, h, T, 1024, Wv, segs, norm=True)
            kb.barrier()
        with kb.scope():
            spec = DiffSpec(self, S, QT, KT, V, O, lam_vecs, lam_init, norm_g, BT, bfar, dmask)
            attn_stage(kb, spec, S, nseq)
            kb.barrier()
        with kb.scope():
            Wv = load_weights(kb, gr, w_out, 1024, 1024, None)
            segs = [dict(kind="TM", n0=0, n1=1024, epi=self.epi.tm_resid(h))]
            gemm_stage(kb, gr, O[:, 0:1024], T, 1024, Wv, segs, x_bf16=True)
            kb.barrier()


class DiffSpec:
    npst = 4

    def __init__(self, m, S, QT, KT, V, O, lam_vecs, lam_init, norm_g, BT, bfar, dmask):
        kb = m.kb
        self.kb, self.gr, self.S = kb, m.gr, S
        self.QT, self.KT, self.V, self.O = QT, KT, V, O
        self.Qa = [[kb.sb("dQ%d_%d" % (i, mm), [64, S], BF16) for mm in range(2)] for i in range(2)]
        self.Ka = [[kb.sb("dK%d_%d" % (i, mm), [64, S], BF16) for mm in range(2)] for i in range(2)]
        self.Vt = [kb.sb("dV%d" % i, [128, (S // 128) * 129], BF16) for i in range(2)]
        for i in range(2):
            kb.memset(self.Vt[i][:], 1.0, e="pool")
        lv = kb.sb("dlv", [128, 256], F32)
        kb.dma(lv[:], lam_vecs.rearrange("a d -> (a d)").partition_broadcast(128))
        pr = kb.sb("dpr", [128, 128], F32)
        lvv = lv[:].rearrange("p (a d) -> p a d", a=4)
        prv = pr[:].rearrange("p (a d) -> p a d", a=2)
        kb.tt(prv[:, 0, :], lvv[:, 0, :], lvv[:, 1, :], ALU.mult)
        kb.tt(prv[:, 1, :], lvv[:, 2, :], lvv[:, 3, :], ALU.mult)
        l2 = kb.sb("dl2", [128, 2], F32)
        kb.red(l2[:, 0:1], prv[:, 0, :])
        kb.red(l2[:, 1:2], prv[:, 1, :])
        kb.act(l2[:], l2[:], AF.Exp)
        self.nlam = kb.sb("dnlam", [128, 1], F32)
        kb.tt(self.nlam[:], l2[:, 1:2], l2[:, 0:1], ALU.subtract)
        kb.ts(self.nlam[:], self.nlam[:], -float(lam_init), None, op0=ALU.add)
        self.gsc = kb.sb("dgsc", [128, 128], F32)
        kb.dma(self.gsc[:], norm_g.partition_broadcast(128))
        kb.ts(self.gsc[:], self.gsc[:], 1.0 - float(lam_init), None, op0=ALU.mult)
        self.Bhl = kb.sb("dBhl", [128, 8 * 2 * 2 * 128], BF16)
        self.Bv = self.Bhl[:].rearrange("p (h d s q) -> p h d s q", h=8, d=2, s=2)
        self.bfar = kb.sb("dbfar", [128, 8], F32)
        kb.dma(self.bfar[:], bfar.partition_broadcast(128))
        self.ones2 = kb.sb("dones2", [2, 128], BF16)
        kb.memset(self.ones2[:], 1.0)
        self.cfar = kb.sb("dcfar", [2, 8 * 128], BF16)
        with kb.scope():
            bt = kb.sb("dbt", [128, 8 * 2 * 128], F32)
            btv = bt[:].rearrange("p (h d q) -> p h d q", h=8, d=2)
            kb.dma(btv, BT.rearrange("h d k q -> k h d q"))
            dm = kb.sb("ddm", [128, 128], F32)
            kb.dma(dm[:], dmask)
            rr = kb.sb("drr", [128, 128], F32)
            for hh in range(8):
                kb.tt(btv[:, hh, 0, :], btv[:, hh, 0, :], dm[:], ALU.add)
                for d in range(2):
                    kb.cp(self.Bv[:, hh, d, 0, :], btv[:, hh, d, :])
                    kb.tt(rr[:], btv[:, hh, d, :], self.Bv[:, hh, d, 0, :], ALU.subtract)
                    kb.cp(self.Bv[:, hh, d, 1, :], rr[:])
            cf = kb.sb("dcf", [2, 8], F32)
            kb.dma(cf[0:1, :], bfar.rearrange("(o h) -> o h", o=1))
            kb.dma(cf[1:2, :], bfar.rearrange("(o h) -> o h", o=1))
            cfb = kb.sb("dcfb", [2, 8], BF16)
            kb.cp(cfb[:], cf[:])
            cr = kb.sb("dcr", [2, 8], F32)
            kb.tt(cr[:], cf[:], cfb[:], ALU.subtract)
            cfl = kb.sb("dcfl", [2, 8], BF16)
            kb.cp(cfl[:], cr[:])
            kb.dma(cfb[1:2, :], cfl[1:2, :])
            cfv = self.cfar[:].rearrange("p (h q) -> p h q", h=8)
            kb.cp(cfv, cfb[:].unsqueeze(2).to_broadcast([2, 8, 128]))
            kb.barrier()
        self.rz = [kb.sb("drz%d" % i, [128, 2], F32) for i in range(3)]
        self.o = [kb.sb("do%d" % i, [128, 128], F32) for i in range(3)]
        self.sq = kb.sb("dsq", [128, 128], F32)
        self.ss = [kb.sb("dss%d" % i, [128, 1], F32) for i in range(3)]
        self.ost = [kb.sb("dost%d" % i, [128, 128], BF16) for i in range(3)]
        self.pi = 0

    def groups(self, seq):
        return [(seq, h) for h in range(8)]

    def subheads(self, g):
        return [0, 1]

    def load(self, g, buf):
        kb, S = self.kb, self.S
        seq, h = g
        t0 = seq * S
        for mm in range(2):
            r0 = 128 * h + 64 * mm
            kb.dma(self.Qa[buf][mm][:], self.QT[r0:r0 + 64, t0:t0 + S])
            kb.dma(self.Ka[buf][mm][:], self.KT[r0:r0 + 64, t0:t0 + S])
        vt = self.Vt[buf][:].rearrange("p (b d) -> p b d", d=129)
        kb.dma(vt[:, :, 0:128], self.V[t0:t0 + S, 128 * h:128 * h + 128].rearrange("(b p) d -> p b d", p=128))

    def qk(self, g, sh, j, i, ps, buf):
        kb = self.kb
        seq, h = g
        r = i - 4 * j
        kb.mm(ps[:], self.Ka[buf][sh][:, i * 128:(i + 1) * 128], self.Qa[buf][sh][:, j * 512:(j + 1) * 512],
              start=True, stop=(r < -1))
        if r >= -1:
            for qb in range(4):
                d = r - qb
                o_ = ps[:, qb * 128:(qb + 1) * 128]
                if d <= -2:
                    kb.mm(o_, self.ones2[:], self.cfar[:, h * 128:(h + 1) * 128], start=False, stop=False)
                elif d <= 0:
                    kb.mm(o_, self.gr.ident[:], self.Bv[:, h, -d, 0, :], start=False, stop=False)
                    kb.mm(o_, self.gr.ident[:], self.Bv[:, h, -d, 1, :], start=False, stop=False)

    def evac(self, g, sh, j, i, ps, P, buf):
        seq, h = g
        if i - 4 * j < -1:
            self.kb.act(P[:], ps[:], AF.Exp, bias=self.bfar[:, h:h + 1])
        else:
            self.kb.act(P[:], ps[:], AF.Exp)

    def ocols(self, g, sh):
        return 256 * sh, 129

    def vblk(self, g, sh, i, buf):
        return self.Vt[buf][:, i * 129:(i + 1) * 129]

    def post(self, g, j, outs, buf):
        kb, S = self.kb, self.S
        seq, h = g
        for qb in range(4):
            k = self.pi % 3
            self.pi += 1
            rz, o, ss, st = self.rz[k], self.o[k], self.ss[k], self.ost[k]
            ob = outs[qb]
            kb.recip(rz[:, 0:1], ob[:, 128:129])
            kb.recip(rz[:, 1:2], ob[:, 384:385])
            kb.tt(rz[:, 1:2], rz[:, 1:2], self.nlam[:], ALU.mult)
            kb.ts(o[:], ob[:, 0:128], rz[:, 0:1], None, op0=ALU.mult)
            kb.stt(o[:], ob[:, 256:384], rz[:, 1:2], o[:], ALU.mult, ALU.add)
            kb.tt(self.sq[:], o[:], o[:], ALU.mult, e="pool")
            kb.red(ss[:], self.sq[:])
            kb.act(ss[:], ss[:], AF.Sqrt, bias=self.gr.epsc[:, 0:1], scale=1.0 / 128)
            kb.recip(ss[:], ss[:])
            kb.stt(st[:], o[:], ss[:, 0:1], self.gsc[:], ALU.mult, ALU.mult)
            tok = seq * S + (4 * j + qb) * 128
            kb.dma(self.O[tok:tok + 128, 128 * h:128 * h + 128], st[:], q="pool")


def _t5_bucket_np(rel):
    nb = 16
    max_exact = 8
    ret = (rel > 0).astype(np.int64) * nb
    n = np.abs(rel)
    nf = np.maximum(n, 1).astype(np.float32)
    large = max_exact + (np.log(nf / max_exact) / math.log(128 / max_exact) * (nb - max_exact)).astype(np.int32)
    large = np.minimum(large, nb - 1)
    return ret + np.where(n < max_exact, n, large)


def diff_tables(rel_bias):
    kl = np.arange(128)[:, None]
    ql = np.arange(128)[None, :]
    idx0 = _t5_bucket_np(kl - ql)
    idx1 = _t5_bucket_np(kl - ql - 128)
    rb = np.asarray(rel_bias, dtype=np.float32)
    BT = np.stack([rb[idx0], rb[idx1]], axis=0)
    BT = np.ascontiguousarray(BT.transpose(3, 0, 1, 2))
    bfar = np.ascontiguousarray(rb[15, :])
    dmask = np.where((kl >= 64) & (ql < 64), NEG, 0.0).astype(np.float32)
    return BT, bfar, dmask


def cmask_np():
    k = np.arange(128)[:, None]
    q = np.arange(512)[None, :]
    return np.concatenate([np.where(128 * r + k <= q, 0.0, NEG) for r in range(4)], axis=1).astype(np.float32)


RET_LG = [math.log1p(-2.0 ** (-5.0 - h)) for h in range(4)]


def ret_tables(S):
    d = np.arange(0, 256, 2, dtype=np.float32) / np.float32(256.0)
    inv = (1.0 / (np.float32(10000.0) ** d)).astype(np.float32)
    ang = np.arange(S, dtype=np.float32)[None, :] * inv[:, None]
    cosT = np.cos(ang).astype(np.float32)
    sinT = np.sin(ang).astype(np.float32)
    t = np.arange(512)
    dq = np.stack([np.exp(RET_LG[h] * t) for h in range(4)]).astype(np.float32)
    dk = np.stack([np.exp(-RET_LG[h] * (t % 128)) / 16.0 for h in range(4)]).astype(np.float32)
    kl = np.arange(128)[:, None]
    q = np.arange(512)[None, :]
    mret = np.zeros((4, 4, 128, 512), np.float32)
    for h in range(4):
        for r in range(4):
            k = 128 * r + kl
            ck, cq = k // 64, q // 64
            f = np.where(ck < cq, 1.0, np.where(ck > cq, 0.0, np.where(k <= q, 1.0, np.exp(2.0 * RET_LG[h] * (k - q)))))
            mret[h, r] = f * np.exp(-RET_LG[h] * 128.0 * r)
    return cosT, sinT, dq, dk, mret


class RetSpec:
    nbuf = 1
    npst = 4

    def __init__(self, m, S, QT, KT, V, GS, O, norm_g, mret):
        kb = m.kb
        self.kb, self.gr, self.S = kb, m.gr, S
        self.QT, self.KT, self.V, self.GS, self.O = QT, KT, V, GS, O
        self.Q = [kb.sb("rQ%d" % c, [128, S], BF16) for c in range(2)]
        self.K = [kb.sb("rK%d" % c, [128, S], BF16) for c in range(2)]
        self.Vt = kb.sb("rV", [128, (S // 128) * 512], BF16)
        self.M = kb.sb("rM", [128, 16 * 512], F32)
        kb.dma(self.M[:].rearrange("p (a q) -> p a q", a=16), mret.rearrange("h r k q -> k (h r) q"))
        self.gb = kb.sb("rgb", [128, 2048], F32)
        kb.dma(self.gb[:], norm_g.partition_broadcast(128))
        self.sq = kb.sb("rsq", [128, 512], F32)
        self.ss = [kb.sb("rss%d" % i, [128, 1], F32) for i in range(2)]
        self.on = [kb.sb("ron%d" % i, [128, 512], F32) for i in range(2)]
        self.gs = [kb.sb("rgs%d" % i, [128, 512], BF16) for i in range(2)]
        self.ost = [kb.sb("rost%d" % i, [128, 512], BF16) for i in range(2)]
        self.pi = 0

    def groups(self, seq):
        return [(seq, h) for h in range(4)]

    def subheads(self, g):
        return [0]

    def load(self, g, buf):
        kb, S = self.kb, self.S
        seq, h = g
        t0 = seq * S
        for c in range(2):
            r0 = 256 * h + 128 * c
            kb.dma(self.Q[c][:], self.QT[r0:r0 + 128, t0:t0 + S])
            kb.dma(self.K[c][:], self.KT[r0:r0 + 128, t0:t0 + S])
        kb.dma(self.Vt[:].rearrange("p (b d) -> p b d", d=512),
               self.V[t0:t0 + S, 512 * h:512 * h + 512].rearrange("(b p) d -> p b d", p=128))

    def qk(self, g, sh, j, i, ps, buf):
        kb = self.kb
        for c in range(2):
            kb.mm(ps[:], self.K[c][:, i * 128:(i + 1) * 128], self.Q[c][:, j * 512:(j + 1) * 512],
                  start=(c == 0), stop=(c == 1))

    def evac(self, g, sh, j, i, ps, P, buf):
        seq, h = g
        r = i - 4 * j
        if r < 0:
            self.kb.act(P[:], ps[:], AF.Copy, scale=float(math.exp(RET_LG[h] * (512 * j - 128 * i))))
        else:
            a = h * 4 + r
            self.kb.tt(P[:], ps[:], self.M[:, a * 512:(a + 1) * 512], ALU.mult)

    def ocols(self, g, sh):
        return 0, 512

    def vblk(self, g, sh, i, buf):
        return self.Vt[:, i * 512:(i + 1) * 512]

    def post(self, g, j, outs, buf):
        kb, S = self.kb, self.S
        seq, h = g
        for qb in range(4):
            k = self.pi % 2
            self.pi += 1
            ob = outs[qb]
            ss, on, gs, st = self.ss[k], self.on[k], self.gs[k], self.ost[k]
            tok = seq * S + (4 * j + qb) * 128
            kb.dma(gs[:], self.GS[tok:tok + 128, 512 * h:512 * h + 512])
            kb.act(self.sq[:], ob[:], AF.Square)
            kb.red(ss[:], self.sq[:])
            kb.act(ss[:], ss[:], AF.Sqrt, bias=self.gr.epsc[:, 0:1], scale=1.0 / 512)
            kb.recip(ss[:], ss[:])
            kb.stt(on[:], ob[:], ss[:, 0:1], self.gb[:, 512 * h:512 * h + 512], ALU.mult, ALU.mult)
            kb.tt(st[:], on[:], gs[:], ALU.mult, e="pool")
            kb.dma(self.O[tok:tok + 128, 512 * h:512 * h + 512], st[:], q="pool")


def ret_mixer(self, h, w_in, gcol, norm_g, w_out, cosT, sinT, dq, dk, mret):
    kb, gr, T, S, nseq = self.kb, self.gr, self.T, self.S, self.nseq
    QT = self.dscr("QT", [1024, T], BF16)
    KT = self.dscr("KT", [1024, T], BF16)
    V = self.dscr("Vtm", [T, 2048], BF16)
    GS = self.dscr("GStm", [T, 2048], BF16)
    O = self.dscr("Otm", [T, 2048], BF16)
    with kb.scope():
        Wv = load_weights(kb, gr, w_in, 1024, 6144, gcol)
        cs = [kb.sb("rcs%d" % i, [128, 1024], F32) for i in range(2)]
        dtab = kb.sb("rdtab", [128, 2 * 4 * 512], F32)
        dtv = dtab[:].rearrange("p (a h t) -> p a h t", a=2, h=4)
        kb.dma(dtv[:, 0], dq.partition_broadcast(128))
        kb.dma(dtv[:, 1], dk.partition_broadcast(128))
        tmp = [kb.sb("rtmp%d" % i, [128, 512], F32) for i in range(4)]
        yst = [kb.sb("ryst%d" % i, [128, 512], BF16) for i in range(4)]
        state = dict(blk=-1, n=0)

        def rot_epi(which, dst):
            def epi(blk, tok0, c0, pss):
                if state["blk"] != blk:
                    state["blk"] = blk
                    p0 = tok0 % S
                    c_ = cs[blk % 2]
                    kb.dma(c_[:, 0:512], cosT[:, p0:p0 + 512])
                    kb.dma(c_[:, 512:1024], sinT[:, p0:p0 + 512])
                c_ = cs[blk % 2]
                cosb, sinb = c_[:, 0:512], c_[:, 512:1024]
                hh = c0 // 2
                d_ = dtv[:, which, hh, :]
                x1, x2 = pss[0], pss[1]
                n = state["n"]
                state["n"] += 1
                ta, tb = tmp[(n % 2) * 2], tmp[(n % 2) * 2 + 1]
                y1, y2 = yst[(n % 2) * 2], yst[(n % 2) * 2 + 1]
                kb.tt(ta[:], x1[:], cosb, ALU.mult)
                kb.tt(tb[:], x2[:], sinb, ALU.mult)
                kb.tt(ta[:], ta[:], tb[:], ALU.subtract, e="pool")
                kb.tt(y1[:], ta[:], d_, ALU.mult, e="pool")
                kb.dma(dst[256 * hh:256 * hh + 128, tok0:tok0 + 512], y1[:], q="sp")
                kb.tt(ta[:], x1[:], sinb, ALU.mult)
                kb.tt(tb[:], x2[:], cosb, ALU.mult)
                kb.tt(ta[:], ta[:], tb[:], ALU.add, e="pool")
                kb.tt(y2[:], ta[:], d_, ALU.mult, e="pool")
                kb.dma(dst[256 * hh + 128:256 * hh + 256, tok0:tok0 + 512], y2[:], q="sp")
            return epi

        segs = [dict(kind="FM", n0=0, n1=1024, group=2, epi=rot_epi(0, QT)),
                dict(kind="FM", n0=1024, n1=2048, group=2, epi=rot_epi(1, KT)),
                dict(kind="TM", n0=2048, n1=4096, epi=self.epi.tm_store(V)),
                dict(kind="TM", n0=4096, n1=6144, epi=self.epi.tm_store(GS, func=AF.Silu))]
        gemm_stage(kb, gr, h, T, 1024, Wv, segs, norm=True)
        kb.barrier()
    with kb.scope():
        spec = RetSpec(self, S, QT, KT, V, GS, O, norm_g, mret)
        attn_stage(kb, spec, S, nseq)
        kb.barrier()
    with kb.scope():
        Wv = load_weights(kb, gr, w_out, 2048, 1024, None)
        segs = [dict(kind="TM", n0=0, n1=1024, epi=self.epi.tm_resid(h))]
        gemm_stage(kb, gr, O, T, 2048, Wv, segs, x_bf16=True)
        kb.barrier()


Model.ret_mixer = ret_mixer


def ssd_tables():
    sa = np.zeros((48, 1024), np.float32)
    for hh in range(8):
        sa[6 * hh:6 * hh + 3, hh * 128:(hh + 1) * 128] = 1.0
    return sa


class SsdSpec:
    nbuf = 1

    def __init__(self, m, S, AQK, BTd, CTd, XSd, DTd, ZS, O, dsk_rep, norm_g, sa_init):
        kb = m.kb
        self.kb, self.gr, self.S, self.m = kb, m.gr, S, m
        self.AQK, self.BTd, self.CTd, self.XSd, self.DTd, self.ZS, self.O = AQK, BTd, CTd, XSd, DTd, ZS, O
        nb = S // 128
        self.nb = nb
        self.TA = kb.sb("sTA", [48, S], BF16)
        kb.memset(self.TA[:], 1.0)
        self.SA = [kb.sb("sSA%d" % i, [48, 1024], BF16) for i in range(3)]
        with kb.scope():
            t = kb.sb("ssat", [48, 1024], F32)
            kb.dma(t[:], sa_init)
            for i in range(3):
                kb.cp(self.SA[i][:], t[:])
            kb.barrier()
        self.Bt = kb.sb("sBt", [128, S], BF16)
        self.Ct = kb.sb("sCt", [128, S], BF16)
        self.XS = kb.sb("sXS", [128, nb * 512], BF16)
        self.XP = kb.sb("sXP", [128, nb * 512], BF16)
        self.DTt = kb.sb("sDT", [128, nb * 8], F32)
        self.E = [kb.sb("sE%d" % i, [128, 512], F32) for i in range(2)]
        self.cbp = [kb.ps("scb%d" % i, [128, 512], F32) for i in range(2)]
        self.dsk = kb.sb("sdsk", [128, 2048], F32)
        kb.dma(self.dsk[:], dsk_rep.partition_broadcast(128))
        self.gn = kb.sb("sgn", [128, 2048], F32)
        kb.dma(self.gn[:], norm_g.partition_broadcast(128))
        self.y1 = [kb.sb("sy1%d" % i, [128, 512], F32) for i in range(2)]
        self.y3 = [kb.sb("sy3%d" % i, [128, 512], F32) for i in range(2)]
        self.zs = [kb.sb("szs%d" % i, [128, 512], BF16) for i in range(2)]
        self.ost = [kb.sb("sost%d" % i, [128, 512], BF16) for i in range(2)]
        self.sq = kb.sb("ssq", [128, 512], F32)
        self.ss = [kb.sb("sss%d" % i, [128, 1], F32) for i in range(2)]
        self.pi = 0
        self.ni = 0
        self.ne = 0

    def groups(self, seq):
        return [(seq, g) for g in range(4)]

    def subheads(self, g):
        return list(range(8))

    def load(self, g, buf):
        kb, S, nb = self.kb, self.S, self.nb
        seq, gg = g
        t0 = seq * S
        for hh in range(8):
            kb.dma(self.TA[6 * hh:6 * hh + 3, :], self.AQK[gg * 8 + hh, 0:3, t0:t0 + S])
        kb.dma(self.Bt[:], self.BTd[gg * 128:(gg + 1) * 128, t0:t0 + S])
        kb.dma(self.Ct[:], self.CTd[gg * 128:(gg + 1) * 128, t0:t0 + S])
        kb.dma(self.XS[:].rearrange("p (b d) -> p b d", d=512),
               self.XSd[t0:t0 + S, gg * 512:(gg + 1) * 512].rearrange("(b p) d -> p b d", p=128))
        kb.dma(self.DTt[:].rearrange("p (b h) -> p b h", h=8),
               self.DTd[t0:t0 + S, gg * 8:(gg + 1) * 8].rearrange("(b p) h -> p b h", p=128))
        kb.tt(self.XP[:].rearrange("p (a d) -> p a d", d=64), self.XS[:].rearrange("p (a d) -> p a d", d=64),
              self.DTt[:].unsqueeze(2).to_broadcast([128, nb * 8, 64]), ALU.mult)

    def pre(self, g, j, i, buf):
        kb, S = self.kb, self.S
        seq, gg = g
        t0 = seq * S
        sa = self.SA[self.ni % 3]
        cb = self.cbp[self.ni % 2]
        self.ni += 1
        for hh in range(8):
            kb.dma(sa[6 * hh + 3:6 * hh + 6, hh * 128:(hh + 1) * 128],
                   self.AQK[gg * 8 + hh, 3:6, t0 + i * 128:t0 + (i + 1) * 128])
        kb.mm(cb[:], self.Bt[:, i * 128:(i + 1) * 128], self.Ct[:, j * 512:(j + 1) * 512])
        return (sa, cb)

    def qk(self, g, sh, j, i, ps, buf):
        kb = self.kb
        sa, cb = self.cur
        r = i - 4 * j
        kb.mm(ps[:], sa[:, sh * 128:(sh + 1) * 128], self.TA[:, j * 512:(j + 1) * 512], start=True, stop=(r < 0))
        if r >= 0:
            kb.mm(ps[:], self.gr.ident[:], self.m.cmask[:, r * 512:(r + 1) * 512], start=False, stop=True)

    def evac(self, g, sh, j, i, ps, P, buf):
        kb = self.kb
        sa, cb = self.cur
        E = self.E[self.ne % 2]
        self.ne += 1
        kb.act(E[:], ps[:], AF.Exp)
        kb.tt(P[:], E[:], cb[:], ALU.mult)

    def ocols(self, g, sh):
        return sh * 64, 64

    def vblk(self, g, sh, i, buf):
        return self.XP[:, i * 512 + sh * 64:i * 512 + sh * 64 + 64]

    def post(self, g, j, outs, buf):
        kb, S = self.kb, self.S
        seq, gg = g
        for qb in range(4):
            k = self.pi % 2
            self.pi += 1
            b = 4 * j + qb
            tok = seq * S + b * 128
            y1, y3, zs, st, ss = self.y1[k], self.y3[k], self.zs[k], self.ost[k], self.ss[k]
            kb.dma(zs[:], self.ZS[tok:tok + 128, gg * 512:(gg + 1) * 512])
            kb.tt(y1[:], self.XS[:, b * 512:(b + 1) * 512], self.dsk[:, gg * 512:(gg + 1) * 512], ALU.mult, e="pool")
            kb.tt(y1[:], outs[qb][:], y1[:], ALU.add)
            kb.tt(y3[:], y1[:], zs[:], ALU.mult, e="pool")
            kb.act(self.sq[:], y3[:], AF.Square)
            kb.red(ss[:], self.sq[:])
            kb.act(ss[:], ss[:], AF.Sqrt, bias=self.gr.epsc[:, 0:1], scale=1.0 / 512)
            kb.recip(ss[:], ss[:])
            kb.stt(st[:], y3[:], ss[:, 0:1], self.gn[:, gg * 512:(gg + 1) * 512], ALU.mult, ALU.mult)
            kb.dma(self.O[tok:tok + 128, gg * 512:(gg + 1) * 512], st[:], q="pool")


def ssd_mixer(self, h, w_in, gcol, conv_wT, conv_b, dt_bias, a_log, dsk_rep, norm_g, w_out, sa_init):
    kb, gr, T, S, nseq = self.kb, self.gr, self.T, self.S, self.nseq
    nc = self.nc
    ZS = self.dscr("GStm", [T, 2048], BF16)
    XSd = self.dscr("Vtm", [T, 2048], BF16)
    BTd = self.dscr("QT", [1024, T], BF16)
    CTd = self.dscr("KT", [1024, T], BF16)
    AQK = self.dscr("CQK", [32, 6, T], BF16)
    DTd = self.dscr("DTd", [T, 32], F32)
    O = self.dscr("Otm", [T, 2048], BF16)
    bps = S // 512
    with kb.scope():
        Wv = load_weights(kb, gr, w_in, 1024, 5152, gcol)
        cw = kb.sb("scw", [128, 24 * 4], F32)
        cwv = cw[:].rearrange("p (c k) -> p c k", k=4)
        kb.dma(cwv, conv_wT.rearrange("(c p) k -> p c k", p=128))
        cbias = kb.sb("scbias", [128, 24], F32)
        kb.dma(cbias[:], conv_b.rearrange("(c p) -> p c", p=128))
        hal = kb.sb("shal", [128, 24 * 3], F32)
        cbuf = [kb.sb("scbuf%d" % i, [128, 515], F32) for i in range(2)]
        accb = [kb.sb("saccb%d" % i, [128, 512], F32) for i in range(2)]
        sil = [kb.sb("ssil%d" % i, [128, 512], BF16) for i in range(2)]
        xst = [kb.sb("sxst%d" % i, [128, 512], BF16) for i in range(2)]
        ptx = kb.ps("sptx", [128, 1024], BF16)
        pdt = kb.ps("spdt", [128, 512], F32)
        dtb = kb.sb("sdtb", [32, 1], F32)
        kb.dma(dtb[:], dt_bias.rearrange("(p o) -> p o", o=1))
        nA = kb.sb("snA", [32, 1], F32)
        kb.dma(nA[:], a_log.rearrange("(p o) -> p o", o=1))
        kb.act(nA[:], nA[:], AF.Exp)
        kb.ts(nA[:], nA[:], -1.0, None, op0=ALU.mult)
        carry = kb.sb("scarry", [32, 1], F32)
        d_ = [kb.sb("sd%d" % i, [32, 512], F32) for i in range(5)]
        cum = kb.sb("scum", [32, 512], F32)
        rr = kb.sb("srr", [32, 512], F32)
        spl = kb.sb("sspl", [32, 6 * 512], BF16)
        splv = spl[:].rearrange("p (s n) -> p s n", s=6)
        dtm = kb.sb("sdtm", [128, 4 * 32], F32)
        cn = [0]

        def conv_epi(blk, tok0, c, pss):
            ps = pss[0]
            n = cn[0]
            cn[0] += 1
            cb, acc, sl = cbuf[n % 2], accb[n % 2], sil[n % 2]
            if blk % bps == 0:
                kb.memset(cb[:, 0:3], 0.0)
            else:
                kb.cp(cb[:, 0:3], hal[:, 3 * c:3 * c + 3])
            kb.act(cb[:, 3:515], ps[:], AF.Copy)
            kb.cp(hal[:, 3 * c:3 * c + 3], cb[:, 512:515])
            kb.ts(acc[:], cb[:, 0:512], cwv[:, c, 0:1], cbias[:, c:c + 1], op0=ALU.mult, op1=ALU.add)
            for k in range(1, 4):
                kb.stt(acc[:], cb[:, k:k + 512], cwv[:, c, k:k + 1], acc[:], ALU.mult, ALU.add)
            kb.act(sl[:], acc[:], AF.Silu)
            if c < 16:
                for s_ in range(4):
                    kb.tr(ptx[:, s_ * 128:(s_ + 1) * 128], sl[:, s_ * 128:(s_ + 1) * 128], gr.ident[:])
                xs_ = xst[n % 2]
                kb.cp(xs_[:], ptx[:, 0:512])
                kb.dma(XSd[tok0:tok0 + 512, c * 128:(c + 1) * 128].rearrange("(s p) ch -> p s ch", p=128),
                       xs_[:].rearrange("p (s ch) -> p s ch", s=4), q="pool")
            elif c < 20:
                kb.dma(BTd[(c - 16) * 128:(c - 15) * 128, tok0:tok0 + 512], sl[:], q="pool")
            else:
                kb.dma(CTd[(c - 20) * 128:(c - 19) * 128, tok0:tok0 + 512], sl[:], q="pool")

        def dt_epi(blk, tok0, c0, pss):
            ps = pss[0]
            xb, ab, e_, r_, dtt = d_
            if blk % bps == 0:
                kb.memset(carry[:], 0.0)
            kb.act(xb[:], ps[0:32, :], AF.Identity, bias=dtb[:, 0:1])
            kb.act(ab[:], xb[:], AF.Abs)
            kb.act(e_[:], ab[:], AF.Exp, scale=-1.0)
            kb.act(e_[:], e_[:], AF.Ln, bias=self.one[0:32, 0:1])
            kb.ts(r_[:], xb[:], 0.0, None, op0=ALU.max)
            kb.tt(dtt[:], r_[:], e_[:], ALU.add)
            kb.ts(ab[:], dtt[:], nA[:, 0:1], None, op0=ALU.mult)
            kb.op("dve", lambda: nc.vector.tensor_tensor_scan(cum[:], ab[:], self.zeros[0:32, :], carry[:, 0:1], ALU.add, ALU.add),
                  [ab, self.zeros, carry], [cum])
            kb.cp(carry[:], cum[:, 511:512])
            split3(kb, cum[:], splv, rr, None, 32, 512)
            kb.dma(AQK[0:32, :, tok0:tok0 + 512], splv, q="pool")
            for s_ in range(4):
                kb.tr(pdt[:, s_ * 32:(s_ + 1) * 32], dtt[:, s_ * 128:(s_ + 1) * 128], gr.ident_f[0:32, 0:32])
            kb.cp(dtm[:], pdt[:, 0:128])
            kb.dma(DTd[tok0:tok0 + 512, :].rearrange("(s p) h -> p s h", p=128), dtm[:].rearrange("p (s h) -> p s h", s=4), q="pool")

        segs = [dict(kind="TM", n0=0, n1=2048, epi=self.epi.tm_store(ZS, func=AF.Silu)),
                dict(kind="FM", n0=2048, n1=5120, epi=conv_epi),
                dict(kind="FM", n0=5120, n1=5152, epi=dt_epi)]
        gemm_stage(kb, gr, h, T, 1024, Wv, segs, norm=True)
        kb.barrier()
    with kb.scope():
        spec = SsdSpec(self, S, AQK, BTd, CTd, XSd, DTd, ZS, O, dsk_rep, norm_g, sa_init)
        attn_stage(kb, spec, S, nseq)
        kb.barrier()
    with kb.scope():
        Wv = load_weights(kb, gr, w_out, 2048, 1024, None)
        segs = [dict(kind="TM", n0=0, n1=1024, epi=self.epi.tm_resid(h))]
        gemm_stage(kb, gr, O, T, 2048, Wv, segs, x_bf16=True)
        kb.barrier()


Model.ssd_mixer = ssd_mixer


DEPTH = 4
SEQ = 4096
NSEQ = 2


def build_program(nseq=NSEQ, S=SEQ):
    T = nseq * S
    nc = bass.Bass("TRN2", target_bir_lowering=False)
    es = ExitStack()
    m = Model(nc, es, nseq, S)
    I = m.inp
    I("ident", [128, 128]); I("cmask", [128, 2048])
    x = I("x", [T, 1024]); p = I("p", [DEPTH, T, 256])
    mix_norm = I("mix_norm", [DEPTH, 1024]); ffn_norm = I("ffn_norm", [DEPTH, 1024]); ple_norm = I("ple_norm", [DEPTH, 1024])
    final_norm = I("final_norm", [1024])
    ssd_w_in = I("ssd_w_in", [1024, 5152]); ssd_cwT = I("ssd_cwT", [3072, 4]); ssd_cb = I("ssd_cb", [3072])
    ssd_dtb = I("ssd_dtb", [32]); ssd_alog = I("ssd_alog", [32]); ssd_dsk = I("ssd_dsk", [2048]); ssd_norm = I("ssd_norm", [2048])
    ssd_w_out = I("ssd_w_out", [2048, 1024]); ssd_sa = I("ssd_sa", [48, 1024])
    ret_w_in = I("ret_w_in", [1024, 6144]); ret_norm = I("ret_norm", [2048]); ret_w_out = I("ret_w_out", [2048, 1024])
    ret_cos = I("ret_cos", [128, S]); ret_sin = I("ret_sin", [128, S]); ret_dq = I("ret_dq", [4, 512]); ret_dk = I("ret_dk", [4, 512])
    ret_m = I("ret_m", [4, 4, 128, 512])
    diff_w_in = I("diff_w_in", [1024, 3072]); diff_lam = I("diff_lam", [4, 64]); diff_norm = I("diff_norm", [128])
    diff_w_out = I("diff_w_out", [1024, 1024]); diff_BT = I("diff_BT", [8, 2, 128, 128]); diff_bfar = I("diff_bfar", [8])
    diff_mask = I("diff_mask", [128, 128])
    fox_w_in = I("fox_w_in", [1024, 3088]); fox_bf = I("fox_bf", [16]); fox_w_out = I("fox_w_out", [1024, 1024])
    peer_wq = I("peer_wq", [DEPTH, 1024, 2048]); peer_kT = I("peer_kT", [DEPTH, 16, 128, 128])
    peer_uT = I("peer_uT", [DEPTH, 1024, 16384]); peer_v = I("peer_v", [DEPTH, 16384, 1024])
    ple_proj = I("ple_proj", [DEPTH, 256, 1024]); ple_gate = I("ple_gate", [DEPTH, 1024, 1024])
    out = nc.dram_tensor("out", [T, 1024], F32, kind="ExternalOutput").ap()
    hb = m.dscr("hbuf", [T, 1024], F32)
    m.setup()
    kb = m.kb
    for r0 in range(0, T, 512):
        kb.dma(hb[r0:r0 + 512, :], x[r0:r0 + 512, :])
    kb.barrier()
    for i in range(DEPTH):
        if i == 0:
            m.ssd_mixer(hb, ssd_w_in, mix_norm[i], ssd_cwT, ssd_cb, ssd_dtb, ssd_alog, ssd_dsk, ssd_norm, ssd_w_out, ssd_sa)
        elif i == 1:
            m.ret_mixer(hb, ret_w_in, mix_norm[i], ret_norm, ret_w_out, ret_cos, ret_sin, ret_dq, ret_dk, ret_m)
        elif i == 2:
            lam_init = 0.8 - 0.6 * math.exp(-0.3 * i)
            m.diff_mixer(hb, diff_w_in, mix_norm[i], diff_lam, lam_init, diff_norm, diff_w_out, diff_BT, diff_bfar, diff_mask)
        else:
            m.fox_mixer(hb, fox_w_in, mix_norm[i], fox_bf, fox_w_out)
        m.peer(hb, ffn_norm[i], peer_wq[i], peer_kT[i], peer_uT[i], peer_v[i])
        m.ple(hb, ple_norm[i], ple_gate[i], p[i], ple_proj[i])
    m.final(hb, final_norm, out)
    es.close()
    return nc, m


def host_inputs(inputs, nseq=NSEQ, S=SEQ, ncores=NCORES):
    f = lambda a: np.ascontiguousarray(np.asarray(a, dtype=np.float32))
    g = {k: np.asarray(v) for k, v in inputs.items()}
    BT, bfar, dmask = diff_tables(g["rel_bias"])
    cosT, sinT, dq, dk, mret = ret_tables(S)
    shared = {
        "ident": np.eye(128, dtype=np.float32), "cmask": cmask_np(),
        "mix_norm": f(g["mix_norm"]), "ffn_norm": f(g["ffn_norm"]), "ple_norm": f(g["ple_norm"]), "final_norm": f(g["final_norm"]),
        "ssd_w_in": f(g["ssd_w_in"][0]), "ssd_cwT": f(g["ssd_conv_w"][0][:, 0, :].T), "ssd_cb": f(g["ssd_conv_b"][0]),
        "ssd_dtb": f(g["ssd_dt_bias"][0]), "ssd_alog": f(g["ssd_a_log"][0]), "ssd_dsk": f(np.repeat(g["ssd_d"][0], 64)),
        "ssd_norm": f(g["ssd_norm"][0]), "ssd_w_out": f(g["ssd_w_out"][0]), "ssd_sa": ssd_tables(),
        "ret_w_in": f(g["ret_w_in"][0]), "ret_norm": f(g["ret_norm"][0]), "ret_w_out": f(g["ret_w_out"][0]),
        "ret_cos": cosT, "ret_sin": sinT, "ret_dq": dq, "ret_dk": dk, "ret_m": mret,
        "diff_w_in": f(g["diff_w_in"][0]), "diff_lam": f(g["diff_lambda"][0]), "diff_norm": f(g["diff_norm"][0]),
        "diff_w_out": f(g["diff_w_out"][0]), "diff_BT": f(BT), "diff_bfar": f(bfar), "diff_mask": dmask,
        "fox_w_in": f(g["fox_w_in"][0]), "fox_bf": f(g["fox_b_f"][0]), "fox_w_out": f(g["fox_w_out"][0]),
        "peer_wq": f(g["peer_w_q"]),
        "peer_kT": f(g["peer_keys"].reshape(DEPTH, 16, 128, 128).transpose(0, 1, 3, 2)),
        "peer_uT": f(g["peer_u"].transpose(0, 2, 1)), "peer_v": f(g["peer_v"]),
        "ple_proj": f(g["ple_proj"]), "ple_gate": f(g["ple_gate"]),
    }
    T = nseq * S
    maps = []
    for c in range(ncores):
        d = dict(shared)
        d["x"] = f(g["x"][c * nseq:(c + 1) * nseq].reshape(T, 1024))
        d["p"] = f(g["p"][:, c * nseq:(c + 1) * nseq].reshape(DEPTH, T, 256))
        maps.append(d)
    return maps


def kernel(**inputs):
    nc, m = build_program()
    maps = host_inputs(inputs)
    res = run_bass_kernel_spmd(nc, maps, core_ids=list(range(NCORES)))
    outs = [np.asarray(r["out"]).reshape(NSEQ, SEQ, 1024) for r in res.results]
    return np.concatenate(outs, axis=0).astype(np.float32)


def peer_merged(self, h, gcol, w_q, keysT, uT, v):
    kb, gr, T = self.kb, self.gr, self.T
    nc = self.nc
    V = nc.vector
    UTb = self.dscr("UTb", [1024, 16384], BF16)
    Vb = self.dscr("Vb", [16384, 1024], BF16)
    Gd = self.dscr("Gd", [T, 16384], BF16)
    with kb.scope():
        kb.dma(gr.gcol[:, 0:8], gcol.rearrange("(c p) -> p c", p=128))
        st = [kb.sb("pst%d" % i, [128, 2048], F32) for i in range(3)]
        sb = [kb.sb("psb%d" % i, [128, 2048], BF16) for i in range(3)]
        n = 0
        for kc in range(8):
            for e0 in range(0, 16384, 2048):
                s_, b_ = st[n % 3], sb[n % 3]
                kb.dma(s_[:], uT[kc * 128:(kc + 1) * 128, e0:e0 + 2048])
                if n % 2 == 0:
                    kb.ts(b_[:], s_[:], gr.gcol[:, kc:kc + 1], None, op0=ALU.mult)
                else:
                    kb.act(b_[:], s_[:], AF.Copy, scale=gr.gcol[:, kc:kc + 1])
                kb.dma(UTb[kc * 128:(kc + 1) * 128, e0:e0 + 2048], b_[:], q="pool")
                n += 1
        for r0 in range(0, 16384, 256):
            s_, b_ = st[n % 3], sb[n % 3]
            kb.dma(s_[:].rearrange("p (c n) -> p c n", c=2), v[r0:r0 + 256, :].rearrange("(c p) n -> p c n", p=128))
            if n % 2 == 0:
                kb.cp(b_[:], s_[:], e="dve")
            else:
                kb.act(b_[:], s_[:], AF.Copy)
            kb.dma(Vb[r0:r0 + 256, :].rearrange("(c p) n -> p c n", p=128), b_[:].rearrange("p (c n) -> p c n", c=2), q="pool")
            n += 1
        kb.barrier()
    with kb.scope():
        Wv = load_weights(kb, gr, w_q, 1024, 2048, gcol)
        kT = kb.sb("keysT", [128, 16 * 128], BF16)
        with kb.scope():
            ktmp = kb.sb("ktmp", [128, 16 * 128], F32)
            kb.dma(ktmp[:].rearrange("p (c k) -> p c k", c=16), keysT.rearrange("c d k -> d c k"))
            kb.cp(kT[:], ktmp[:])
            kb.barrier()
        kTv = kT[:].rearrange("p (c k) -> p c k", c=16)
        qT = [kb.sb("pqT%d" % i, [128, 512], BF16) for i in range(2)]
        psc = kb.ps("psc", [128, 512], F32)
        sc_all = kb.sb("sc_all", [128, 4 * 16 * 128], F32)
        scv = sc_all[:].rearrange("p (s c k) -> p s c k", s=4, c=16)
        a16 = kb.sb("a16", [128, 16], F32)
        b16 = kb.sb("b16", [128, 16], F32)
        c16 = kb.sb("c16", [128, 16], F32)
        e16 = kb.sb("e16", [128, 16], F32)
        t128 = kb.sb("t128", [128, 128], F32)
        cand = kb.sb("cand", [128, 256], F32)
        cand2 = kb.sb("cand2", [128, 256], F32)
        tau = kb.sb("tau", [128, 8], F32)
        nb = kb.sb("nb", [128, 8], F32)
        zz = kb.sb("zz", [128, 1], F32)
        Sc = [kb.sb("Sc%d" % i, [128, 2048], F32) for i in range(2)]
        Ec = [kb.sb("Ec%d" % i, [128, 2048], BF16) for i in range(2)]
        Gb = [kb.sb("Gb%d" % i, [128, 2048], BF16) for i in range(2)]
        ut = [kb.sb("ut%d" % i, [128, 8 * 512], BF16) for i in range(2)]
        vt = [kb.sb("vt%d" % i, [128, 4 * 1024], BF16) for i in range(2)]
        gt = [kb.sb("gt%d" % i, [128, 512], BF16) for i in range(2)]
        gel = [kb.sb("gel%d" % i, [128, 512], F32) for i in range(1)]
        gh = [kb.sb("gh%d" % i, [128, 512], BF16) for i in range(2)]
        ghT = [kb.sb("ghT%d" % i, [128, 512], BF16) for i in range(1)]
        acc = [kb.sb("acc%d" % i, [128, 1024], F32) for i in range(4)]
        hst = gr.xin
        php = kb.ps("php", [128, 512], F32)
        ptp = kb.ps("ptp", [128, 1024], BF16)
        pop = [kb.ps("pop%d" % i, [128, 512], F32) for i in range(2)]
        state = dict(pending=None, nq=0, nd=0, gchunk=0)

        def top16(dst, src, tmp):
            kb.op("dve", lambda: V.max(out=dst[:, 0:8], in_=src), [src], [dst])
            kb.op("dve", lambda: V.match_replace(out=tmp, in_to_replace=dst[:, 0:8], in_values=src, imm_value=-1e30),
                  [dst, src], [tmp])
            kb.op("dve", lambda: V.max(out=dst[:, 8:16], in_=tmp), [tmp], [dst])

        def gates_gen(blk, tokb, sub):
            gkey = ("Gd", blk)
            for hh in range(8):
                s1 = scv[:, sub, 2 * hh, :]
                s2 = scv[:, sub, 2 * hh + 1, :]
                top16(a16, s1, t128[:])
                top16(b16, s2, t128[:])
                kb.tt(cand[:].rearrange("p (a b) -> p a b", a=16),
                      a16[:].unsqueeze(2).to_broadcast([128, 16, 16]),
                      b16[:].unsqueeze(1).to_broadcast([128, 16, 16]), ALU.add)
                top16(c16, cand[:], cand2[:])
                kb.cp(tau[:, hh:hh + 1], c16[:, 15:16])
                kb.ts(nb[:, hh:hh + 1], c16[:, 0:1], -1.0, None, op0=ALU.mult)
                kb.act(e16[:], c16[:], AF.Exp, bias=nb[:, hh:hh + 1])
                kb.red(zz[:], e16[:])
                kb.act(zz[:], zz[:], AF.Ln)
                kb.tt(nb[:, hh:hh + 1], nb[:, hh:hh + 1], zz[:], ALU.subtract)
                yield
            items = [(c, hh) for c in range(8) for hh in range(8)]

            def stage1(n):
                c, hh = items[n]
                S_, E_ = Sc[n % 2], Ec[n % 2]
                s1 = scv[:, sub, 2 * hh, 16 * c:16 * c + 16]
                s2 = scv[:, sub, 2 * hh + 1, :]
                S3 = S_[:].rearrange("p (a b) -> p a b", a=16)
                kb.tt(S3, s1.unsqueeze(2).to_broadcast([128, 16, 128]),
                      s2.unsqueeze(1).to_broadcast([128, 16, 128]), ALU.add)
                kb.act(E_[:], S_[:], AF.Exp, bias=nb[:, hh:hh + 1])

            def stage2(n):
                c, hh = items[n]
                S_, E_ = Sc[n % 2], Ec[n % 2]
                G = Gb[(state["gchunk"] + c) % 2]
                dst = G if hh == 0 else E_
                kb.stt(dst[:], S_[:], tau[:, hh:hh + 1], E_[:], ALU.is_ge, ALU.mult)
                if hh > 0:
                    kb.tt(G[:], G[:], E_[:], ALU.add)
                if hh == 7:
                    prev = state.get("gstore")
                    if prev is not None:
                        kb.dma(prev[0], prev[1], q="act", wk=prev[2])
                    state["gstore"] = (Gd[tokb:tokb + 128, c * 2048:(c + 1) * 2048], G[:], gkey)

            stage1(0)
            for n in range(64):
                if n + 1 < 64:
                    stage1(n + 1)
                stage2(n)
                yield
            state["gchunk"] += 8
            if sub == 3:
                prev = state.get("gstore")
                kb.dma(prev[0], prev[1], q="sp", wk=prev[2])
                state["gstore"] = None

        def dense_gen(blk, xTv):
            tokb = blk * 512
            gkey = ("Gd", blk)
            for ec in range(32):
                u_ = ut[ec % 2]
                v_ = vt[ec % 2]
                uv = u_[:].rearrange("p (c e) -> p c e", c=8)
                vv = v_[:].rearrange("p (c n) -> p c n", c=4)
                kb.dma(uv, UTb[:, ec * 512:(ec + 1) * 512].rearrange("(c p) e -> p c e", p=128))
                kb.dma(vv, Vb[ec * 512:(ec + 1) * 512, :].rearrange("(c p) n -> p c n", p=128))
                for sub in range(4):
                    k = state["nd"]
                    state["nd"] += 1
                    g_ = gt[k % 2]
                    kb.dma(g_[:], Gd[tokb + sub * 128:tokb + sub * 128 + 128, ec * 512:(ec + 1) * 512], rk=gkey)
                    for kc in range(8):
                        kb.mm(php[:], xTv[:, kc, sub * 128:(sub + 1) * 128], uv[:, kc, :], start=(kc == 0), stop=(kc == 7))
                    yield
                    ge = gel[0]
                    kb.act(ge[:], php[:], AF.Gelu)
                    gh_ = gh[k % 2]
                    kb.tt(gh_[:], ge[:], g_[:], ALU.mult, e="pool")
                    yield
                    for c4 in range(4):
                        kb.tr(ptp[:, c4 * 128:(c4 + 1) * 128], gh_[:, c4 * 128:(c4 + 1) * 128], gr.ident[:])
                    if state.get("padd") is not None:
                        state["padd"]()
                        state["padd"] = None
                    yield
                    gT = ghT[0]
                    kb.act(gT[:], ptp[:, 0:512], AF.Copy)
                    pop4 = [pop[0], pop[1], gr.pm[0], gr.pm[1]]
                    pos = []
                    for nh in range(2):
                        po = pop4[(k % 2) * 2 + nh]
                        pos.append(po)
                        for c4 in range(4):
                            kb.mm(po[:], gT[:, c4 * 128:(c4 + 1) * 128], vv[:, c4, nh * 512:(nh + 1) * 512],
                                  start=(c4 == 0), stop=(c4 == 3))

                    def do_add(pos=pos, sub=sub, ec=ec):
                        for nh in range(2):
                            a_ = acc[sub][:, nh * 512:(nh + 1) * 512]
                            if ec == 0:
                                kb.act(a_, pos[nh][:], AF.Copy)
                            else:
                                kb.tt(a_, pos[nh][:], a_, ALU.add)
                    state["padd"] = do_add
                    yield
            if state.get("padd") is not None:
                state["padd"]()
                state["padd"] = None
            for sub in range(4):
                t0 = tokb + sub * 128
                hs = hst[sub % 3]
                key = ("h", t0)
                kb.dma(hs[:], h[t0:t0 + 128, :], rk=key)
                kb.tt(hs[:], acc[sub][:], hs[:], ALU.add, e="pool")
                kb.dma(h[t0:t0 + 128, :], hs[:], q="pool", wk=key)
            yield

        def run_interleaved(gg, dg, ng=2, nd=4):
            alive_g, alive_d = gg is not None, dg is not None
            while alive_g or alive_d:
                if alive_g:
                    for _ in range(ng):
                        try:
                            next(gg)
                        except StopIteration:
                            alive_g = False
                            break
                if alive_d:
                    for _ in range(nd):
                        try:
                            next(dg)
                        except StopIteration:
                            alive_d = False
                            break

        def q_epi(blk, tok0, c0, pss):
            qt = qT[c0 % 2]
            kb.act(qt[:], pss[0][:], AF.Copy)
            for sub in range(4):
                kb.mm(psc[:, sub * 128:(sub + 1) * 128], qt[:, sub * 128:(sub + 1) * 128], kTv[:, c0, :])
            kb.act(scv[:, :, c0, :], psc[:].rearrange("p (s k) -> p s k", s=4), AF.Copy)

        def chain(blk):
            for sub in range(4):
                for _ in gates_gen(blk, blk * 512 + sub * 128, sub):
                    yield

        def merged(blk, xTv):
            pend = state["pending"]
            dg = dense_gen(*pend) if pend is not None else None
            run_interleaved(chain(blk), dg)
            state["pending"] = (blk, xTv)

        segs = [dict(kind="FM", n0=0, n1=2048, epi=q_epi), dict(kind="custom", fn=merged)]
        gemm_stage(kb, gr, h, T, 1024, Wv, segs, norm=True, n_psT=1, n_pm=2)
        run_interleaved(None, dense_gen(*state["pending"]))
        kb.barrier()


Model.peer = peer_merged
```
